# Optimizing a Trainium2 kernel written in Bass

```python
import jax, jax.numpy as jnp
from jax import lax
import numpy as np

D_MODEL = 1024
BATCH = 8
SEQ = 2048
DEPTH = 1

RWKV_HEADS = 8
RWKV_HEAD_DIM = 64
RWKV_WIDTH = RWKV_HEADS * RWKV_HEAD_DIM
RET_HEADS = 4
RET_HEAD_DIM = 128
RET_WIDTH = RET_HEADS * RET_HEAD_DIM
MIX_WIDTH = RWKV_WIDTH + RET_WIDTH
DECAY_LORA = 64
AAA_LORA = 64
GATE_LORA = 128
IN_COLS = 3 * RWKV_WIDTH + 4 * RET_WIDTH
IN_SPLITS = (RWKV_WIDTH, 2 * RWKV_WIDTH, 3 * RWKV_WIDTH,
             3 * RWKV_WIDTH + RET_WIDTH, 3 * RWKV_WIDTH + 2 * RET_WIDTH,
             3 * RWKV_WIDTH + 3 * RET_WIDTH)
RET_CHUNK = 128
ROPE_BASE = 10000.0
D_FF = 2816
CONV_WIDTH = 3
NORM_EPS = 1e-6
RWKV_GN_EPS = 64e-5
RET_GN_EPS = 1e-5

kernel_name = "hymba_rwkv7_retnet_convglu"


def rms_norm(x, g):
    xf = x.astype(jnp.float32)
    y = xf * lax.rsqrt(jnp.mean(xf * xf, axis=-1, keepdims=True) + NORM_EPS)
    return (y * g.astype(jnp.float32)).astype(x.dtype)


def token_shift(x):
    return jnp.pad(x, ((0, 0), (1, 0), (0, 0)))[:, :-1, :]


def head_norm(y, eps):
    mu = jnp.mean(y, axis=-1, keepdims=True)
    var = jnp.mean(jnp.square(y - mu), axis=-1, keepdims=True)
    yn = (y - mu) * lax.rsqrt(var + eps)
    return yn.reshape(y.shape[0], y.shape[1], -1)


def rotary(x, positions):
    half = x.shape[-1] // 2
    inv_freq = ROPE_BASE ** (-jnp.arange(half, dtype=jnp.float32) / half)
    ang = positions.astype(jnp.float32)[:, None] * inv_freq[None, :]
    cos = jnp.cos(ang)[None, :, None, :]
    sin = jnp.sin(ang)[None, :, None, :]
    x1, x2 = x[..., :half], x[..., half:]
    return jnp.concatenate([x1 * cos - x2 * sin, x1 * sin + x2 * cos], axis=-1)


def rwkv7_scan(r, w, k, v, a, b):
    Bsz, T, H, D = r.shape
    seq_first = lambda z: jnp.swapaxes(z, 0, 1)

    def step(S, inp):
        r_t, w_t, k_t, v_t, a_t, b_t = inp
        sa = jnp.einsum('bhij,bhj->bhi', S, a_t)
        S = (S * w_t[:, :, None, :] + sa[..., None] * b_t[:, :, None, :]
             + v_t[..., None] * k_t[:, :, None, :])
        y = jnp.einsum('bhij,bhj->bhi', S, r_t)
        return S, y

    S0 = jnp.zeros((Bsz, H, D, D), jnp.float32)
    _, ys = lax.scan(step, S0, tuple(seq_first(z) for z in (r, w, k, v, a, b)))
    return seq_first(ys)


def rwkv7_group(xn, p_r, p_k, p_v, mu_r, mu_k, mu_v, mu_w, mu_a, mu_g,
                w0, w1, w2, a0, a1, a2, g1, g2, k_k, k_a, r_k, lnx_w, lnx_b):
    Bsz, T, _ = xn.shape
    f32 = jnp.float32
    heads = lambda z: z.reshape(Bsz, T, RWKV_HEADS, RWKV_HEAD_DIM)
    dx = token_shift(xn) - xn
    xw = xn + dx * mu_w
    xa = xn + dx * mu_a
    xg = xn + dx * mu_g
    r = (p_r + (token_shift(p_r) - p_r) * mu_r).astype(f32)
    k = (p_k + (token_shift(p_k) - p_k) * mu_k).astype(f32)
    v = (p_v + (token_shift(p_v) - p_v) * mu_v).astype(f32)
    w_log = -jax.nn.softplus(-(w0 + jnp.tanh(xw @ w1) @ w2).astype(f32)) - 0.5
    decay = jnp.exp(-jnp.exp(w_log))
    a = jax.nn.sigmoid((a0 + (xa @ a1) @ a2).astype(f32))
    g = (jax.nn.sigmoid(xg @ g1) @ g2).astype(f32)
    kk = heads(k * k_k.astype(f32))
    kk = kk / jnp.maximum(jnp.linalg.norm(kk, axis=-1, keepdims=True), 1e-12)
    k = k * (1.0 + (a - 1.0) * k_a.astype(f32))
    rh, kh, vh, ah = heads(r), heads(k), heads(v), heads(a)
    y = rwkv7_scan(rh, heads(decay), kh, vh, -kk, kk * ah)
    y = head_norm(y, RWKV_GN_EPS) * lnx_w.astype(f32) + lnx_b.astype(f32)
    bonus = jnp.sum(rh * kh * r_k.astype(f32), axis=-1, keepdims=True) * vh
    y = (y + bonus.reshape(Bsz, T, RWKV_WIDTH)) * g
    return y


def retention_group(q_p, k_p, v_p, g_p, gn_w):
    Bsz, T, _ = q_p.shape
    f32 = jnp.float32
    H, D, C = RET_HEADS, RET_HEAD_DIM, RET_CHUNK
    n = T // C
    pos = jnp.arange(T)
    q = rotary(q_p.astype(f32).reshape(Bsz, T, H, D), pos)
    k = rotary(k_p.astype(f32).reshape(Bsz, T, H, D), pos) * (D ** -0.5)
    v = v_p.astype(f32).reshape(Bsz, T, H, D)
    log_gamma = jnp.log(1.0 - 2.0 ** (-5.0 - jnp.arange(H, dtype=f32)))
    qc = q.reshape(Bsz, n, C, H, D)
    kc = k.reshape(Bsz, n, C, H, D)
    vc = v.reshape(Bsz, n, C, H, D)
    idx = jnp.arange(C, dtype=f32)
    diff = idx[:, None] - idx[None, :]
    dmask = jnp.where(diff[None] >= 0,
                      jnp.exp(jnp.maximum(diff, 0.0)[None] * log_gamma[:, None, None]), 0.0)
    scores = jnp.einsum('bnihd,bnjhd->bnhij', qc, kc) * dmask
    intra = jnp.einsum('bnhij,bnjhe->bnihe', scores, vc)
    zeta = jnp.exp((C - 1.0 - idx)[None, :] * log_gamma[:, None])
    kv = jnp.einsum('bnjhd,hj,bnjhe->bnhde', kc, zeta, vc)
    gamma_c = jnp.exp(C * log_gamma)[None, :, None, None]

    def step(R, kv_i):
        return R * gamma_c + kv_i, R

    R0 = jnp.zeros((Bsz, H, D, D), f32)
    _, R_prev = lax.scan(step, R0, jnp.swapaxes(kv, 0, 1))
    R_prev = jnp.swapaxes(R_prev, 0, 1)
    xi = jnp.exp((idx + 1.0)[None, :] * log_gamma[:, None])
    inter = jnp.einsum('bnihd,bnhde,hi->bnihe', qc, R_prev, xi)
    y = (intra + inter).reshape(Bsz, T, H, D)
    y = head_norm(y, RET_GN_EPS) * gn_w.astype(f32)
    return jax.nn.silu(g_p.astype(f32)) * y


def hybrid_mixer(xn, w_in, mu_r, mu_k, mu_v, mu_w, mu_a, mu_g, w0, w1, w2,
                 a0, a1, a2, g1, g2, k_k, k_a, r_k, lnx_w, lnx_b, ret_gn_w, w_out):
    proj = xn @ w_in
    p_r, p_k, p_v, q_ret, k_ret, v_ret, g_ret = jnp.split(proj, IN_SPLITS, axis=-1)
    y_rwkv = rwkv7_group(xn, p_r, p_k, p_v, mu_r, mu_k, mu_v, mu_w, mu_a, mu_g,
                         w0, w1, w2, a0, a1, a2, g1, g2, k_k, k_a, r_k, lnx_w, lnx_b)
    y_ret = retention_group(q_ret, k_ret, v_ret, g_ret, ret_gn_w)
    y = jnp.concatenate([y_rwkv, y_ret], axis=-1).astype(xn.dtype)
    return y @ w_out


def conv_glu(xn, w_gate, w_up, conv_w, conv_b, w_down):
    gate = xn @ w_gate
    up = xn @ w_up
    gate = lax.conv_general_dilated(
        gate, conv_w, window_strides=(1,), padding=[(CONV_WIDTH - 1, 0)],
        dimension_numbers=('NWC', 'WIO', 'NWC'), feature_group_count=D_FF) + conv_b
    return (jax.nn.silu(gate) * up) @ w_down


def setup_inputs(seed: int = 0) -> dict:
    key = jax.random.key(seed)
    ks = iter(jax.random.split(key, 40))
    L, D = DEPTH, D_MODEL
    nrm = lambda shape, s: jax.random.normal(next(ks), shape, jnp.float32) * s
    uni = lambda shape, lo, hi: jax.random.uniform(next(ks), shape, jnp.float32, lo, hi)
    gain = lambda shape: 1.0 + nrm(shape, 0.02)
    return {
        "x": nrm((BATCH, SEQ, D), 1.0),
        "norm_mix_g": gain((L, D)),
        "w_in": nrm((L, D, IN_COLS), D ** -0.5),
        "rwkv_mu_r": uni((L, RWKV_WIDTH), 0.0, 1.0),
        "rwkv_mu_k": uni((L, RWKV_WIDTH), 0.0, 1.0),
        "rwkv_mu_v": uni((L, RWKV_WIDTH), 0.0, 1.0),
        "rwkv_mu_w": uni((L, D), 0.0, 1.0),
        "rwkv_mu_a": uni((L, D), 0.0, 1.0),
        "rwkv_mu_g": uni((L, D), 0.0, 1.0),
        "rwkv_w0": uni((L, RWKV_WIDTH), -6.0, -1.0),
        "rwkv_w1": nrm((L, D, DECAY_LORA), D ** -0.5),
        "rwkv_w2": nrm((L, DECAY_LORA, RWKV_WIDTH), 0.5 * DECAY_LORA ** -0.5),
        "rwkv_a0": nrm((L, RWKV_WIDTH), 0.1),
        "rwkv_a1": nrm((L, D, AAA_LORA), D ** -0.5),
        "rwkv_a2": nrm((L, AAA_LORA, RWKV_WIDTH), 0.5 * AAA_LORA ** -0.5),
        "rwkv_g1": nrm((L, D, GATE_LORA), D ** -0.5),
        "rwkv_g2": nrm((L, GATE_LORA, RWKV_WIDTH), GATE_LORA ** -0.5),
        "rwkv_k_k": 0.85 + nrm((L, RWKV_WIDTH), 0.05),
        "rwkv_k_a": 1.0 + nrm((L, RWKV_WIDTH), 0.05),
        "rwkv_r_k": nrm((L, RWKV_HEADS, RWKV_HEAD_DIM), 0.1),
        "rwkv_lnx_w": gain((L, RWKV_WIDTH)),
        "rwkv_lnx_b": nrm((L, RWKV_WIDTH), 0.02),
        "ret_gn_w": gain((L, RET_WIDTH)),
        "w_out": nrm((L, MIX_WIDTH, D), MIX_WIDTH ** -0.5),
        "norm_ffn_g": gain((L, D)),
        "ffn_w_gate": nrm((L, D, D_FF), D ** -0.5),
        "ffn_w_up": nrm((L, D, D_FF), D ** -0.5),
        "ffn_conv_w": nrm((L, CONV_WIDTH, 1, D_FF), CONV_WIDTH ** -0.5),
        "ffn_conv_b": nrm((L, D_FF), 0.02),
        "ffn_w_down": nrm((L, D_FF, D), D_FF ** -0.5),
        "norm_final_g": gain((D,)),
    }


def reference(x, norm_mix_g, w_in, rwkv_mu_r, rwkv_mu_k, rwkv_mu_v, rwkv_mu_w,
              rwkv_mu_a, rwkv_mu_g, rwkv_w0, rwkv_w1, rwkv_w2, rwkv_a0, rwkv_a1,
              rwkv_a2, rwkv_g1, rwkv_g2, rwkv_k_k, rwkv_k_a, rwkv_r_k, rwkv_lnx_w,
              rwkv_lnx_b, ret_gn_w, w_out, norm_ffn_g, ffn_w_gate, ffn_w_up,
              ffn_conv_w, ffn_conv_b, ffn_w_down, norm_final_g):
    for l in range(DEPTH):
        h = rms_norm(x, norm_mix_g[l])
        x = x + hybrid_mixer(h, w_in[l], rwkv_mu_r[l], rwkv_mu_k[l], rwkv_mu_v[l],
                             rwkv_mu_w[l], rwkv_mu_a[l], rwkv_mu_g[l], rwkv_w0[l],
                             rwkv_w1[l], rwkv_w2[l], rwkv_a0[l], rwkv_a1[l], rwkv_a2[l],
                             rwkv_g1[l], rwkv_g2[l], rwkv_k_k[l], rwkv_k_a[l], rwkv_r_k[l],
                             rwkv_lnx_w[l], rwkv_lnx_b[l], ret_gn_w[l], w_out[l])
        h = rms_norm(x, norm_ffn_g[l])
        x = x + conv_glu(h, ffn_w_gate[l], ffn_w_up[l], ffn_conv_w[l], ffn_conv_b[l],
                         ffn_w_down[l])
    return rms_norm(x, norm_final_g)
```

```python
import numpy as np
import ml_dtypes
from contextlib import ExitStack
import concourse.bass as bass
import concourse.mybir as mybir
from concourse.bass_utils import run_bass_kernel_spmd

F32 = mybir.dt.float32
BF16 = mybir.dt.bfloat16
AF = mybir.ActivationFunctionType
ALU = mybir.AluOpType
AX = mybir.AxisListType

QUEUES = ('sp', 'act', 'pool', 'pe', 'dve')

T = 2048
D = 1024
NT = 16
DFF = 2816
NFF = 22
C0 = float(np.exp(-0.5))
NORM_EPS = 1e-6
RWKV_GN_EPS = 64e-5
RET_GN_EPS = 1e-5


class Buf:
    __slots__ = ('name', 'w', 'r')

    def __init__(self, name=''):
        self.name = name
        self.w = None
        self.r = {}


class _Op:
    __slots__ = ('q', 's', 'idx', 'fn', 'waits', 'inc', 'dma')


class Sched:
    def __init__(self, nc):
        self.nc = nc
        self.ops = {q: [] for q in QUEUES}
        self.streams = {}
        self.clock = {q: {} for q in QUEUES}
        self.opclock = {}
        self.nwaits = 0
        self.nops = 0

    def op(self, q, fn, reads=(), writes=(), dma=False):
        s = ('dq_' + q) if dma else q
        deps = {}

        def need(st, i):
            if deps.get(st, 0) < i:
                deps[st] = i
        for b in reads:
            if b.w is not None:
                st, i = b.w
                if st == q and q == 'pe':
                    continue
                need(st, i)
        for b in writes:
            if b.w is not None:
                st, i = b.w
                if not (st == q and not dma):
                    need(st, i)
            for st, i in b.r.items():
                if st == q and not dma:
                    continue
                need(st, i)
        ck = self.clock[q]
        waits = []
        for st, i in deps.items():
            if ck.get(st, 0) >= i:
                continue
            waits.append((st, i))
            oc = self.opclock[(st, i)]
            for k, v in oc.items():
                if ck.get(k, 0) < v:
                    ck[k] = v
            if ck.get(st, 0) < i:
                ck[st] = i
            self.streams[st][i - 1].inc = True
        o = _Op()
        o.q = q
        o.s = s
        o.fn = fn
        o.waits = waits
        o.inc = dma
        o.dma = dma
        lst = self.streams.setdefault(s, [])
        lst.append(o)
        o.idx = len(lst)
        self.opclock[(s, o.idx)] = dict(ck)
        self.ops[q].append(o)
        self.nwaits += len(waits)
        self.nops += 1
        for b in writes:
            b.w = (s, o.idx)
            b.r = {}
        for b in reads:
            if b.r.get(s, 0) < o.idx:
                b.r[s] = o.idx
        return o

    def barrier(self, queues=QUEUES):
        tips = {s: len(l) for s, l in self.streams.items() if l}
        for q in queues:
            ck = self.clock[q]
            waits = []
            for s, i in tips.items():
                if s == q and q == 'pe':
                    continue
                if ck.get(s, 0) >= i:
                    continue
                waits.append((s, i))
                self.streams[s][i - 1].inc = True
            for s, i in waits:
                oc = self.opclock[(s, i)]
                for k, v in oc.items():
                    if ck.get(k, 0) < v:
                        ck[k] = v
                ck[s] = i
            if waits:
                o = _Op()
                o.q = q
                o.s = None
                o.fn = None
                o.waits = waits
                o.inc = False
                o.dma = False
                self.ops[q].append(o)

    def emit(self, stack):
        nc = self.nc
        sems = {s: stack.enter_context(nc.semaphore('sem_' + s)) for s in self.streams}
        cnt = {}
        for s, lst in self.streams.items():
            c = 0
            for o in lst:
                if o.dma:
                    c += 16
                elif o.inc:
                    c += 1
                cnt[(s, o.idx)] = c
        self.final_counts = {s: (cnt[(s, len(l))] if l else 0) for s, l in self.streams.items()}
        block = stack.enter_context(nc.Block())

        def run(q, eng):
            for o in self.ops[q]:
                for st, i in o.waits:
                    eng.wait_ge(sems[st], cnt[(st, i)])
                if o.fn is None:
                    continue
                ins = o.fn(eng)
                if o.dma:
                    ins.then_inc(sems[o.s], 16)
                elif o.inc:
                    ins.then_inc(sems[o.s], 1)

        @block.sync
        def _(e):
            run('sp', e)

        @block.scalar
        def _(e):
            run('act', e)

        @block.gpsimd
        def _(e):
            run('pool', e)

        @block.tensor
        def _(e):
            run('pe', e)

        @block.vector
        def _(e):
            run('dve', e)


class Arena:
    def __init__(self, ap, nbytes):
        self.ap = ap
        self.nbytes = nbytes
        self.off = 0
        self.peak = 0

    def take(self, shape, dt):
        esz = 4 if dt == F32 else 2
        n = int(np.prod(shape))
        nb = (n * esz + 63) // 64 * 64
        assert self.off + nb <= self.nbytes, ("arena overflow", self.off, nb, self.nbytes)
        v = self.ap[:, self.off // 4:(self.off + nb) // 4]
        if dt != F32:
            v = v.bitcast(dt)
        v = v[:, 0:n]
        if len(shape) == 2:
            v = v.rearrange("p (a b) -> p a b", a=shape[0])
        elif len(shape) == 3:
            v = v.rearrange("p (a b c) -> p a b c", a=shape[0], b=shape[1])
        elif len(shape) == 4:
            v = v.rearrange("p (a b c d) -> p a b c d", a=shape[0], b=shape[1], c=shape[2])
        self.off += nb
        self.peak = max(self.peak, self.off)
        return v

    def mark(self):
        return self.off

    def reset(self, m):
        self.off = m


V_MUW, V_MUA, V_MUG = 0, 8, 16
V_PAIR = 24
V_FFN = 56
NV = 56 + 4 * NFF
CB_ID, CB_MRET, CB_M4, CB_ML, CB_SEL, CB_ONES = 0, 128, 256, 768, 896, 900
NCB = 1028
CF_COS, CF_SIN, CF_NSIN, CF_XIT, CF_KAT, CF_KAPG, CF_GC = 0, 1024, 2048, 3072, 3584, 4096, 4100
NCF = 4104


def make_consts():
    f32 = np.float32
    p = np.arange(128)
    cf = np.zeros((128, NCF), f32)
    half = 64
    inv_freq = (10000.0 ** (-np.arange(half, dtype=np.float64) / half))
    pos = (np.arange(NT)[None, :] * 128 + p[:, None]).astype(np.float64)
    ang = pos[:, :, None] * inv_freq[None, None, :]
    cf[:, CF_COS:CF_COS + 1024] = np.cos(ang).reshape(128, -1)
    cf[:, CF_SIN:CF_SIN + 1024] = np.sin(ang).reshape(128, -1)
    cf[:, CF_NSIN:CF_NSIN + 1024] = -np.sin(ang).reshape(128, -1)
    lg = np.log(1.0 - 2.0 ** (-5.0 - np.arange(4, dtype=np.float64)))
    i = np.arange(128, dtype=np.float64)
    xi = np.exp((i[None, :] + 1.0) * lg[:, None])
    ka = np.exp(-(i[None, :] + 1.0) * lg[:, None]) * (128.0 ** -0.5)
    cf[:, CF_XIT:CF_XIT + 512] = np.broadcast_to(xi.reshape(1, 512), (128, 512))
    cf[:, CF_KAT:CF_KAT + 512] = np.broadcast_to(ka.reshape(1, 512), (128, 512))
    gC = np.exp(128.0 * lg)
    cf[:, CF_KAPG:CF_KAPG + 4] = (ka.T * gC[None, :])
    cf[:, CF_GC:CF_GC + 4] = gC[None, :]
    cb = np.zeros((128, NCB), f32)
    cb[:, CB_ID:CB_ID + 128] = np.eye(128)
    r = p[:, None]
    c = p[None, :]
    cb[:, CB_MRET:CB_MRET + 128] = (r <= c)
    strict = (r < c).astype(f32)
    incl = (r <= c).astype(f32)
    cb[:, CB_M4:CB_M4 + 512] = np.concatenate([strict, incl, strict, incl], axis=1)
    cb[:, CB_ML:CB_ML + 128] = (c < r)
    cb[0:64, CB_SEL] = 1.0
    cb[64:128, CB_SEL + 1] = 1.0
    cb[0:64, CB_ONES:CB_ONES + 64] = 1.0
    cb[64:128, CB_ONES + 64:CB_ONES + 128] = 1.0
    return cf, cb.astype(ml_dtypes.bfloat16)


DBG = {'ret_chunks': NT, 'ret_steps': 99}


def build_program(taps=None, phases=(1, 2, 3, 4, 5)):
    nc = bass.Bass("TRN2", target_bir_lowering=False)

    def din(name, shape, dt=F32):
        return nc.dram_tensor(name, list(shape), dt, kind="ExternalInput").ap()
    x = din("x", [T, D])
    w_in = din("w_in", [D, 3584])
    w_out = din("w_out", [D, D])
    wg_d = din("ffn_w_gate", [D, DFF])
    wu_d = din("ffn_w_up", [D, DFF])
    wd_d = din("ffn_w_down", [DFF, D])
    w1_d = din("rwkv_w1", [D, 64])
    a1_d = din("rwkv_a1", [D, 64])
    g1_d = din("rwkv_g1", [D, 128])
    w2_d = din("rwkv_w2", [64, 512])
    a2_d = din("rwkv_a2", [64, 512])
    g2_d = din("rwkv_g2", [128, 512])
    vecs_d = din("vecs", [128, NV])
    bct_d = din("bct", [128, 4608])
    cf_d = din("cf", [128, NCF])
    cb_d = din("cb", [128, NCB], BF16)
    out = nc.dram_tensor("out", [T, D], F32, kind="ExternalOutput").ap()
    tap_out = {}
    taps = taps or {}

    S = Sched(nc)
    st = ExitStack()
    ARENA_BYTES = 200 * 1024
    arena_t = st.enter_context(nc.sbuf_tensor("arena", [128, ARENA_BYTES // 4], F32))
    A = Arena(arena_t[:], ARENA_BYTES)
    pp = [st.enter_context(nc.psum_tensor(f"pp{i}", [128, 1024], F32)) for i in range(4)]
    bank = [pp[i // 2][:, (i % 2) * 512:(i % 2) * 512 + 512] for i in range(8)]
    bankB = [Buf(f"bank{i}") for i in range(8)]

    def bankbf(i):
        return bank[i].bitcast(BF16)

    def act(out_, in_, func, r, w, bias=None, scale=None, accum=None):
        kw = {}
        if bias is not None:
            kw['bias'] = bias
        if scale is not None:
            kw['scale'] = scale
        if accum is not None:
            kw['accum_out'] = accum
        S.op('act', lambda e: e.activation(out=out_, in_=in_, func=func, **kw), reads=r, writes=w)

    def tt(out_, a, b, op, r, w, q='dve'):
        S.op(q, lambda e: e.tensor_tensor(out=out_, in0=a, in1=b, op=op), reads=r, writes=w)

    def ts(out_, a, s1, s2, op0, op1, r, w, q='dve'):
        if s2 is None:
            S.op(q, lambda e: e.tensor_scalar(out=out_, in0=a, scalar1=s1, scalar2=None, op0=op0), reads=r, writes=w)
        else:
            S.op(q, lambda e: e.tensor_scalar(out=out_, in0=a, scalar1=s1, scalar2=s2, op0=op0, op1=op1), reads=r, writes=w)

    def stt(out_, a, s, b, op0, op1, r, w):
        S.op('dve', lambda e: e.scalar_tensor_tensor(out=out_, in0=a, scalar=s, in1=b, op0=op0, op1=op1), reads=r, writes=w)

    def mm(out_, lhsT, rhs, start, stop, r, w):
        S.op('pe', lambda e: e.matmul(out=out_, lhsT=lhsT, rhs=rhs, start=start, stop=stop), reads=r, writes=w)

    def mm2(out_, lhsT, rhs, start, stop, r, w):
        if lhsT.shape[0] == 128:
            mm(out_, lhsT[0:64], rhs[0:64], start, False, r, w)
            mm(out_, lhsT[64:128], rhs[64:128], False, stop, r, w)
        else:
            mm(out_, lhsT, rhs, start, stop, r, w)

    def dma(q, out_, in_, r, w, **kw):
        S.op(q, lambda e: e.dma_start(out=out_, in_=in_, **kw), reads=r, writes=w, dma=True)

    def cp(q, out_, in_, r, w):
        if q == 'act':
            act(out_, in_, AF.Copy, r, w)
        else:
            S.op(q, lambda e: e.tensor_copy(out=out_, in_=in_), reads=r, writes=w)

    def rsqrt_tiny(dst, src, scale, eps, r, w):
        ts(dst, src, scale, eps, ALU.mult, ALU.add, r, w)
        act(dst, dst, AF.Ln, w, w)
        act(dst, dst, AF.Exp, w, w, scale=-0.5)

    hT = A.take([8, T + 1], BF16)
    yT = A.take([8, T], BF16)
    cb = A.take([NCB], BF16)
    vecs = A.take([NV], F32)
    om = A.take([NV], F32)
    mhalf = A.take([4], F32)
    gtab = A.take([1024], F32)
    stat = A.take([64], F32)
    ss_all = A.take([3, NT], F32)
    rstd_all = A.take([3, NT], F32)
    B_const = Buf('const')
    B_gtab = Buf('gtab')
    hTb = [Buf(f'hT{n}') for n in range(NT)]
    yTb = [[Buf(f'yT{c}_{n}') for n in range(NT)] for c in range(8)]
    ident = cb[:, CB_ID:CB_ID + 128]
    PERSIST = A.mark()

    def tap(name, ap, shape, reads):
        if name in taps:
            d = nc.dram_tensor("tap_" + name, list(shape), ap.dtype, kind="ExternalOutput").ap()
            tap_out[name] = d
            dma('sp', d, ap, reads, [])

    dma('sp', cb, cb_d, [], [B_const])
    dma('sp', vecs, vecs_d, [], [B_const])
    dma('sp', gtab, bct_d[:, 0:1024], [], [B_gtab])
    S.op('pool', lambda e: e.memset(mhalf, -0.5), writes=[B_const])
    ts(om, vecs, -1.0, 1.0, ALU.mult, ALU.add, [B_const], [B_const])
    S.op('pool', lambda e: e.memset(hT[:, :, 0:1], 0.0), writes=[hTb[0]])

    xst = [A.take([D], F32) for _ in range(3)]
    xstB = [Buf(f'xst{i}') for i in range(3)]
    hb = [A.take([D], BF16) for _ in range(2)]
    hbB = [Buf(f'hb{i}') for i in range(2)]
    sqj = A.take([D], BF16)
    sqjB = Buf('sqj')
    statB = [Buf(f'stat{i}') for i in range(4)]
    NORM_END = A.mark()

    def norm_to_hT(n, src, srcB, which, pbank):
        ssn = ss_all[:, which, n:n + 1]
        rsn = rstd_all[:, which, n:n + 1]
        sB = statB[n % 4]
        act(sqj, src, AF.Square, [srcB], [sqjB, sB], accum=ssn)
        rsqrt_tiny(rsn, ssn, 1.0 / D, NORM_EPS, [sB], [sB])
        h = hb[n % 2]
        stt(h, src, rsn, gtab, ALU.mult, ALU.mult, [srcB, sB, B_gtab], [hbB[n % 2]])
        pt = bankbf(pbank).rearrange("p (c t) -> p c t", c=8)
        for c in range(8):
            S.op('pe', lambda e, c=c: e.transpose(out=pt[:, c, :], in_=h[:, c * 128:(c + 1) * 128], identity=ident),
                 reads=[hbB[n % 2], B_const], writes=[bankB[pbank]])
        cp('act', hT[:, :, 1 + n * 128:1 + (n + 1) * 128], pt, [bankB[pbank]], [hTb[n]])

    xv = x.rearrange("(n p) d -> n p d", p=128)
    ov = out.rearrange("(n p) d -> n p d", p=128)
    for n in range(NT):
        dma('sp', xst[n % 3], xv[n], [], [xstB[n % 3]])
        norm_to_hT(n, xst[n % 3], xstB[n % 3], 0, n % 2)
    tap('hT', hT, [128, 8, T + 1], hTb)

    if 2 in phases:
        A.reset(NORM_END)
        cf = A.take([NCF], F32)
        dma('sp', cf, cf_d, [], [B_const])
        wret = A.take([8, 2048], BF16)
        wretB = Buf('wret')
        wv = w_in.rearrange("(c p) n -> p c n", p=128)
        for c in range(8):
            dma('pool', wret[:, c, :], wv[:, c, 1536:3584], [], [wretB])
        gnw = A.take([512], F32)
        dma('sp', gnw, bct_d[:, 4096:4608], [], [B_const])
        qa = A.take([512], F32)
        qb = A.take([512], F32)
        qrot = A.take([512], BF16)
        krot = A.take([512], BF16)
        qT = A.take([4, 128], BF16)
        kT = A.take([4, 128], BF16)
        PT = A.take([4, 128], BF16)
        Vb = A.take([512], BF16)
        Vk = A.take([512], BF16)
        R = A.take([512], F32)
        Rt = A.take([512], F32)
        Rb = A.take([512], BF16)
        sqy = A.take([512], F32)
        yn = A.take([512], F32)
        sgt = A.take([512], F32)
        yo = A.take([512], BF16)
        rst = A.take([32], F32)
        Bq = {k: Buf('r_' + k) for k in ['qa', 'qb', 'qrot', 'krot', 'qT', 'kT', 'PT', 'Vb', 'Vk', 'R', 'Rt', 'Rb', 'sqy', 'yn', 'sgt', 'yo', 'rst']}
        kapg_bc = cf[:, CF_KAPG:CF_KAPG + 4].unsqueeze(2).to_broadcast([128, 4, 128])
        gC_bc = cf[:, CF_GC:CF_GC + 4].unsqueeze(2).to_broadcast([128, 4, 128])
        xiT = cf[:, CF_XIT:CF_XIT + 512].rearrange("p (h t) -> p h t", h=4)
        kaT = cf[:, CF_KAT:CF_KAT + 512].rearrange("p (h t) -> p h t", h=4)
        mret_bc = cb[:, CB_MRET:CB_MRET + 128].unsqueeze(1).to_broadcast([128, 4, 128])
        PQ, PK, PV, PG, PTB, PS, PY, PKV = range(8)

        def v4(ap):
            return ap.rearrange("p (h e) -> p h e", h=4)

        def rot(ps, psB, dst, dstB, n):
            cosb = cf[:, CF_COS + n * 64:CF_COS + (n + 1) * 64].unsqueeze(1).unsqueeze(1).to_broadcast([128, 4, 2, 64])
            sinb = cf[:, CF_SIN + n * 64:CF_SIN + (n + 1) * 64].unsqueeze(1).to_broadcast([128, 4, 64])
            nsinb = cf[:, CF_NSIN + n * 64:CF_NSIN + (n + 1) * 64].unsqueeze(1).to_broadcast([128, 4, 64])
            p4 = ps.rearrange("p (h two f) -> p h two f", h=4, two=2)
            tt(qa.rearrange("p (h two f) -> p h two f", h=4, two=2), p4, cosb, ALU.mult, [psB, B_const], [Bq['qa']])
            qb4 = qb.rearrange("p (h two f) -> p h two f", h=4, two=2)
            tt(qb4[:, :, 0, :], p4[:, :, 1, :], nsinb, ALU.mult, [psB, B_const], [Bq['qb']])
            tt(qb4[:, :, 1, :], p4[:, :, 0, :], sinb, ALU.mult, [psB, B_const], [Bq['qb']])
            tt(dst, qa, qb, ALU.add, [Bq['qa'], Bq['qb']], [dstB])

        for n in range(DBG['ret_chunks']):
            RS = DBG['ret_steps']
            tok = slice(1 + n * 128, 1 + (n + 1) * 128)
            for j, pb in enumerate((PQ, PK, PV, PG)):
                for c in range(8):
                    mm(bank[pb], hT[:, c, tok], wret[:, c, j * 512:(j + 1) * 512], c == 0, c == 7,
                       [hTb[n], wretB], [bankB[pb]])
            if RS < 2:
                continue
            rot(bank[PQ], bankB[PQ], qrot, Bq['qrot'], n)
            rot(bank[PK], bankB[PK], krot, Bq['krot'], n)
            if RS < 3:
                continue
            ptb = bankbf(PTB).rearrange("p (c t) -> p c t", c=8)
            for h in range(4):
                S.op('pe', lambda e, h=h: e.transpose(out=ptb[:, h, :], in_=qrot[:, h * 128:(h + 1) * 128], identity=ident),
                     reads=[Bq['qrot'], B_const], writes=[bankB[PTB]])
            for h in range(4):
                S.op('pe', lambda e, h=h: e.transpose(out=ptb[:, 4 + h, :], in_=krot[:, h * 128:(h + 1) * 128], identity=ident),
                     reads=[Bq['krot'], B_const], writes=[bankB[PTB]])
            tt(qT, ptb[:, 0:4, :], xiT, ALU.mult, [bankB[PTB], B_const], [Bq['qT']])
            tt(kT, ptb[:, 4:8, :], kaT, ALU.mult, [bankB[PTB], B_const], [Bq['kT']])
            if RS < 4:
                continue
            ps4 = v4(bank[PS])
            for h in range(4):
                mm(ps4[:, h, :], kT[:, h, :], qT[:, h, :], True, True, [Bq['kT'], Bq['qT']], [bankB[PS]])
            tt(PT, ps4, mret_bc, ALU.mult, [bankB[PS], B_const], [Bq['PT']])
            if RS < 5:
                continue
            cp('act', Vb, bank[PV], [bankB[PV]], [Bq['Vb']])
            tt(v4(Vk), v4(bank[PV]), kapg_bc, ALU.mult, [bankB[PV], B_const], [Bq['Vk']])
            if RS < 6:
                continue
            py4 = v4(bank[PY])
            for h in range(4):
                mm(py4[:, h, :], PT[:, h, :], Vb[:, h * 128:(h + 1) * 128], True, n == 0, [Bq['PT'], Bq['Vb']], [bankB[PY]])
                if n > 0:
                    mm(py4[:, h, :], qT[:, h, :], Rb[:, h * 128:(h + 1) * 128], False, True, [Bq['qT'], Bq['Rb']], [bankB[PY]])
            if RS < 7:
                continue
            if n < NT - DBG.get('skiplast', 0):
                pkv4 = v4(bank[PKV])
                for h in range(4):
                    mm(pkv4[:, h, :], krot[:, h * 128:(h + 1) * 128], Vk[:, h * 128:(h + 1) * 128], True, True,
                       [Bq['krot'], Bq['Vk']], [bankB[PKV]])
                if n == 0:
                    cp('dve', R, bank[PKV], [bankB[PKV]], [Bq['R']])
                else:
                    tt(v4(Rt), v4(R), gC_bc, ALU.mult, [Bq['R'], B_const], [Bq['Rt']])
                    tt(R, Rt, bank[PKV], ALU.add, [Bq['Rt'], bankB[PKV]], [Bq['R']])
                cp('pool', Rb, R, [Bq['R']], [Bq['Rb']])
            if RS < 8:
                continue
            s1 = rst[:, 0:4]
            s2 = rst[:, 4:8]
            mean = rst[:, 8:12]
            msq = rst[:, 12:16]
            rstd = rst[:, 16:20]
            S.op('dve', lambda e: e.tensor_reduce(out=s1, in_=py4, axis=AX.X, op=ALU.add), reads=[bankB[PY]], writes=[Bq['rst']])
            act(sqy, bank[PY], AF.Square, [bankB[PY]], [Bq['sqy']])
            S.op('dve', lambda e: e.tensor_reduce(out=s2, in_=v4(sqy), axis=AX.X, op=ALU.add), reads=[Bq['sqy']], writes=[Bq['rst']])
            ts(mean, s1, 1.0 / 128, None, ALU.mult, None, [Bq['rst']], [Bq['rst']])
            tt(msq, mean, mean, ALU.mult, [Bq['rst']], [Bq['rst']])
            stt(rstd, s2, 1.0 / 128, msq, ALU.mult, ALU.subtract, [Bq['rst']], [Bq['rst']])
            rsqrt_tiny(rstd, rstd, 1.0, RET_GN_EPS, [Bq['rst']], [Bq['rst']])
            tt(v4(yn), py4, mean.unsqueeze(2).to_broadcast([128, 4, 128]), ALU.subtract, [bankB[PY], Bq['rst']], [Bq['yn']])
            tt(v4(yn), v4(yn), rstd.unsqueeze(2).to_broadcast([128, 4, 128]), ALU.mult, [Bq['yn'], Bq['rst']], [Bq['yn']])
            tt(yn, yn, gnw, ALU.mult, [Bq['yn'], B_const], [Bq['yn']])
            act(sgt, bank[PG], AF.Silu, [bankB[PG]], [Bq['sgt']])
            tt(yo, yn, sgt, ALU.mult, [Bq['yn'], Bq['sgt']], [Bq['yo']])
            if RS < 9:
                continue
            for h in range(4):
                S.op('pe', lambda e, h=h: e.transpose(out=ptb[:, h, :], in_=yo[:, h * 128:(h + 1) * 128], identity=ident),
                     reads=[Bq['yo'], B_const], writes=[bankB[PTB]])
            cp('act', yT[:, 4:8, n * 128:(n + 1) * 128], ptb[:, 0:4, :], [bankB[PTB]], [yTb[4 + h][n] for h in range(4)])
        if 3 not in phases:
            tap('yT', yT, [128, 8, T], [b for l in yTb for b in l])
        S.barrier()


    if 3 in phases:
        S.barrier()
        A.reset(PERSIST)
        wl_f = A.take([8, 128], F32)
        gl_f = A.take([8, 128], F32)
        W1A = A.take([8, 128], BF16)
        W1B = A.take([8, 128], BF16)
        G1A = A.take([8, 128], BF16)
        G1B = A.take([8, 128], BF16)
        W2sb = A.take([512], BF16)
        A2sb = A.take([512], BF16)
        G2sb = A.take([512], BF16)
        L1 = A.take([T], BF16)
        L1g = A.take([T], BF16)
        lnxw = A.take([512], F32)
        lnxb = A.take([512], F32)
        B_lw = Buf('loraw')
        L1B = [Buf(f'L1_{i}') for i in range(4)]
        dma('sp', wl_f[:, :, 0:64], w1_d.rearrange("(c p) k -> p c k", p=128), [], [B_lw])
        dma('sp', wl_f[:, :, 64:128], a1_d.rearrange("(c p) k -> p c k", p=128), [], [B_lw])
        dma('sp', gl_f, g1_d.rearrange("(c p) k -> p c k", p=128), [], [B_lw])
        dma('sp', lnxw, bct_d[:, 3072:3584], [], [B_lw])
        dma('sp', lnxb, bct_d[:, 3584:4096], [], [B_lw])
        S.op('pool', lambda e: e.memset(W2sb, 0.0), writes=[B_lw])
        S.op('pool', lambda e: e.memset(A2sb, 0.0), writes=[B_lw])
        dma('pool', W2sb[0:64, :], w2_d, [B_lw], [B_lw])
        dma('pool', A2sb[64:128, :], a2_d, [B_lw], [B_lw])
        dma('pool', G2sb, g2_d, [], [B_lw])

        def vb(tab, col, k):
            return tab[:, col:col + 8].unsqueeze(2).to_broadcast([128, 8, k])
        tt(W1A[:, :, 0:64], wl_f[:, :, 0:64], vb(om, V_MUW, 64), ALU.mult, [B_lw, B_const], [B_lw])
        tt(W1A[:, :, 64:128], wl_f[:, :, 64:128], vb(om, V_MUA, 64), ALU.mult, [B_lw, B_const], [B_lw])
        tt(W1B[:, :, 0:64], wl_f[:, :, 0:64], vb(vecs, V_MUW, 64), ALU.mult, [B_lw, B_const], [B_lw])
        tt(W1B[:, :, 64:128], wl_f[:, :, 64:128], vb(vecs, V_MUA, 64), ALU.mult, [B_lw, B_const], [B_lw])
        tt(G1A, gl_f, vb(om, V_MUG, 128), ALU.mult, [B_lw, B_const], [B_lw])
        tt(G1B, gl_f, vb(vecs, V_MUG, 128), ALU.mult, [B_lw, B_const], [B_lw])
        for tb in range(4):
            rd = [hTb[4 * tb + i] for i in range(4)] + ([hTb[4 * tb - 1]] if tb > 0 else []) + [B_lw]
            for (WA, WB, pb) in ((W1A, W1B, 0), (G1A, G1B, 1)):
                for c in range(8):
                    mm(bank[pb], WA[:, c, :], hT[:, c, 1 + tb * 512:1 + (tb + 1) * 512], c == 0, False, rd, [bankB[pb]])
                    mm(bank[pb], WB[:, c, :], hT[:, c, tb * 512:(tb + 1) * 512], False, c == 7, rd, [bankB[pb]])
            blk = slice(tb * 512, (tb + 1) * 512)
            act(L1[0:64, blk], bank[0][0:64, :], AF.Tanh, [bankB[0]], [L1B[tb]])
            act(L1[64:128, blk], bank[0][64:128, :], AF.Copy, [bankB[0]], [L1B[tb]])
            act(L1g[:, blk], bank[1], AF.Sigmoid, [bankB[1]], [L1B[tb]])

        wrkv = A.take([8, 3, 128], BF16)
        AR = A.take([NT, 2, 128], BF16)
        BT = A.take([T], BF16)
        KT = A.take([T], BF16)
        vT = A.take([T], BF16)
        rkrT = A.take([T], BF16)
        rm = A.take([513], F32)
        km = A.take([513], F32)
        vm = A.take([513], F32)
        tnames = ['r', 'k0', 'sg', 'asg', 'cum', 'P', 'invP', 'Pp', 'ssk', 'kk', 't1']
        tmp = {k: A.take([512], F32) for k in tnames}
        sqk = A.take([512], BF16)
        PCt = A.take([NT], F32)
        Xb = [A.take([2, 2, 128], BF16) for _ in range(2)]
        Nn = [A.take([2, 128], BF16) for _ in range(2)]
        W1s = [A.take([2, 3, 128], BF16) for _ in range(2)]
        TTs = [A.take([2, 128], BF16) for _ in range(2)]
        BK = [A.take([2, 128], BF16) for _ in range(2)]
        Vt = [A.take([4, 128], BF16) for _ in range(2)]
        Xs = A.take([128], BF16)
        Us = A.take([128], BF16)
        Hs = A.take([64], F32)
        HP = A.take([64], F32)
        Hbz = A.take([2, 64], BF16)
        BKz = [A.take([2, 2, 128], BF16) for _ in range(2)]
        Yp = [A.take([4, 128], F32) for _ in range(2)]
        sqp = A.take([512], F32)
        ynp = A.take([512], F32)
        bon = A.take([512], F32)
        sB = A.take([8], F32)
        yop = A.take([4, 128], BF16)
        rstp = A.take([64], F32)
        Bw = Buf('wrkv')
        Bt_ = {k: Buf('t_' + k) for k in tnames + ['rm', 'km', 'vm', 'sqk', 'PCt']}
        ARb = [Buf(f'AR{n}') for n in range(NT)]
        BTb = [Buf(f'BT{i}') for i in range(4)]
        KTb = [Buf(f'KT{i}') for i in range(4)]
        vTb = [Buf(f'vT{i}') for i in range(4)]
        rkb = [Buf(f'rk{i}') for i in range(4)]
        Bs = {k: Buf('s_' + k) for k in ['X0', 'X1', 'N0', 'N1', 'W10', 'W11', 'TT0', 'TT1', 'BK0', 'BK1', 'Vt0', 'Vt1', 'Xs', 'Us', 'H', 'HP', 'Hb', 'Yp0', 'Yp1', 'BKz0', 'BKz1',
                                          'sqp', 'ynp', 'bon', 'sB', 'yop', 'rstp',
                                          'ps1', 'ps2', 'psN', 'psL', 'pT', 'pT2', 'psX', 'psU', 'psH', 'psY', 'psB', 'psG']}
        for k_, b_ in (('ps1', 0), ('ps2', 2), ('psN', 2), ('psL', 3), ('pT', 4), ('pT2', 4), ('psX', 5), ('psU', 5), ('psH', 5),
                       ('psY', 6), ('psB', 6), ('psG', 7)):
            Bs[k_] = bankB[b_]
        wv3 = w_in.rearrange("(c p) n -> p c n", p=128)
        m4 = cb[:, CB_M4:CB_M4 + 512]
        mS_bc = cb[:, CB_M4:CB_M4 + 128].unsqueeze(1).to_broadcast([128, 2, 128])
        m3_bc = cb[:, CB_M4 + 128:CB_M4 + 512].unsqueeze(1).to_broadcast([128, 2, 384])
        mL_bc = cb[:, CB_ML:CB_ML + 128].unsqueeze(1).to_broadcast([128, 2, 128])
        id_bc = ident.unsqueeze(1).to_broadcast([128, 2, 128])
        sel = cb[:, CB_SEL:CB_SEL + 2]
        ones_bd = cb[:, CB_ONES:CB_ONES + 128]
        ps1 = pp[0][:].rearrange("p (h c) -> p h c", h=2)
        ps2 = bank[2][:, 0:256].rearrange("p (h s) -> p h s", h=2)
        psN = bank[2][:, 256:512].rearrange("p (h s) -> p h s", h=2)
        psL = bank[3].rearrange("p (h c) -> p h c", h=2)
        pTb = bankbf(4)
        pT3 = pTb[:, 0:384].rearrange("p (j t) -> p j t", j=3)
        pT2 = pTb[:, 512:1024].rearrange("p (j t) -> p j t", j=4)
        psX = bank[5][:, 0:128]
        psU = bank[5][:, 128:256]
        psH = bank[5][:, 256:384]
        psY = bank[6][:, 0:128]
        psB = bank[6][:, 128:136]
        psG = bank[7]

        for p in range(DBG.get('pairs', 4)):
            vp = V_PAIR + 8 * p
            col = lambda j: vecs[:, vp + j:vp + j + 1]
            ocol = lambda j: om[:, vp + j:vp + j + 1]
            for j in range(3):
                dma('pool', wrkv[:, :, j, :], wv3[:, :, j * 512 + p * 128:j * 512 + (p + 1) * 128], [], [Bw])
            for nm_ in ('rm', 'km', 'vm'):
                tl = {'rm': rm, 'km': km, 'vm': vm}[nm_]
                S.op('pool', lambda e, tl=tl: e.memset(tl[:, 0:1], 0.0), writes=[Bt_[nm_]])
            for tb in range(4):
                blk = slice(tb * 512, (tb + 1) * 512)
                rd = [hTb[4 * tb + i] for i in range(4)] + [Bw]
                for j in range(3):
                    for c in range(8):
                        mm(bank[j], wrkv[:, c, j, :], hT[:, c, 1 + tb * 512:1 + (tb + 1) * 512], c == 0, c == 7, rd, [bankB[j]])
                mm(bank[3], W2sb[:, p * 128:(p + 1) * 128], L1[:, blk], True, True, [B_lw, L1B[tb]], [bankB[3]])
                mm(bank[4], A2sb[:, p * 128:(p + 1) * 128], L1[:, blk], True, True, [B_lw, L1B[tb]], [bankB[4]])
                for j, (tl, nm_, dst, dstB) in enumerate(((rm, 'rm', tmp['r'], Bt_['r']), (km, 'km', tmp['k0'], Bt_['k0']), (vm, 'vm', vT[:, blk], vTb[tb]))):
                    act(tl[:, 1:513], bank[j], AF.Copy, [bankB[j], B_const], [Bt_[nm_]], scale=col(j))
                    stt(dst, bank[j], ocol(j), tl[:, 0:512], ALU.mult, ALU.add, [bankB[j], Bt_[nm_], B_const], [dstB])
                    S.op('pool', lambda e, tl=tl: e.tensor_copy(out=tl[:, 0:1], in_=tl[:, 512:513]), reads=[Bt_[nm_]], writes=[Bt_[nm_]])
                r_, k0 = tmp['r'], tmp['k0']
                act(tmp['sg'], bank[3], AF.Sigmoid, [bankB[3], B_const], [Bt_['sg']], bias=col(3))
                act(tmp['asg'], bank[4], AF.Sigmoid, [bankB[4], B_const], [Bt_['asg']], bias=col(4))
                for ch in range(4):
                    cs = slice(ch * 128, (ch + 1) * 128)
                    S.op('dve', lambda e, cs=cs: e.tensor_tensor_scan(out=tmp['cum'][:, cs], data0=tmp['sg'][:, cs], data1=tmp['sg'][:, cs],
                                                                       initial=0.0, op0=ALU.add, op1=ALU.bypass),
                         reads=[Bt_['sg']], writes=[Bt_['cum']])
                act(tmp['P'], tmp['cum'], AF.Exp, [Bt_['cum']], [Bt_['P']], scale=-C0)
                act(tmp['invP'], tmp['cum'], AF.Exp, [Bt_['cum']], [Bt_['invP']], scale=C0)
                tt(tmp['sg'], tmp['cum'], tmp['sg'], ALU.subtract, [Bt_['cum'], Bt_['sg']], [Bt_['sg']])
                act(tmp['Pp'], tmp['sg'], AF.Exp, [Bt_['sg']], [Bt_['Pp']], scale=-C0)
                S.op('pool', lambda e, tb=tb: e.tensor_copy(out=PCt[:, tb * 4:(tb + 1) * 4],
                                                            in_=tmp['P'].rearrange("p (c t) -> p c t", c=4)[:, :, 127]),
                     reads=[Bt_['P']], writes=[Bt_['PCt']])
                act(sqk, k0, AF.Square, [Bt_['k0'], B_const], [Bt_['sqk']], scale=col(5))
                mm(bank[5], ones_bd, sqk, True, True, [Bt_['sqk'], B_const], [bankB[5]])
                act(tmp['ssk'], bank[5], AF.Ln, [bankB[5]], [Bt_['ssk']])
                act(tmp['ssk'], tmp['ssk'], AF.Exp, [Bt_['ssk']], [Bt_['ssk']], scale=-0.5)
                stt(tmp['kk'], k0, col(5), tmp['ssk'], ALU.mult, ALU.mult, [Bt_['k0'], Bt_['ssk'], B_const], [Bt_['kk']])
                ts(tmp['t1'], tmp['asg'], col(6), ocol(6), ALU.mult, ALU.add, [Bt_['asg'], B_const], [Bt_['t1']])
                tt(tmp['t1'], tmp['t1'], k0, ALU.mult, [Bt_['t1'], Bt_['k0']], [Bt_['t1']])
                arv = AR[:, 4 * tb:4 * tb + 4, :, :]
                c4 = lambda a: a.rearrange("p (c t) -> p c t", c=4)
                stt(arv[:, :, 0, :], c4(tmp['kk']), -1.0, c4(tmp['Pp']), ALU.mult, ALU.mult, [Bt_['kk'], Bt_['Pp']], [ARb[4 * tb + i] for i in range(4)])
                tt(arv[:, :, 1, :], c4(r_), c4(tmp['P']), ALU.mult, [Bt_['r'], Bt_['P']], [ARb[4 * tb + i] for i in range(4)])
                tt(tmp['kk'], tmp['kk'], tmp['asg'], ALU.mult, [Bt_['kk'], Bt_['asg']], [Bt_['kk']])
                tt(BT[:, blk], tmp['kk'], tmp['invP'], ALU.mult, [Bt_['kk'], Bt_['invP']], [BTb[tb]])
                tt(KT[:, blk], tmp['t1'], tmp['invP'], ALU.mult, [Bt_['t1'], Bt_['invP']], [KTb[tb]])
                stt(rkrT[:, blk], r_, col(7), tmp['t1'], ALU.mult, ALU.mult, [Bt_['r'], Bt_['t1'], B_const], [rkb[tb]])
            if 'AR' in taps and p == 0:
                tap('AR', AR, [128, NT, 2, 128], ARb)
                tap('BT', BT, [128, T], BTb)
                tap('KT', KT, [128, T], KTb)
                tap('vT', vT, [128, T], vTb)

            S.barrier()
            S.op('pool', lambda e: e.memset(Hs, 0.0), writes=[Bs['H']])
            S.op('pool', lambda e: e.memset(Hbz, 0.0), writes=[Bs['Hb']])
            if p == 0:
                for i in range(2):
                    S.op('pool', lambda e, i=i: e.memset(BKz[i], 0.0), writes=[Bs[f'BKz{i}']])

            def local(n):
                cs = slice(n * 128, (n + 1) * 128)
                tb = n // 4
                bz = BKz[n % 2]
                W1, TT, W1B, TTB = W1s[n % 2], TTs[n % 2], Bs[f'W1{n % 2}'], Bs[f'TT{n % 2}']
                bzB = Bs[f'BKz{n % 2}']
                for h in range(2):
                    hp = slice(64 * h, 64 * h + 64)
                    S.op('pool', lambda e, h=h, hp=hp: e.tensor_copy(out=bz[hp, 0, h, :], in_=BT[hp, cs]), reads=[BTb[tb]], writes=[bzB])
                    S.op('pool', lambda e, h=h, hp=hp: e.tensor_copy(out=bz[hp, 1, h, :], in_=KT[hp, cs]), reads=[KTb[tb]], writes=[bzB])
                for h in range(2):
                    mm(ps1[:, h, 0:256], bz[:, 0, h, :], AR[:, n, :, :], True, True, [bzB, ARb[n]], [Bs['ps1']])
                    mm(ps1[:, h, 256:512], bz[:, 1, h, :], AR[:, n, :, :], True, True, [bzB, ARb[n]], [Bs['ps1']])
                    mm(ps2[:, h, :], AR[:, n, 0, :], bz[:, 0, h, :], True, True, [bzB, ARb[n]], [Bs['ps2']])
                tt(Xb[0][:, :, 0, :], ps1[:, :, 0:128], mS_bc, ALU.mult, [Bs['ps1'], B_const], [Bs['X0']])
                tt(W1, ps1[:, :, 128:512], m3_bc, ALU.mult, [Bs['ps1'], B_const], [W1B])
                tt(Nn[0], ps2, mL_bc, ALU.mult, [Bs['ps2'], B_const], [Bs['N0']])
                S.op('pool', lambda e: e.tensor_copy(out=Xb[0][:, :, 1, :], in_=id_bc), reads=[B_const], writes=[Bs['X0']])
                yield
                cur = 0
                for k in range(4):
                    nx = 1 - cur
                    for h in range(2):
                        mm(psL[:, h, :], Nn[cur][:, h, :], Xb[cur][:, h, :, :], True, True, [Bs[f'N{cur}'], Bs[f'X{cur}']], [Bs['psL']])
                        mm(psN[:, h, :], Xb[cur][:, h, 0, :], Nn[cur][:, h, :], True, True, [Bs[f'N{cur}'], Bs[f'X{cur}']], [Bs['psN']])
                    cp('act', Xb[nx][:, :, 0, :], psL[:, :, 0:128], [Bs['psL']], [Bs[f'X{nx}']])
                    cp('act', Nn[nx], psN, [Bs['psN']], [Bs[f'N{nx}']])
                    tt(Xb[nx][:, :, 1, :], Xb[cur][:, :, 1, :], psL[:, :, 128:256], ALU.add, [Bs['psL'], Bs[f'X{cur}']], [Bs[f'X{nx}']])
                    cur = nx
                    yield
                for h in range(2):
                    mm(psL[:, h, 0:128], Nn[cur][:, h, :], Xb[cur][:, h, 1, :], True, True, [Bs[f'N{cur}'], Bs[f'X{cur}']], [Bs['psL']])
                tt(TT, Xb[cur][:, :, 1, :], psL[:, :, 0:128], ALU.add, [Bs['psL'], Bs[f'X{cur}']], [TTB])
                yield

            def chain(n):
                cs = slice(n * 128, (n + 1) * 128)
                tb = n // 4
                g = (n // 4) % 2
                bk = BK[n % 2]
                bkB = Bs[f'BK{n % 2}']
                vt = Vt[g][:, n % 4, :]
                vtB = Bs[f'Vt{g}']
                W1, TT, W1B, TTB = W1s[n % 2], TTs[n % 2], Bs[f'W1{n % 2}'], Bs[f'TT{n % 2}']
                S.op('pe', lambda e: e.transpose(out=pT3[:, 0, :], in_=vT[:, cs], identity=ident), reads=[vTb[tb], B_const], writes=[Bs['pT']])
                S.op('pe', lambda e: e.transpose(out=pT3[:, 1, :], in_=BT[:, cs], identity=ident), reads=[BTb[tb], B_const], writes=[Bs['pT']])
                S.op('pe', lambda e: e.transpose(out=pT3[:, 2, :], in_=KT[:, cs], identity=ident), reads=[KTb[tb], B_const], writes=[Bs['pT']])
                cp('act', vt, pT3[:, 0, :], [Bs['pT']], [vtB])
                cp('act', bk, pT3[:, 1:3, :], [Bs['pT']], [bkB])
                yield
                for h in range(2):
                    hs = slice(64 * h, 64 * h + 64)
                    if n > 0:
                        mm(psX[:, hs], AR[:, n, 0, :], Hbz[:, h, :], True, False, [ARb[n], Bs['Hb']], [Bs['psX']])
                    mm(psX[:, hs], W1[:, h, 1, :], vt[:, hs], n == 0, True, [W1B, vtB], [Bs['psX']])
                cp('act', Xs, psX, [Bs['psX']], [Bs['Xs']])
                yield
                for h in range(2):
                    hs = slice(64 * h, 64 * h + 64)
                    mm(psU[:, hs], TT[:, h, :], Xs[:, hs], True, True, [TTB, Bs['Xs']], [Bs['psU']])
                cp('dve', Us, psU, [Bs['psU']], [Bs['Us']])
                yield
                for h in range(2):
                    hs = slice(64 * h, 64 * h + 64)
                    if n > 0:
                        mm(psY[:, hs], AR[:, n, 1, :], Hbz[:, h, :], True, False, [ARb[n], Bs['Hb']], [Bs['psY']])
                    mm(psY[:, hs], W1[:, h, 0, :], Us[:, hs], n == 0, False, [W1B, Bs['Us']], [Bs['psY']])
                    mm(psY[:, hs], W1[:, h, 2, :], vt[:, hs], False, True, [W1B, vtB], [Bs['psY']])
                mm(psH, bk[:, 0, :], Us, True, False, [bkB, Bs['Us']], [Bs['psH']])
                mm(psH, bk[:, 1, :], vt, False, True, [bkB, vtB], [Bs['psH']])
                cp('act', Yp[g][:, n % 4, :], psY, [Bs['psY']], [Bs[f'Yp{g}']])
                if n > 0:
                    ts(HP, Hs, PCt[:, n:n + 1], None, ALU.mult, None, [Bs['H'], Bt_['PCt']], [Bs['HP']])
                for h in range(2):
                    hp = slice(64 * h, 64 * h + 64)
                    hs = slice(64 * h, 64 * h + 64)
                    if n > 0:
                        stt(Hs[hp, :], psH[hp, hs], PCt[hp, n:n + 1], HP[hp, :], ALU.mult, ALU.add, [Bs['psH'], Bs['HP'], Bt_['PCt']], [Bs['H']])
                    else:
                        ts(Hs[hp, :], psH[hp, hs], PCt[hp, n:n + 1], None, ALU.mult, None, [Bs['psH'], Bt_['PCt']], [Bs['H']])
                for h in range(2):
                    hp = slice(64 * h, 64 * h + 64)
                    cp('act', Hbz[hp, h, :], Hs[hp, :], [Bs['H']], [Bs['Hb']])
                yield

            def post(tg):
                g = tg % 2
                y3 = Yp[g].rearrange("p j (h e) -> p (j h) e", h=2)
                yB = Bs[f'Yp{g}']
                s1, s2, mean, msq, rstd = (rstp[:, 8 * i:8 * i + 8] for i in range(5))
                v8 = lambda a: a.rearrange("p (j e) -> p j e", j=8)
                S.op('dve', lambda e: e.tensor_reduce(out=s1, in_=y3, axis=AX.X, op=ALU.add), reads=[yB], writes=[Bs['rstp']])
                act(sqp, Yp[g].rearrange("p j c -> p (j c)"), AF.Square, [yB], [Bs['sqp']])
                S.op('dve', lambda e: e.tensor_reduce(out=s2, in_=v8(sqp), axis=AX.X, op=ALU.add), reads=[Bs['sqp']], writes=[Bs['rstp']])
                ts(mean, s1, 1.0 / 64, None, ALU.mult, None, [Bs['rstp']], [Bs['rstp']])
                tt(msq, mean, mean, ALU.mult, [Bs['rstp']], [Bs['rstp']])
                stt(rstd, s2, 1.0 / 64, msq, ALU.mult, ALU.subtract, [Bs['rstp']], [Bs['rstp']])
                rsqrt_tiny(rstd, rstd, 1.0, RWKV_GN_EPS, [Bs['rstp']], [Bs['rstp']])
                tt(v8(ynp), y3, mean.unsqueeze(2).to_broadcast([128, 8, 64]), ALU.subtract, [yB, Bs['rstp']], [Bs['ynp']])
                tt(v8(ynp), v8(ynp), rstd.unsqueeze(2).to_broadcast([128, 8, 64]), ALU.mult, [Bs['ynp'], Bs['rstp']], [Bs['ynp']])
                y4 = ynp.rearrange("p (j c) -> p j c", j=4)
                tt(y4, y4, lnxw[:, p * 128:(p + 1) * 128].unsqueeze(1).to_broadcast([128, 4, 128]), ALU.mult, [Bs['ynp'], B_lw], [Bs['ynp']])
                tt(y4, y4, lnxb[:, p * 128:(p + 1) * 128].unsqueeze(1).to_broadcast([128, 4, 128]), ALU.add, [Bs['ynp'], B_lw], [Bs['ynp']])
                for j in range(4):
                    n = 4 * tg + j
                    cs = slice(n * 128, (n + 1) * 128)
                    mm(psB[:, 2 * j:2 * j + 2], rkrT[:, cs], sel, True, True, [rkb[tg], B_const], [Bs['psB']])
                    mm(psG[:, j * 128:(j + 1) * 128], L1g[:, cs], G2sb[:, p * 128:(p + 1) * 128], True, True, [L1B[tg], B_lw], [Bs['psG']])
                cp('act', sB, psB, [Bs['psB']], [Bs['sB']])
                tt(v8(bon), Vt[g].rearrange("p j (h e) -> p (j h) e", h=2), sB.unsqueeze(2).to_broadcast([128, 8, 64]), ALU.mult,
                   [Bs[f'Vt{g}'], Bs['sB']], [Bs['bon']])
                tt(ynp, ynp, bon, ALU.add, [Bs['ynp'], Bs['bon']], [Bs['ynp']])
                tt(yop.rearrange("p j c -> p (j c)"), ynp, psG, ALU.mult, [Bs['ynp'], Bs['psG']], [Bs['yop']])
                for j in range(4):
                    S.op('pe', lambda e, j=j: e.transpose(out=pT2[:, j, :], in_=yop[:, j, :], identity=ident), reads=[Bs['yop'], B_const], writes=[Bs['pT2']])
                cp('act', yT[:, p, tg * 512:(tg + 1) * 512], pT2.rearrange("p j t -> p (j t)"), [Bs['pT2']], [yTb[p][4 * tg + j] for j in range(4)])

            nch = DBG.get('scan_chunks', NT)
            gens = []
            def lim(gen, k):
                for i, _ in enumerate(gen):
                    if i + 1 >= k:
                        break
            if nch > 0:
                lim(local(0), DBG.get('lsteps', 99))
            for n in range(nch):
                if DBG.get('csteps', 99) < 99 or DBG.get('lsteps', 99) < 99:
                    lim(chain(n), DBG.get('csteps', 99))
                    if n == 0 and p == 0 and 'sX0' in taps:
                        tap('sX0', Xb[0], [128, 2, 2, 128], [Bs['X0']])
                        tap('sX1', Xb[1], [128, 2, 2, 128], [Bs['X1']])
                        tap('sN0', Nn[0], [128, 2, 128], [Bs['N0']])
                        tap('sN1', Nn[1], [128, 2, 128], [Bs['N1']])
                        tap('sTT', TTs[0], [128, 2, 128], [Bs['TT0']])
                    continue
                a = chain(n)
                b = local(n + 1) if n + 1 < nch else iter(())
                done_a = done_b = False
                while not (done_a and done_b):
                    if not done_a:
                        try:
                            next(a)
                        except StopIteration:
                            done_a = True
                    if not done_b:
                        try:
                            next(b)
                        except StopIteration:
                            done_b = True
                if n == 0 and p == 0 and 'sX0' in taps:
                    tap('sX0', Xb[0], [128, 2, 2, 128], [Bs['X0']])
                    tap('sX1', Xb[1], [128, 2, 2, 128], [Bs['X1']])
                    tap('sN0', Nn[0], [128, 2, 128], [Bs['N0']])
                    tap('sN1', Nn[1], [128, 2, 128], [Bs['N1']])
                if n == 0 and p == 0 and 'sW1' in taps:
                    tap('sW1', W1s[0], [128, 2, 3, 128], [Bs['W10']])
                    tap('sTT', TTs[0], [128, 2, 128], [Bs['TT0']])
                    tap('sXs', Xs, [128, 128], [Bs['Xs']])
                    tap('sUs', Us, [128, 128], [Bs['Us']])
                    tap('sYp', Yp[0], [128, 4, 128], [Bs['Yp0']])
                    tap('sVt', Vt[0], [128, 4, 128], [Bs['Vt0']])
                    tap('sHs', Hs, [128, 64], [Bs['H']])
                if n % 4 == 3:
                    post(n // 4)
            S.barrier()
        tap('yT', yT, [128, 8, T], [b for l in yTb for b in l])
        S.barrier()


    if 4 in phases:
        S.barrier()
        A.reset(NORM_END)
        xres = A.take([NT, D], F32)
        xresB = [Buf(f'xres{n}') for n in range(NT)]
        P4 = A.mark()
        wout = A.take([8, D], BF16)
        woutB = Buf('wout')
        wo_v = w_out.rearrange("(c p) n -> p c n", p=128)
        for c in range(8):
            dma('pool', wout[:, c, :], wo_v[:, c, :], [], [woutB])
        dma('sp', gtab, bct_d[:, 1024:2048], [], [B_gtab])
        for n in range(NT):
            dma('sp', xst[n % 3], xv[n], [], [xstB[n % 3]])
            pb = 2 * (n % 2)
            for half in range(2):
                for c in range(8):
                    mm(bank[pb + half], yT[:, c, n * 128:(n + 1) * 128], wout[:, c, half * 512:(half + 1) * 512], c == 0, c == 7,
                       [yTb[c][n], woutB], [bankB[pb + half]])
            tt(xres[:, n, :], pp[n % 2][:], xst[n % 3], ALU.add, [bankB[pb], bankB[pb + 1], xstB[n % 3]], [xresB[n]])
            norm_to_hT(n, xres[:, n, :], xresB[n], 1, 4 + n % 2)
        tap('xres', xres, [128, NT, D], xresB)

    if 5 in phases:
        S.barrier()
        A.reset(P4)
        hid = yT[:, 0:6, :]
        hidB = [Buf(f'hid{i}') for i in range(4)]
        wgu = [A.take([2, 8, 256], BF16) for _ in range(2)]
        wguB = [Buf(f'wgu{i}') for i in range(2)]
        wd = A.take([6, D], BF16)
        wdB = Buf('wd')
        gs = [A.take([514], F32) for _ in range(2)]
        gsB = [Buf(f'gs{i}') for i in range(2)]
        acc = [A.take([512], F32) for _ in range(2)]
        accB = [Buf(f'acc{i}') for i in range(2)]
        sl = [A.take([512], F32) for _ in range(2)]
        slB = [Buf(f'sl{i}') for i in range(2)]
        ost = [xst[0], xst[1]]
        ostB = [xstB[0], xstB[1]]
        dma('sp', gtab, bct_d[:, 2048:3072], [], [B_gtab])
        wg_v = wg_d.rearrange("(c p) n -> p c n", p=128)
        wu_v = wu_d.rearrange("(c p) n -> p c n", p=128)
        wd_v = wd_d.rearrange("(m p) n -> p m n", p=128)
        quarters = [(0, 6), (6, 6), (12, 5), (17, 5)]
        it = 0
        for qi, (m0, nq) in enumerate(quarters):
            dma('pool', wd[:, 0:nq, :], wd_v[:, m0:m0 + nq, :], [], [wdB])
            for ml in range(nq):
                m = m0 + ml
                wb = (m // 2) % 2
                if m % 2 == 0:
                    dma('pool', wgu[wb][:, 0, :, :], wg_v[:, :, m * 128:(m + 2) * 128], [], [wguB[wb]])
                    dma('pool', wgu[wb][:, 1, :, :], wu_v[:, :, m * 128:(m + 2) * 128], [], [wguB[wb]])
                mc = slice((m % 2) * 128, (m % 2) * 128 + 128)
                vf = V_FFN + 4 * m
                cw = lambda j: vecs[:, vf + j:vf + j + 1]
                for blk in range(4):
                    g_, gB = gs[blk % 2], gsB[blk % 2]
                    a_, aB = acc[it % 2], accB[it % 2]
                    s_, sB_ = sl[it % 2], slB[it % 2]
                    pg, pu = 2 * (it % 2), 2 * (it % 2) + 1
                    it += 1
                    rd = [hTb[4 * blk + i] for i in range(4)] + [wguB[wb]]
                    for c in range(8):
                        mm(bank[pg], wgu[wb][:, 0, c, mc], hT[:, c, 1 + blk * 512:1 + (blk + 1) * 512], c == 0, c == 7, rd, [bankB[pg]])
                    for c in range(8):
                        mm(bank[pu], wgu[wb][:, 1, c, mc], hT[:, c, 1 + blk * 512:1 + (blk + 1) * 512], c == 0, c == 7, rd, [bankB[pu]])
                    if blk == 0:
                        S.op('pool', lambda e, g_=g_: e.memset(g_[:, 0:2], 0.0), writes=[gB])
                    else:
                        gp = gs[(blk - 1) % 2]
                        S.op('pool', lambda e, g_=g_, gp=gp: e.tensor_copy(out=g_[:, 0:2], in_=gp[:, 512:514]), reads=[gsB[(blk - 1) % 2]], writes=[gB])
                    act(g_[:, 2:514], bank[pg], AF.Copy, [bankB[pg]], [gB])
                    act(a_, bank[pg], AF.Identity, [bankB[pg], B_const], [aB], bias=cw(3), scale=cw(2))
                    stt(a_, g_[:, 1:513], cw(1), a_, ALU.mult, ALU.add, [gB, aB, B_const], [aB])
                    stt(a_, g_[:, 0:512], cw(0), a_, ALU.mult, ALU.add, [gB, aB, B_const], [aB])
                    act(s_, a_, AF.Silu, [aB], [sB_])
                    tt(hid[:, ml, blk * 512:(blk + 1) * 512], s_, bank[pu], ALU.mult, [sB_, bankB[pu]], [hidB[blk]])
            last = qi == len(quarters) - 1
            for n in range(NT):
                pb = 4 + 2 * (n % 2)
                for half in range(2):
                    for ml in range(nq):
                        mm(bank[pb + half], hid[:, ml, n * 128:(n + 1) * 128], wd[:, ml, half * 512:(half + 1) * 512], ml == 0, ml == nq - 1,
                           [hidB[n // 4], wdB], [bankB[pb + half]])
                tt(xres[:, n, :], pp[2 + n % 2][:], xres[:, n, :], ALU.add, [bankB[pb], bankB[pb + 1], xresB[n]], [xresB[n]])
                if last:
                    ssn = ss_all[:, 2, n:n + 1]
                    rsn = rstd_all[:, 2, n:n + 1]
                    sB2 = statB[n % 4]
                    act(sqj, xres[:, n, :], AF.Square, [xresB[n]], [sqjB, sB2], accum=ssn)
                    rsqrt_tiny(rsn, ssn, 1.0 / D, NORM_EPS, [sB2], [sB2])
                    stt(ost[n % 2], xres[:, n, :], rsn, gtab, ALU.mult, ALU.mult, [xresB[n], sB2, B_gtab], [ostB[n % 2]])
                    dma('sp', ov[n], ost[n % 2], [ostB[n % 2]], [])

    S.barrier(('sp',))
    S.emit(st)
    st.close()
    return nc, tap_out, S, A


def _chunkcols(v):
    v = np.asarray(v, np.float32).reshape(-1, 128)
    return np.ascontiguousarray(v.T)


def prep_shared(inp):
    f = lambda k: np.ascontiguousarray(np.asarray(inp[k], np.float32)[0])
    vecs = np.zeros((128, NV), np.float32)
    vecs[:, V_MUW:V_MUW + 8] = _chunkcols(f("rwkv_mu_w"))
    vecs[:, V_MUA:V_MUA + 8] = _chunkcols(f("rwkv_mu_a"))
    vecs[:, V_MUG:V_MUG + 8] = _chunkcols(f("rwkv_mu_g"))
    names = ["rwkv_mu_r", "rwkv_mu_k", "rwkv_mu_v", "rwkv_w0", "rwkv_a0", "rwkv_k_k", "rwkv_k_a", "rwkv_r_k"]
    for j, nm in enumerate(names):
        cc = _chunkcols(f(nm).reshape(-1))
        for p in range(4):
            vecs[:, V_PAIR + 8 * p + j] = cc[:, p]
    cw = f("ffn_conv_w").reshape(3, DFF)
    cbias = f("ffn_conv_b")
    for j in range(3):
        cc = _chunkcols(cw[j])
        for m in range(NFF):
            vecs[:, V_FFN + 4 * m + j] = cc[:, m]
    cc = _chunkcols(cbias)
    for m in range(NFF):
        vecs[:, V_FFN + 4 * m + 3] = cc[:, m]
    row = np.concatenate([f("norm_mix_g"), f("norm_ffn_g"), np.asarray(inp["norm_final_g"], np.float32),
                          f("rwkv_lnx_w"), f("rwkv_lnx_b"), f("ret_gn_w")])
    bct = np.ascontiguousarray(np.broadcast_to(row[None, :], (128, row.shape[0])))
    cf, cb = make_consts()
    shared = {
        "w_in": f("w_in"), "w_out": f("w_out"), "ffn_w_gate": f("ffn_w_gate"), "ffn_w_up": f("ffn_w_up"),
        "ffn_w_down": f("ffn_w_down"), "rwkv_w1": f("rwkv_w1"), "rwkv_a1": f("rwkv_a1"), "rwkv_g1": f("rwkv_g1"),
        "rwkv_w2": f("rwkv_w2"), "rwkv_a2": f("rwkv_a2"), "rwkv_g2": f("rwkv_g2"),
        "vecs": vecs, "bct": bct, "cf": cf, "cb": cb,
    }
    return shared


_PROG = None


def kernel(**inputs):
    global _PROG
    if _PROG is None:
        _PROG = build_program()[0]
    shared = prep_shared(inputs)
    xs = np.asarray(inputs["x"], np.float32)
    in_maps = [dict(shared, x=np.ascontiguousarray(xs[b])) for b in range(8)]
    res = run_bass_kernel_spmd(_PROG, in_maps, core_ids=list(range(8)))
    return np.stack([np.asarray(r["out"], np.float32) for r in res.results], axis=0)
```

```python
import numpy as np
import ml_dtypes
from contextlib import ExitStack
import concourse.bass as bass
import concourse.mybir as mybir
from concourse.bass_utils import run_bass_kernel_spmd

F32 = mybir.dt.float32
BF16 = mybir.dt.bfloat16
AF = mybir.ActivationFunctionType
ALU = mybir.AluOpType
AX = mybir.AxisListType

QUEUES = ('sp', 'act', 'pool', 'pe', 'dve')

T = 2048
D = 1024
NT = 16
DFF = 2816
NFF = 22
C0 = float(np.exp(-0.5))
NORM_EPS = 1e-6
RWKV_GN_EPS = 64e-5
RET_GN_EPS = 1e-5


class Buf:
    __slots__ = ('name', 'w', 'r')

    def __init__(self, name=''):
        self.name = name
        self.w = None
        self.r = {}


class _Op:
    __slots__ = ('q', 's', 'idx', 'fn', 'waits', 'inc', 'dma')


class Sched:
    def __init__(self, nc):
        self.nc = nc
        self.ops = {q: [] for q in QUEUES}
        self.streams = {}
        self.clock = {q: {} for q in QUEUES}
        self.opclock = {}
        self.nwaits = 0
        self.nops = 0

    def op(self, q, fn, reads=(), writes=(), dma=False):
        s = ('dq_' + q) if dma else q
        deps = {}

        def need(st, i):
            if deps.get(st, 0) < i:
                deps[st] = i
        for b in reads:
            if b.w is not None:
                st, i = b.w
                if st == q and q == 'pe':
                    continue
                need(st, i)
        for b in writes:
            if b.w is not None:
                st, i = b.w
                if not (st == q and not dma):
                    need(st, i)
            for st, i in b.r.items():
                if st == q and not dma:
                    continue
                need(st, i)
        ck = self.clock[q]
        waits = []
        for st, i in deps.items():
            if ck.get(st, 0) >= i:
                continue
            waits.append((st, i))
            oc = self.opclock[(st, i)]
            for k, v in oc.items():
                if ck.get(k, 0) < v:
                    ck[k] = v
            if ck.get(st, 0) < i:
                ck[st] = i
            self.streams[st][i - 1].inc = True
        o = _Op()
        o.q = q
        o.s = s
        o.fn = fn
        o.waits = waits
        o.inc = dma
        o.dma = dma
        lst = self.streams.setdefault(s, [])
        lst.append(o)
        o.idx = len(lst)
        self.opclock[(s, o.idx)] = dict(ck)
        self.ops[q].append(o)
        self.nwaits += len(waits)
        self.nops += 1
        for b in writes:
            b.w = (s, o.idx)
            b.r = {}
        for b in reads:
            if b.r.get(s, 0) < o.idx:
                b.r[s] = o.idx
        return o

    def barrier(self, queues=QUEUES):
        tips = {s: len(l) for s, l in self.streams.items() if l}
        for q in queues:
            ck = self.clock[q]
            waits = []
            for s, i in tips.items():
                if s == q and q == 'pe':
                    continue
                if ck.get(s, 0) >= i:
                    continue
                waits.append((s, i))
                self.streams[s][i - 1].inc = True
            for s, i in waits:
                oc = self.opclock[(s, i)]
                for k, v in oc.items():
                    if ck.get(k, 0) < v:
                        ck[k] = v
                ck[s] = i
            if waits:
                o = _Op()
                o.q = q
                o.s = None
                o.fn = None
                o.waits = waits
                o.inc = False
                o.dma = False
                self.ops[q].append(o)

    def emit(self, stack):
        nc = self.nc
        sems = {s: stack.enter_context(nc.semaphore('sem_' + s)) for s in self.streams}
        cnt = {}
        for s, lst in self.streams.items():
            c = 0
            for o in lst:
                if o.dma:
                    c += 16
                elif o.inc:
                    c += 1
                cnt[(s, o.idx)] = c
        self.final_counts = {s: (cnt[(s, len(l))] if l else 0) for s, l in self.streams.items()}
        block = stack.enter_context(nc.Block())

        def run(q, eng):
            for o in self.ops[q]:
                for st, i in o.waits:
                    eng.wait_ge(sems[st], cnt[(st, i)])
                if o.fn is None:
                    continue
                ins = o.fn(eng)
                if o.dma:
                    ins.then_inc(sems[o.s], 16)
                elif o.inc:
                    ins.then_inc(sems[o.s], 1)

        @block.sync
        def _(e):
            run('sp', e)

        @block.scalar
        def _(e):
            run('act', e)

        @block.gpsimd
        def _(e):
            run('pool', e)

        @block.tensor
        def _(e):
            run('pe', e)

        @block.vector
        def _(e):
            run('dve', e)


class Arena:
    def __init__(self, ap, nbytes):
        self.ap = ap
        self.nbytes = nbytes
        self.off = 0
        self.peak = 0

    def take(self, shape, dt):
        esz = 4 if dt == F32 else 2
        n = int(np.prod(shape))
        nb = (n * esz + 63) // 64 * 64
        assert self.off + nb <= self.nbytes, ("arena overflow", self.off, nb, self.nbytes)
        v = self.ap[:, self.off // 4:(self.off + nb) // 4]
        if dt != F32:
            v = v.bitcast(dt)
        v = v[:, 0:n]
        if len(shape) == 2:
            v = v.rearrange("p (a b) -> p a b", a=shape[0])
        elif len(shape) == 3:
            v = v.rearrange("p (a b c) -> p a b c", a=shape[0], b=shape[1])
        elif len(shape) == 4:
            v = v.rearrange("p (a b c d) -> p a b c d", a=shape[0], b=shape[1], c=shape[2])
        self.off += nb
        self.peak = max(self.peak, self.off)
        return v

    def mark(self):
        return self.off

    def reset(self, m):
        self.off = m


V_MUW, V_MUA, V_MUG = 0, 8, 16
V_PAIR = 24
V_FFN = 56
NV = 56 + 4 * NFF
CB_ID, CB_MRET, CB_M4, CB_ML, CB_SEL, CB_ONES = 0, 128, 256, 768, 896, 900
NCB = 1028
CF_COS, CF_SIN, CF_NSIN, CF_XIT, CF_KAT, CF_KAPG, CF_GC = 0, 1024, 2048, 3072, 3584, 4096, 4100
NCF = 4104


def make_consts():
    f32 = np.float32
    p = np.arange(128)
    cf = np.zeros((128, NCF), f32)
    half = 64
    inv_freq = (10000.0 ** (-np.arange(half, dtype=np.float64) / half))
    pos = (np.arange(NT)[None, :] * 128 + p[:, None]).astype(np.float64)
    ang = pos[:, :, None] * inv_freq[None, None, :]
    cf[:, CF_COS:CF_COS + 1024] = np.cos(ang).reshape(128, -1)
    cf[:, CF_SIN:CF_SIN + 1024] = np.sin(ang).reshape(128, -1)
    cf[:, CF_NSIN:CF_NSIN + 1024] = -np.sin(ang).reshape(128, -1)
    lg = np.log(1.0 - 2.0 ** (-5.0 - np.arange(4, dtype=np.float64)))
    i = np.arange(128, dtype=np.float64)
    xi = np.exp((i[None, :] + 1.0) * lg[:, None])
    ka = np.exp(-(i[None, :] + 1.0) * lg[:, None]) * (128.0 ** -0.5)
    cf[:, CF_XIT:CF_XIT + 512] = np.broadcast_to(xi.reshape(1, 512), (128, 512))
    cf[:, CF_KAT:CF_KAT + 512] = np.broadcast_to(ka.reshape(1, 512), (128, 512))
    gC = np.exp(128.0 * lg)
    cf[:, CF_KAPG:CF_KAPG + 4] = (ka.T * gC[None, :])
    cf[:, CF_GC:CF_GC + 4] = gC[None, :]
    cb = np.zeros((128, NCB), f32)
    cb[:, CB_ID:CB_ID + 128] = np.eye(128)
    r = p[:, None]
    c = p[None, :]
    cb[:, CB_MRET:CB_MRET + 128] = (r <= c)
    strict = (r < c).astype(f32)
    incl = (r <= c).astype(f32)
    cb[:, CB_M4:CB_M4 + 512] = np.concatenate([strict, incl, strict, incl], axis=1)
    cb[:, CB_ML:CB_ML + 128] = (c < r)
    cb[0:64, CB_SEL] = 1.0
    cb[64:128, CB_SEL + 1] = 1.0
    cb[0:64, CB_ONES:CB_ONES + 64] = 1.0
    cb[64:128, CB_ONES + 64:CB_ONES + 128] = 1.0
    return cf, cb.astype(ml_dtypes.bfloat16)


DBG = {'ret_chunks': NT, 'ret_steps': 99}


def build_program(taps=None, phases=(1, 2, 3, 4, 5)):
    nc = bass.Bass("TRN2", target_bir_lowering=False)

    def din(name, shape, dt=F32):
        return nc.dram_tensor(name, list(shape), dt, kind="ExternalInput").ap()
    x = din("x", [T, D])
    w_in = din("w_in", [D, 3584])
    w_out = din("w_out", [D, D])
    wg_d = din("ffn_w_gate", [D, DFF])
    wu_d = din("ffn_w_up", [D, DFF])
    wd_d = din("ffn_w_down", [DFF, D])
    w1_d = din("rwkv_w1", [D, 64])
    a1_d = din("rwkv_a1", [D, 64])
    g1_d = din("rwkv_g1", [D, 128])
    w2_d = din("rwkv_w2", [64, 512])
    a2_d = din("rwkv_a2", [64, 512])
    g2_d = din("rwkv_g2", [128, 512])
    vecs_d = din("vecs", [128, NV])
    bct_d = din("bct", [128, 4608])
    cf_d = din("cf", [128, NCF])
    cb_d = din("cb", [128, NCB], BF16)
    out = nc.dram_tensor("out", [T, D], F32, kind="ExternalOutput").ap()
    tap_out = {}
    taps = taps or {}

    S = Sched(nc)
    st = ExitStack()
    ARENA_BYTES = 200 * 1024
    arena_t = st.enter_context(nc.sbuf_tensor("arena", [128, ARENA_BYTES // 4], F32))
    A = Arena(arena_t[:], ARENA_BYTES)
    pp = [st.enter_context(nc.psum_tensor(f"pp{i}", [128, 1024], F32)) for i in range(4)]
    bank = [pp[i // 2][:, (i % 2) * 512:(i % 2) * 512 + 512] for i in range(8)]
    bankB = [Buf(f"bank{i}") for i in range(8)]

    def bankbf(i):
        return bank[i].bitcast(BF16)

    def act(out_, in_, func, r, w, bias=None, scale=None, accum=None):
        kw = {}
        if bias is not None:
            kw['bias'] = bias
        if scale is not None:
            kw['scale'] = scale
        if accum is not None:
            kw['accum_out'] = accum
        S.op('act', lambda e: e.activation(out=out_, in_=in_, func=func, **kw), reads=r, writes=w)

    def tt(out_, a, b, op, r, w, q='dve'):
        S.op(q, lambda e: e.tensor_tensor(out=out_, in0=a, in1=b, op=op), reads=r, writes=w)

    def ts(out_, a, s1, s2, op0, op1, r, w, q='dve'):
        if s2 is None:
            S.op(q, lambda e: e.tensor_scalar(out=out_, in0=a, scalar1=s1, scalar2=None, op0=op0), reads=r, writes=w)
        else:
            S.op(q, lambda e: e.tensor_scalar(out=out_, in0=a, scalar1=s1, scalar2=s2, op0=op0, op1=op1), reads=r, writes=w)

    def stt(out_, a, s, b, op0, op1, r, w):
        S.op('dve', lambda e: e.scalar_tensor_tensor(out=out_, in0=a, scalar=s, in1=b, op0=op0, op1=op1), reads=r, writes=w)

    def mm(out_, lhsT, rhs, start, stop, r, w):
        S.op('pe', lambda e: e.matmul(out=out_, lhsT=lhsT, rhs=rhs, start=start, stop=stop), reads=r, writes=w)

    def mm2(out_, lhsT, rhs, start, stop, r, w):
        if lhsT.shape[0] == 128:
            mm(out_, lhsT[0:64], rhs[0:64], start, False, r, w)
            mm(out_, lhsT[64:128], rhs[64:128], False, stop, r, w)
        else:
            mm(out_, lhsT, rhs, start, stop, r, w)

    def dma(q, out_, in_, r, w, **kw):
        S.op(q, lambda e: e.dma_start(out=out_, in_=in_, **kw), reads=r, writes=w, dma=True)

    def cp(q, out_, in_, r, w):
        if q == 'act':
            act(out_, in_, AF.Copy, r, w)
        else:
            S.op(q, lambda e: e.tensor_copy(out=out_, in_=in_), reads=r, writes=w)

    def rsqrt_tiny(dst, src, scale, eps, r, w):
        ts(dst, src, scale, eps, ALU.mult, ALU.add, r, w)
        act(dst, dst, AF.Ln, w, w)
        act(dst, dst, AF.Exp, w, w, scale=-0.5)

    hT = A.take([8, T + 1], BF16)
    yT = A.take([8, T], BF16)
    cb = A.take([NCB], BF16)
    vecs = A.take([NV], F32)
    om = A.take([NV], F32)
    mhalf = A.take([4], F32)
    gtab = A.take([1024], F32)
    stat = A.take([64], F32)
    ss_all = A.take([3, NT], F32)
    rstd_all = A.take([3, NT], F32)
    B_const = Buf('const')
    B_gtab = Buf('gtab')
    hTb = [Buf(f'hT{n}') for n in range(NT)]
    yTb = [[Buf(f'yT{c}_{n}') for n in range(NT)] for c in range(8)]
    ident = cb[:, CB_ID:CB_ID + 128]
    PERSIST = A.mark()

    def tap(name, ap, shape, reads):
        if name in taps:
            d = nc.dram_tensor("tap_" + name, list(shape), ap.dtype, kind="ExternalOutput").ap()
            tap_out[name] = d
            dma('sp', d, ap, reads, [])

    dma('sp', cb, cb_d, [], [B_const])
    dma('sp', vecs, vecs_d, [], [B_const])
    dma('sp', gtab, bct_d[:, 0:1024], [], [B_gtab])
    S.op('pool', lambda e: e.memset(mhalf, -0.5), writes=[B_const])
    ts(om, vecs, -1.0, 1.0, ALU.mult, ALU.add, [B_const], [B_const])
    S.op('pool', lambda e: e.memset(hT[:, :, 0:1], 0.0), writes=[hTb[0]])

    xst = [A.take([D], F32) for _ in range(3)]
    xstB = [Buf(f'xst{i}') for i in range(3)]
    hb = [A.take([D], BF16) for _ in range(2)]
    hbB = [Buf(f'hb{i}') for i in range(2)]
    sqj = A.take([D], BF16)
    sqjB = Buf('sqj')
    statB = [Buf(f'stat{i}') for i in range(4)]
    NORM_END = A.mark()

    def norm_to_hT(n, src, srcB, which, pbank):
        ssn = ss_all[:, which, n:n + 1]
        rsn = rstd_all[:, which, n:n + 1]
        sB = statB[n % 4]
        act(sqj, src, AF.Square, [srcB], [sqjB, sB], accum=ssn)
        rsqrt_tiny(rsn, ssn, 1.0 / D, NORM_EPS, [sB], [sB])
        h = hb[n % 2]
        stt(h, src, rsn, gtab, ALU.mult, ALU.mult, [srcB, sB, B_gtab], [hbB[n % 2]])
        pt = bankbf(pbank).rearrange("p (c t) -> p c t", c=8)
        for c in range(8):
            S.op('pe', lambda e, c=c: e.transpose(out=pt[:, c, :], in_=h[:, c * 128:(c + 1) * 128], identity=ident),
                 reads=[hbB[n % 2], B_const], writes=[bankB[pbank]])
        cp('act', hT[:, :, 1 + n * 128:1 + (n + 1) * 128], pt, [bankB[pbank]], [hTb[n]])

    xv = x.rearrange("(n p) d -> n p d", p=128)
    ov = out.rearrange("(n p) d -> n p d", p=128)
    for n in range(NT):
        dma('sp', xst[n % 3], xv[n], [], [xstB[n % 3]])
        norm_to_hT(n, xst[n % 3], xstB[n % 3], 0, n % 2)
    tap('hT', hT, [128, 8, T + 1], hTb)

    if 2 in phases:
        A.reset(NORM_END)
        cf = A.take([NCF], F32)
        dma('sp', cf, cf_d, [], [B_const])
        wret = A.take([8, 2048], BF16)
        wretB = Buf('wret')
        wv = w_in.rearrange("(c p) n -> p c n", p=128)
        for c in range(8):
            dma('pool', wret[:, c, :], wv[:, c, 1536:3584], [], [wretB])
        gnw = A.take([512], F32)
        dma('sp', gnw, bct_d[:, 4096:4608], [], [B_const])
        qa = A.take([512], F32)
        qb = A.take([512], F32)
        qrot = A.take([512], BF16)
        krot = A.take([512], BF16)
        qT = A.take([4, 128], BF16)
        kT = A.take([4, 128], BF16)
        PT = A.take([4, 128], BF16)
        Vb = A.take([512], BF16)
        Vk = A.take([512], BF16)
        R = A.take([512], F32)
        Rt = A.take([512], F32)
        Rb = A.take([512], BF16)
        sqy = A.take([512], F32)
        yn = A.take([512], F32)
        sgt = A.take([512], F32)
        yo = A.take([512], BF16)
        rst = A.take([32], F32)
        Bq = {k: Buf('r_' + k) for k in ['qa', 'qb', 'qrot', 'krot', 'qT', 'kT', 'PT', 'Vb', 'Vk', 'R', 'Rt', 'Rb', 'sqy', 'yn', 'sgt', 'yo', 'rst']}
        kapg_bc = cf[:, CF_KAPG:CF_KAPG + 4].unsqueeze(2).to_broadcast([128, 4, 128])
        gC_bc = cf[:, CF_GC:CF_GC + 4].unsqueeze(2).to_broadcast([128, 4, 128])
        xiT = cf[:, CF_XIT:CF_XIT + 512].rearrange("p (h t) -> p h t", h=4)
        kaT = cf[:, CF_KAT:CF_KAT + 512].rearrange("p (h t) -> p h t", h=4)
        mret_bc = cb[:, CB_MRET:CB_MRET + 128].unsqueeze(1).to_broadcast([128, 4, 128])
        PQ, PK, PV, PG, PTB, PS, PY, PKV = range(8)

        def v4(ap):
            return ap.rearrange("p (h e) -> p h e", h=4)

        def rot(ps, psB, dst, dstB, n):
            cosb = cf[:, CF_COS + n * 64:CF_COS + (n + 1) * 64].unsqueeze(1).unsqueeze(1).to_broadcast([128, 4, 2, 64])
            sinb = cf[:, CF_SIN + n * 64:CF_SIN + (n + 1) * 64].unsqueeze(1).to_broadcast([128, 4, 64])
            nsinb = cf[:, CF_NSIN + n * 64:CF_NSIN + (n + 1) * 64].unsqueeze(1).to_broadcast([128, 4, 64])
            p4 = ps.rearrange("p (h two f) -> p h two f", h=4, two=2)
            tt(qa.rearrange("p (h two f) -> p h two f", h=4, two=2), p4, cosb, ALU.mult, [psB, B_const], [Bq['qa']])
            qb4 = qb.rearrange("p (h two f) -> p h two f", h=4, two=2)
            tt(qb4[:, :, 0, :], p4[:, :, 1, :], nsinb, ALU.mult, [psB, B_const], [Bq['qb']])
            tt(qb4[:, :, 1, :], p4[:, :, 0, :], sinb, ALU.mult, [psB, B_const], [Bq['qb']])
            tt(dst, qa, qb, ALU.add, [Bq['qa'], Bq['qb']], [dstB])

        for n in range(DBG['ret_chunks']):
            RS = DBG['ret_steps']
            tok = slice(1 + n * 128, 1 + (n + 1) * 128)
            for j, pb in enumerate((PQ, PK, PV, PG)):
                for c in range(8):
                    mm(bank[pb], hT[:, c, tok], wret[:, c, j * 512:(j + 1) * 512], c == 0, c == 7,
                       [hTb[n], wretB], [bankB[pb]])
            if RS < 2:
                continue
            rot(bank[PQ], bankB[PQ], qrot, Bq['qrot'], n)
            rot(bank[PK], bankB[PK], krot, Bq['krot'], n)
            if RS < 3:
                continue
            ptb = bankbf(PTB).rearrange("p (c t) -> p c t", c=8)
            for h in range(4):
                S.op('pe', lambda e, h=h: e.transpose(out=ptb[:, h, :], in_=qrot[:, h * 128:(h + 1) * 128], identity=ident),
                     reads=[Bq['qrot'], B_const], writes=[bankB[PTB]])
            for h in range(4):
                S.op('pe', lambda e, h=h: e.transpose(out=ptb[:, 4 + h, :], in_=krot[:, h * 128:(h + 1) * 128], identity=ident),
                     reads=[Bq['krot'], B_const], writes=[bankB[PTB]])
            tt(qT, ptb[:, 0:4, :], xiT, ALU.mult, [bankB[PTB], B_const], [Bq['qT']])
            tt(kT, ptb[:, 4:8, :], kaT, ALU.mult, [bankB[PTB], B_const], [Bq['kT']])
            if RS < 4:
                continue
            ps4 = v4(bank[PS])
            for h in range(4):
                mm(ps4[:, h, :], kT[:, h, :], qT[:, h, :], True, True, [Bq['kT'], Bq['qT']], [bankB[PS]])
            tt(PT, ps4, mret_bc, ALU.mult, [bankB[PS], B_const], [Bq['PT']])
            if RS < 5:
                continue
            cp('act', Vb, bank[PV], [bankB[PV]], [Bq['Vb']])
            tt(v4(Vk), v4(bank[PV]), kapg_bc, ALU.mult, [bankB[PV], B_const], [Bq['Vk']])
            if RS < 6:
                continue
            py4 = v4(bank[PY])
            for h in range(4):
                mm(py4[:, h, :], PT[:, h, :], Vb[:, h * 128:(h + 1) * 128], True, n == 0, [Bq['PT'], Bq['Vb']], [bankB[PY]])
                if n > 0:
                    mm(py4[:, h, :], qT[:, h, :], Rb[:, h * 128:(h + 1) * 128], False, True, [Bq['qT'], Bq['Rb']], [bankB[PY]])
            if RS < 7:
                continue
            if n < NT - DBG.get('skiplast', 0):
                pkv4 = v4(bank[PKV])
                for h in range(4):
                    mm(pkv4[:, h, :], krot[:, h * 128:(h + 1) * 128], Vk[:, h * 128:(h + 1) * 128], True, True,
                       [Bq['krot'], Bq['Vk']], [bankB[PKV]])
                if n == 0:
                    cp('dve', R, bank[PKV], [bankB[PKV]], [Bq['R']])
                else:
                    tt(v4(Rt), v4(R), gC_bc, ALU.mult, [Bq['R'], B_const], [Bq['Rt']])
                    tt(R, Rt, bank[PKV], ALU.add, [Bq['Rt'], bankB[PKV]], [Bq['R']])
                cp('pool', Rb, R, [Bq['R']], [Bq['Rb']])
            if RS < 8:
                continue
            s1 = rst[:, 0:4]
            s2 = rst[:, 4:8]
            mean = rst[:, 8:12]
            msq = rst[:, 12:16]
            rstd = rst[:, 16:20]
            S.op('dve', lambda e: e.tensor_reduce(out=s1, in_=py4, axis=AX.X, op=ALU.add), reads=[bankB[PY]], writes=[Bq['rst']])
            act(sqy, bank[PY], AF.Square, [bankB[PY]], [Bq['sqy']])
            S.op('dve', lambda e: e.tensor_reduce(out=s2, in_=v4(sqy), axis=AX.X, op=ALU.add), reads=[Bq['sqy']], writes=[Bq['rst']])
            ts(mean, s1, 1.0 / 128, None, ALU.mult, None, [Bq['rst']], [Bq['rst']])
            tt(msq, mean, mean, ALU.mult, [Bq['rst']], [Bq['rst']])
            stt(rstd, s2, 1.0 / 128, msq, ALU.mult, ALU.subtract, [Bq['rst']], [Bq['rst']])
            rsqrt_tiny(rstd, rstd, 1.0, RET_GN_EPS, [Bq['rst']], [Bq['rst']])
            tt(v4(yn), py4, mean.unsqueeze(2).to_broadcast([128, 4, 128]), ALU.subtract, [bankB[PY], Bq['rst']], [Bq['yn']])
            tt(v4(yn), v4(yn), rstd.unsqueeze(2).to_broadcast([128, 4, 128]), ALU.mult, [Bq['yn'], Bq['rst']], [Bq['yn']])
            tt(yn, yn, gnw, ALU.mult, [Bq['yn'], B_const], [Bq['yn']])
            act(sgt, bank[PG], AF.Silu, [bankB[PG]], [Bq['sgt']])
            tt(yo, yn, sgt, ALU.mult, [Bq['yn'], Bq['sgt']], [Bq['yo']])
            if RS < 9:
                continue
            for h in range(4):
                S.op('pe', lambda e, h=h: e.transpose(out=ptb[:, h, :], in_=yo[:, h * 128:(h + 1) * 128], identity=ident),
                     reads=[Bq['yo'], B_const], writes=[bankB[PTB]])
            cp('act', yT[:, 4:8, n * 128:(n + 1) * 128], ptb[:, 0:4, :], [bankB[PTB]], [yTb[4 + h][n] for h in range(4)])
        if 3 not in phases:
            tap('yT', yT, [128, 8, T], [b for l in yTb for b in l])
        S.barrier()


    if 3 in phases:
        S.barrier()
        A.reset(PERSIST)
        wl_f = A.take([8, 128], F32)
        gl_f = A.take([8, 128], F32)
        W1A = A.take([8, 128], BF16)
        W1B = A.take([8, 128], BF16)
        G1A = A.take([8, 128], BF16)
        G1B = A.take([8, 128], BF16)
        W2sb = A.take([512], BF16)
        A2sb = A.take([512], BF16)
        G2sb = A.take([512], BF16)
        L1 = A.take([T], BF16)
        L1g = A.take([T], BF16)
        lnxw = A.take([512], F32)
        lnxb = A.take([512], F32)
        B_lw = Buf('loraw')
        L1B = [Buf(f'L1_{i}') for i in range(4)]
        dma('sp', wl_f[:, :, 0:64], w1_d.rearrange("(c p) k -> p c k", p=128), [], [B_lw])
        dma('sp', wl_f[:, :, 64:128], a1_d.rearrange("(c p) k -> p c k", p=128), [], [B_lw])
        dma('sp', gl_f, g1_d.rearrange("(c p) k -> p c k", p=128), [], [B_lw])
        dma('sp', lnxw, bct_d[:, 3072:3584], [], [B_lw])
        dma('sp', lnxb, bct_d[:, 3584:4096], [], [B_lw])
        S.op('pool', lambda e: e.memset(W2sb, 0.0), writes=[B_lw])
        S.op('pool', lambda e: e.memset(A2sb, 0.0), writes=[B_lw])
        dma('pool', W2sb[0:64, :], w2_d, [B_lw], [B_lw])
        dma('pool', A2sb[64:128, :], a2_d, [B_lw], [B_lw])
        dma('pool', G2sb, g2_d, [], [B_lw])

        def vb(tab, col, k):
            return tab[:, col:col + 8].unsqueeze(2).to_broadcast([128, 8, k])
        tt(W1A[:, :, 0:64], wl_f[:, :, 0:64], vb(om, V_MUW, 64), ALU.mult, [B_lw, B_const], [B_lw])
        tt(W1A[:, :, 64:128], wl_f[:, :, 64:128], vb(om, V_MUA, 64), ALU.mult, [B_lw, B_const], [B_lw])
        tt(W1B[:, :, 0:64], wl_f[:, :, 0:64], vb(vecs, V_MUW, 64), ALU.mult, [B_lw, B_const], [B_lw])
        tt(W1B[:, :, 64:128], wl_f[:, :, 64:128], vb(vecs, V_MUA, 64), ALU.mult, [B_lw, B_const], [B_lw])
        tt(G1A, gl_f, vb(om, V_MUG, 128), ALU.mult, [B_lw, B_const], [B_lw])
        tt(G1B, gl_f, vb(vecs, V_MUG, 128), ALU.mult, [B_lw, B_const], [B_lw])
        for tb in range(4):
            rd = [hTb[4 * tb + i] for i in range(4)] + ([hTb[4 * tb - 1]] if tb > 0 else []) + [B_lw]
            for (WA, WB, pb) in ((W1A, W1B, 0), (G1A, G1B, 1)):
                for c in range(8):
                    mm(bank[pb], WA[:, c, :], hT[:, c, 1 + tb * 512:1 + (tb + 1) * 512], c == 0, False, rd, [bankB[pb]])
                    mm(bank[pb], WB[:, c, :], hT[:, c, tb * 512:(tb + 1) * 512], False, c == 7, rd, [bankB[pb]])
            blk = slice(tb * 512, (tb + 1) * 512)
            act(L1[0:64, blk], bank[0][0:64, :], AF.Tanh, [bankB[0]], [L1B[tb]])
            act(L1[64:128, blk], bank[0][64:128, :], AF.Copy, [bankB[0]], [L1B[tb]])
            act(L1g[:, blk], bank[1], AF.Sigmoid, [bankB[1]], [L1B[tb]])

        wrkv = A.take([8, 3, 128], BF16)
        AR = A.take([NT, 2, 128], BF16)
        BT = A.take([T], BF16)
        KT = A.take([T], BF16)
        vT = A.take([T], BF16)
        rkrT = A.take([T], BF16)
        rm = A.take([513], F32)
        km = A.take([513], F32)
        vm = A.take([513], F32)
        tnames = ['r', 'k0', 'sg', 'asg', 'cum', 'P', 'invP', 'Pp', 'ssk', 'kk', 't1']
        tmp = {k: A.take([512], F32) for k in tnames}
        sqk = A.take([512], BF16)
        PCt = A.take([NT], F32)
        Xb = [A.take([2, 2, 128], BF16) for _ in range(2)]
        Nn = [A.take([2, 128], BF16) for _ in range(2)]
        W1s = [A.take([2, 3, 128], BF16) for _ in range(2)]
        TTs = [A.take([2, 128], BF16) for _ in range(2)]
        BK = [A.take([2, 128], BF16) for _ in range(2)]
        Vt = [A.take([4, 128], BF16) for _ in range(2)]
        Xs = A.take([128], BF16)
        Us = A.take([128], BF16)
        Hs = A.take([64], F32)
        HP = A.take([64], F32)
        Hbz = A.take([2, 64], BF16)
        BKz = [A.take([2, 2, 128], BF16) for _ in range(2)]
        Yp = [A.take([4, 128], F32) for _ in range(2)]
        sqp = A.take([512], F32)
        ynp = A.take([512], F32)
        bon = A.take([512], F32)
        sB = A.take([8], F32)
        yop = A.take([4, 128], BF16)
        rstp = A.take([64], F32)
        Bw = Buf('wrkv')
        Bt_ = {k: Buf('t_' + k) for k in tnames + ['rm', 'km', 'vm', 'sqk', 'PCt']}
        ARb = [Buf(f'AR{n}') for n in range(NT)]
        BTb = [Buf(f'BT{i}') for i in range(4)]
        KTb = [Buf(f'KT{i}') for i in range(4)]
        vTb = [Buf(f'vT{i}') for i in range(4)]
        rkb = [Buf(f'rk{i}') for i in range(4)]
        Bs = {k: Buf('s_' + k) for k in ['X0', 'X1', 'N0', 'N1', 'W10', 'W11', 'TT0', 'TT1', 'BK0', 'BK1', 'Vt0', 'Vt1', 'Xs', 'Us', 'H', 'HP', 'Hb', 'Yp0', 'Yp1', 'BKz0', 'BKz1',
                                          'sqp', 'ynp', 'bon', 'sB', 'yop', 'rstp',
                                          'ps1', 'ps2', 'psN', 'psL', 'pT', 'pT2', 'psX', 'psU', 'psH', 'psY', 'psB', 'psG']}
        for k_, b_ in (('ps1', 0), ('ps2', 2), ('psN', 2), ('psL', 3), ('pT', 4), ('pT2', 4), ('psX', 5), ('psU', 5), ('psH', 5),
                       ('psY', 6), ('psB', 6), ('psG', 7)):
            Bs[k_] = bankB[b_]
        wv3 = w_in.rearrange("(c p) n -> p c n", p=128)
        m4 = cb[:, CB_M4:CB_M4 + 512]
        mS_bc = cb[:, CB_M4:CB_M4 + 128].unsqueeze(1).to_broadcast([128, 2, 128])
        m3_bc = cb[:, CB_M4 + 128:CB_M4 + 512].unsqueeze(1).to_broadcast([128, 2, 384])
        mL_bc = cb[:, CB_ML:CB_ML + 128].unsqueeze(1).to_broadcast([128, 2, 128])
        id_bc = ident.unsqueeze(1).to_broadcast([128, 2, 128])
        sel = cb[:, CB_SEL:CB_SEL + 2]
        ones_bd = cb[:, CB_ONES:CB_ONES + 128]
        ps1 = pp[0][:].rearrange("p (h c) -> p h c", h=2)
        ps2 = bank[2][:, 0:256].rearrange("p (h s) -> p h s", h=2)
        psN = bank[2][:, 256:512].rearrange("p (h s) -> p h s", h=2)
        psL = bank[3].rearrange("p (h c) -> p h c", h=2)
        pTb = bankbf(4)
        pT3 = pTb[:, 0:384].rearrange("p (j t) -> p j t", j=3)
        pT2 = pTb[:, 512:1024].rearrange("p (j t) -> p j t", j=4)
        psX = bank[5][:, 0:128]
        psU = bank[5][:, 128:256]
        psH = bank[5][:, 256:384]
        psY = bank[6][:, 0:128]
        psB = bank[6][:, 128:136]
        psG = bank[7]

        def pair_setup(p):
            vp = V_PAIR + 8 * p
            col = lambda j: vecs[:, vp + j:vp + j + 1]
            ocol = lambda j: om[:, vp + j:vp + j + 1]
            return col, ocol

        def prep_block(p, tb):
            col, ocol = pair_setup(p)
            if tb == 0:
                for j in range(3):
                    dma('pool', wrkv[:, :, j, :], wv3[:, :, j * 512 + p * 128:j * 512 + (p + 1) * 128], [], [Bw])
                for nm_ in ('rm', 'km', 'vm'):
                    tl = {'rm': rm, 'km': km, 'vm': vm}[nm_]
                    S.op('pool', lambda e, tl=tl: e.memset(tl[:, 0:1], 0.0), writes=[Bt_[nm_]])
            blk = slice(tb * 512, (tb + 1) * 512)
            rd = [hTb[4 * tb + i] for i in range(4)] + [Bw]
            for j in range(3):
                for c in range(8):
                    mm(bank[j], wrkv[:, c, j, :], hT[:, c, 1 + tb * 512:1 + (tb + 1) * 512], c == 0, c == 7, rd, [bankB[j]])
            mm(bank[3], W2sb[:, p * 128:(p + 1) * 128], L1[:, blk], True, True, [B_lw, L1B[tb]], [bankB[3]])
            mm(bank[4], A2sb[:, p * 128:(p + 1) * 128], L1[:, blk], True, True, [B_lw, L1B[tb]], [bankB[4]])
            for j, (tl, nm_, dst, dstB) in enumerate(((rm, 'rm', tmp['r'], Bt_['r']), (km, 'km', tmp['k0'], Bt_['k0']), (vm, 'vm', vT[:, blk], vTb[tb]))):
                act(tl[:, 1:513], bank[j], AF.Copy, [bankB[j], B_const], [Bt_[nm_]], scale=col(j))
                stt(dst, bank[j], ocol(j), tl[:, 0:512], ALU.mult, ALU.add, [bankB[j], Bt_[nm_], B_const], [dstB])
                S.op('pool', lambda e, tl=tl: e.tensor_copy(out=tl[:, 0:1], in_=tl[:, 512:513]), reads=[Bt_[nm_]], writes=[Bt_[nm_]])
            r_, k0 = tmp['r'], tmp['k0']
            act(tmp['sg'], bank[3], AF.Sigmoid, [bankB[3], B_const], [Bt_['sg']], bias=col(3))
            act(tmp['asg'], bank[4], AF.Sigmoid, [bankB[4], B_const], [Bt_['asg']], bias=col(4))
            for ch in range(4):
                cs = slice(ch * 128, (ch + 1) * 128)
                S.op('dve', lambda e, cs=cs: e.tensor_tensor_scan(out=tmp['cum'][:, cs], data0=tmp['sg'][:, cs], data1=tmp['sg'][:, cs],
                                                                   initial=0.0, op0=ALU.add, op1=ALU.bypass),
                     reads=[Bt_['sg']], writes=[Bt_['cum']])
            act(tmp['P'], tmp['cum'], AF.Exp, [Bt_['cum']], [Bt_['P']], scale=-C0)
            act(tmp['invP'], tmp['cum'], AF.Exp, [Bt_['cum']], [Bt_['invP']], scale=C0)
            tt(tmp['sg'], tmp['cum'], tmp['sg'], ALU.subtract, [Bt_['cum'], Bt_['sg']], [Bt_['sg']])
            act(tmp['Pp'], tmp['sg'], AF.Exp, [Bt_['sg']], [Bt_['Pp']], scale=-C0)
            S.op('pool', lambda e, tb=tb: e.tensor_copy(out=PCt[:, tb * 4:(tb + 1) * 4],
                                                        in_=tmp['P'].rearrange("p (c t) -> p c t", c=4)[:, :, 127]),
                 reads=[Bt_['P']], writes=[Bt_['PCt']])
            act(sqk, k0, AF.Square, [Bt_['k0'], B_const], [Bt_['sqk']], scale=col(5))
            mm(bank[5], ones_bd, sqk, True, True, [Bt_['sqk'], B_const], [bankB[5]])
            act(tmp['ssk'], bank[5], AF.Ln, [bankB[5]], [Bt_['ssk']])
            act(tmp['ssk'], tmp['ssk'], AF.Exp, [Bt_['ssk']], [Bt_['ssk']], scale=-0.5)
            stt(tmp['kk'], k0, col(5), tmp['ssk'], ALU.mult, ALU.mult, [Bt_['k0'], Bt_['ssk'], B_const], [Bt_['kk']])
            ts(tmp['t1'], tmp['asg'], col(6), ocol(6), ALU.mult, ALU.add, [Bt_['asg'], B_const], [Bt_['t1']])
            tt(tmp['t1'], tmp['t1'], k0, ALU.mult, [Bt_['t1'], Bt_['k0']], [Bt_['t1']])
            arv = AR[:, 4 * tb:4 * tb + 4, :, :]
            c4 = lambda a: a.rearrange("p (c t) -> p c t", c=4)
            stt(arv[:, :, 0, :], c4(tmp['kk']), -1.0, c4(tmp['Pp']), ALU.mult, ALU.mult, [Bt_['kk'], Bt_['Pp']], [ARb[4 * tb + i] for i in range(4)])
            tt(arv[:, :, 1, :], c4(r_), c4(tmp['P']), ALU.mult, [Bt_['r'], Bt_['P']], [ARb[4 * tb + i] for i in range(4)])
            tt(tmp['kk'], tmp['kk'], tmp['asg'], ALU.mult, [Bt_['kk'], Bt_['asg']], [Bt_['kk']])
            tt(BT[:, blk], tmp['kk'], tmp['invP'], ALU.mult, [Bt_['kk'], Bt_['invP']], [BTb[tb]])
            tt(KT[:, blk], tmp['t1'], tmp['invP'], ALU.mult, [Bt_['t1'], Bt_['invP']], [KTb[tb]])
            stt(rkrT[:, blk], r_, col(7), tmp['t1'], ALU.mult, ALU.mult, [Bt_['r'], Bt_['t1'], B_const], [rkb[tb]])

        def make_scan(p):
            col, ocol = pair_setup(p)
            def local(n):
                cs = slice(n * 128, (n + 1) * 128)
                tb = n // 4
                bz = BKz[n % 2]
                W1, TT, W1B, TTB = W1s[n % 2], TTs[n % 2], Bs[f'W1{n % 2}'], Bs[f'TT{n % 2}']
                bzB = Bs[f'BKz{n % 2}']
                for h in range(2):
                    hp = slice(64 * h, 64 * h + 64)
                    S.op('pool', lambda e, h=h, hp=hp: e.tensor_copy(out=bz[hp, 0, h, :], in_=BT[hp, cs]), reads=[BTb[tb]], writes=[bzB])
                    S.op('pool', lambda e, h=h, hp=hp: e.tensor_copy(out=bz[hp, 1, h, :], in_=KT[hp, cs]), reads=[KTb[tb]], writes=[bzB])
                for h in range(2):
                    mm(ps1[:, h, 0:256], bz[:, 0, h, :], AR[:, n, :, :], True, True, [bzB, ARb[n]], [Bs['ps1']])
                    mm(ps1[:, h, 256:512], bz[:, 1, h, :], AR[:, n, :, :], True, True, [bzB, ARb[n]], [Bs['ps1']])
                    mm(ps2[:, h, :], AR[:, n, 0, :], bz[:, 0, h, :], True, True, [bzB, ARb[n]], [Bs['ps2']])
                tt(Xb[0][:, :, 0, :], ps1[:, :, 0:128], mS_bc, ALU.mult, [Bs['ps1'], B_const], [Bs['X0']])
                tt(W1, ps1[:, :, 128:512], m3_bc, ALU.mult, [Bs['ps1'], B_const], [W1B])
                tt(Nn[0], ps2, mL_bc, ALU.mult, [Bs['ps2'], B_const], [Bs['N0']])
                S.op('pool', lambda e: e.tensor_copy(out=Xb[0][:, :, 1, :], in_=id_bc), reads=[B_const], writes=[Bs['X0']])
                yield
                cur = 0
                for k in range(4):
                    nx = 1 - cur
                    lastk = (k == 3)
                    for h in range(2):
                        if lastk:
                            mm(psL[:, h, 128:256], Nn[cur][:, h, :], Xb[cur][:, h, 1, :], True, True, [Bs[f'N{cur}'], Bs[f'X{cur}']], [Bs['psL']])
                        else:
                            mm(psL[:, h, :], Nn[cur][:, h, :], Xb[cur][:, h, :, :], True, True, [Bs[f'N{cur}'], Bs[f'X{cur}']], [Bs['psL']])
                            mm(psN[:, h, :], Xb[cur][:, h, 0, :], Nn[cur][:, h, :], True, True, [Bs[f'N{cur}'], Bs[f'X{cur}']], [Bs['psN']])
                    if lastk:
                        tt(TT, Xb[cur][:, :, 1, :], psL[:, :, 128:256], ALU.add, [Bs['psL'], Bs[f'X{cur}']], [TTB])
                    else:
                        cp('act', Xb[nx][:, :, 0, :], psL[:, :, 0:128], [Bs['psL']], [Bs[f'X{nx}']])
                        cp('act', Nn[nx], psN, [Bs['psN']], [Bs[f'N{nx}']])
                        tt(Xb[nx][:, :, 1, :], Xb[cur][:, :, 1, :], psL[:, :, 128:256], ALU.add, [Bs['psL'], Bs[f'X{cur}']], [Bs[f'X{nx}']])
                    cur = nx
                    yield

            def chain(n):
                cs = slice(n * 128, (n + 1) * 128)
                tb = n // 4
                g = (n // 4) % 2
                bk = BK[n % 2]
                bkB = Bs[f'BK{n % 2}']
                vt = Vt[g][:, n % 4, :]
                vtB = Bs[f'Vt{g}']
                W1, TT, W1B, TTB = W1s[n % 2], TTs[n % 2], Bs[f'W1{n % 2}'], Bs[f'TT{n % 2}']
                S.op('pe', lambda e: e.transpose(out=pT3[:, 0, :], in_=vT[:, cs], identity=ident), reads=[vTb[tb], B_const], writes=[Bs['pT']])
                S.op('pe', lambda e: e.transpose(out=pT3[:, 1, :], in_=BT[:, cs], identity=ident), reads=[BTb[tb], B_const], writes=[Bs['pT']])
                S.op('pe', lambda e: e.transpose(out=pT3[:, 2, :], in_=KT[:, cs], identity=ident), reads=[KTb[tb], B_const], writes=[Bs['pT']])
                cp('act', vt, pT3[:, 0, :], [Bs['pT']], [vtB])
                cp('act', bk, pT3[:, 1:3, :], [Bs['pT']], [bkB])
                yield
                for h in range(2):
                    hs = slice(64 * h, 64 * h + 64)
                    if n > 0:
                        mm(psX[:, hs], AR[:, n, 0, :], Hbz[:, h, :], True, False, [ARb[n], Bs['Hb']], [Bs['psX']])
                    mm(psX[:, hs], W1[:, h, 1, :], vt[:, hs], n == 0, True, [W1B, vtB], [Bs['psX']])
                cp('act', Xs, psX, [Bs['psX']], [Bs['Xs']])
                yield
                for h in range(2):
                    hs = slice(64 * h, 64 * h + 64)
                    mm(psU[:, hs], TT[:, h, :], Xs[:, hs], True, True, [TTB, Bs['Xs']], [Bs['psU']])
                cp('dve', Us, psU, [Bs['psU']], [Bs['Us']])
                yield
                for h in range(2):
                    hs = slice(64 * h, 64 * h + 64)
                    if n > 0:
                        mm(psY[:, hs], AR[:, n, 1, :], Hbz[:, h, :], True, False, [ARb[n], Bs['Hb']], [Bs['psY']])
                    mm(psY[:, hs], W1[:, h, 0, :], Us[:, hs], n == 0, False, [W1B, Bs['Us']], [Bs['psY']])
                    mm(psY[:, hs], W1[:, h, 2, :], vt[:, hs], False, True, [W1B, vtB], [Bs['psY']])
                mm(psH, bk[:, 0, :], Us, True, False, [bkB, Bs['Us']], [Bs['psH']])
                mm(psH, bk[:, 1, :], vt, False, True, [bkB, vtB], [Bs['psH']])
                cp('act', Yp[g][:, n % 4, :], psY, [Bs['psY']], [Bs[f'Yp{g}']])
                if n > 0:
                    ts(HP, Hs, PCt[:, n:n + 1], None, ALU.mult, None, [Bs['H'], Bt_['PCt']], [Bs['HP']])
                for h in range(2):
                    hp = slice(64 * h, 64 * h + 64)
                    hs = slice(64 * h, 64 * h + 64)
                    if n > 0:
                        stt(Hs[hp, :], psH[hp, hs], PCt[hp, n:n + 1], HP[hp, :], ALU.mult, ALU.add, [Bs['psH'], Bs['HP'], Bt_['PCt']], [Bs['H']])
                    else:
                        ts(Hs[hp, :], psH[hp, hs], PCt[hp, n:n + 1], None, ALU.mult, None, [Bs['psH'], Bt_['PCt']], [Bs['H']])
                for h in range(2):
                    hp = slice(64 * h, 64 * h + 64)
                    cp('act', Hbz[hp, h, :], Hs[hp, :], [Bs['H']], [Bs['Hb']])
                yield

            def post(tg):
                g = tg % 2
                y3 = Yp[g].rearrange("p j (h e) -> p (j h) e", h=2)
                yB = Bs[f'Yp{g}']
                s1, s2, mean, msq, rstd = (rstp[:, 8 * i:8 * i + 8] for i in range(5))
                v8 = lambda a: a.rearrange("p (j e) -> p j e", j=8)
                S.op('dve', lambda e: e.tensor_reduce(out=s1, in_=y3, axis=AX.X, op=ALU.add), reads=[yB], writes=[Bs['rstp']])
                act(sqp, Yp[g].rearrange("p j c -> p (j c)"), AF.Square, [yB], [Bs['sqp']])
                S.op('dve', lambda e: e.tensor_reduce(out=s2, in_=v8(sqp), axis=AX.X, op=ALU.add), reads=[Bs['sqp']], writes=[Bs['rstp']])
                ts(mean, s1, 1.0 / 64, None, ALU.mult, None, [Bs['rstp']], [Bs['rstp']])
                tt(msq, mean, mean, ALU.mult, [Bs['rstp']], [Bs['rstp']])
                stt(rstd, s2, 1.0 / 64, msq, ALU.mult, ALU.subtract, [Bs['rstp']], [Bs['rstp']])
                rsqrt_tiny(rstd, rstd, 1.0, RWKV_GN_EPS, [Bs['rstp']], [Bs['rstp']])
                tt(v8(ynp), y3, mean.unsqueeze(2).to_broadcast([128, 8, 64]), ALU.subtract, [yB, Bs['rstp']], [Bs['ynp']])
                tt(v8(ynp), v8(ynp), rstd.unsqueeze(2).to_broadcast([128, 8, 64]), ALU.mult, [Bs['ynp'], Bs['rstp']], [Bs['ynp']])
                y4 = ynp.rearrange("p (j c) -> p j c", j=4)
                tt(y4, y4, lnxw[:, p * 128:(p + 1) * 128].unsqueeze(1).to_broadcast([128, 4, 128]), ALU.mult, [Bs['ynp'], B_lw], [Bs['ynp']])
                tt(y4, y4, lnxb[:, p * 128:(p + 1) * 128].unsqueeze(1).to_broadcast([128, 4, 128]), ALU.add, [Bs['ynp'], B_lw], [Bs['ynp']])
                for j in range(4):
                    n = 4 * tg + j
                    cs = slice(n * 128, (n + 1) * 128)
                    mm(psB[:, 2 * j:2 * j + 2], rkrT[:, cs], sel, True, True, [rkb[tg], B_const], [Bs['psB']])
                    mm(psG[:, j * 128:(j + 1) * 128], L1g[:, cs], G2sb[:, p * 128:(p + 1) * 128], True, True, [L1B[tg], B_lw], [Bs['psG']])
                cp('act', sB, psB, [Bs['psB']], [Bs['sB']])
                tt(v8(bon), Vt[g].rearrange("p j (h e) -> p (j h) e", h=2), sB.unsqueeze(2).to_broadcast([128, 8, 64]), ALU.mult,
                   [Bs[f'Vt{g}'], Bs['sB']], [Bs['bon']])
                tt(ynp, ynp, bon, ALU.add, [Bs['ynp'], Bs['bon']], [Bs['ynp']])
                tt(yop.rearrange("p j c -> p (j c)"), ynp, psG, ALU.mult, [Bs['ynp'], Bs['psG']], [Bs['yop']])
                for j in range(4):
                    S.op('pe', lambda e, j=j: e.transpose(out=pT2[:, j, :], in_=yop[:, j, :], identity=ident), reads=[Bs['yop'], B_const], writes=[Bs['pT2']])
                cp('act', yT[:, p, tg * 512:(tg + 1) * 512], pT2.rearrange("p j t -> p (j t)"), [Bs['pT2']], [yTb[p][4 * tg + j] for j in range(4)])

            return local, chain, post

        for i_ in range(2):
            S.op('pool', lambda e, i_=i_: e.memset(BKz[i_], 0.0), writes=[Bs[f'BKz{i_}']])
        NP = DBG.get('pairs', 4)
        for tb in range(4):
            prep_block(0, tb)
        scans = [make_scan(p) for p in range(NP)]

        def drain(g):
            for _ in g:
                pass
        S.op('pool', lambda e: e.memset(Hs, 0.0), writes=[Bs['H']])
        S.op('pool', lambda e: e.memset(Hbz, 0.0), writes=[Bs['Hb']])
        drain(scans[0][0](0))
        for p in range(NP):
            local, chain, post = scans[p]
            for n in range(NT):
                a = chain(n)
                if n + 1 < NT:
                    b = local(n + 1)
                elif p + 1 < NP:
                    b = scans[p + 1][0](0)
                else:
                    b = iter(())
                done_a = done_b = False
                while not (done_a and done_b):
                    if not done_a:
                        try:
                            next(a)
                        except StopIteration:
                            done_a = True
                    if not done_b:
                        try:
                            next(b)
                        except StopIteration:
                            done_b = True
                if n % 4 == 3:
                    post(n // 4)
                    if p + 1 < NP:
                        prep_block(p + 1, n // 4)
            if p + 1 < NP:
                S.op('dve', lambda e: e.memset(Hs, 0.0), writes=[Bs['H']])
                S.op('pool', lambda e: e.memset(Hbz, 0.0), writes=[Bs['Hb']])
        tap('yT', yT, [128, 8, T], [b for l in yTb for b in l])
        S.barrier()


    if 4 in phases:
        S.barrier()
        A.reset(NORM_END)
        xres = A.take([NT, D], F32)
        xresB = [Buf(f'xres{n}') for n in range(NT)]
        P4 = A.mark()
        wout = A.take([8, D], BF16)
        woutB = Buf('wout')
        wo_v = w_out.rearrange("(c p) n -> p c n", p=128)
        for c in range(8):
            dma('pool', wout[:, c, :], wo_v[:, c, :], [], [woutB])
        dma('sp', gtab, bct_d[:, 1024:2048], [], [B_gtab])
        for n in range(NT):
            dma('sp', xst[n % 3], xv[n], [], [xstB[n % 3]])
            pb = 2 * (n % 2)
            for half in range(2):
                for c in range(8):
                    mm(bank[pb + half], yT[:, c, n * 128:(n + 1) * 128], wout[:, c, half * 512:(half + 1) * 512], c == 0, c == 7,
                       [yTb[c][n], woutB], [bankB[pb + half]])
            tt(xres[:, n, :], pp[n % 2][:], xst[n % 3], ALU.add, [bankB[pb], bankB[pb + 1], xstB[n % 3]], [xresB[n]])
            norm_to_hT(n, xres[:, n, :], xresB[n], 1, 4 + n % 2)
        tap('xres', xres, [128, NT, D], xresB)

    if 5 in phases:
        S.barrier()
        A.reset(P4)
        hid = yT[:, 0:6, :]
        hidB = [Buf(f'hid{i}') for i in range(4)]
        wgu = [A.take([2, 8, 256], BF16) for _ in range(2)]
        wguB = [Buf(f'wgu{i}') for i in range(2)]
        wd = A.take([6, D], BF16)
        wdB = Buf('wd')
        gs = [A.take([514], F32) for _ in range(2)]
        gsB = [Buf(f'gs{i}') for i in range(2)]
        acc = [A.take([512], F32) for _ in range(2)]
        accB = [Buf(f'acc{i}') for i in range(2)]
        sl = [A.take([512], F32) for _ in range(2)]
        slB = [Buf(f'sl{i}') for i in range(2)]
        ost = [xst[0], xst[1]]
        ostB = [xstB[0], xstB[1]]
        dma('sp', gtab, bct_d[:, 2048:3072], [], [B_gtab])
        wg_v = wg_d.rearrange("(c p) n -> p c n", p=128)
        wu_v = wu_d.rearrange("(c p) n -> p c n", p=128)
        wd_v = wd_d.rearrange("(m p) n -> p m n", p=128)
        quarters = [(0, 6), (6, 6), (12, 5), (17, 5)]

        def load_wgu(m):
            wb_ = (m // 2) % 2
            dma('pool', wgu[wb_][:, 0, :, :], wg_v[:, :, m * 128:(m + 2) * 128], [], [wguB[wb_]])
            dma('pool', wgu[wb_][:, 1, :, :], wu_v[:, :, m * 128:(m + 2) * 128], [], [wguB[wb_]])
        load_wgu(0)
        it = 0
        for qi, (m0, nq) in enumerate(quarters):
            dma('pool', wd[:, 0:nq, :], wd_v[:, m0:m0 + nq, :], [], [wdB])
            for ml in range(nq):
                m = m0 + ml
                wb = (m // 2) % 2
                if m % 2 == 0 and m + 2 < NFF:
                    load_wgu(m + 2)
                mc = slice((m % 2) * 128, (m % 2) * 128 + 128)
                vf = V_FFN + 4 * m
                cw = lambda j: vecs[:, vf + j:vf + j + 1]
                for blk in range(4):
                    g_, gB = gs[blk % 2], gsB[blk % 2]
                    a_, aB = acc[it % 2], accB[it % 2]
                    s_, sB_ = sl[it % 2], slB[it % 2]
                    pg, pu = 2 * (it % 4), 2 * (it % 4) + 1
                    it += 1
                    rd = [hTb[4 * blk + i] for i in range(4)] + [wguB[wb]]
                    for c in range(8):
                        mm(bank[pg], wgu[wb][:, 0, c, mc], hT[:, c, 1 + blk * 512:1 + (blk + 1) * 512], c == 0, c == 7, rd, [bankB[pg]])
                    for c in range(8):
                        mm(bank[pu], wgu[wb][:, 1, c, mc], hT[:, c, 1 + blk * 512:1 + (blk + 1) * 512], c == 0, c == 7, rd, [bankB[pu]])
                    if blk == 0:
                        S.op('pool', lambda e, g_=g_: e.memset(g_[:, 0:2], 0.0), writes=[gB])
                    else:
                        gp = gs[(blk - 1) % 2]
                        S.op('pool', lambda e, g_=g_, gp=gp: e.tensor_copy(out=g_[:, 0:2], in_=gp[:, 512:514]), reads=[gsB[(blk - 1) % 2]], writes=[gB])
                    act(g_[:, 2:514], bank[pg], AF.Copy, [bankB[pg]], [gB])
                    act(a_, bank[pg], AF.Identity, [bankB[pg], B_const], [aB], bias=cw(3), scale=cw(2))
                    stt(a_, g_[:, 1:513], cw(1), a_, ALU.mult, ALU.add, [gB, aB, B_const], [aB])
                    stt(a_, g_[:, 0:512], cw(0), a_, ALU.mult, ALU.add, [gB, aB, B_const], [aB])
                    act(s_, a_, AF.Silu, [aB], [sB_])
                    tt(hid[:, ml, blk * 512:(blk + 1) * 512], s_, bank[pu], ALU.mult, [sB_, bankB[pu]], [hidB[blk]])
            last = qi == len(quarters) - 1
            for n in range(NT):
                pb = 2 * (n % 4)
                for half in range(2):
                    for ml in range(nq):
                        mm(bank[pb + half], hid[:, ml, n * 128:(n + 1) * 128], wd[:, ml, half * 512:(half + 1) * 512], ml == 0, ml == nq - 1,
                           [hidB[n // 4], wdB], [bankB[pb + half]])
                tt(xres[:, n, :], pp[n % 4][:], xres[:, n, :], ALU.add, [bankB[pb], bankB[pb + 1], xresB[n]], [xresB[n]])
                if last:
                    ssn = ss_all[:, 2, n:n + 1]
                    rsn = rstd_all[:, 2, n:n + 1]
                    sB2 = statB[n % 4]
                    act(sqj, xres[:, n, :], AF.Square, [xresB[n]], [sqjB, sB2], accum=ssn)
                    rsqrt_tiny(rsn, ssn, 1.0 / D, NORM_EPS, [sB2], [sB2])
                    stt(ost[n % 2], xres[:, n, :], rsn, gtab, ALU.mult, ALU.mult, [xresB[n], sB2, B_gtab], [ostB[n % 2]])
                    dma('sp', ov[n], ost[n % 2], [ostB[n % 2]], [])

    S.barrier(('sp',))
    S.emit(st)
    st.close()
    return nc, tap_out, S, A


def _chunkcols(v):
    v = np.asarray(v, np.float32).reshape(-1, 128)
    return np.ascontiguousarray(v.T)


def prep_shared(inp):
    f = lambda k: np.ascontiguousarray(np.asarray(inp[k], np.float32)[0])
    vecs = np.zeros((128, NV), np.float32)
    vecs[:, V_MUW:V_MUW + 8] = _chunkcols(f("rwkv_mu_w"))
    vecs[:, V_MUA:V_MUA + 8] = _chunkcols(f("rwkv_mu_a"))
    vecs[:, V_MUG:V_MUG + 8] = _chunkcols(f("rwkv_mu_g"))
    names = ["rwkv_mu_r", "rwkv_mu_k", "rwkv_mu_v", "rwkv_w0", "rwkv_a0", "rwkv_k_k", "rwkv_k_a", "rwkv_r_k"]
    for j, nm in enumerate(names):
        cc = _chunkcols(f(nm).reshape(-1))
        for p in range(4):
            vecs[:, V_PAIR + 8 * p + j] = cc[:, p]
    cw = f("ffn_conv_w").reshape(3, DFF)
    cbias = f("ffn_conv_b")
    for j in range(3):
        cc = _chunkcols(cw[j])
        for m in range(NFF):
            vecs[:, V_FFN + 4 * m + j] = cc[:, m]
    cc = _chunkcols(cbias)
    for m in range(NFF):
        vecs[:, V_FFN + 4 * m + 3] = cc[:, m]
    row = np.concatenate([f("norm_mix_g"), f("norm_ffn_g"), np.asarray(inp["norm_final_g"], np.float32),
                          f("rwkv_lnx_w"), f("rwkv_lnx_b"), f("ret_gn_w")])
    bct = np.ascontiguousarray(np.broadcast_to(row[None, :], (128, row.shape[0])))
    cf, cb = make_consts()
    shared = {
        "w_in": f("w_in"), "w_out": f("w_out"), "ffn_w_gate": f("ffn_w_gate"), "ffn_w_up": f("ffn_w_up"),
        "ffn_w_down": f("ffn_w_down"), "rwkv_w1": f("rwkv_w1"), "rwkv_a1": f("rwkv_a1"), "rwkv_g1": f("rwkv_g1"),
        "rwkv_w2": f("rwkv_w2"), "rwkv_a2": f("rwkv_a2"), "rwkv_g2": f("rwkv_g2"),
        "vecs": vecs, "bct": bct, "cf": cf, "cb": cb,
    }
    return shared


_PROG = None


def kernel(**inputs):
    global _PROG
    if _PROG is None:
        _PROG = build_program()[0]
    shared = prep_shared(inputs)
    xs = np.asarray(inputs["x"], np.float32)
    in_maps = [dict(shared, x=np.ascontiguousarray(xs[b])) for b in range(8)]
    res = run_bass_kernel_spmd(_PROG, in_maps, core_ids=list(range(8)))
    return np.stack([np.asarray(r["out"], np.float32) for r in res.results], axis=0)
```

```python
import numpy as np
import ml_dtypes
from contextlib import ExitStack
import concourse.bass as bass
import concourse.mybir as mybir
from concourse.bass_utils import run_bass_kernel_spmd

F32 = mybir.dt.float32
BF16 = mybir.dt.bfloat16
AF = mybir.ActivationFunctionType
ALU = mybir.AluOpType
AX = mybir.AxisListType

QUEUES = ('sp', 'act', 'pool', 'pe', 'dve')

T = 2048
D = 1024
NT = 16
DFF = 2816
NFF = 22
C0 = float(np.exp(-0.5))
NORM_EPS = 1e-6
RWKV_GN_EPS = 64e-5
RET_GN_EPS = 1e-5


class Buf:
    __slots__ = ('name', 'w', 'r')

    def __init__(self, name=''):
        self.name = name
        self.w = None
        self.r = {}


class _Op:
    __slots__ = ('q', 's', 'idx', 'fn', 'waits', 'inc', 'dma')


class Sched:
    def __init__(self, nc):
        self.nc = nc
        self.ops = {q: [] for q in QUEUES}
        self.streams = {}
        self.clock = {q: {} for q in QUEUES}
        self.opclock = {}
        self.nwaits = 0
        self.nops = 0

    def op(self, q, fn, reads=(), writes=(), dma=False):
        if dma:
            ref = writes[0] if len(writes) else (reads[0] if len(reads) else None)
            s = 'dq_' + (ref.name if ref is not None and ref.name else q)
        else:
            s = q
        deps = {}

        def need(st, i):
            if deps.get(st, 0) < i:
                deps[st] = i
        for b in reads:
            if b.w is not None:
                st, i = b.w
                if st == q and q == 'pe':
                    continue
                need(st, i)
        for b in writes:
            if b.w is not None:
                st, i = b.w
                if not (st == q and not dma):
                    need(st, i)
            for st, i in b.r.items():
                if st == q and not dma:
                    continue
                need(st, i)
        ck = self.clock[q]
        waits = []
        for st, i in deps.items():
            if ck.get(st, 0) >= i:
                continue
            waits.append((st, i))
            oc = self.opclock[(st, i)]
            for k, v in oc.items():
                if ck.get(k, 0) < v:
                    ck[k] = v
            if ck.get(st, 0) < i:
                ck[st] = i
            self.streams[st][i - 1].inc = True
        o = _Op()
        o.q = q
        o.s = s
        o.fn = fn
        o.waits = waits
        o.inc = dma
        o.dma = dma
        lst = self.streams.setdefault(s, [])
        lst.append(o)
        o.idx = len(lst)
        self.opclock[(s, o.idx)] = dict(ck)
        self.ops[q].append(o)
        self.nwaits += len(waits)
        self.nops += 1
        for b in writes:
            b.w = (s, o.idx)
            b.r = {}
        for b in reads:
            if b.r.get(s, 0) < o.idx:
                b.r[s] = o.idx
        return o

    def barrier(self, queues=QUEUES):
        tips = {s: len(l) for s, l in self.streams.items() if l}
        for q in queues:
            ck = self.clock[q]
            waits = []
            for s, i in tips.items():
                if s == q and q == 'pe':
                    continue
                if ck.get(s, 0) >= i:
                    continue
                waits.append((s, i))
                self.streams[s][i - 1].inc = True
            for s, i in waits:
                oc = self.opclock[(s, i)]
                for k, v in oc.items():
                    if ck.get(k, 0) < v:
                        ck[k] = v
                ck[s] = i
            if waits:
                o = _Op()
                o.q = q
                o.s = None
                o.fn = None
                o.waits = waits
                o.inc = False
                o.dma = False
                self.ops[q].append(o)

    def emit(self, stack):
        nc = self.nc
        sems = {s: stack.enter_context(nc.semaphore('sem_' + s)) for s in self.streams}
        cnt = {}
        for s, lst in self.streams.items():
            c = 0
            for o in lst:
                if o.dma:
                    c += 16
                elif o.inc:
                    c += 1
                cnt[(s, o.idx)] = c
        self.final_counts = {s: (cnt[(s, len(l))] if l else 0) for s, l in self.streams.items()}
        block = stack.enter_context(nc.Block())

        def run(q, eng):
            for o in self.ops[q]:
                for st, i in o.waits:
                    eng.wait_ge(sems[st], cnt[(st, i)])
                if o.fn is None:
                    continue
                ins = o.fn(eng)
                if o.dma:
                    ins.then_inc(sems[o.s], 16)
                elif o.inc:
                    ins.then_inc(sems[o.s], 1)

        @block.sync
        def _(e):
            run('sp', e)

        @block.scalar
        def _(e):
            run('act', e)

        @block.gpsimd
        def _(e):
            run('pool', e)

        @block.tensor
        def _(e):
            run('pe', e)

        @block.vector
        def _(e):
            run('dve', e)


class Arena:
    def __init__(self, ap, nbytes):
        self.ap = ap
        self.nbytes = nbytes
        self.off = 0
        self.peak = 0

    def take(self, shape, dt):
        esz = 4 if dt == F32 else 2
        n = int(np.prod(shape))
        nb = (n * esz + 63) // 64 * 64
        assert self.off + nb <= self.nbytes, ("arena overflow", self.off, nb, self.nbytes)
        v = self.ap[:, self.off // 4:(self.off + nb) // 4]
        if dt != F32:
            v = v.bitcast(dt)
        v = v[:, 0:n]
        if len(shape) == 2:
            v = v.rearrange("p (a b) -> p a b", a=shape[0])
        elif len(shape) == 3:
            v = v.rearrange("p (a b c) -> p a b c", a=shape[0], b=shape[1])
        elif len(shape) == 4:
            v = v.rearrange("p (a b c d) -> p a b c d", a=shape[0], b=shape[1], c=shape[2])
        self.off += nb
        self.peak = max(self.peak, self.off)
        return v

    def mark(self):
        return self.off

    def reset(self, m):
        self.off = m


V_MUW, V_MUA, V_MUG = 0, 8, 16
V_PAIR = 24
V_FFN = 56
NV = 56 + 4 * NFF
CB_ID, CB_MRET, CB_M4, CB_ML, CB_SEL, CB_ONES = 0, 128, 256, 768, 896, 900
NCB = 1028
CF_COS, CF_SIN, CF_NSIN, CF_XIT, CF_KAT, CF_KAPG, CF_GC = 0, 1024, 2048, 3072, 3584, 4096, 4100
NCF = 4104


def make_consts():
    f32 = np.float32
    p = np.arange(128)
    cf = np.zeros((128, NCF), f32)
    half = 64
    inv_freq = (10000.0 ** (-np.arange(half, dtype=np.float64) / half))
    pos = (np.arange(NT)[None, :] * 128 + p[:, None]).astype(np.float64)
    ang = pos[:, :, None] * inv_freq[None, None, :]
    cf[:, CF_COS:CF_COS + 1024] = np.cos(ang).reshape(128, -1)
    cf[:, CF_SIN:CF_SIN + 1024] = np.sin(ang).reshape(128, -1)
    cf[:, CF_NSIN:CF_NSIN + 1024] = -np.sin(ang).reshape(128, -1)
    lg = np.log(1.0 - 2.0 ** (-5.0 - np.arange(4, dtype=np.float64)))
    i = np.arange(128, dtype=np.float64)
    xi = np.exp((i[None, :] + 1.0) * lg[:, None])
    ka = np.exp(-(i[None, :] + 1.0) * lg[:, None]) * (128.0 ** -0.5)
    cf[:, CF_XIT:CF_XIT + 512] = np.broadcast_to(xi.reshape(1, 512), (128, 512))
    cf[:, CF_KAT:CF_KAT + 512] = np.broadcast_to(ka.reshape(1, 512), (128, 512))
    gC = np.exp(128.0 * lg)
    cf[:, CF_KAPG:CF_KAPG + 4] = (ka.T * gC[None, :])
    cf[:, CF_GC:CF_GC + 4] = gC[None, :]
    cb = np.zeros((128, NCB), f32)
    cb[:, CB_ID:CB_ID + 128] = np.eye(128)
    r = p[:, None]
    c = p[None, :]
    cb[:, CB_MRET:CB_MRET + 128] = (r <= c)
    strict = (r < c).astype(f32)
    incl = (r <= c).astype(f32)
    cb[:, CB_M4:CB_M4 + 512] = np.concatenate([strict, incl, strict, incl], axis=1)
    cb[:, CB_ML:CB_ML + 128] = (c < r)
    cb[0:64, CB_SEL] = 1.0
    cb[64:128, CB_SEL + 1] = 1.0
    cb[0:64, CB_ONES:CB_ONES + 64] = 1.0
    cb[64:128, CB_ONES + 64:CB_ONES + 128] = 1.0
    return cf, cb.astype(ml_dtypes.bfloat16)


DBG = {'ret_chunks': NT, 'ret_steps': 99}


def build_program(taps=None, phases=(1, 2, 3, 4, 5)):
    nc = bass.Bass("TRN2", target_bir_lowering=False)

    def din(name, shape, dt=F32):
        return nc.dram_tensor(name, list(shape), dt, kind="ExternalInput").ap()
    x = din("x", [T, D])
    w_in = din("w_in", [D, 3584])
    w_out = din("w_out", [D, D])
    wg_d = din("ffn_w_gate", [D, DFF])
    wu_d = din("ffn_w_up", [D, DFF])
    wd_d = din("ffn_w_down", [DFF, D])
    w1_d = din("rwkv_w1", [D, 64])
    a1_d = din("rwkv_a1", [D, 64])
    g1_d = din("rwkv_g1", [D, 128])
    w2_d = din("rwkv_w2", [64, 512])
    a2_d = din("rwkv_a2", [64, 512])
    g2_d = din("rwkv_g2", [128, 512])
    vecs_d = din("vecs", [128, NV])
    bct_d = din("bct", [128, 4608])
    cf_d = din("cf", [128, NCF])
    cb_d = din("cb", [128, NCB], BF16)
    out = nc.dram_tensor("out", [T, D], F32, kind="ExternalOutput").ap()
    tap_out = {}
    taps = taps or {}

    S = Sched(nc)
    st = ExitStack()
    ARENA_BYTES = 200 * 1024
    arena_t = st.enter_context(nc.sbuf_tensor("arena", [128, ARENA_BYTES // 4], F32))
    A = Arena(arena_t[:], ARENA_BYTES)
    pp = [st.enter_context(nc.psum_tensor(f"pp{i}", [128, 1024], F32)) for i in range(4)]
    bank = [pp[i // 2][:, (i % 2) * 512:(i % 2) * 512 + 512] for i in range(8)]
    bankB = [Buf(f"bank{i}") for i in range(8)]

    def bankbf(i):
        return bank[i].bitcast(BF16)

    def act(out_, in_, func, r, w, bias=None, scale=None, accum=None):
        kw = {}
        if bias is not None:
            kw['bias'] = bias
        if scale is not None:
            kw['scale'] = scale
        if accum is not None:
            kw['accum_out'] = accum
        S.op('act', lambda e: e.activation(out=out_, in_=in_, func=func, **kw), reads=r, writes=w)

    def tt(out_, a, b, op, r, w, q='dve'):
        S.op(q, lambda e: e.tensor_tensor(out=out_, in0=a, in1=b, op=op), reads=r, writes=w)

    def ts(out_, a, s1, s2, op0, op1, r, w, q='dve'):
        if s2 is None:
            S.op(q, lambda e: e.tensor_scalar(out=out_, in0=a, scalar1=s1, scalar2=None, op0=op0), reads=r, writes=w)
        else:
            S.op(q, lambda e: e.tensor_scalar(out=out_, in0=a, scalar1=s1, scalar2=s2, op0=op0, op1=op1), reads=r, writes=w)

    def stt(out_, a, s, b, op0, op1, r, w):
        S.op('dve', lambda e: e.scalar_tensor_tensor(out=out_, in0=a, scalar=s, in1=b, op0=op0, op1=op1), reads=r, writes=w)

    def mm(out_, lhsT, rhs, start, stop, r, w):
        S.op('pe', lambda e: e.matmul(out=out_, lhsT=lhsT, rhs=rhs, start=start, stop=stop), reads=r, writes=w)

    def mm2(out_, lhsT, rhs, start, stop, r, w):
        if lhsT.shape[0] == 128:
            mm(out_, lhsT[0:64], rhs[0:64], start, False, r, w)
            mm(out_, lhsT[64:128], rhs[64:128], False, stop, r, w)
        else:
            mm(out_, lhsT, rhs, start, stop, r, w)

    def dma(q, out_, in_, r, w, **kw):
        S.op(q, lambda e: e.dma_start(out=out_, in_=in_, **kw), reads=r, writes=w, dma=True)

    def cp(q, out_, in_, r, w):
        if q == 'act':
            act(out_, in_, AF.Copy, r, w)
        else:
            S.op(q, lambda e: e.tensor_copy(out=out_, in_=in_), reads=r, writes=w)

    def rsqrt_tiny(dst, src, scale, eps, r, w):
        ts(dst, src, scale, eps, ALU.mult, ALU.add, r, w)
        act(dst, dst, AF.Ln, w, w)
        act(dst, dst, AF.Exp, w, w, scale=-0.5)

    hT = A.take([8, T + 1], BF16)
    yT = A.take([8, T], BF16)
    cb = A.take([NCB], BF16)
    vecs = A.take([NV], F32)
    om = A.take([NV], F32)
    mhalf = A.take([4], F32)
    gtab = A.take([1024], F32)
    stat = A.take([64], F32)
    ss_all = A.take([3, NT], F32)
    rstd_all = A.take([3, NT], F32)
    B_const = Buf('const')
    B_gtab = Buf('gtab')
    hTb = [Buf(f'hT{n}') for n in range(NT)]
    yTb = [[Buf(f'yT{c}_{n}') for n in range(NT)] for c in range(8)]
    ident = cb[:, CB_ID:CB_ID + 128]
    PERSIST = A.mark()

    def tap(name, ap, shape, reads):
        if name in taps:
            d = nc.dram_tensor("tap_" + name, list(shape), ap.dtype, kind="ExternalOutput").ap()
            tap_out[name] = d
            dma('sp', d, ap, reads, [])

    dma('sp', cb, cb_d, [], [B_const])
    dma('sp', vecs, vecs_d, [], [B_const])
    dma('sp', gtab, bct_d[:, 0:1024], [], [B_gtab])
    S.op('pool', lambda e: e.memset(mhalf, -0.5), writes=[B_const])
    ts(om, vecs, -1.0, 1.0, ALU.mult, ALU.add, [B_const], [B_const])
    S.op('pool', lambda e: e.memset(hT[:, :, 0:1], 0.0), writes=[hTb[0]])

    xst = [A.take([D], F32) for _ in range(3)]
    xstB = [Buf(f'xst{i}') for i in range(3)]
    hb = [A.take([D], BF16) for _ in range(2)]
    hbB = [Buf(f'hb{i}') for i in range(2)]
    sqj = A.take([D], BF16)
    sqjB = Buf('sqj')
    statB = [Buf(f'stat{i}') for i in range(4)]
    NORM_END = A.mark()

    ssB = [Buf(f'ss{i}') for i in range(3)]
    rsB = [Buf(f'rs{i}') for i in range(3)]

    def norm_stats(n, src, srcB, which):
        act(sqj, src, AF.Square, [srcB], [sqjB, ssB[which]], accum=ss_all[:, which, n:n + 1])

    def norm_rstd(which):
        rsqrt_tiny(rstd_all[:, which, :], ss_all[:, which, :], 1.0 / D, NORM_EPS, [ssB[which]], [rsB[which]])

    def norm_apply(n, src, srcB, which, pbank):
        h = hb[n % 2]
        stt(h, src, rstd_all[:, which, n:n + 1], gtab, ALU.mult, ALU.mult, [srcB, rsB[which], B_gtab], [hbB[n % 2]])
        pt = bankbf(pbank).rearrange("p (c t) -> p c t", c=8)
        for c in range(8):
            S.op('pe', lambda e, c=c: e.transpose(out=pt[:, c, :], in_=h[:, c * 128:(c + 1) * 128], identity=ident),
                 reads=[hbB[n % 2], B_const], writes=[bankB[pbank]])
        cp('act', hT[:, :, 1 + n * 128:1 + (n + 1) * 128], pt, [bankB[pbank]], [hTb[n]])

    xv = x.rearrange("(n p) d -> n p d", p=128)
    ov = out.rearrange("(n p) d -> n p d", p=128)
    if 2 in phases:
        cf = A.take([NCF], F32)
        dma('sp', cf, cf_d, [], [B_const])
        wret = A.take([8, 2048], BF16)
        wretB = Buf('wret')
        wv = w_in.rearrange("(c p) n -> p c n", p=128)
        for c in range(8):
            dma('pool', wret[:, c, :], wv[:, c, 1536:3584], [], [wretB])
        P2START = A.mark()
    for n in range(NT):
        dma('sp', xst[n % 3], xv[n], [], [xstB[n % 3]])
        norm_stats(n, xst[n % 3], xstB[n % 3], 0)
    norm_rstd(0)
    for n in range(NT):
        dma('sp', xst[n % 3], xv[n], [], [xstB[n % 3]])
        norm_apply(n, xst[n % 3], xstB[n % 3], 0, n % 2)
    tap('hT', hT, [128, 8, T + 1], hTb)

    if 2 in phases:
        A.reset(P2START)
        gnw = A.take([512], F32)
        dma('sp', gnw, bct_d[:, 4096:4608], [], [B_const])
        qa = A.take([512], F32)
        qb = A.take([512], F32)
        qrot = A.take([512], BF16)
        krot = A.take([512], BF16)
        qT = A.take([4, 128], BF16)
        kT = A.take([4, 128], BF16)
        PT = A.take([4, 128], BF16)
        Vb = A.take([512], BF16)
        Vk = A.take([512], BF16)
        R = A.take([512], F32)
        Rt = A.take([512], F32)
        Rb = A.take([512], BF16)
        sqy = A.take([512], F32)
        yn = A.take([512], F32)
        sgt = A.take([512], F32)
        yo = A.take([512], BF16)
        rst = A.take([32], F32)
        Bq = {k: Buf('r_' + k) for k in ['qa', 'qb', 'qrot', 'krot', 'qT', 'kT', 'PT', 'Vb', 'Vk', 'R', 'Rt', 'Rb', 'sqy', 'yn', 'sgt', 'yo', 'rst']}
        kapg_bc = cf[:, CF_KAPG:CF_KAPG + 4].unsqueeze(2).to_broadcast([128, 4, 128])
        gC_bc = cf[:, CF_GC:CF_GC + 4].unsqueeze(2).to_broadcast([128, 4, 128])
        xiT = cf[:, CF_XIT:CF_XIT + 512].rearrange("p (h t) -> p h t", h=4)
        kaT = cf[:, CF_KAT:CF_KAT + 512].rearrange("p (h t) -> p h t", h=4)
        mret_bc = cb[:, CB_MRET:CB_MRET + 128].unsqueeze(1).to_broadcast([128, 4, 128])
        PQ, PK, PV, PG, PTB, PS, PY, PKV = range(8)

        def v4(ap):
            return ap.rearrange("p (h e) -> p h e", h=4)

        def rot(ps, psB, dst, dstB, n):
            cosb = cf[:, CF_COS + n * 64:CF_COS + (n + 1) * 64].unsqueeze(1).unsqueeze(1).to_broadcast([128, 4, 2, 64])
            sinb = cf[:, CF_SIN + n * 64:CF_SIN + (n + 1) * 64].unsqueeze(1).to_broadcast([128, 4, 64])
            nsinb = cf[:, CF_NSIN + n * 64:CF_NSIN + (n + 1) * 64].unsqueeze(1).to_broadcast([128, 4, 64])
            p4 = ps.rearrange("p (h two f) -> p h two f", h=4, two=2)
            tt(qa.rearrange("p (h two f) -> p h two f", h=4, two=2), p4, cosb, ALU.mult, [psB, B_const], [Bq['qa']])
            qb4 = qb.rearrange("p (h two f) -> p h two f", h=4, two=2)
            tt(qb4[:, :, 0, :], p4[:, :, 1, :], nsinb, ALU.mult, [psB, B_const], [Bq['qb']])
            tt(qb4[:, :, 1, :], p4[:, :, 0, :], sinb, ALU.mult, [psB, B_const], [Bq['qb']])
            tt(dst, qa, qb, ALU.add, [Bq['qa'], Bq['qb']], [dstB])

        for n in range(DBG['ret_chunks']):
            RS = DBG['ret_steps']
            tok = slice(1 + n * 128, 1 + (n + 1) * 128)
            for j, pb in enumerate((PQ, PK, PV, PG)):
                for c in range(8):
                    mm(bank[pb], hT[:, c, tok], wret[:, c, j * 512:(j + 1) * 512], c == 0, c == 7,
                       [hTb[n], wretB], [bankB[pb]])
            if RS < 2:
                continue
            rot(bank[PQ], bankB[PQ], qrot, Bq['qrot'], n)
            rot(bank[PK], bankB[PK], krot, Bq['krot'], n)
            if RS < 3:
                continue
            ptb = bankbf(PTB).rearrange("p (c t) -> p c t", c=8)
            for h in range(4):
                S.op('pe', lambda e, h=h: e.transpose(out=ptb[:, h, :], in_=qrot[:, h * 128:(h + 1) * 128], identity=ident),
                     reads=[Bq['qrot'], B_const], writes=[bankB[PTB]])
            for h in range(4):
                S.op('pe', lambda e, h=h: e.transpose(out=ptb[:, 4 + h, :], in_=krot[:, h * 128:(h + 1) * 128], identity=ident),
                     reads=[Bq['krot'], B_const], writes=[bankB[PTB]])
            tt(qT, ptb[:, 0:4, :], xiT, ALU.mult, [bankB[PTB], B_const], [Bq['qT']])
            tt(kT, ptb[:, 4:8, :], kaT, ALU.mult, [bankB[PTB], B_const], [Bq['kT']])
            if RS < 4:
                continue
            ps4 = v4(bank[PS])
            for h in range(4):
                mm(ps4[:, h, :], kT[:, h, :], qT[:, h, :], True, True, [Bq['kT'], Bq['qT']], [bankB[PS]])
            tt(PT, ps4, mret_bc, ALU.mult, [bankB[PS], B_const], [Bq['PT']])
            if RS < 5:
                continue
            cp('act', Vb, bank[PV], [bankB[PV]], [Bq['Vb']])
            tt(v4(Vk), v4(bank[PV]), kapg_bc, ALU.mult, [bankB[PV], B_const], [Bq['Vk']])
            if RS < 6:
                continue
            py4 = v4(bank[PY])
            for h in range(4):
                mm(py4[:, h, :], PT[:, h, :], Vb[:, h * 128:(h + 1) * 128], True, n == 0, [Bq['PT'], Bq['Vb']], [bankB[PY]])
                if n > 0:
                    mm(py4[:, h, :], qT[:, h, :], Rb[:, h * 128:(h + 1) * 128], False, True, [Bq['qT'], Bq['Rb']], [bankB[PY]])
            if RS < 7:
                continue
            if n < NT - DBG.get('skiplast', 0):
                pkv4 = v4(bank[PKV])
                for h in range(4):
                    mm(pkv4[:, h, :], krot[:, h * 128:(h + 1) * 128], Vk[:, h * 128:(h + 1) * 128], True, True,
                       [Bq['krot'], Bq['Vk']], [bankB[PKV]])
                if n == 0:
                    cp('dve', R, bank[PKV], [bankB[PKV]], [Bq['R']])
                else:
                    tt(v4(Rt), v4(R), gC_bc, ALU.mult, [Bq['R'], B_const], [Bq['Rt']])
                    tt(R, Rt, bank[PKV], ALU.add, [Bq['Rt'], bankB[PKV]], [Bq['R']])
                cp('pool', Rb, R, [Bq['R']], [Bq['Rb']])
            if RS < 8:
                continue
            s1 = rst[:, 0:4]
            s2 = rst[:, 4:8]
            mean = rst[:, 8:12]
            msq = rst[:, 12:16]
            rstd = rst[:, 16:20]
            S.op('dve', lambda e: e.tensor_reduce(out=s1, in_=py4, axis=AX.X, op=ALU.add), reads=[bankB[PY]], writes=[Bq['rst']])
            act(sqy, bank[PY], AF.Square, [bankB[PY]], [Bq['sqy']])
            S.op('dve', lambda e: e.tensor_reduce(out=s2, in_=v4(sqy), axis=AX.X, op=ALU.add), reads=[Bq['sqy']], writes=[Bq['rst']])
            ts(mean, s1, 1.0 / 128, None, ALU.mult, None, [Bq['rst']], [Bq['rst']])
            tt(msq, mean, mean, ALU.mult, [Bq['rst']], [Bq['rst']])
            stt(rstd, s2, 1.0 / 128, msq, ALU.mult, ALU.subtract, [Bq['rst']], [Bq['rst']])
            rsqrt_tiny(rstd, rstd, 1.0, RET_GN_EPS, [Bq['rst']], [Bq['rst']])
            tt(v4(yn), py4, mean.unsqueeze(2).to_broadcast([128, 4, 128]), ALU.subtract, [bankB[PY], Bq['rst']], [Bq['yn']])
            tt(v4(yn), v4(yn), rstd.unsqueeze(2).to_broadcast([128, 4, 128]), ALU.mult, [Bq['yn'], Bq['rst']], [Bq['yn']])
            tt(yn, yn, gnw, ALU.mult, [Bq['yn'], B_const], [Bq['yn']])
            act(sgt, bank[PG], AF.Silu, [bankB[PG]], [Bq['sgt']])
            tt(yo, yn, sgt, ALU.mult, [Bq['yn'], Bq['sgt']], [Bq['yo']])
            if RS < 9:
                continue
            for h in range(4):
                S.op('pe', lambda e, h=h: e.transpose(out=ptb[:, h, :], in_=yo[:, h * 128:(h + 1) * 128], identity=ident),
                     reads=[Bq['yo'], B_const], writes=[bankB[PTB]])
            cp('act', yT[:, 4:8, n * 128:(n + 1) * 128], ptb[:, 0:4, :], [bankB[PTB]], [yTb[4 + h][n] for h in range(4)])
        if 3 not in phases:
            tap('yT', yT, [128, 8, T], [b for l in yTb for b in l])
        S.barrier()


    if 3 in phases:
        S.barrier()
        A.reset(PERSIST)
        wl_f = A.take([8, 128], F32)
        gl_f = A.take([8, 128], F32)
        W1A = A.take([8, 128], BF16)
        W1B = A.take([8, 128], BF16)
        G1A = A.take([8, 128], BF16)
        G1B = A.take([8, 128], BF16)
        W2sb = A.take([512], BF16)
        A2sb = A.take([512], BF16)
        G2sb = A.take([512], BF16)
        L1 = A.take([T], BF16)
        L1g = A.take([T], BF16)
        lnxw = A.take([512], F32)
        lnxb = A.take([512], F32)
        B_lw = Buf('loraw')
        L1B = [Buf(f'L1_{i}') for i in range(4)]
        dma('sp', wl_f[:, :, 0:64], w1_d.rearrange("(c p) k -> p c k", p=128), [], [B_lw])
        dma('sp', wl_f[:, :, 64:128], a1_d.rearrange("(c p) k -> p c k", p=128), [], [B_lw])
        dma('sp', gl_f, g1_d.rearrange("(c p) k -> p c k", p=128), [], [B_lw])
        dma('sp', lnxw, bct_d[:, 3072:3584], [], [B_lw])
        dma('sp', lnxb, bct_d[:, 3584:4096], [], [B_lw])
        S.op('pool', lambda e: e.memset(W2sb, 0.0), writes=[B_lw])
        S.op('pool', lambda e: e.memset(A2sb, 0.0), writes=[B_lw])
        dma('pool', W2sb[0:64, :], w2_d, [B_lw], [B_lw])
        dma('pool', A2sb[64:128, :], a2_d, [B_lw], [B_lw])
        dma('pool', G2sb, g2_d, [], [B_lw])

        def vb(tab, col, k):
            return tab[:, col:col + 8].unsqueeze(2).to_broadcast([128, 8, k])
        tt(W1A[:, :, 0:64], wl_f[:, :, 0:64], vb(om, V_MUW, 64), ALU.mult, [B_lw, B_const], [B_lw])
        tt(W1A[:, :, 64:128], wl_f[:, :, 64:128], vb(om, V_MUA, 64), ALU.mult, [B_lw, B_const], [B_lw])
        tt(W1B[:, :, 0:64], wl_f[:, :, 0:64], vb(vecs, V_MUW, 64), ALU.mult, [B_lw, B_const], [B_lw])
        tt(W1B[:, :, 64:128], wl_f[:, :, 64:128], vb(vecs, V_MUA, 64), ALU.mult, [B_lw, B_const], [B_lw])
        tt(G1A, gl_f, vb(om, V_MUG, 128), ALU.mult, [B_lw, B_const], [B_lw])
        tt(G1B, gl_f, vb(vecs, V_MUG, 128), ALU.mult, [B_lw, B_const], [B_lw])
        for tb in range(4):
            rd = [hTb[4 * tb + i] for i in range(4)] + ([hTb[4 * tb - 1]] if tb > 0 else []) + [B_lw]
            for (WA, WB, pb) in ((W1A, W1B, 0), (G1A, G1B, 1)):
                for c in range(8):
                    mm(bank[pb], WA[:, c, :], hT[:, c, 1 + tb * 512:1 + (tb + 1) * 512], c == 0, False, rd, [bankB[pb]])
                    mm(bank[pb], WB[:, c, :], hT[:, c, tb * 512:(tb + 1) * 512], False, c == 7, rd, [bankB[pb]])
            blk = slice(tb * 512, (tb + 1) * 512)
            act(L1[0:64, blk], bank[0][0:64, :], AF.Tanh, [bankB[0]], [L1B[tb]])
            act(L1[64:128, blk], bank[0][64:128, :], AF.Copy, [bankB[0]], [L1B[tb]])
            act(L1g[:, blk], bank[1], AF.Sigmoid, [bankB[1]], [L1B[tb]])

        wrkv = A.take([8, 3, 128], BF16)
        AR = A.take([NT, 2, 128], BF16)
        BT = A.take([T], BF16)
        KT = A.take([T], BF16)
        vT = A.take([T], BF16)
        rkrT = A.take([T], BF16)
        rm = A.take([513], F32)
        km = A.take([513], F32)
        vm = A.take([513], F32)
        tnames = ['r', 'k0', 'sg', 'asg', 'cum', 'P', 'invP', 'Pp', 'ssk', 'kk', 't1']
        tmp = {k: A.take([512], F32) for k in tnames}
        sqk = A.take([512], BF16)
        PCt = A.take([NT], F32)
        Xb = [A.take([2, 2, 128], BF16) for _ in range(2)]
        Nn = [A.take([2, 128], BF16) for _ in range(2)]
        W1s = [A.take([2, 3, 128], BF16) for _ in range(2)]
        TTs = [A.take([2, 128], BF16) for _ in range(2)]
        BK = [A.take([2, 128], BF16) for _ in range(2)]
        Vt = [A.take([4, 128], BF16) for _ in range(2)]
        Xs = A.take([128], BF16)
        Us = A.take([128], BF16)
        Hs = A.take([64], F32)
        HP = A.take([64], F32)
        Hbz = A.take([2, 64], BF16)
        BKz = [A.take([2, 2, 128], BF16) for _ in range(2)]
        Yp = [A.take([4, 128], F32) for _ in range(2)]
        sqp = A.take([512], F32)
        ynp = A.take([512], F32)
        bon = A.take([512], F32)
        sB = A.take([8], F32)
        yop = A.take([4, 128], BF16)
        rstp = A.take([64], F32)
        Bw = Buf('wrkv')
        Bt_ = {k: Buf('t_' + k) for k in tnames + ['rm', 'km', 'vm', 'sqk', 'PCt']}
        ARb = [Buf(f'AR{n}') for n in range(NT)]
        BTb = [Buf(f'BT{i}') for i in range(4)]
        KTb = [Buf(f'KT{i}') for i in range(4)]
        vTb = [Buf(f'vT{i}') for i in range(4)]
        rkb = [Buf(f'rk{i}') for i in range(4)]
        Bs = {k: Buf('s_' + k) for k in ['X0', 'X1', 'N0', 'N1', 'W10', 'W11', 'TT0', 'TT1', 'BK0', 'BK1', 'Vt0', 'Vt1', 'Xs', 'Us', 'H', 'HP', 'Hb', 'Yp0', 'Yp1', 'BKz0', 'BKz1',
                                          'sqp', 'ynp', 'bon', 'sB', 'yop', 'rstp',
                                          'ps1', 'ps2', 'psN', 'psL', 'pT', 'pT2', 'psX', 'psU', 'psH', 'psY', 'psB', 'psG']}
        for k_, b_ in (('ps1', 0), ('ps2', 2), ('psN', 2), ('psL', 3), ('pT', 4), ('pT2', 4), ('psX', 5), ('psU', 5), ('psH', 5),
                       ('psY', 6), ('psB', 6), ('psG', 7)):
            Bs[k_] = bankB[b_]
        wv3 = w_in.rearrange("(c p) n -> p c n", p=128)
        m4 = cb[:, CB_M4:CB_M4 + 512]
        mS_bc = cb[:, CB_M4:CB_M4 + 128].unsqueeze(1).to_broadcast([128, 2, 128])
        m3_bc = cb[:, CB_M4 + 128:CB_M4 + 512].unsqueeze(1).to_broadcast([128, 2, 384])
        mL_bc = cb[:, CB_ML:CB_ML + 128].unsqueeze(1).to_broadcast([128, 2, 128])
        id_bc = ident.unsqueeze(1).to_broadcast([128, 2, 128])
        sel = cb[:, CB_SEL:CB_SEL + 2]
        ones_bd = cb[:, CB_ONES:CB_ONES + 128]
        ps1 = pp[0][:].rearrange("p (h c) -> p h c", h=2)
        ps2 = bank[2][:, 0:256].rearrange("p (h s) -> p h s", h=2)
        psN = bank[2][:, 256:512].rearrange("p (h s) -> p h s", h=2)
        psL = bank[3].rearrange("p (h c) -> p h c", h=2)
        pTb = bankbf(4)
        pT3 = pTb[:, 0:384].rearrange("p (j t) -> p j t", j=3)
        pT2 = pTb[:, 512:1024].rearrange("p (j t) -> p j t", j=4)
        psX = bank[5][:, 0:128]
        psU = bank[5][:, 128:256]
        psH = bank[5][:, 256:384]
        psY = bank[6][:, 0:128]
        psB = bank[6][:, 128:136]
        psG = bank[7]

        def pair_setup(p):
            vp = V_PAIR + 8 * p
            col = lambda j: vecs[:, vp + j:vp + j + 1]
            ocol = lambda j: om[:, vp + j:vp + j + 1]
            return col, ocol

        def prep_block(p, tb):
            col, ocol = pair_setup(p)
            if tb == 0:
                for j in range(3):
                    dma('pool', wrkv[:, :, j, :], wv3[:, :, j * 512 + p * 128:j * 512 + (p + 1) * 128], [], [Bw])
                for nm_ in ('rm', 'km', 'vm'):
                    tl = {'rm': rm, 'km': km, 'vm': vm}[nm_]
                    S.op('pool', lambda e, tl=tl: e.memset(tl[:, 0:1], 0.0), writes=[Bt_[nm_]])
            blk = slice(tb * 512, (tb + 1) * 512)
            rd = [hTb[4 * tb + i] for i in range(4)] + [Bw]
            for j in range(3):
                for c in range(8):
                    mm(bank[j], wrkv[:, c, j, :], hT[:, c, 1 + tb * 512:1 + (tb + 1) * 512], c == 0, c == 7, rd, [bankB[j]])
            mm(bank[3], W2sb[:, p * 128:(p + 1) * 128], L1[:, blk], True, True, [B_lw, L1B[tb]], [bankB[3]])
            mm(bank[4], A2sb[:, p * 128:(p + 1) * 128], L1[:, blk], True, True, [B_lw, L1B[tb]], [bankB[4]])
            for j, (tl, nm_, dst, dstB) in enumerate(((rm, 'rm', tmp['r'], Bt_['r']), (km, 'km', tmp['k0'], Bt_['k0']), (vm, 'vm', vT[:, blk], vTb[tb]))):
                act(tl[:, 1:513], bank[j], AF.Copy, [bankB[j], B_const], [Bt_[nm_]], scale=col(j))
                stt(dst, bank[j], ocol(j), tl[:, 0:512], ALU.mult, ALU.add, [bankB[j], Bt_[nm_], B_const], [dstB])
                S.op('pool', lambda e, tl=tl: e.tensor_copy(out=tl[:, 0:1], in_=tl[:, 512:513]), reads=[Bt_[nm_]], writes=[Bt_[nm_]])
            r_, k0 = tmp['r'], tmp['k0']
            act(tmp['sg'], bank[3], AF.Sigmoid, [bankB[3], B_const], [Bt_['sg']], bias=col(3))
            act(tmp['asg'], bank[4], AF.Sigmoid, [bankB[4], B_const], [Bt_['asg']], bias=col(4))
            for ch in range(4):
                cs = slice(ch * 128, (ch + 1) * 128)
                S.op('dve', lambda e, cs=cs: e.tensor_tensor_scan(out=tmp['cum'][:, cs], data0=tmp['sg'][:, cs], data1=tmp['sg'][:, cs],
                                                                   initial=0.0, op0=ALU.add, op1=ALU.bypass),
                     reads=[Bt_['sg']], writes=[Bt_['cum']])
            act(tmp['P'], tmp['cum'], AF.Exp, [Bt_['cum']], [Bt_['P']], scale=-C0)
            act(tmp['invP'], tmp['cum'], AF.Exp, [Bt_['cum']], [Bt_['invP']], scale=C0)
            tt(tmp['sg'], tmp['cum'], tmp['sg'], ALU.subtract, [Bt_['cum'], Bt_['sg']], [Bt_['sg']])
            act(tmp['Pp'], tmp['sg'], AF.Exp, [Bt_['sg']], [Bt_['Pp']], scale=-C0)
            S.op('pool', lambda e, tb=tb: e.tensor_copy(out=PCt[:, tb * 4:(tb + 1) * 4],
                                                        in_=tmp['P'].rearrange("p (c t) -> p c t", c=4)[:, :, 127]),
                 reads=[Bt_['P']], writes=[Bt_['PCt']])
            act(sqk, k0, AF.Square, [Bt_['k0'], B_const], [Bt_['sqk']], scale=col(5))
            mm(bank[5], ones_bd, sqk, True, True, [Bt_['sqk'], B_const], [bankB[5]])
            act(tmp['ssk'], bank[5], AF.Ln, [bankB[5]], [Bt_['ssk']])
            act(tmp['ssk'], tmp['ssk'], AF.Exp, [Bt_['ssk']], [Bt_['ssk']], scale=-0.5)
            stt(tmp['kk'], k0, col(5), tmp['ssk'], ALU.mult, ALU.mult, [Bt_['k0'], Bt_['ssk'], B_const], [Bt_['kk']])
            ts(tmp['t1'], tmp['asg'], col(6), ocol(6), ALU.mult, ALU.add, [Bt_['asg'], B_const], [Bt_['t1']])
            tt(tmp['t1'], tmp['t1'], k0, ALU.mult, [Bt_['t1'], Bt_['k0']], [Bt_['t1']])
            arv = AR[:, 4 * tb:4 * tb + 4, :, :]
            c4 = lambda a: a.rearrange("p (c t) -> p c t", c=4)
            stt(arv[:, :, 0, :], c4(tmp['kk']), -1.0, c4(tmp['Pp']), ALU.mult, ALU.mult, [Bt_['kk'], Bt_['Pp']], [ARb[4 * tb + i] for i in range(4)])
            tt(arv[:, :, 1, :], c4(r_), c4(tmp['P']), ALU.mult, [Bt_['r'], Bt_['P']], [ARb[4 * tb + i] for i in range(4)])
            tt(tmp['kk'], tmp['kk'], tmp['asg'], ALU.mult, [Bt_['kk'], Bt_['asg']], [Bt_['kk']])
            tt(BT[:, blk], tmp['kk'], tmp['invP'], ALU.mult, [Bt_['kk'], Bt_['invP']], [BTb[tb]])
            tt(KT[:, blk], tmp['t1'], tmp['invP'], ALU.mult, [Bt_['t1'], Bt_['invP']], [KTb[tb]])
            stt(rkrT[:, blk], r_, col(7), tmp['t1'], ALU.mult, ALU.mult, [Bt_['r'], Bt_['t1'], B_const], [rkb[tb]])

        def make_scan(p):
            col, ocol = pair_setup(p)
            def local(n):
                cs = slice(n * 128, (n + 1) * 128)
                tb = n // 4
                bz = BKz[n % 2]
                W1, TT, W1B, TTB = W1s[n % 2], TTs[n % 2], Bs[f'W1{n % 2}'], Bs[f'TT{n % 2}']
                bzB = Bs[f'BKz{n % 2}']
                for h in range(2):
                    hp = slice(64 * h, 64 * h + 64)
                    S.op('pool', lambda e, h=h, hp=hp: e.tensor_copy(out=bz[hp, 0, h, :], in_=BT[hp, cs]), reads=[BTb[tb]], writes=[bzB])
                    S.op('pool', lambda e, h=h, hp=hp: e.tensor_copy(out=bz[hp, 1, h, :], in_=KT[hp, cs]), reads=[KTb[tb]], writes=[bzB])
                for h in range(2):
                    mm(ps1[:, h, 0:256], bz[:, 0, h, :], AR[:, n, :, :], True, True, [bzB, ARb[n]], [Bs['ps1']])
                    mm(ps1[:, h, 256:512], bz[:, 1, h, :], AR[:, n, :, :], True, True, [bzB, ARb[n]], [Bs['ps1']])
                    mm(ps2[:, h, :], AR[:, n, 0, :], bz[:, 0, h, :], True, True, [bzB, ARb[n]], [Bs['ps2']])
                tt(Xb[0][:, :, 0, :], ps1[:, :, 0:128], mS_bc, ALU.mult, [Bs['ps1'], B_const], [Bs['X0']])
                tt(W1, ps1[:, :, 128:512], m3_bc, ALU.mult, [Bs['ps1'], B_const], [W1B])
                tt(Nn[0], ps2, mL_bc, ALU.mult, [Bs['ps2'], B_const], [Bs['N0']])
                S.op('pool', lambda e: e.tensor_copy(out=Xb[0][:, :, 1, :], in_=id_bc), reads=[B_const], writes=[Bs['X0']])
                yield
                cur = 0
                for k in range(4):
                    nx = 1 - cur
                    lastk = (k == 3)
                    for h in range(2):
                        if lastk:
                            mm(psL[:, h, 128:256], Nn[cur][:, h, :], Xb[cur][:, h, 1, :], True, True, [Bs[f'N{cur}'], Bs[f'X{cur}']], [Bs['psL']])
                        else:
                            mm(psL[:, h, :], Nn[cur][:, h, :], Xb[cur][:, h, :, :], True, True, [Bs[f'N{cur}'], Bs[f'X{cur}']], [Bs['psL']])
                            mm(psN[:, h, :], Xb[cur][:, h, 0, :], Nn[cur][:, h, :], True, True, [Bs[f'N{cur}'], Bs[f'X{cur}']], [Bs['psN']])
                    if lastk:
                        tt(TT, Xb[cur][:, :, 1, :], psL[:, :, 128:256], ALU.add, [Bs['psL'], Bs[f'X{cur}']], [TTB])
                    else:
                        cp('act', Xb[nx][:, :, 0, :], psL[:, :, 0:128], [Bs['psL']], [Bs[f'X{nx}']])
                        cp('act', Nn[nx], psN, [Bs['psN']], [Bs[f'N{nx}']])
                        tt(Xb[nx][:, :, 1, :], Xb[cur][:, :, 1, :], psL[:, :, 128:256], ALU.add, [Bs['psL'], Bs[f'X{cur}']], [Bs[f'X{nx}']])
                    cur = nx
                    yield

            def chain(n):
                cs = slice(n * 128, (n + 1) * 128)
                tb = n // 4
                g = (n // 4) % 2
                bk = BK[n % 2]
                bkB = Bs[f'BK{n % 2}']
                vt = Vt[g][:, n % 4, :]
                vtB = Bs[f'Vt{g}']
                W1, TT, W1B, TTB = W1s[n % 2], TTs[n % 2], Bs[f'W1{n % 2}'], Bs[f'TT{n % 2}']
                S.op('pe', lambda e: e.transpose(out=pT3[:, 0, :], in_=vT[:, cs], identity=ident), reads=[vTb[tb], B_const], writes=[Bs['pT']])
                S.op('pe', lambda e: e.transpose(out=pT3[:, 1, :], in_=BT[:, cs], identity=ident), reads=[BTb[tb], B_const], writes=[Bs['pT']])
                S.op('pe', lambda e: e.transpose(out=pT3[:, 2, :], in_=KT[:, cs], identity=ident), reads=[KTb[tb], B_const], writes=[Bs['pT']])
                cp('act', vt, pT3[:, 0, :], [Bs['pT']], [vtB])
                cp('act', bk, pT3[:, 1:3, :], [Bs['pT']], [bkB])
                yield
                for h in range(2):
                    hs = slice(64 * h, 64 * h + 64)
                    if n > 0:
                        mm(psX[:, hs], AR[:, n, 0, :], Hbz[:, h, :], True, False, [ARb[n], Bs['Hb']], [Bs['psX']])
                    mm(psX[:, hs], W1[:, h, 1, :], vt[:, hs], n == 0, True, [W1B, vtB], [Bs['psX']])
                cp('act', Xs, psX, [Bs['psX']], [Bs['Xs']])
                yield
                for h in range(2):
                    hs = slice(64 * h, 64 * h + 64)
                    mm(psU[:, hs], TT[:, h, :], Xs[:, hs], True, True, [TTB, Bs['Xs']], [Bs['psU']])
                cp('dve', Us, psU, [Bs['psU']], [Bs['Us']])
                yield
                for h in range(2):
                    hs = slice(64 * h, 64 * h + 64)
                    if n > 0:
                        mm(psY[:, hs], AR[:, n, 1, :], Hbz[:, h, :], True, False, [ARb[n], Bs['Hb']], [Bs['psY']])
                    mm(psY[:, hs], W1[:, h, 0, :], Us[:, hs], n == 0, False, [W1B, Bs['Us']], [Bs['psY']])
                    mm(psY[:, hs], W1[:, h, 2, :], vt[:, hs], False, True, [W1B, vtB], [Bs['psY']])
                mm(psH, bk[:, 0, :], Us, True, False, [bkB, Bs['Us']], [Bs['psH']])
                mm(psH, bk[:, 1, :], vt, False, True, [bkB, vtB], [Bs['psH']])
                cp('act', Yp[g][:, n % 4, :], psY, [Bs['psY']], [Bs[f'Yp{g}']])
                if n > 0:
                    ts(HP, Hs, PCt[:, n:n + 1], None, ALU.mult, None, [Bs['H'], Bt_['PCt']], [Bs['HP']])
                for h in range(2):
                    hp = slice(64 * h, 64 * h + 64)
                    hs = slice(64 * h, 64 * h + 64)
                    if n > 0:
                        stt(Hs[hp, :], psH[hp, hs], PCt[hp, n:n + 1], HP[hp, :], ALU.mult, ALU.add, [Bs['psH'], Bs['HP'], Bt_['PCt']], [Bs['H']])
                    else:
                        ts(Hs[hp, :], psH[hp, hs], PCt[hp, n:n + 1], None, ALU.mult, None, [Bs['psH'], Bt_['PCt']], [Bs['H']])
                for h in range(2):
                    hp = slice(64 * h, 64 * h + 64)
                    cp('act', Hbz[hp, h, :], Hs[hp, :], [Bs['H']], [Bs['Hb']])
                yield

            def post(tg):
                g = tg % 2
                y3 = Yp[g].rearrange("p j (h e) -> p (j h) e", h=2)
                yB = Bs[f'Yp{g}']
                s1, s2, mean, msq, rstd = (rstp[:, 8 * i:8 * i + 8] for i in range(5))
                v8 = lambda a: a.rearrange("p (j e) -> p j e", j=8)
                S.op('dve', lambda e: e.tensor_reduce(out=s1, in_=y3, axis=AX.X, op=ALU.add), reads=[yB], writes=[Bs['rstp']])
                act(sqp, Yp[g].rearrange("p j c -> p (j c)"), AF.Square, [yB], [Bs['sqp']])
                S.op('dve', lambda e: e.tensor_reduce(out=s2, in_=v8(sqp), axis=AX.X, op=ALU.add), reads=[Bs['sqp']], writes=[Bs['rstp']])
                ts(mean, s1, 1.0 / 64, None, ALU.mult, None, [Bs['rstp']], [Bs['rstp']])
                tt(msq, mean, mean, ALU.mult, [Bs['rstp']], [Bs['rstp']])
                stt(rstd, s2, 1.0 / 64, msq, ALU.mult, ALU.subtract, [Bs['rstp']], [Bs['rstp']])
                rsqrt_tiny(rstd, rstd, 1.0, RWKV_GN_EPS, [Bs['rstp']], [Bs['rstp']])
                tt(v8(ynp), y3, mean.unsqueeze(2).to_broadcast([128, 8, 64]), ALU.subtract, [yB, Bs['rstp']], [Bs['ynp']])
                tt(v8(ynp), v8(ynp), rstd.unsqueeze(2).to_broadcast([128, 8, 64]), ALU.mult, [Bs['ynp'], Bs['rstp']], [Bs['ynp']])
                y4 = ynp.rearrange("p (j c) -> p j c", j=4)
                tt(y4, y4, lnxw[:, p * 128:(p + 1) * 128].unsqueeze(1).to_broadcast([128, 4, 128]), ALU.mult, [Bs['ynp'], B_lw], [Bs['ynp']])
                tt(y4, y4, lnxb[:, p * 128:(p + 1) * 128].unsqueeze(1).to_broadcast([128, 4, 128]), ALU.add, [Bs['ynp'], B_lw], [Bs['ynp']])
                for j in range(4):
                    n = 4 * tg + j
                    cs = slice(n * 128, (n + 1) * 128)
                    mm(psB[:, 2 * j:2 * j + 2], rkrT[:, cs], sel, True, True, [rkb[tg], B_const], [Bs['psB']])
                    mm(psG[:, j * 128:(j + 1) * 128], L1g[:, cs], G2sb[:, p * 128:(p + 1) * 128], True, True, [L1B[tg], B_lw], [Bs['psG']])
                cp('act', sB, psB, [Bs['psB']], [Bs['sB']])
                tt(v8(bon), Vt[g].rearrange("p j (h e) -> p (j h) e", h=2), sB.unsqueeze(2).to_broadcast([128, 8, 64]), ALU.mult,
                   [Bs[f'Vt{g}'], Bs['sB']], [Bs['bon']])
                tt(ynp, ynp, bon, ALU.add, [Bs['ynp'], Bs['bon']], [Bs['ynp']])
                tt(yop.rearrange("p j c -> p (j c)"), ynp, psG, ALU.mult, [Bs['ynp'], Bs['psG']], [Bs['yop']])
                for j in range(4):
                    S.op('pe', lambda e, j=j: e.transpose(out=pT2[:, j, :], in_=yop[:, j, :], identity=ident), reads=[Bs['yop'], B_const], writes=[Bs['pT2']])
                cp('act', yT[:, p, tg * 512:(tg + 1) * 512], pT2.rearrange("p j t -> p (j t)"), [Bs['pT2']], [yTb[p][4 * tg + j] for j in range(4)])

            return local, chain, post

        for i_ in range(2):
            S.op('pool', lambda e, i_=i_: e.memset(BKz[i_], 0.0), writes=[Bs[f'BKz{i_}']])
        NP = DBG.get('pairs', 4)
        for tb in range(4):
            prep_block(0, tb)
        scans = [make_scan(p) for p in range(NP)]

        def drain(g):
            for _ in g:
                pass
        S.op('pool', lambda e: e.memset(Hs, 0.0), writes=[Bs['H']])
        S.op('pool', lambda e: e.memset(Hbz, 0.0), writes=[Bs['Hb']])
        drain(scans[0][0](0))
        for p in range(NP):
            local, chain, post = scans[p]
            for n in range(NT):
                a = chain(n)
                if n + 1 < NT:
                    b = local(n + 1)
                elif p + 1 < NP:
                    b = scans[p + 1][0](0)
                else:
                    b = iter(())
                done_a = done_b = False
                while not (done_a and done_b):
                    if not done_a:
                        try:
                            next(a)
                        except StopIteration:
                            done_a = True
                    if not done_b:
                        try:
                            next(b)
                        except StopIteration:
                            done_b = True
                if n % 4 == 3:
                    post(n // 4)
                    if p + 1 < NP:
                        prep_block(p + 1, n // 4)
            if p + 1 < NP:
                S.op('dve', lambda e: e.memset(Hs, 0.0), writes=[Bs['H']])
                S.op('pool', lambda e: e.memset(Hbz, 0.0), writes=[Bs['Hb']])
        tap('yT', yT, [128, 8, T], [b for l in yTb for b in l])
        S.barrier()


    if 4 in phases:
        S.barrier()
        A.reset(NORM_END)
        xres = A.take([NT, D], F32)
        xresB = [Buf(f'xres{n}') for n in range(NT)]
        P4 = A.mark()
        wout = A.take([8, D], BF16)
        woutB = Buf('wout')
        wo_v = w_out.rearrange("(c p) n -> p c n", p=128)
        for c in range(8):
            dma('pool', wout[:, c, :], wo_v[:, c, :], [], [woutB])
        dma('sp', gtab, bct_d[:, 1024:2048], [], [B_gtab])
        for n in range(NT):
            dma('sp', xst[n % 3], xv[n], [], [xstB[n % 3]])
            pb = 2 * (n % 2)
            for half in range(2):
                for c in range(8):
                    mm(bank[pb + half], yT[:, c, n * 128:(n + 1) * 128], wout[:, c, half * 512:(half + 1) * 512], c == 0, c == 7,
                       [yTb[c][n], woutB], [bankB[pb + half]])
            tt(xres[:, n, :], pp[n % 2][:], xst[n % 3], ALU.add, [bankB[pb], bankB[pb + 1], xstB[n % 3]], [xresB[n]])
            norm_stats(n, xres[:, n, :], xresB[n], 1)
        norm_rstd(1)
        for n in range(NT):
            norm_apply(n, xres[:, n, :], xresB[n], 1, 4 + n % 2)
        tap('xres', xres, [128, NT, D], xresB)

    if 5 in phases:
        S.barrier()
        A.reset(P4)
        hid = yT[:, 0:6, :]
        hidB = [Buf(f'hid{i}') for i in range(4)]
        wgu = [A.take([2, 8, 256], BF16) for _ in range(2)]
        wguB = [Buf(f'wgu{i}') for i in range(2)]
        wd = A.take([6, D], BF16)
        wdB = Buf('wd')
        gs = [A.take([514], F32) for _ in range(2)]
        gsB = [Buf(f'gs{i}') for i in range(2)]
        acc = [A.take([512], F32) for _ in range(2)]
        accB = [Buf(f'acc{i}') for i in range(2)]
        sl = [A.take([512], F32) for _ in range(2)]
        slB = [Buf(f'sl{i}') for i in range(2)]
        ost = [xst[0], xst[1]]
        ostB = [xstB[0], xstB[1]]
        dma('sp', gtab, bct_d[:, 2048:3072], [], [B_gtab])
        wg_v = wg_d.rearrange("(c p) n -> p c n", p=128)
        wu_v = wu_d.rearrange("(c p) n -> p c n", p=128)
        wd_v = wd_d.rearrange("(m p) n -> p m n", p=128)
        quarters = [(0, 6), (6, 6), (12, 5), (17, 5)]

        def load_wgu(m):
            wb_ = (m // 2) % 2
            dma('pool', wgu[wb_][:, 0, :, :], wg_v[:, :, m * 128:(m + 2) * 128], [], [wguB[wb_]])
            dma('pool', wgu[wb_][:, 1, :, :], wu_v[:, :, m * 128:(m + 2) * 128], [], [wguB[wb_]])
        load_wgu(0)
        it = 0
        for qi, (m0, nq) in enumerate(quarters):
            dma('pool', wd[:, 0:nq, :], wd_v[:, m0:m0 + nq, :], [], [wdB])
            for ml in range(nq):
                m = m0 + ml
                wb = (m // 2) % 2
                if m % 2 == 0 and m + 2 < NFF:
                    load_wgu(m + 2)
                mc = slice((m % 2) * 128, (m % 2) * 128 + 128)
                vf = V_FFN + 4 * m
                cw = lambda j: vecs[:, vf + j:vf + j + 1]
                for blk in range(4):
                    g_, gB = gs[blk % 2], gsB[blk % 2]
                    a_, aB = acc[it % 2], accB[it % 2]
                    s_, sB_ = sl[it % 2], slB[it % 2]
                    pg, pu = 2 * (it % 4), 2 * (it % 4) + 1
                    it += 1
                    rd = [hTb[4 * blk + i] for i in range(4)] + [wguB[wb]]
                    for c in range(8):
                        mm(bank[pg], wgu[wb][:, 0, c, mc], hT[:, c, 1 + blk * 512:1 + (blk + 1) * 512], c == 0, c == 7, rd, [bankB[pg]])
                    for c in range(8):
                        mm(bank[pu], wgu[wb][:, 1, c, mc], hT[:, c, 1 + blk * 512:1 + (blk + 1) * 512], c == 0, c == 7, rd, [bankB[pu]])
                    if blk == 0:
                        S.op('pool', lambda e, g_=g_: e.memset(g_[:, 0:2], 0.0), writes=[gB])
                    else:
                        gp = gs[(blk - 1) % 2]
                        S.op('pool', lambda e, g_=g_, gp=gp: e.tensor_copy(out=g_[:, 0:2], in_=gp[:, 512:514]), reads=[gsB[(blk - 1) % 2]], writes=[gB])
                    act(g_[:, 2:514], bank[pg], AF.Copy, [bankB[pg]], [gB])
                    act(a_, bank[pg], AF.Identity, [bankB[pg], B_const], [aB], bias=cw(3), scale=cw(2))
                    stt(a_, g_[:, 1:513], cw(1), a_, ALU.mult, ALU.add, [gB, aB, B_const], [aB])
                    stt(a_, g_[:, 0:512], cw(0), a_, ALU.mult, ALU.add, [gB, aB, B_const], [aB])
                    act(s_, a_, AF.Silu, [aB], [sB_])
                    tt(hid[:, ml, blk * 512:(blk + 1) * 512], s_, bank[pu], ALU.mult, [sB_, bankB[pu]], [hidB[blk]])
            last = qi == len(quarters) - 1
            for n in range(NT):
                pb = 2 * (n % 4)
                for half in range(2):
                    for ml in range(nq):
                        mm(bank[pb + half], hid[:, ml, n * 128:(n + 1) * 128], wd[:, ml, half * 512:(half + 1) * 512], ml == 0, ml == nq - 1,
                           [hidB[n // 4], wdB], [bankB[pb + half]])
                tt(xres[:, n, :], pp[n % 4][:], xres[:, n, :], ALU.add, [bankB[pb], bankB[pb + 1], xresB[n]], [xresB[n]])
                if last:
                    ssn = ss_all[:, 2, n:n + 1]
                    rsn = rstd_all[:, 2, n:n + 1]
                    sB2 = statB[n % 4]
                    act(sqj, xres[:, n, :], AF.Square, [xresB[n]], [sqjB, sB2], accum=ssn)
                    rsqrt_tiny(rsn, ssn, 1.0 / D, NORM_EPS, [sB2], [sB2])
                    stt(ost[n % 2], xres[:, n, :], rsn, gtab, ALU.mult, ALU.mult, [xresB[n], sB2, B_gtab], [ostB[n % 2]])
                    dma('sp', ov[n], ost[n % 2], [ostB[n % 2]], [])

    S.barrier(('sp',))
    S.emit(st)
    st.close()
    return nc, tap_out, S, A


def _chunkcols(v):
    v = np.asarray(v, np.float32).reshape(-1, 128)
    return np.ascontiguousarray(v.T)


def prep_shared(inp):
    f = lambda k: np.ascontiguousarray(np.asarray(inp[k], np.float32)[0])
    vecs = np.zeros((128, NV), np.float32)
    vecs[:, V_MUW:V_MUW + 8] = _chunkcols(f("rwkv_mu_w"))
    vecs[:, V_MUA:V_MUA + 8] = _chunkcols(f("rwkv_mu_a"))
    vecs[:, V_MUG:V_MUG + 8] = _chunkcols(f("rwkv_mu_g"))
    names = ["rwkv_mu_r", "rwkv_mu_k", "rwkv_mu_v", "rwkv_w0", "rwkv_a0", "rwkv_k_k", "rwkv_k_a", "rwkv_r_k"]
    for j, nm in enumerate(names):
        cc = _chunkcols(f(nm).reshape(-1))
        for p in range(4):
            vecs[:, V_PAIR + 8 * p + j] = cc[:, p]
    cw = f("ffn_conv_w").reshape(3, DFF)
    cbias = f("ffn_conv_b")
    for j in range(3):
        cc = _chunkcols(cw[j])
        for m in range(NFF):
            vecs[:, V_FFN + 4 * m + j] = cc[:, m]
    cc = _chunkcols(cbias)
    for m in range(NFF):
        vecs[:, V_FFN + 4 * m + 3] = cc[:, m]
    row = np.concatenate([f("norm_mix_g"), f("norm_ffn_g"), np.asarray(inp["norm_final_g"], np.float32),
                          f("rwkv_lnx_w"), f("rwkv_lnx_b"), f("ret_gn_w")])
    bct = np.ascontiguousarray(np.broadcast_to(row[None, :], (128, row.shape[0])))
    cf, cb = make_consts()
    shared = {
        "w_in": f("w_in"), "w_out": f("w_out"), "ffn_w_gate": f("ffn_w_gate"), "ffn_w_up": f("ffn_w_up"),
        "ffn_w_down": f("ffn_w_down"), "rwkv_w1": f("rwkv_w1"), "rwkv_a1": f("rwkv_a1"), "rwkv_g1": f("rwkv_g1"),
        "rwkv_w2": f("rwkv_w2"), "rwkv_a2": f("rwkv_a2"), "rwkv_g2": f("rwkv_g2"),
        "vecs": vecs, "bct": bct, "cf": cf, "cb": cb,
    }
    return shared


_PROG = None


def kernel(**inputs):
    global _PROG
    if _PROG is None:
        _PROG = build_program()[0]
    shared = prep_shared(inputs)
    xs = np.asarray(inputs["x"], np.float32)
    in_maps = [dict(shared, x=np.ascontiguousarray(xs[b])) for b in range(8)]
    res = run_bass_kernel_spmd(_PROG, in_maps, core_ids=list(range(8)))
    return np.stack([np.asarray(r["out"], np.float32) for r in res.results], axis=0)
```

```python
import numpy as np
import ml_dtypes
from contextlib import ExitStack
import concourse.bass as bass
import concourse.mybir as mybir
from concourse.bass_utils import run_bass_kernel_spmd

F32 = mybir.dt.float32
BF16 = mybir.dt.bfloat16
AF = mybir.ActivationFunctionType
ALU = mybir.AluOpType
AX = mybir.AxisListType

QUEUES = ('sp', 'act', 'pool', 'pe', 'dve')

T = 2048
D = 1024
NT = 16
DFF = 2816
NFF = 22
C0 = float(np.exp(-0.5))
NORM_EPS = 1e-6
RWKV_GN_EPS = 64e-5
RET_GN_EPS = 1e-5


class Buf:
    __slots__ = ('name', 'w', 'r')

    def __init__(self, name=''):
        self.name = name
        self.w = None
        self.r = {}


class _Op:
    __slots__ = ('q', 's', 'idx', 'fn', 'waits', 'inc', 'dma')


class Sched:
    def __init__(self, nc):
        self.nc = nc
        self.ops = {q: [] for q in QUEUES}
        self.streams = {}
        self.clock = {q: {} for q in QUEUES}
        self.opclock = {}
        self.nwaits = 0
        self.nops = 0

    def op(self, q, fn, reads=(), writes=(), dma=False):
        if dma:
            ref = writes[0] if len(writes) else (reads[0] if len(reads) else None)
            s = 'dq_' + (ref.name if ref is not None and ref.name else q)
        else:
            s = q
        deps = {}

        def need(st, i):
            if deps.get(st, 0) < i:
                deps[st] = i
        for b in reads:
            if b.w is not None:
                st, i = b.w
                if st == q and q == 'pe':
                    continue
                need(st, i)
        for b in writes:
            if b.w is not None:
                st, i = b.w
                if not (st == q and not dma):
                    need(st, i)
            for st, i in b.r.items():
                if st == q and not dma:
                    continue
                need(st, i)
        ck = self.clock[q]
        waits = []
        for st, i in deps.items():
            if ck.get(st, 0) >= i:
                continue
            waits.append((st, i))
            oc = self.opclock[(st, i)]
            for k, v in oc.items():
                if ck.get(k, 0) < v:
                    ck[k] = v
            if ck.get(st, 0) < i:
                ck[st] = i
            self.streams[st][i - 1].inc = True
        o = _Op()
        o.q = q
        o.s = s
        o.fn = fn
        o.waits = waits
        o.inc = dma
        o.dma = dma
        lst = self.streams.setdefault(s, [])
        lst.append(o)
        o.idx = len(lst)
        self.opclock[(s, o.idx)] = dict(ck)
        self.ops[q].append(o)
        self.nwaits += len(waits)
        self.nops += 1
        for b in writes:
            b.w = (s, o.idx)
            b.r = {}
        for b in reads:
            if b.r.get(s, 0) < o.idx:
                b.r[s] = o.idx
        return o

    def barrier(self, queues=QUEUES):
        tips = {s: len(l) for s, l in self.streams.items() if l}
        for q in queues:
            ck = self.clock[q]
            waits = []
            for s, i in tips.items():
                if s == q and q == 'pe':
                    continue
                if ck.get(s, 0) >= i:
                    continue
                waits.append((s, i))
                self.streams[s][i - 1].inc = True
            for s, i in waits:
                oc = self.opclock[(s, i)]
                for k, v in oc.items():
                    if ck.get(k, 0) < v:
                        ck[k] = v
                ck[s] = i
            if waits:
                o = _Op()
                o.q = q
                o.s = None
                o.fn = None
                o.waits = waits
                o.inc = False
                o.dma = False
                self.ops[q].append(o)

    def emit(self, stack):
        nc = self.nc
        sems = {s: stack.enter_context(nc.semaphore('sem_' + s)) for s in self.streams}
        cnt = {}
        for s, lst in self.streams.items():
            c = 0
            for o in lst:
                if o.dma:
                    c += 16
                elif o.inc:
                    c += 1
                cnt[(s, o.idx)] = c
        self.final_counts = {s: (cnt[(s, len(l))] if l else 0) for s, l in self.streams.items()}
        block = stack.enter_context(nc.Block())

        def run(q, eng):
            for o in self.ops[q]:
                for st, i in o.waits:
                    eng.wait_ge(sems[st], cnt[(st, i)])
                if o.fn is None:
                    continue
                ins = o.fn(eng)
                if o.dma:
                    ins.then_inc(sems[o.s], 16)
                elif o.inc:
                    ins.then_inc(sems[o.s], 1)

        @block.sync
        def _(e):
            run('sp', e)

        @block.scalar
        def _(e):
            run('act', e)

        @block.gpsimd
        def _(e):
            run('pool', e)

        @block.tensor
        def _(e):
            run('pe', e)

        @block.vector
        def _(e):
            run('dve', e)


class Arena:
    def __init__(self, ap, nbytes):
        self.ap = ap
        self.nbytes = nbytes
        self.off = 0
        self.peak = 0

    def take(self, shape, dt):
        esz = 4 if dt == F32 else 2
        n = int(np.prod(shape))
        nb = (n * esz + 63) // 64 * 64
        assert self.off + nb <= self.nbytes, ("arena overflow", self.off, nb, self.nbytes)
        v = self.ap[:, self.off // 4:(self.off + nb) // 4]
        if dt != F32:
            v = v.bitcast(dt)
        v = v[:, 0:n]
        if len(shape) == 2:
            v = v.rearrange("p (a b) -> p a b", a=shape[0])
        elif len(shape) == 3:
            v = v.rearrange("p (a b c) -> p a b c", a=shape[0], b=shape[1])
        elif len(shape) == 4:
            v = v.rearrange("p (a b c d) -> p a b c d", a=shape[0], b=shape[1], c=shape[2])
        self.off += nb
        self.peak = max(self.peak, self.off)
        return v

    def mark(self):
        return self.off

    def reset(self, m):
        self.off = m


V_MUW, V_MUA, V_MUG = 0, 8, 16
V_PAIR = 24
V_FFN = 56
NV = 56 + 4 * NFF
CB_ID, CB_MRET, CB_M4, CB_ML, CB_SEL, CB_ONES = 0, 128, 256, 768, 896, 900
NCB = 1028
CF_COS, CF_SIN, CF_NSIN, CF_XIT, CF_KAT, CF_KAPG, CF_GC = 0, 1024, 2048, 3072, 3584, 4096, 4100
NCF = 4104


def make_consts():
    f32 = np.float32
    p = np.arange(128)
    cf = np.zeros((128, NCF), f32)
    half = 64
    inv_freq = (10000.0 ** (-np.arange(half, dtype=np.float64) / half))
    pos = (np.arange(NT)[None, :] * 128 + p[:, None]).astype(np.float64)
    ang = pos[:, :, None] * inv_freq[None, None, :]
    cf[:, CF_COS:CF_COS + 1024] = np.cos(ang).reshape(128, -1)
    cf[:, CF_SIN:CF_SIN + 1024] = np.sin(ang).reshape(128, -1)
    cf[:, CF_NSIN:CF_NSIN + 1024] = -np.sin(ang).reshape(128, -1)
    lg = np.log(1.0 - 2.0 ** (-5.0 - np.arange(4, dtype=np.float64)))
    i = np.arange(128, dtype=np.float64)
    xi = np.exp((i[None, :] + 1.0) * lg[:, None])
    ka = np.exp(-(i[None, :] + 1.0) * lg[:, None]) * (128.0 ** -0.5)
    cf[:, CF_XIT:CF_XIT + 512] = np.broadcast_to(xi.reshape(1, 512), (128, 512))
    cf[:, CF_KAT:CF_KAT + 512] = np.broadcast_to(ka.reshape(1, 512), (128, 512))
    gC = np.exp(128.0 * lg)
    cf[:, CF_KAPG:CF_KAPG + 4] = (ka.T * gC[None, :])
    cf[:, CF_GC:CF_GC + 4] = gC[None, :]
    cb = np.zeros((128, NCB), f32)
    cb[:, CB_ID:CB_ID + 128] = np.eye(128)
    r = p[:, None]
    c = p[None, :]
    cb[:, CB_MRET:CB_MRET + 128] = (r <= c)
    strict = (r < c).astype(f32)
    incl = (r <= c).astype(f32)
    cb[:, CB_M4:CB_M4 + 512] = np.concatenate([strict, incl, strict, incl], axis=1)
    cb[:, CB_ML:CB_ML + 128] = (c < r)
    cb[0:64, CB_SEL] = 1.0
    cb[64:128, CB_SEL + 1] = 1.0
    cb[0:64, CB_ONES:CB_ONES + 64] = 1.0
    cb[64:128, CB_ONES + 64:CB_ONES + 128] = 1.0
    return cf, cb.astype(ml_dtypes.bfloat16)


DBG = {'ret_chunks': NT, 'ret_steps': 99}


def build_program(taps=None, phases=(1, 2, 3, 4, 5)):
    nc = bass.Bass("TRN2", target_bir_lowering=False)

    def din(name, shape, dt=F32):
        return nc.dram_tensor(name, list(shape), dt, kind="ExternalInput").ap()
    x = din("x", [T, D])
    w_in = din("w_in", [D, 3584])
    w_out = din("w_out", [D, D])
    wg_d = din("ffn_w_gate", [D, DFF])
    wu_d = din("ffn_w_up", [D, DFF])
    wd_d = din("ffn_w_down", [DFF, D])
    w1_d = din("rwkv_w1", [D, 64])
    a1_d = din("rwkv_a1", [D, 64])
    g1_d = din("rwkv_g1", [D, 128])
    w2_d = din("rwkv_w2", [64, 512])
    a2_d = din("rwkv_a2", [64, 512])
    g2_d = din("rwkv_g2", [128, 512])
    vecs_d = din("vecs", [128, NV])
    bct_d = din("bct", [128, 4608])
    cf_d = din("cf", [128, NCF])
    cb_d = din("cb", [128, NCB], BF16)
    out = nc.dram_tensor("out", [T, D], F32, kind="ExternalOutput").ap()
    tap_out = {}
    taps = taps or {}

    S = Sched(nc)
    st = ExitStack()
    ARENA_BYTES = 200 * 1024
    arena_t = st.enter_context(nc.sbuf_tensor("arena", [128, ARENA_BYTES // 4], F32))
    A = Arena(arena_t[:], ARENA_BYTES)
    pp = [st.enter_context(nc.psum_tensor(f"pp{i}", [128, 1024], F32)) for i in range(4)]
    bank = [pp[i // 2][:, (i % 2) * 512:(i % 2) * 512 + 512] for i in range(8)]
    bankB = [Buf(f"bank{i}") for i in range(8)]

    def bankbf(i):
        return bank[i].bitcast(BF16)

    def act(out_, in_, func, r, w, bias=None, scale=None, accum=None):
        kw = {}
        if bias is not None:
            kw['bias'] = bias
        if scale is not None:
            kw['scale'] = scale
        if accum is not None:
            kw['accum_out'] = accum
        S.op('act', lambda e: e.activation(out=out_, in_=in_, func=func, **kw), reads=r, writes=w)

    def tt(out_, a, b, op, r, w, q='dve'):
        S.op(q, lambda e: e.tensor_tensor(out=out_, in0=a, in1=b, op=op), reads=r, writes=w)

    def ts(out_, a, s1, s2, op0, op1, r, w, q='dve'):
        if s2 is None:
            S.op(q, lambda e: e.tensor_scalar(out=out_, in0=a, scalar1=s1, scalar2=None, op0=op0), reads=r, writes=w)
        else:
            S.op(q, lambda e: e.tensor_scalar(out=out_, in0=a, scalar1=s1, scalar2=s2, op0=op0, op1=op1), reads=r, writes=w)

    def stt(out_, a, s, b, op0, op1, r, w):
        S.op('dve', lambda e: e.scalar_tensor_tensor(out=out_, in0=a, scalar=s, in1=b, op0=op0, op1=op1), reads=r, writes=w)

    def mm(out_, lhsT, rhs, start, stop, r, w):
        S.op('pe', lambda e: e.matmul(out=out_, lhsT=lhsT, rhs=rhs, start=start, stop=stop), reads=r, writes=w)

    def mm2(out_, lhsT, rhs, start, stop, r, w):
        if lhsT.shape[0] == 128:
            mm(out_, lhsT[0:64], rhs[0:64], start, False, r, w)
            mm(out_, lhsT[64:128], rhs[64:128], False, stop, r, w)
        else:
            mm(out_, lhsT, rhs, start, stop, r, w)

    def dma(q, out_, in_, r, w, **kw):
        S.op(q, lambda e: e.dma_start(out=out_, in_=in_, **kw), reads=r, writes=w, dma=True)

    def cp(q, out_, in_, r, w):
        if q == 'act':
            act(out_, in_, AF.Copy, r, w)
        else:
            S.op(q, lambda e: e.tensor_copy(out=out_, in_=in_), reads=r, writes=w)

    def rsqrt_tiny(dst, src, scale, eps, r, w):
        ts(dst, src, scale, eps, ALU.mult, ALU.add, r, w)
        act(dst, dst, AF.Ln, w, w)
        act(dst, dst, AF.Exp, w, w, scale=-0.5)

    hT = A.take([8, T + 1], BF16)
    yT = A.take([8, T], BF16)
    cb = A.take([NCB], BF16)
    vecs = A.take([NV], F32)
    om = A.take([NV], F32)
    mhalf = A.take([4], F32)
    gtab = A.take([1024], F32)
    stat = A.take([64], F32)
    ss_all = A.take([3, NT], F32)
    rstd_all = A.take([3, NT], F32)
    B_const = Buf('const')
    B_gtab = Buf('gtab')
    hTb = [Buf(f'hT{n}') for n in range(NT)]
    yTb = [[Buf(f'yT{c}_{n}') for n in range(NT)] for c in range(8)]
    ident = cb[:, CB_ID:CB_ID + 128]
    PERSIST = A.mark()

    def tap(name, ap, shape, reads):
        if name in taps:
            d = nc.dram_tensor("tap_" + name, list(shape), ap.dtype, kind="ExternalOutput").ap()
            tap_out[name] = d
            dma('sp', d, ap, reads, [])

    dma('sp', cb, cb_d, [], [B_const])
    dma('sp', vecs, vecs_d, [], [B_const])
    dma('sp', gtab, bct_d[:, 0:1024], [], [B_gtab])
    S.op('pool', lambda e: e.memset(mhalf, -0.5), writes=[B_const])
    ts(om, vecs, -1.0, 1.0, ALU.mult, ALU.add, [B_const], [B_const])
    S.op('pool', lambda e: e.memset(hT[:, :, 0:1], 0.0), writes=[hTb[0]])

    xst = [A.take([D], F32) for _ in range(3)]
    xstB = [Buf(f'xst{i}') for i in range(3)]
    hb = [A.take([D], BF16) for _ in range(2)]
    hbB = [Buf(f'hb{i}') for i in range(2)]
    sqj = A.take([D], BF16)
    sqjB = Buf('sqj')
    statB = [Buf(f'stat{i}') for i in range(4)]
    NORM_END = A.mark()

    ssB = [Buf(f'ss{i}') for i in range(3)]
    rsB = [Buf(f'rs{i}') for i in range(3)]

    def norm_stats(n, src, srcB, which):
        act(sqj, src, AF.Square, [srcB], [sqjB, ssB[which]], accum=ss_all[:, which, n:n + 1])

    def norm_rstd(which):
        rsqrt_tiny(rstd_all[:, which, :], ss_all[:, which, :], 1.0 / D, NORM_EPS, [ssB[which]], [rsB[which]])

    def norm_apply(n, src, srcB, which, pbank):
        h = hb[n % 2]
        stt(h, src, rstd_all[:, which, n:n + 1], gtab, ALU.mult, ALU.mult, [srcB, rsB[which], B_gtab], [hbB[n % 2]])
        pt = bankbf(pbank).rearrange("p (c t) -> p c t", c=8)
        for c in range(8):
            S.op('pe', lambda e, c=c: e.transpose(out=pt[:, c, :], in_=h[:, c * 128:(c + 1) * 128], identity=ident),
                 reads=[hbB[n % 2], B_const], writes=[bankB[pbank]])
        cp('act', hT[:, :, 1 + n * 128:1 + (n + 1) * 128], pt, [bankB[pbank]], [hTb[n]])

    xv = x.rearrange("(n p) d -> n p d", p=128)
    ov = out.rearrange("(n p) d -> n p d", p=128)
    if 2 in phases:
        cf = A.take([NCF], F32)
        dma('sp', cf, cf_d, [], [B_const])
        wret = A.take([8, 2048], BF16)
        wretB = Buf('wret')
        wv = w_in.rearrange("(c p) n -> p c n", p=128)
        for c in range(8):
            dma('pool', wret[:, c, :], wv[:, c, 1536:3584], [], [wretB])
        P2START = A.mark()
    for n in range(NT):
        dma('sp', xst[n % 3], xv[n], [], [xstB[n % 3]])
        norm_stats(n, xst[n % 3], xstB[n % 3], 0)
    norm_rstd(0)
    for n in range(NT):
        dma('sp', xst[n % 3], xv[n], [], [xstB[n % 3]])
        norm_apply(n, xst[n % 3], xstB[n % 3], 0, n % 2)
    tap('hT', hT, [128, 8, T + 1], hTb)

    if 2 in phases:
        A.reset(P2START)
        gnw = A.take([512], F32)
        dma('sp', gnw, bct_d[:, 4096:4608], [], [B_const])
        qa = A.take([512], F32)
        qb = A.take([512], F32)
        qrot = A.take([512], BF16)
        krot = A.take([512], BF16)
        qT = A.take([4, 128], BF16)
        kT = A.take([4, 128], BF16)
        PT = A.take([4, 128], BF16)
        Vb = A.take([512], BF16)
        Vk = A.take([512], BF16)
        R = A.take([512], F32)
        Rt = A.take([512], F32)
        Rb = A.take([512], BF16)
        sqy = A.take([512], F32)
        yn = A.take([512], F32)
        sgt = A.take([512], F32)
        yo = A.take([512], BF16)
        rst = A.take([32], F32)
        Bq = {k: Buf('r_' + k) for k in ['qa', 'qb', 'qrot', 'krot', 'qT', 'kT', 'PT', 'Vb', 'Vk', 'R', 'Rt', 'Rb', 'sqy', 'yn', 'sgt', 'yo', 'rst']}
        kapg_bc = cf[:, CF_KAPG:CF_KAPG + 4].unsqueeze(2).to_broadcast([128, 4, 128])
        gC_bc = cf[:, CF_GC:CF_GC + 4].unsqueeze(2).to_broadcast([128, 4, 128])
        xiT = cf[:, CF_XIT:CF_XIT + 512].rearrange("p (h t) -> p h t", h=4)
        kaT = cf[:, CF_KAT:CF_KAT + 512].rearrange("p (h t) -> p h t", h=4)
        mret_bc = cb[:, CB_MRET:CB_MRET + 128].unsqueeze(1).to_broadcast([128, 4, 128])
        PQ, PK, PV, PG, PTB, PS, PY, PKV = range(8)

        def v4(ap):
            return ap.rearrange("p (h e) -> p h e", h=4)

        def rot(ps, psB, dst, dstB, n):
            cosb = cf[:, CF_COS + n * 64:CF_COS + (n + 1) * 64].unsqueeze(1).unsqueeze(1).to_broadcast([128, 4, 2, 64])
            sinb = cf[:, CF_SIN + n * 64:CF_SIN + (n + 1) * 64].unsqueeze(1).to_broadcast([128, 4, 64])
            nsinb = cf[:, CF_NSIN + n * 64:CF_NSIN + (n + 1) * 64].unsqueeze(1).to_broadcast([128, 4, 64])
            p4 = ps.rearrange("p (h two f) -> p h two f", h=4, two=2)
            tt(qa.rearrange("p (h two f) -> p h two f", h=4, two=2), p4, cosb, ALU.mult, [psB, B_const], [Bq['qa']])
            qb4 = qb.rearrange("p (h two f) -> p h two f", h=4, two=2)
            tt(qb4[:, :, 0, :], p4[:, :, 1, :], nsinb, ALU.mult, [psB, B_const], [Bq['qb']])
            tt(qb4[:, :, 1, :], p4[:, :, 0, :], sinb, ALU.mult, [psB, B_const], [Bq['qb']])
            tt(dst, qa, qb, ALU.add, [Bq['qa'], Bq['qb']], [dstB])

        for n in range(DBG['ret_chunks']):
            RS = DBG['ret_steps']
            tok = slice(1 + n * 128, 1 + (n + 1) * 128)
            for j, pb in enumerate((PQ, PK, PV, PG)):
                for c in range(8):
                    mm(bank[pb], hT[:, c, tok], wret[:, c, j * 512:(j + 1) * 512], c == 0, c == 7,
                       [hTb[n], wretB], [bankB[pb]])
            if RS < 2:
                continue
            rot(bank[PQ], bankB[PQ], qrot, Bq['qrot'], n)
            rot(bank[PK], bankB[PK], krot, Bq['krot'], n)
            if RS < 3:
                continue
            ptb = bankbf(PTB).rearrange("p (c t) -> p c t", c=8)
            for h in range(4):
                S.op('pe', lambda e, h=h: e.transpose(out=ptb[:, h, :], in_=qrot[:, h * 128:(h + 1) * 128], identity=ident),
                     reads=[Bq['qrot'], B_const], writes=[bankB[PTB]])
            for h in range(4):
                S.op('pe', lambda e, h=h: e.transpose(out=ptb[:, 4 + h, :], in_=krot[:, h * 128:(h + 1) * 128], identity=ident),
                     reads=[Bq['krot'], B_const], writes=[bankB[PTB]])
            tt(qT, ptb[:, 0:4, :], xiT, ALU.mult, [bankB[PTB], B_const], [Bq['qT']])
            tt(kT, ptb[:, 4:8, :], kaT, ALU.mult, [bankB[PTB], B_const], [Bq['kT']])
            if RS < 4:
                continue
            ps4 = v4(bank[PS])
            for h in range(4):
                mm(ps4[:, h, :], kT[:, h, :], qT[:, h, :], True, True, [Bq['kT'], Bq['qT']], [bankB[PS]])
            tt(PT, ps4, mret_bc, ALU.mult, [bankB[PS], B_const], [Bq['PT']])
            if RS < 5:
                continue
            cp('act', Vb, bank[PV], [bankB[PV]], [Bq['Vb']])
            tt(v4(Vk), v4(bank[PV]), kapg_bc, ALU.mult, [bankB[PV], B_const], [Bq['Vk']])
            if RS < 6:
                continue
            py4 = v4(bank[PY])
            for h in range(4):
                mm(py4[:, h, :], PT[:, h, :], Vb[:, h * 128:(h + 1) * 128], True, n == 0, [Bq['PT'], Bq['Vb']], [bankB[PY]])
                if n > 0:
                    mm(py4[:, h, :], qT[:, h, :], Rb[:, h * 128:(h + 1) * 128], False, True, [Bq['qT'], Bq['Rb']], [bankB[PY]])
            if RS < 7:
                continue
            if n < NT - DBG.get('skiplast', 0):
                pkv4 = v4(bank[PKV])
                for h in range(4):
                    mm(pkv4[:, h, :], krot[:, h * 128:(h + 1) * 128], Vk[:, h * 128:(h + 1) * 128], True, True,
                       [Bq['krot'], Bq['Vk']], [bankB[PKV]])
                if n == 0:
                    cp('dve', R, bank[PKV], [bankB[PKV]], [Bq['R']])
                else:
                    tt(v4(Rt), v4(R), gC_bc, ALU.mult, [Bq['R'], B_const], [Bq['Rt']])
                    tt(R, Rt, bank[PKV], ALU.add, [Bq['Rt'], bankB[PKV]], [Bq['R']])
                cp('pool', Rb, R, [Bq['R']], [Bq['Rb']])
            if RS < 8:
                continue
            s1 = rst[:, 0:4]
            s2 = rst[:, 4:8]
            mean = rst[:, 8:12]
            msq = rst[:, 12:16]
            rstd = rst[:, 16:20]
            S.op('dve', lambda e: e.tensor_reduce(out=s1, in_=py4, axis=AX.X, op=ALU.add), reads=[bankB[PY]], writes=[Bq['rst']])
            act(sqy, bank[PY], AF.Square, [bankB[PY]], [Bq['sqy']])
            S.op('dve', lambda e: e.tensor_reduce(out=s2, in_=v4(sqy), axis=AX.X, op=ALU.add), reads=[Bq['sqy']], writes=[Bq['rst']])
            ts(mean, s1, 1.0 / 128, None, ALU.mult, None, [Bq['rst']], [Bq['rst']])
            tt(msq, mean, mean, ALU.mult, [Bq['rst']], [Bq['rst']])
            stt(rstd, s2, 1.0 / 128, msq, ALU.mult, ALU.subtract, [Bq['rst']], [Bq['rst']])
            rsqrt_tiny(rstd, rstd, 1.0, RET_GN_EPS, [Bq['rst']], [Bq['rst']])
            tt(v4(yn), py4, mean.unsqueeze(2).to_broadcast([128, 4, 128]), ALU.subtract, [bankB[PY], Bq['rst']], [Bq['yn']])
            tt(v4(yn), v4(yn), rstd.unsqueeze(2).to_broadcast([128, 4, 128]), ALU.mult, [Bq['yn'], Bq['rst']], [Bq['yn']])
            tt(yn, yn, gnw, ALU.mult, [Bq['yn'], B_const], [Bq['yn']])
            act(sgt, bank[PG], AF.Silu, [bankB[PG]], [Bq['sgt']])
            tt(yo, yn, sgt, ALU.mult, [Bq['yn'], Bq['sgt']], [Bq['yo']])
            if RS < 9:
                continue
            for h in range(4):
                S.op('pe', lambda e, h=h: e.transpose(out=ptb[:, h, :], in_=yo[:, h * 128:(h + 1) * 128], identity=ident),
                     reads=[Bq['yo'], B_const], writes=[bankB[PTB]])
            cp('act', yT[:, 4:8, n * 128:(n + 1) * 128], ptb[:, 0:4, :], [bankB[PTB]], [yTb[4 + h][n] for h in range(4)])
        if 3 not in phases:
            tap('yT', yT, [128, 8, T], [b for l in yTb for b in l])
        S.barrier()


    if 3 in phases:
        S.barrier()
        A.reset(PERSIST)
        wl_f = A.take([8, 128], F32)
        gl_f = A.take([8, 128], F32)
        W1A = A.take([8, 128], BF16)
        W1B = A.take([8, 128], BF16)
        G1A = A.take([8, 128], BF16)
        G1B = A.take([8, 128], BF16)
        W2sb = A.take([512], BF16)
        A2sb = A.take([512], BF16)
        G2sb = A.take([512], BF16)
        L1 = A.take([T], BF16)
        L1g = A.take([T], BF16)
        lnxw = A.take([512], F32)
        lnxb = A.take([512], F32)
        B_lw = Buf('loraw')
        L1B = [Buf(f'L1_{i}') for i in range(4)]
        dma('sp', wl_f[:, :, 0:64], w1_d.rearrange("(c p) k -> p c k", p=128), [], [B_lw])
        dma('sp', wl_f[:, :, 64:128], a1_d.rearrange("(c p) k -> p c k", p=128), [], [B_lw])
        dma('sp', gl_f, g1_d.rearrange("(c p) k -> p c k", p=128), [], [B_lw])
        dma('sp', lnxw, bct_d[:, 3072:3584], [], [B_lw])
        dma('sp', lnxb, bct_d[:, 3584:4096], [], [B_lw])
        S.op('pool', lambda e: e.memset(W2sb, 0.0), writes=[B_lw])
        S.op('pool', lambda e: e.memset(A2sb, 0.0), writes=[B_lw])
        dma('pool', W2sb[0:64, :], w2_d, [B_lw], [B_lw])
        dma('pool', A2sb[64:128, :], a2_d, [B_lw], [B_lw])
        dma('pool', G2sb, g2_d, [], [B_lw])

        def vb(tab, col, k):
            return tab[:, col:col + 8].unsqueeze(2).to_broadcast([128, 8, k])
        tt(W1A[:, :, 0:64], wl_f[:, :, 0:64], vb(om, V_MUW, 64), ALU.mult, [B_lw, B_const], [B_lw])
        tt(W1A[:, :, 64:128], wl_f[:, :, 64:128], vb(om, V_MUA, 64), ALU.mult, [B_lw, B_const], [B_lw])
        tt(W1B[:, :, 0:64], wl_f[:, :, 0:64], vb(vecs, V_MUW, 64), ALU.mult, [B_lw, B_const], [B_lw])
        tt(W1B[:, :, 64:128], wl_f[:, :, 64:128], vb(vecs, V_MUA, 64), ALU.mult, [B_lw, B_const], [B_lw])
        tt(G1A, gl_f, vb(om, V_MUG, 128), ALU.mult, [B_lw, B_const], [B_lw])
        tt(G1B, gl_f, vb(vecs, V_MUG, 128), ALU.mult, [B_lw, B_const], [B_lw])
        for tb in range(4):
            rd = [hTb[4 * tb + i] for i in range(4)] + ([hTb[4 * tb - 1]] if tb > 0 else []) + [B_lw]
            for (WA, WB, pb) in ((W1A, W1B, 0), (G1A, G1B, 1)):
                for c in range(8):
                    mm(bank[pb], WA[:, c, :], hT[:, c, 1 + tb * 512:1 + (tb + 1) * 512], c == 0, False, rd, [bankB[pb]])
                    mm(bank[pb], WB[:, c, :], hT[:, c, tb * 512:(tb + 1) * 512], False, c == 7, rd, [bankB[pb]])
            blk = slice(tb * 512, (tb + 1) * 512)
            act(L1[0:64, blk], bank[0][0:64, :], AF.Tanh, [bankB[0]], [L1B[tb]])
            act(L1[64:128, blk], bank[0][64:128, :], AF.Copy, [bankB[0]], [L1B[tb]])
            act(L1g[:, blk], bank[1], AF.Sigmoid, [bankB[1]], [L1B[tb]])

        wrkv = A.take([8, 3, 128], BF16)
        AR = A.take([NT, 2, 128], BF16)
        BT = A.take([T], BF16)
        KT = A.take([T], BF16)
        vT = A.take([T], BF16)
        rkrT = A.take([T], BF16)
        rm = A.take([513], F32)
        km = A.take([513], F32)
        vm = A.take([513], F32)
        tnames = ['r', 'k0', 'sg', 'asg', 'cum', 'P', 'invP', 'Pp', 'ssk', 'kk', 't1']
        tmp = {k: A.take([512], F32) for k in tnames}
        sqk = A.take([512], BF16)
        PCt = A.take([NT], F32)
        Xb = [A.take([2, 2, 128], BF16) for _ in range(2)]
        Nn = [A.take([2, 128], BF16) for _ in range(2)]
        W1s = [A.take([2, 3, 128], BF16) for _ in range(2)]
        TTs = [A.take([2, 128], BF16) for _ in range(2)]
        BK = [A.take([2, 128], BF16) for _ in range(2)]
        Vt = [A.take([4, 128], BF16) for _ in range(2)]
        Xs = A.take([128], BF16)
        Us = A.take([128], BF16)
        Hs = A.take([64], F32)
        HP = A.take([64], F32)
        Hbz = A.take([2, 64], BF16)
        BKz = [A.take([2, 2, 128], BF16) for _ in range(2)]
        Yp = [A.take([4, 128], F32) for _ in range(2)]
        sqp = A.take([512], F32)
        ynp = A.take([512], F32)
        bon = A.take([512], F32)
        sB = A.take([8], F32)
        yop = A.take([4, 128], BF16)
        rstp = A.take([64], F32)
        Bw = Buf('wrkv')
        Bt_ = {k: Buf('t_' + k) for k in tnames + ['rm', 'km', 'vm', 'sqk', 'PCt']}
        ARb = [Buf(f'AR{n}') for n in range(NT)]
        BTb = [Buf(f'BT{i}') for i in range(4)]
        KTb = [Buf(f'KT{i}') for i in range(4)]
        vTb = [Buf(f'vT{i}') for i in range(4)]
        rkb = [Buf(f'rk{i}') for i in range(4)]
        Bs = {k: Buf('s_' + k) for k in ['X0', 'X1', 'N0', 'N1', 'W10', 'W11', 'TT0', 'TT1', 'BK0', 'BK1', 'Vt0', 'Vt1', 'Xs', 'Us', 'H', 'HP', 'Hb', 'Yp0', 'Yp1', 'BKz0', 'BKz1',
                                          'sqp', 'ynp', 'bon', 'sB', 'yop', 'rstp',
                                          'ps1', 'ps2', 'psN', 'psL', 'pT', 'pT2', 'psX', 'psU', 'psH', 'psY', 'psB', 'psG']}
        for k_, b_ in (('ps1', 0), ('ps2', 2), ('psN', 2), ('psL', 3), ('pT', 4), ('pT2', 4), ('psX', 5), ('psU', 5), ('psH', 5),
                       ('psY', 6), ('psB', 6), ('psG', 7)):
            Bs[k_] = bankB[b_]
        wv3 = w_in.rearrange("(c p) n -> p c n", p=128)
        m4 = cb[:, CB_M4:CB_M4 + 512]
        mS_bc = cb[:, CB_M4:CB_M4 + 128].unsqueeze(1).to_broadcast([128, 2, 128])
        m3_bc = cb[:, CB_M4 + 128:CB_M4 + 512].unsqueeze(1).to_broadcast([128, 2, 384])
        mL_bc = cb[:, CB_ML:CB_ML + 128].unsqueeze(1).to_broadcast([128, 2, 128])
        id_bc = ident.unsqueeze(1).to_broadcast([128, 2, 128])
        sel = cb[:, CB_SEL:CB_SEL + 2]
        ones_bd = cb[:, CB_ONES:CB_ONES + 128]
        ps1 = pp[0][:].rearrange("p (h c) -> p h c", h=2)
        ps2 = bank[2][:, 0:256].rearrange("p (h s) -> p h s", h=2)
        psN = bank[2][:, 256:512].rearrange("p (h s) -> p h s", h=2)
        psL = bank[3].rearrange("p (h c) -> p h c", h=2)
        pTb = bankbf(4)
        pT3 = pTb[:, 0:384].rearrange("p (j t) -> p j t", j=3)
        pT2 = pTb[:, 512:1024].rearrange("p (j t) -> p j t", j=4)
        psX = bank[5][:, 0:128]
        psU = bank[5][:, 128:256]
        psH = bank[5][:, 256:384]
        psY = bank[6][:, 0:128]
        psB = bank[6][:, 128:136]
        psG = bank[7]

        def pair_setup(p):
            vp = V_PAIR + 8 * p
            col = lambda j: vecs[:, vp + j:vp + j + 1]
            ocol = lambda j: om[:, vp + j:vp + j + 1]
            return col, ocol

        def prep_block(p, tb):
            col, ocol = pair_setup(p)
            if tb == 0:
                for j in range(3):
                    dma('pool', wrkv[:, :, j, :], wv3[:, :, j * 512 + p * 128:j * 512 + (p + 1) * 128], [], [Bw])
                for nm_ in ('rm', 'km', 'vm'):
                    tl = {'rm': rm, 'km': km, 'vm': vm}[nm_]
                    S.op('pool', lambda e, tl=tl: e.memset(tl[:, 0:1], 0.0), writes=[Bt_[nm_]])
            blk = slice(tb * 512, (tb + 1) * 512)
            rd = [hTb[4 * tb + i] for i in range(4)] + [Bw]
            for j in range(3):
                for c in range(8):
                    mm(bank[j], wrkv[:, c, j, :], hT[:, c, 1 + tb * 512:1 + (tb + 1) * 512], c == 0, c == 7, rd, [bankB[j]])
            mm(bank[3], W2sb[:, p * 128:(p + 1) * 128], L1[:, blk], True, True, [B_lw, L1B[tb]], [bankB[3]])
            mm(bank[4], A2sb[:, p * 128:(p + 1) * 128], L1[:, blk], True, True, [B_lw, L1B[tb]], [bankB[4]])
            for j, (tl, nm_, dst, dstB) in enumerate(((rm, 'rm', tmp['r'], Bt_['r']), (km, 'km', tmp['k0'], Bt_['k0']), (vm, 'vm', vT[:, blk], vTb[tb]))):
                act(tl[:, 1:513], bank[j], AF.Copy, [bankB[j], B_const], [Bt_[nm_]], scale=col(j))
                stt(dst, bank[j], ocol(j), tl[:, 0:512], ALU.mult, ALU.add, [bankB[j], Bt_[nm_], B_const], [dstB])
                S.op('pool', lambda e, tl=tl: e.tensor_copy(out=tl[:, 0:1], in_=tl[:, 512:513]), reads=[Bt_[nm_]], writes=[Bt_[nm_]])
            r_, k0 = tmp['r'], tmp['k0']
            act(tmp['sg'], bank[3], AF.Sigmoid, [bankB[3], B_const], [Bt_['sg']], bias=col(3))
            act(tmp['asg'], bank[4], AF.Sigmoid, [bankB[4], B_const], [Bt_['asg']], bias=col(4))
            for ch in range(4):
                cs = slice(ch * 128, (ch + 1) * 128)
                S.op('dve', lambda e, cs=cs: e.tensor_tensor_scan(out=tmp['cum'][:, cs], data0=tmp['sg'][:, cs], data1=tmp['sg'][:, cs],
                                                                   initial=0.0, op0=ALU.add, op1=ALU.bypass),
                     reads=[Bt_['sg']], writes=[Bt_['cum']])
            act(tmp['P'], tmp['cum'], AF.Exp, [Bt_['cum']], [Bt_['P']], scale=-C0)
            act(tmp['invP'], tmp['cum'], AF.Exp, [Bt_['cum']], [Bt_['invP']], scale=C0)
            tt(tmp['sg'], tmp['cum'], tmp['sg'], ALU.subtract, [Bt_['cum'], Bt_['sg']], [Bt_['sg']])
            act(tmp['Pp'], tmp['sg'], AF.Exp, [Bt_['sg']], [Bt_['Pp']], scale=-C0)
            S.op('pool', lambda e, tb=tb: e.tensor_copy(out=PCt[:, tb * 4:(tb + 1) * 4],
                                                        in_=tmp['P'].rearrange("p (c t) -> p c t", c=4)[:, :, 127]),
                 reads=[Bt_['P']], writes=[Bt_['PCt']])
            act(sqk, k0, AF.Square, [Bt_['k0'], B_const], [Bt_['sqk']], scale=col(5))
            mm(bank[5], ones_bd, sqk, True, True, [Bt_['sqk'], B_const], [bankB[5]])
            act(tmp['ssk'], bank[5], AF.Ln, [bankB[5]], [Bt_['ssk']])
            act(tmp['ssk'], tmp['ssk'], AF.Exp, [Bt_['ssk']], [Bt_['ssk']], scale=-0.5)
            stt(tmp['kk'], k0, col(5), tmp['ssk'], ALU.mult, ALU.mult, [Bt_['k0'], Bt_['ssk'], B_const], [Bt_['kk']])
            ts(tmp['t1'], tmp['asg'], col(6), ocol(6), ALU.mult, ALU.add, [Bt_['asg'], B_const], [Bt_['t1']])
            tt(tmp['t1'], tmp['t1'], k0, ALU.mult, [Bt_['t1'], Bt_['k0']], [Bt_['t1']])
            arv = AR[:, 4 * tb:4 * tb + 4, :, :]
            c4 = lambda a: a.rearrange("p (c t) -> p c t", c=4)
            stt(arv[:, :, 0, :], c4(tmp['kk']), -1.0, c4(tmp['Pp']), ALU.mult, ALU.mult, [Bt_['kk'], Bt_['Pp']], [ARb[4 * tb + i] for i in range(4)])
            tt(arv[:, :, 1, :], c4(r_), c4(tmp['P']), ALU.mult, [Bt_['r'], Bt_['P']], [ARb[4 * tb + i] for i in range(4)])
            tt(tmp['kk'], tmp['kk'], tmp['asg'], ALU.mult, [Bt_['kk'], Bt_['asg']], [Bt_['kk']])
            tt(BT[:, blk], tmp['kk'], tmp['invP'], ALU.mult, [Bt_['kk'], Bt_['invP']], [BTb[tb]])
            tt(KT[:, blk], tmp['t1'], tmp['invP'], ALU.mult, [Bt_['t1'], Bt_['invP']], [KTb[tb]])
            stt(rkrT[:, blk], r_, col(7), tmp['t1'], ALU.mult, ALU.mult, [Bt_['r'], Bt_['t1'], B_const], [rkb[tb]])

        def make_scan(p):
            col, ocol = pair_setup(p)
            def local(n):
                cs = slice(n * 128, (n + 1) * 128)
                tb = n // 4
                bz = BKz[n % 2]
                W1, TT, W1B, TTB = W1s[n % 2], TTs[n % 2], Bs[f'W1{n % 2}'], Bs[f'TT{n % 2}']
                bzB = Bs[f'BKz{n % 2}']
                for h in range(2):
                    hp = slice(64 * h, 64 * h + 64)
                    S.op('pool', lambda e, h=h, hp=hp: e.tensor_copy(out=bz[hp, 0, h, :], in_=BT[hp, cs]), reads=[BTb[tb]], writes=[bzB])
                    S.op('pool', lambda e, h=h, hp=hp: e.tensor_copy(out=bz[hp, 1, h, :], in_=KT[hp, cs]), reads=[KTb[tb]], writes=[bzB])
                for h in range(2):
                    mm(ps1[:, h, 0:256], bz[:, 0, h, :], AR[:, n, :, :], True, True, [bzB, ARb[n]], [Bs['ps1']])
                    mm(ps1[:, h, 256:512], bz[:, 1, h, :], AR[:, n, :, :], True, True, [bzB, ARb[n]], [Bs['ps1']])
                    mm(ps2[:, h, :], AR[:, n, 0, :], bz[:, 0, h, :], True, True, [bzB, ARb[n]], [Bs['ps2']])
                tt(Xb[0][:, :, 0, :], ps1[:, :, 0:128], mS_bc, ALU.mult, [Bs['ps1'], B_const], [Bs['X0']])
                tt(W1, ps1[:, :, 128:512], m3_bc, ALU.mult, [Bs['ps1'], B_const], [W1B])
                tt(Nn[0], ps2, mL_bc, ALU.mult, [Bs['ps2'], B_const], [Bs['N0']])
                tt(Xb[1][:, :, 1, :], Xb[0][:, :, 0, :], id_bc, ALU.add, [Bs['X0'], B_const], [Bs['X1']])
                yield
                cur = 0
                for k in range(4):
                    nx = 1 - cur
                    lastk = (k == 3)
                    for h in range(2):
                        if lastk:
                            mm(psL[:, h, 128:256], Nn[cur][:, h, :], Xb[cur][:, h, 1, :], True, True, [Bs[f'N{cur}'], Bs[f'X{cur}']], [Bs['psL']])
                        elif k == 0:
                            mm(psL[:, h, 0:128], Nn[cur][:, h, :], Xb[cur][:, h, 0, :], True, True, [Bs[f'N{cur}'], Bs[f'X{cur}']], [Bs['psL']])
                            mm(psN[:, h, :], Xb[cur][:, h, 0, :], Nn[cur][:, h, :], True, True, [Bs[f'N{cur}'], Bs[f'X{cur}']], [Bs['psN']])
                        else:
                            mm(psL[:, h, :], Nn[cur][:, h, :], Xb[cur][:, h, :, :], True, True, [Bs[f'N{cur}'], Bs[f'X{cur}']], [Bs['psL']])
                            mm(psN[:, h, :], Xb[cur][:, h, 0, :], Nn[cur][:, h, :], True, True, [Bs[f'N{cur}'], Bs[f'X{cur}']], [Bs['psN']])
                    if lastk:
                        tt(TT, Xb[cur][:, :, 1, :], psL[:, :, 128:256], ALU.add, [Bs['psL'], Bs[f'X{cur}']], [TTB])
                    else:
                        cp('act', Xb[nx][:, :, 0, :], psL[:, :, 0:128], [Bs['psL']], [Bs[f'X{nx}']])
                        cp('dve', Nn[nx], psN, [Bs['psN']], [Bs[f'N{nx}']])
                        if k > 0:
                            tt(Xb[nx][:, :, 1, :], Xb[cur][:, :, 1, :], psL[:, :, 128:256], ALU.add, [Bs['psL'], Bs[f'X{cur}']], [Bs[f'X{nx}']])
                    cur = nx
                    yield

            def chain(n):
                cs = slice(n * 128, (n + 1) * 128)
                tb = n // 4
                g = (n // 4) % 2
                bk = BK[n % 2]
                bkB = Bs[f'BK{n % 2}']
                vt = Vt[g][:, n % 4, :]
                vtB = Bs[f'Vt{g}']
                W1, TT, W1B, TTB = W1s[n % 2], TTs[n % 2], Bs[f'W1{n % 2}'], Bs[f'TT{n % 2}']
                S.op('pe', lambda e: e.transpose(out=pT3[:, 0, :], in_=vT[:, cs], identity=ident), reads=[vTb[tb], B_const], writes=[Bs['pT']])
                S.op('pe', lambda e: e.transpose(out=pT3[:, 1, :], in_=BT[:, cs], identity=ident), reads=[BTb[tb], B_const], writes=[Bs['pT']])
                S.op('pe', lambda e: e.transpose(out=pT3[:, 2, :], in_=KT[:, cs], identity=ident), reads=[KTb[tb], B_const], writes=[Bs['pT']])
                cp('act', vt, pT3[:, 0, :], [Bs['pT']], [vtB])
                cp('act', bk, pT3[:, 1:3, :], [Bs['pT']], [bkB])
                yield
                for h in range(2):
                    hs = slice(64 * h, 64 * h + 64)
                    if n > 0:
                        mm(psX[:, hs], AR[:, n, 0, :], Hbz[:, h, :], True, False, [ARb[n], Bs['Hb']], [Bs['psX']])
                    mm(psX[:, hs], W1[:, h, 1, :], vt[:, hs], n == 0, True, [W1B, vtB], [Bs['psX']])
                cp('act', Xs, psX, [Bs['psX']], [Bs['Xs']])
                yield
                for h in range(2):
                    hs = slice(64 * h, 64 * h + 64)
                    mm(psU[:, hs], TT[:, h, :], Xs[:, hs], True, True, [TTB, Bs['Xs']], [Bs['psU']])
                cp('dve', Us, psU, [Bs['psU']], [Bs['Us']])
                yield
                for h in range(2):
                    hs = slice(64 * h, 64 * h + 64)
                    if n > 0:
                        mm(psY[:, hs], AR[:, n, 1, :], Hbz[:, h, :], True, False, [ARb[n], Bs['Hb']], [Bs['psY']])
                    mm(psY[:, hs], W1[:, h, 0, :], Us[:, hs], n == 0, False, [W1B, Bs['Us']], [Bs['psY']])
                    mm(psY[:, hs], W1[:, h, 2, :], vt[:, hs], False, True, [W1B, vtB], [Bs['psY']])
                mm(psH, bk[:, 0, :], Us, True, False, [bkB, Bs['Us']], [Bs['psH']])
                mm(psH, bk[:, 1, :], vt, False, True, [bkB, vtB], [Bs['psH']])
                cp('act', Yp[g][:, n % 4, :], psY, [Bs['psY']], [Bs[f'Yp{g}']])
                if n > 0:
                    ts(HP, Hs, PCt[:, n:n + 1], None, ALU.mult, None, [Bs['H'], Bt_['PCt']], [Bs['HP']])
                for h in range(2):
                    hp = slice(64 * h, 64 * h + 64)
                    hs = slice(64 * h, 64 * h + 64)
                    if n > 0:
                        stt(Hs[hp, :], psH[hp, hs], PCt[hp, n:n + 1], HP[hp, :], ALU.mult, ALU.add, [Bs['psH'], Bs['HP'], Bt_['PCt']], [Bs['H']])
                    else:
                        ts(Hs[hp, :], psH[hp, hs], PCt[hp, n:n + 1], None, ALU.mult, None, [Bs['psH'], Bt_['PCt']], [Bs['H']])
                for h in range(2):
                    hp = slice(64 * h, 64 * h + 64)
                    cp('act', Hbz[hp, h, :], Hs[hp, :], [Bs['H']], [Bs['Hb']])
                yield

            def post(tg):
                g = tg % 2
                y3 = Yp[g].rearrange("p j (h e) -> p (j h) e", h=2)
                yB = Bs[f'Yp{g}']
                s1, s2, mean, msq, rstd = (rstp[:, 8 * i:8 * i + 8] for i in range(5))
                v8 = lambda a: a.rearrange("p (j e) -> p j e", j=8)
                S.op('dve', lambda e: e.tensor_reduce(out=s1, in_=y3, axis=AX.X, op=ALU.add), reads=[yB], writes=[Bs['rstp']])
                act(sqp, Yp[g].rearrange("p j c -> p (j c)"), AF.Square, [yB], [Bs['sqp']])
                S.op('dve', lambda e: e.tensor_reduce(out=s2, in_=v8(sqp), axis=AX.X, op=ALU.add), reads=[Bs['sqp']], writes=[Bs['rstp']])
                ts(mean, s1, 1.0 / 64, None, ALU.mult, None, [Bs['rstp']], [Bs['rstp']])
                tt(msq, mean, mean, ALU.mult, [Bs['rstp']], [Bs['rstp']])
                stt(rstd, s2, 1.0 / 64, msq, ALU.mult, ALU.subtract, [Bs['rstp']], [Bs['rstp']])
                rsqrt_tiny(rstd, rstd, 1.0, RWKV_GN_EPS, [Bs['rstp']], [Bs['rstp']])
                tt(v8(ynp), y3, mean.unsqueeze(2).to_broadcast([128, 8, 64]), ALU.subtract, [yB, Bs['rstp']], [Bs['ynp']])
                tt(v8(ynp), v8(ynp), rstd.unsqueeze(2).to_broadcast([128, 8, 64]), ALU.mult, [Bs['ynp'], Bs['rstp']], [Bs['ynp']])
                y4 = ynp.rearrange("p (j c) -> p j c", j=4)
                tt(y4, y4, lnxw[:, p * 128:(p + 1) * 128].unsqueeze(1).to_broadcast([128, 4, 128]), ALU.mult, [Bs['ynp'], B_lw], [Bs['ynp']])
                tt(y4, y4, lnxb[:, p * 128:(p + 1) * 128].unsqueeze(1).to_broadcast([128, 4, 128]), ALU.add, [Bs['ynp'], B_lw], [Bs['ynp']])
                for j in range(4):
                    n = 4 * tg + j
                    cs = slice(n * 128, (n + 1) * 128)
                    mm(psB[:, 2 * j:2 * j + 2], rkrT[:, cs], sel, True, True, [rkb[tg], B_const], [Bs['psB']])
                    mm(psG[:, j * 128:(j + 1) * 128], L1g[:, cs], G2sb[:, p * 128:(p + 1) * 128], True, True, [L1B[tg], B_lw], [Bs['psG']])
                cp('act', sB, psB, [Bs['psB']], [Bs['sB']])
                tt(v8(bon), Vt[g].rearrange("p j (h e) -> p (j h) e", h=2), sB.unsqueeze(2).to_broadcast([128, 8, 64]), ALU.mult,
                   [Bs[f'Vt{g}'], Bs['sB']], [Bs['bon']])
                tt(ynp, ynp, bon, ALU.add, [Bs['ynp'], Bs['bon']], [Bs['ynp']])
                tt(yop.rearrange("p j c -> p (j c)"), ynp, psG, ALU.mult, [Bs['ynp'], Bs['psG']], [Bs['yop']])
                for j in range(4):
                    S.op('pe', lambda e, j=j: e.transpose(out=pT2[:, j, :], in_=yop[:, j, :], identity=ident), reads=[Bs['yop'], B_const], writes=[Bs['pT2']])
                cp('act', yT[:, p, tg * 512:(tg + 1) * 512], pT2.rearrange("p j t -> p (j t)"), [Bs['pT2']], [yTb[p][4 * tg + j] for j in range(4)])

            return local, chain, post

        for i_ in range(2):
            S.op('pool', lambda e, i_=i_: e.memset(BKz[i_], 0.0), writes=[Bs[f'BKz{i_}']])
        NP = DBG.get('pairs', 4)
        for tb in range(4):
            prep_block(0, tb)
        scans = [make_scan(p) for p in range(NP)]

        def drain(g):
            for _ in g:
                pass
        S.op('pool', lambda e: e.memset(Hs, 0.0), writes=[Bs['H']])
        S.op('pool', lambda e: e.memset(Hbz, 0.0), writes=[Bs['Hb']])
        drain(scans[0][0](0))
        for p in range(NP):
            local, chain, post = scans[p]
            for n in range(NT):
                a = chain(n)
                if n + 1 < NT:
                    b = local(n + 1)
                elif p + 1 < NP:
                    b = scans[p + 1][0](0)
                else:
                    b = iter(())
                done_a = done_b = False
                while not (done_a and done_b):
                    if not done_a:
                        try:
                            next(a)
                        except StopIteration:
                            done_a = True
                    if not done_b:
                        try:
                            next(b)
                        except StopIteration:
                            done_b = True
                if n % 4 == 3:
                    post(n // 4)
                    if p + 1 < NP:
                        prep_block(p + 1, n // 4)
            if p + 1 < NP:
                S.op('dve', lambda e: e.memset(Hs, 0.0), writes=[Bs['H']])
                S.op('pool', lambda e: e.memset(Hbz, 0.0), writes=[Bs['Hb']])
        tap('yT', yT, [128, 8, T], [b for l in yTb for b in l])
        S.barrier()


    if 4 in phases:
        S.barrier()
        A.reset(NORM_END)
        xres = A.take([NT, D], F32)
        xresB = [Buf(f'xres{n}') for n in range(NT)]
        P4 = A.mark()
        wout = A.take([8, D], BF16)
        woutB = Buf('wout')
        wo_v = w_out.rearrange("(c p) n -> p c n", p=128)
        for c in range(8):
            dma('pool', wout[:, c, :], wo_v[:, c, :], [], [woutB])
        dma('sp', gtab, bct_d[:, 1024:2048], [], [B_gtab])
        for n in range(NT):
            dma('sp', xst[n % 3], xv[n], [], [xstB[n % 3]])
            pb = 2 * (n % 2)
            for half in range(2):
                for c in range(8):
                    mm(bank[pb + half], yT[:, c, n * 128:(n + 1) * 128], wout[:, c, half * 512:(half + 1) * 512], c == 0, c == 7,
                       [yTb[c][n], woutB], [bankB[pb + half]])
            tt(xres[:, n, :], pp[n % 2][:], xst[n % 3], ALU.add, [bankB[pb], bankB[pb + 1], xstB[n % 3]], [xresB[n]])
            norm_stats(n, xres[:, n, :], xresB[n], 1)
        norm_rstd(1)
        for n in range(NT):
            norm_apply(n, xres[:, n, :], xresB[n], 1, 4 + n % 2)
        tap('xres', xres, [128, NT, D], xresB)

    if 5 in phases:
        S.barrier()
        A.reset(P4)
        hid = yT[:, 0:6, :]
        hidB = [Buf(f'hid{i}') for i in range(4)]
        wgu = [A.take([2, 8, 256], BF16) for _ in range(2)]
        wguB = [Buf(f'wgu{i}') for i in range(2)]
        wd = A.take([6, D], BF16)
        wdB = Buf('wd')
        gs = [A.take([514], F32) for _ in range(2)]
        gsB = [Buf(f'gs{i}') for i in range(2)]
        acc = [A.take([512], F32) for _ in range(2)]
        accB = [Buf(f'acc{i}') for i in range(2)]
        sl = [A.take([512], F32) for _ in range(2)]
        slB = [Buf(f'sl{i}') for i in range(2)]
        ost = [xst[0], xst[1]]
        ostB = [xstB[0], xstB[1]]
        dma('sp', gtab, bct_d[:, 2048:3072], [], [B_gtab])
        wg_v = wg_d.rearrange("(c p) n -> p c n", p=128)
        wu_v = wu_d.rearrange("(c p) n -> p c n", p=128)
        wd_v = wd_d.rearrange("(m p) n -> p m n", p=128)
        quarters = [(0, 6), (6, 6), (12, 5), (17, 5)]

        def load_wgu(m):
            wb_ = (m // 2) % 2
            dma('pool', wgu[wb_][:, 0, :, :], wg_v[:, :, m * 128:(m + 2) * 128], [], [wguB[wb_]])
            dma('pool', wgu[wb_][:, 1, :, :], wu_v[:, :, m * 128:(m + 2) * 128], [], [wguB[wb_]])
        load_wgu(0)
        it = 0
        for qi, (m0, nq) in enumerate(quarters):
            dma('pool', wd[:, 0:nq, :], wd_v[:, m0:m0 + nq, :], [], [wdB])
            for ml in range(nq):
                m = m0 + ml
                wb = (m // 2) % 2
                if m % 2 == 0 and m + 2 < NFF:
                    load_wgu(m + 2)
                mc = slice((m % 2) * 128, (m % 2) * 128 + 128)
                vf = V_FFN + 4 * m
                cw = lambda j: vecs[:, vf + j:vf + j + 1]
                for blk in range(4):
                    g_, gB = gs[blk % 2], gsB[blk % 2]
                    a_, aB = acc[it % 2], accB[it % 2]
                    s_, sB_ = sl[it % 2], slB[it % 2]
                    pg, pu = 2 * (it % 4), 2 * (it % 4) + 1
                    it += 1
                    rd = [hTb[4 * blk + i] for i in range(4)] + [wguB[wb]]
                    for c in range(8):
                        mm(bank[pg], wgu[wb][:, 0, c, mc], hT[:, c, 1 + blk * 512:1 + (blk + 1) * 512], c == 0, c == 7, rd, [bankB[pg]])
                    for c in range(8):
                        mm(bank[pu], wgu[wb][:, 1, c, mc], hT[:, c, 1 + blk * 512:1 + (blk + 1) * 512], c == 0, c == 7, rd, [bankB[pu]])
                    if blk == 0:
                        S.op('pool', lambda e, g_=g_: e.memset(g_[:, 0:2], 0.0), writes=[gB])
                    else:
                        gp = gs[(blk - 1) % 2]
                        S.op('pool', lambda e, g_=g_, gp=gp: e.tensor_copy(out=g_[:, 0:2], in_=gp[:, 512:514]), reads=[gsB[(blk - 1) % 2]], writes=[gB])
                    act(g_[:, 2:514], bank[pg], AF.Copy, [bankB[pg]], [gB])
                    act(a_, bank[pg], AF.Identity, [bankB[pg], B_const], [aB], bias=cw(3), scale=cw(2))
                    stt(a_, g_[:, 1:513], cw(1), a_, ALU.mult, ALU.add, [gB, aB, B_const], [aB])
                    stt(a_, g_[:, 0:512], cw(0), a_, ALU.mult, ALU.add, [gB, aB, B_const], [aB])
                    act(s_, a_, AF.Silu, [aB], [sB_])
                    tt(hid[:, ml, blk * 512:(blk + 1) * 512], s_, bank[pu], ALU.mult, [sB_, bankB[pu]], [hidB[blk]])
            last = qi == len(quarters) - 1
            for n in range(NT):
                pb = 2 * (n % 4)
                for half in range(2):
                    for ml in range(nq):
                        mm(bank[pb + half], hid[:, ml, n * 128:(n + 1) * 128], wd[:, ml, half * 512:(half + 1) * 512], ml == 0, ml == nq - 1,
                           [hidB[n // 4], wdB], [bankB[pb + half]])
                tt(xres[:, n, :], pp[n % 4][:], xres[:, n, :], ALU.add, [bankB[pb], bankB[pb + 1], xresB[n]], [xresB[n]])
                if last:
                    ssn = ss_all[:, 2, n:n + 1]
                    rsn = rstd_all[:, 2, n:n + 1]
                    sB2 = statB[n % 4]
                    act(sqj, xres[:, n, :], AF.Square, [xresB[n]], [sqjB, sB2], accum=ssn)
                    rsqrt_tiny(rsn, ssn, 1.0 / D, NORM_EPS, [sB2], [sB2])
                    stt(ost[n % 2], xres[:, n, :], rsn, gtab, ALU.mult, ALU.mult, [xresB[n], sB2, B_gtab], [ostB[n % 2]])
                    dma('sp', ov[n], ost[n % 2], [ostB[n % 2]], [])

    S.barrier(('sp',))
    S.emit(st)
    st.close()
    return nc, tap_out, S, A


def _chunkcols(v):
    v = np.asarray(v, np.float32).reshape(-1, 128)
    return np.ascontiguousarray(v.T)


def prep_shared(inp):
    f = lambda k: np.ascontiguousarray(np.asarray(inp[k], np.float32)[0])
    vecs = np.zeros((128, NV), np.float32)
    vecs[:, V_MUW:V_MUW + 8] = _chunkcols(f("rwkv_mu_w"))
    vecs[:, V_MUA:V_MUA + 8] = _chunkcols(f("rwkv_mu_a"))
    vecs[:, V_MUG:V_MUG + 8] = _chunkcols(f("rwkv_mu_g"))
    names = ["rwkv_mu_r", "rwkv_mu_k", "rwkv_mu_v", "rwkv_w0", "rwkv_a0", "rwkv_k_k", "rwkv_k_a", "rwkv_r_k"]
    for j, nm in enumerate(names):
        cc = _chunkcols(f(nm).reshape(-1))
        for p in range(4):
            vecs[:, V_PAIR + 8 * p + j] = cc[:, p]
    cw = f("ffn_conv_w").reshape(3, DFF)
    cbias = f("ffn_conv_b")
    for j in range(3):
        cc = _chunkcols(cw[j])
        for m in range(NFF):
            vecs[:, V_FFN + 4 * m + j] = cc[:, m]
    cc = _chunkcols(cbias)
    for m in range(NFF):
        vecs[:, V_FFN + 4 * m + 3] = cc[:, m]
    row = np.concatenate([f("norm_mix_g"), f("norm_ffn_g"), np.asarray(inp["norm_final_g"], np.float32),
                          f("rwkv_lnx_w"), f("rwkv_lnx_b"), f("ret_gn_w")])
    bct = np.ascontiguousarray(np.broadcast_to(row[None, :], (128, row.shape[0])))
    cf, cb = make_consts()
    shared = {
        "w_in": f("w_in"), "w_out": f("w_out"), "ffn_w_gate": f("ffn_w_gate"), "ffn_w_up": f("ffn_w_up"),
        "ffn_w_down": f("ffn_w_down"), "rwkv_w1": f("rwkv_w1"), "rwkv_a1": f("rwkv_a1"), "rwkv_g1": f("rwkv_g1"),
        "rwkv_w2": f("rwkv_w2"), "rwkv_a2": f("rwkv_a2"), "rwkv_g2": f("rwkv_g2"),
        "vecs": vecs, "bct": bct, "cf": cf, "cb": cb,
    }
    return shared


_PROG = None


def kernel(**inputs):
    global _PROG
    if _PROG is None:
        _PROG = build_program()[0]
    shared = prep_shared(inputs)
    xs = np.asarray(inputs["x"], np.float32)
    in_maps = [dict(shared, x=np.ascontiguousarray(xs[b])) for b in range(8)]
    res = run_bass_kernel_spmd(_PROG, in_maps, core_ids=list(range(8)))
    return np.stack([np.asarray(r["out"], np.float32) for r in res.results], axis=0)
```

```python
import numpy as np
import ml_dtypes
from contextlib import ExitStack
import concourse.bass as bass
import concourse.mybir as mybir
from concourse.bass_utils import run_bass_kernel_spmd

F32 = mybir.dt.float32
BF16 = mybir.dt.bfloat16
AF = mybir.ActivationFunctionType
ALU = mybir.AluOpType
AX = mybir.AxisListType

QUEUES = ('sp', 'act', 'pool', 'pe', 'dve')

T = 2048
D = 1024
NT = 16
DFF = 2816
NFF = 22
C0 = float(np.exp(-0.5))
NORM_EPS = 1e-6
RWKV_GN_EPS = 64e-5
RET_GN_EPS = 1e-5


class Buf:
    __slots__ = ('name', 'w', 'r')

    def __init__(self, name=''):
        self.name = name
        self.w = None
        self.r = {}


class _Op:
    __slots__ = ('q', 's', 'idx', 'fn', 'waits', 'inc', 'dma')


class Sched:
    def __init__(self, nc):
        self.nc = nc
        self.ops = {q: [] for q in QUEUES}
        self.streams = {}
        self.clock = {q: {} for q in QUEUES}
        self.opclock = {}
        self.nwaits = 0
        self.nops = 0

    def op(self, q, fn, reads=(), writes=(), dma=False):
        if dma:
            ref = writes[0] if len(writes) else (reads[0] if len(reads) else None)
            s = 'dq_' + (ref.name if ref is not None and ref.name else q)
        else:
            s = q
        deps = {}

        def need(st, i):
            if deps.get(st, 0) < i:
                deps[st] = i
        for b in reads:
            if b.w is not None:
                st, i = b.w
                if st == q and q == 'pe':
                    continue
                need(st, i)
        for b in writes:
            if b.w is not None:
                st, i = b.w
                if not (st == q and not dma):
                    need(st, i)
            for st, i in b.r.items():
                if st == q and not dma:
                    continue
                need(st, i)
        ck = self.clock[q]
        waits = []
        for st, i in deps.items():
            if ck.get(st, 0) >= i:
                continue
            waits.append((st, i))
            oc = self.opclock[(st, i)]
            for k, v in oc.items():
                if ck.get(k, 0) < v:
                    ck[k] = v
            if ck.get(st, 0) < i:
                ck[st] = i
            self.streams[st][i - 1].inc = True
        o = _Op()
        o.q = q
        o.s = s
        o.fn = fn
        o.waits = waits
        o.inc = dma
        o.dma = dma
        lst = self.streams.setdefault(s, [])
        lst.append(o)
        o.idx = len(lst)
        self.opclock[(s, o.idx)] = dict(ck)
        self.ops[q].append(o)
        self.nwaits += len(waits)
        self.nops += 1
        for b in writes:
            b.w = (s, o.idx)
            b.r = {}
        for b in reads:
            if b.r.get(s, 0) < o.idx:
                b.r[s] = o.idx
        return o

    def barrier(self, queues=QUEUES):
        tips = {s: len(l) for s, l in self.streams.items() if l}
        for q in queues:
            ck = self.clock[q]
            waits = []
            for s, i in tips.items():
                if s == q and q == 'pe':
                    continue
                if ck.get(s, 0) >= i:
                    continue
                waits.append((s, i))
                self.streams[s][i - 1].inc = True
            for s, i in waits:
                oc = self.opclock[(s, i)]
                for k, v in oc.items():
                    if ck.get(k, 0) < v:
                        ck[k] = v
                ck[s] = i
            if waits:
                o = _Op()
                o.q = q
                o.s = None
                o.fn = None
                o.waits = waits
                o.inc = False
                o.dma = False
                self.ops[q].append(o)

    def emit(self, stack):
        nc = self.nc
        sems = {s: stack.enter_context(nc.semaphore('sem_' + s)) for s in self.streams}
        cnt = {}
        for s, lst in self.streams.items():
            c = 0
            for o in lst:
                if o.dma:
                    c += 16
                elif o.inc:
                    c += 1
                cnt[(s, o.idx)] = c
        self.final_counts = {s: (cnt[(s, len(l))] if l else 0) for s, l in self.streams.items()}
        block = stack.enter_context(nc.Block())

        def run(q, eng):
            for o in self.ops[q]:
                for st, i in o.waits:
                    eng.wait_ge(sems[st], cnt[(st, i)])
                if o.fn is None:
                    continue
                ins = o.fn(eng)
                if o.dma:
                    ins.then_inc(sems[o.s], 16)
                elif o.inc:
                    ins.then_inc(sems[o.s], 1)

        @block.sync
        def _(e):
            run('sp', e)

        @block.scalar
        def _(e):
            run('act', e)

        @block.gpsimd
        def _(e):
            run('pool', e)

        @block.tensor
        def _(e):
            run('pe', e)

        @block.vector
        def _(e):
            run('dve', e)


class Arena:
    def __init__(self, ap, nbytes):
        self.ap = ap
        self.nbytes = nbytes
        self.off = 0
        self.peak = 0

    def take(self, shape, dt):
        esz = 4 if dt == F32 else 2
        n = int(np.prod(shape))
        nb = (n * esz + 63) // 64 * 64
        assert self.off + nb <= self.nbytes, ("arena overflow", self.off, nb, self.nbytes)
        v = self.ap[:, self.off // 4:(self.off + nb) // 4]
        if dt != F32:
            v = v.bitcast(dt)
        v = v[:, 0:n]
        if len(shape) == 2:
            v = v.rearrange("p (a b) -> p a b", a=shape[0])
        elif len(shape) == 3:
            v = v.rearrange("p (a b c) -> p a b c", a=shape[0], b=shape[1])
        elif len(shape) == 4:
            v = v.rearrange("p (a b c d) -> p a b c d", a=shape[0], b=shape[1], c=shape[2])
        self.off += nb
        self.peak = max(self.peak, self.off)
        return v

    def mark(self):
        return self.off

    def reset(self, m):
        self.off = m


V_MUW, V_MUA, V_MUG = 0, 8, 16
V_PAIR = 24
V_FFN = 56
NV = 56 + 4 * NFF
CB_ID, CB_MRET, CB_M4, CB_ML, CB_SEL, CB_ONES = 0, 128, 256, 768, 896, 900
NCB = 1028
CF_COS, CF_SIN, CF_NSIN, CF_XIT, CF_KAT, CF_KAPG, CF_GC = 0, 1024, 2048, 3072, 3584, 4096, 4100
NCF = 4104


def make_consts():
    f32 = np.float32
    p = np.arange(128)
    cf = np.zeros((128, NCF), f32)
    half = 64
    inv_freq = (10000.0 ** (-np.arange(half, dtype=np.float64) / half))
    pos = (np.arange(NT)[None, :] * 128 + p[:, None]).astype(np.float64)
    ang = pos[:, :, None] * inv_freq[None, None, :]
    cf[:, CF_COS:CF_COS + 1024] = np.cos(ang).reshape(128, -1)
    cf[:, CF_SIN:CF_SIN + 1024] = np.sin(ang).reshape(128, -1)
    cf[:, CF_NSIN:CF_NSIN + 1024] = -np.sin(ang).reshape(128, -1)
    lg = np.log(1.0 - 2.0 ** (-5.0 - np.arange(4, dtype=np.float64)))
    i = np.arange(128, dtype=np.float64)
    xi = np.exp((i[None, :] + 1.0) * lg[:, None])
    ka = np.exp(-(i[None, :] + 1.0) * lg[:, None]) * (128.0 ** -0.5)
    cf[:, CF_XIT:CF_XIT + 512] = np.broadcast_to(xi.reshape(1, 512), (128, 512))
    cf[:, CF_KAT:CF_KAT + 512] = np.broadcast_to(ka.reshape(1, 512), (128, 512))
    gC = np.exp(128.0 * lg)
    cf[:, CF_KAPG:CF_KAPG + 4] = (ka.T * gC[None, :])
    cf[:, CF_GC:CF_GC + 4] = gC[None, :]
    cb = np.zeros((128, NCB), f32)
    cb[:, CB_ID:CB_ID + 128] = np.eye(128)
    r = p[:, None]
    c = p[None, :]
    cb[:, CB_MRET:CB_MRET + 128] = (r <= c)
    strict = (r < c).astype(f32)
    incl = (r <= c).astype(f32)
    cb[:, CB_M4:CB_M4 + 512] = np.concatenate([strict, incl, strict, incl], axis=1)
    cb[:, CB_ML:CB_ML + 128] = (c < r)
    cb[0:64, CB_SEL] = 1.0
    cb[64:128, CB_SEL + 1] = 1.0
    cb[0:64, CB_ONES:CB_ONES + 64] = 1.0
    cb[64:128, CB_ONES + 64:CB_ONES + 128] = 1.0
    return cf, cb.astype(ml_dtypes.bfloat16)


DBG = {'ret_chunks': NT, 'ret_steps': 99}


def build_program(taps=None, phases=(1, 2, 3, 4, 5)):
    nc = bass.Bass("TRN2", target_bir_lowering=False)

    def din(name, shape, dt=F32):
        return nc.dram_tensor(name, list(shape), dt, kind="ExternalInput").ap()
    x = din("x", [T, D])
    w_in = din("w_in", [D, 3584])
    w_out = din("w_out", [D, D])
    wg_d = din("ffn_w_gate", [D, DFF])
    wu_d = din("ffn_w_up", [D, DFF])
    wd_d = din("ffn_w_down", [DFF, D])
    w1_d = din("rwkv_w1", [D, 64])
    a1_d = din("rwkv_a1", [D, 64])
    g1_d = din("rwkv_g1", [D, 128])
    w2_d = din("rwkv_w2", [64, 512])
    a2_d = din("rwkv_a2", [64, 512])
    g2_d = din("rwkv_g2", [128, 512])
    vecs_d = din("vecs", [128, NV])
    bct_d = din("bct", [128, 4608])
    cf_d = din("cf", [128, NCF])
    cb_d = din("cb", [128, NCB], BF16)
    out = nc.dram_tensor("out", [T, D], F32, kind="ExternalOutput").ap()
    tap_out = {}
    taps = taps or {}

    S = Sched(nc)
    st = ExitStack()
    ARENA_BYTES = 200 * 1024
    arena_t = st.enter_context(nc.sbuf_tensor("arena", [128, ARENA_BYTES // 4], F32))
    A = Arena(arena_t[:], ARENA_BYTES)
    pp = [st.enter_context(nc.psum_tensor(f"pp{i}", [128, 1024], F32)) for i in range(4)]
    bank = [pp[i // 2][:, (i % 2) * 512:(i % 2) * 512 + 512] for i in range(8)]
    bankB = [Buf(f"bank{i}") for i in range(8)]

    def bankbf(i):
        return bank[i].bitcast(BF16)

    def act(out_, in_, func, r, w, bias=None, scale=None, accum=None):
        kw = {}
        if bias is not None:
            kw['bias'] = bias
        if scale is not None:
            kw['scale'] = scale
        if accum is not None:
            kw['accum_out'] = accum
        S.op('act', lambda e: e.activation(out=out_, in_=in_, func=func, **kw), reads=r, writes=w)

    def tt(out_, a, b, op, r, w, q='dve'):
        S.op(q, lambda e: e.tensor_tensor(out=out_, in0=a, in1=b, op=op), reads=r, writes=w)

    def ts(out_, a, s1, s2, op0, op1, r, w, q='dve'):
        if s2 is None:
            S.op(q, lambda e: e.tensor_scalar(out=out_, in0=a, scalar1=s1, scalar2=None, op0=op0), reads=r, writes=w)
        else:
            S.op(q, lambda e: e.tensor_scalar(out=out_, in0=a, scalar1=s1, scalar2=s2, op0=op0, op1=op1), reads=r, writes=w)

    def stt(out_, a, s, b, op0, op1, r, w):
        S.op('dve', lambda e: e.scalar_tensor_tensor(out=out_, in0=a, scalar=s, in1=b, op0=op0, op1=op1), reads=r, writes=w)

    def mm(out_, lhsT, rhs, start, stop, r, w):
        S.op('pe', lambda e: e.matmul(out=out_, lhsT=lhsT, rhs=rhs, start=start, stop=stop), reads=r, writes=w)

    def mm2(out_, lhsT, rhs, start, stop, r, w):
        if lhsT.shape[0] == 128:
            mm(out_, lhsT[0:64], rhs[0:64], start, False, r, w)
            mm(out_, lhsT[64:128], rhs[64:128], False, stop, r, w)
        else:
            mm(out_, lhsT, rhs, start, stop, r, w)

    def dma(q, out_, in_, r, w, **kw):
        S.op(q, lambda e: e.dma_start(out=out_, in_=in_, **kw), reads=r, writes=w, dma=True)

    def cp(q, out_, in_, r, w):
        if q == 'act':
            act(out_, in_, AF.Copy, r, w)
        else:
            S.op(q, lambda e: e.tensor_copy(out=out_, in_=in_), reads=r, writes=w)

    def rsqrt_tiny(dst, src, scale, eps, r, w):
        ts(dst, src, scale, eps, ALU.mult, ALU.add, r, w)
        act(dst, dst, AF.Ln, w, w)
        act(dst, dst, AF.Exp, w, w, scale=-0.5)

    hT = A.take([8, T + 1], BF16)
    yT = A.take([8, T], BF16)
    cb = A.take([NCB], BF16)
    vecs = A.take([NV], F32)
    om = A.take([NV], F32)
    mhalf = A.take([4], F32)
    gtab = A.take([1024], F32)
    stat = A.take([64], F32)
    ss_all = A.take([3, NT], F32)
    rstd_all = A.take([3, NT], F32)
    B_const = Buf('const')
    B_gtab = Buf('gtab')
    hTb = [Buf(f'hT{n}') for n in range(NT)]
    yTb = [[Buf(f'yT{c}_{n}') for n in range(NT)] for c in range(8)]
    ident = cb[:, CB_ID:CB_ID + 128]
    PERSIST = A.mark()

    def tap(name, ap, shape, reads):
        if name in taps:
            d = nc.dram_tensor("tap_" + name, list(shape), ap.dtype, kind="ExternalOutput").ap()
            tap_out[name] = d
            dma('sp', d, ap, reads, [])

    dma('sp', cb, cb_d, [], [B_const])
    dma('sp', vecs, vecs_d, [], [B_const])
    dma('sp', gtab, bct_d[:, 0:1024], [], [B_gtab])
    S.op('pool', lambda e: e.memset(mhalf, -0.5), writes=[B_const])
    ts(om, vecs, -1.0, 1.0, ALU.mult, ALU.add, [B_const], [B_const])
    S.op('pool', lambda e: e.memset(hT[:, :, 0:1], 0.0), writes=[hTb[0]])

    xst = [A.take([D], F32) for _ in range(3)]
    xstB = [Buf(f'xst{i}') for i in range(3)]
    hb = [A.take([D], BF16) for _ in range(2)]
    hbB = [Buf(f'hb{i}') for i in range(2)]
    sqj = A.take([D], BF16)
    sqjB = Buf('sqj')
    statB = [Buf(f'stat{i}') for i in range(4)]
    NORM_END = A.mark()

    ssB = [Buf(f'ss{i}') for i in range(3)]
    rsB = [Buf(f'rs{i}') for i in range(3)]

    def norm_stats(n, src, srcB, which):
        act(sqj, src, AF.Square, [srcB], [sqjB, ssB[which]], accum=ss_all[:, which, n:n + 1])

    def norm_rstd(which):
        rsqrt_tiny(rstd_all[:, which, :], ss_all[:, which, :], 1.0 / D, NORM_EPS, [ssB[which]], [rsB[which]])

    def norm_apply(n, src, srcB, which, pbank):
        h = hb[n % 2]
        stt(h, src, rstd_all[:, which, n:n + 1], gtab, ALU.mult, ALU.mult, [srcB, rsB[which], B_gtab], [hbB[n % 2]])
        pt = bankbf(pbank).rearrange("p (c t) -> p c t", c=8)
        for c in range(8):
            S.op('pe', lambda e, c=c: e.transpose(out=pt[:, c, :], in_=h[:, c * 128:(c + 1) * 128], identity=ident),
                 reads=[hbB[n % 2], B_const], writes=[bankB[pbank]])
        cp('act', hT[:, :, 1 + n * 128:1 + (n + 1) * 128], pt, [bankB[pbank]], [hTb[n]])

    xv = x.rearrange("(n p) d -> n p d", p=128)
    ov = out.rearrange("(n p) d -> n p d", p=128)
    if 2 in phases:
        cf = A.take([NCF], F32)
        dma('sp', cf, cf_d, [], [B_const])
        wret = A.take([8, 2048], BF16)
        wretB = Buf('wret')
        wv = w_in.rearrange("(c p) n -> p c n", p=128)
        for c in range(8):
            dma('pool', wret[:, c, :], wv[:, c, 1536:3584], [], [wretB])
        P2START = A.mark()
    for n in range(NT):
        dma('sp', xst[n % 3], xv[n], [], [xstB[n % 3]])
        norm_stats(n, xst[n % 3], xstB[n % 3], 0)
    norm_rstd(0)
    for n in range(NT):
        dma('sp', xst[n % 3], xv[n], [], [xstB[n % 3]])
        norm_apply(n, xst[n % 3], xstB[n % 3], 0, n % 2)
    tap('hT', hT, [128, 8, T + 1], hTb)

    if 2 in phases:
        A.reset(P2START)
        gnw = A.take([512], F32)
        dma('sp', gnw, bct_d[:, 4096:4608], [], [B_const])
        qa = A.take([512], F32)
        qb = A.take([512], F32)
        qrot = A.take([512], BF16)
        krot = A.take([512], BF16)
        qT = A.take([4, 128], BF16)
        kT = A.take([4, 128], BF16)
        PT = A.take([4, 128], BF16)
        Vb = A.take([512], BF16)
        Vk = A.take([512], BF16)
        R = A.take([512], F32)
        Rt = A.take([512], F32)
        Rb = A.take([512], BF16)
        sqy = A.take([512], F32)
        yn = A.take([512], F32)
        sgt = A.take([512], F32)
        yo = A.take([512], BF16)
        rst = A.take([32], F32)
        Bq = {k: Buf('r_' + k) for k in ['qa', 'qb', 'qrot', 'krot', 'qT', 'kT', 'PT', 'Vb', 'Vk', 'R', 'Rt', 'Rb', 'sqy', 'yn', 'sgt', 'yo', 'rst']}
        kapg_bc = cf[:, CF_KAPG:CF_KAPG + 4].unsqueeze(2).to_broadcast([128, 4, 128])
        gC_bc = cf[:, CF_GC:CF_GC + 4].unsqueeze(2).to_broadcast([128, 4, 128])
        xiT = cf[:, CF_XIT:CF_XIT + 512].rearrange("p (h t) -> p h t", h=4)
        kaT = cf[:, CF_KAT:CF_KAT + 512].rearrange("p (h t) -> p h t", h=4)
        mret_bc = cb[:, CB_MRET:CB_MRET + 128].unsqueeze(1).to_broadcast([128, 4, 128])
        PQ, PK, PV, PG, PTB, PS, PY, PKV = range(8)

        def v4(ap):
            return ap.rearrange("p (h e) -> p h e", h=4)

        def rot(ps, psB, dst, dstB, n):
            cosb = cf[:, CF_COS + n * 64:CF_COS + (n + 1) * 64].unsqueeze(1).unsqueeze(1).to_broadcast([128, 4, 2, 64])
            sinb = cf[:, CF_SIN + n * 64:CF_SIN + (n + 1) * 64].unsqueeze(1).to_broadcast([128, 4, 64])
            nsinb = cf[:, CF_NSIN + n * 64:CF_NSIN + (n + 1) * 64].unsqueeze(1).to_broadcast([128, 4, 64])
            p4 = ps.rearrange("p (h two f) -> p h two f", h=4, two=2)
            tt(qa.rearrange("p (h two f) -> p h two f", h=4, two=2), p4, cosb, ALU.mult, [psB, B_const], [Bq['qa']])
            qb4 = qb.rearrange("p (h two f) -> p h two f", h=4, two=2)
            tt(qb4[:, :, 0, :], p4[:, :, 1, :], nsinb, ALU.mult, [psB, B_const], [Bq['qb']])
            tt(qb4[:, :, 1, :], p4[:, :, 0, :], sinb, ALU.mult, [psB, B_const], [Bq['qb']])
            tt(dst, qa, qb, ALU.add, [Bq['qa'], Bq['qb']], [dstB])

        for n in range(DBG['ret_chunks']):
            RS = DBG['ret_steps']
            tok = slice(1 + n * 128, 1 + (n + 1) * 128)
            for j, pb in enumerate((PQ, PK, PV, PG)):
                for c in range(8):
                    mm(bank[pb], hT[:, c, tok], wret[:, c, j * 512:(j + 1) * 512], c == 0, c == 7,
                       [hTb[n], wretB], [bankB[pb]])
            if RS < 2:
                continue
            rot(bank[PQ], bankB[PQ], qrot, Bq['qrot'], n)
            rot(bank[PK], bankB[PK], krot, Bq['krot'], n)
            if RS < 3:
                continue
            ptb = bankbf(PTB).rearrange("p (c t) -> p c t", c=8)
            for h in range(4):
                S.op('pe', lambda e, h=h: e.transpose(out=ptb[:, h, :], in_=qrot[:, h * 128:(h + 1) * 128], identity=ident),
                     reads=[Bq['qrot'], B_const], writes=[bankB[PTB]])
            for h in range(4):
                S.op('pe', lambda e, h=h: e.transpose(out=ptb[:, 4 + h, :], in_=krot[:, h * 128:(h + 1) * 128], identity=ident),
                     reads=[Bq['krot'], B_const], writes=[bankB[PTB]])
            tt(qT, ptb[:, 0:4, :], xiT, ALU.mult, [bankB[PTB], B_const], [Bq['qT']])
            tt(kT, ptb[:, 4:8, :], kaT, ALU.mult, [bankB[PTB], B_const], [Bq['kT']])
            if RS < 4:
                continue
            ps4 = v4(bank[PS])
            for h in range(4):
                mm(ps4[:, h, :], kT[:, h, :], qT[:, h, :], True, True, [Bq['kT'], Bq['qT']], [bankB[PS]])
            tt(PT, ps4, mret_bc, ALU.mult, [bankB[PS], B_const], [Bq['PT']])
            if RS < 5:
                continue
            cp('act', Vb, bank[PV], [bankB[PV]], [Bq['Vb']])
            tt(v4(Vk), v4(bank[PV]), kapg_bc, ALU.mult, [bankB[PV], B_const], [Bq['Vk']])
            if RS < 6:
                continue
            py4 = v4(bank[PY])
            for h in range(4):
                mm(py4[:, h, :], PT[:, h, :], Vb[:, h * 128:(h + 1) * 128], True, n == 0, [Bq['PT'], Bq['Vb']], [bankB[PY]])
                if n > 0:
                    mm(py4[:, h, :], qT[:, h, :], Rb[:, h * 128:(h + 1) * 128], False, True, [Bq['qT'], Bq['Rb']], [bankB[PY]])
            if RS < 7:
                continue
            if n < NT - DBG.get('skiplast', 0):
                pkv4 = v4(bank[PKV])
                for h in range(4):
                    mm(pkv4[:, h, :], krot[:, h * 128:(h + 1) * 128], Vk[:, h * 128:(h + 1) * 128], True, True,
                       [Bq['krot'], Bq['Vk']], [bankB[PKV]])
                if n == 0:
                    cp('dve', R, bank[PKV], [bankB[PKV]], [Bq['R']])
                else:
                    tt(v4(Rt), v4(R), gC_bc, ALU.mult, [Bq['R'], B_const], [Bq['Rt']])
                    tt(R, Rt, bank[PKV], ALU.add, [Bq['Rt'], bankB[PKV]], [Bq['R']])
                cp('pool', Rb, R, [Bq['R']], [Bq['Rb']])
            if RS < 8:
                continue
            s1 = rst[:, 0:4]
            s2 = rst[:, 4:8]
            mean = rst[:, 8:12]
            msq = rst[:, 12:16]
            rstd = rst[:, 16:20]
            S.op('dve', lambda e: e.tensor_reduce(out=s1, in_=py4, axis=AX.X, op=ALU.add), reads=[bankB[PY]], writes=[Bq['rst']])
            act(sqy, bank[PY], AF.Square, [bankB[PY]], [Bq['sqy']])
            S.op('dve', lambda e: e.tensor_reduce(out=s2, in_=v4(sqy), axis=AX.X, op=ALU.add), reads=[Bq['sqy']], writes=[Bq['rst']])
            ts(mean, s1, 1.0 / 128, None, ALU.mult, None, [Bq['rst']], [Bq['rst']])
            tt(msq, mean, mean, ALU.mult, [Bq['rst']], [Bq['rst']])
            stt(rstd, s2, 1.0 / 128, msq, ALU.mult, ALU.subtract, [Bq['rst']], [Bq['rst']])
            rsqrt_tiny(rstd, rstd, 1.0, RET_GN_EPS, [Bq['rst']], [Bq['rst']])
            tt(v4(yn), py4, mean.unsqueeze(2).to_broadcast([128, 4, 128]), ALU.subtract, [bankB[PY], Bq['rst']], [Bq['yn']])
            tt(v4(yn), v4(yn), rstd.unsqueeze(2).to_broadcast([128, 4, 128]), ALU.mult, [Bq['yn'], Bq['rst']], [Bq['yn']])
            tt(yn, yn, gnw, ALU.mult, [Bq['yn'], B_const], [Bq['yn']])
            act(sgt, bank[PG], AF.Silu, [bankB[PG]], [Bq['sgt']])
            tt(yo, yn, sgt, ALU.mult, [Bq['yn'], Bq['sgt']], [Bq['yo']])
            if RS < 9:
                continue
            for h in range(4):
                S.op('pe', lambda e, h=h: e.transpose(out=ptb[:, h, :], in_=yo[:, h * 128:(h + 1) * 128], identity=ident),
                     reads=[Bq['yo'], B_const], writes=[bankB[PTB]])
            cp('act', yT[:, 4:8, n * 128:(n + 1) * 128], ptb[:, 0:4, :], [bankB[PTB]], [yTb[4 + h][n] for h in range(4)])
        if 3 not in phases:
            tap('yT', yT, [128, 8, T], [b for l in yTb for b in l])
        S.barrier()


    if 3 in phases:
        S.barrier()
        A.reset(PERSIST)
        wl_f = A.take([8, 128], F32)
        gl_f = A.take([8, 128], F32)
        W1A = A.take([8, 128], BF16)
        W1B = A.take([8, 128], BF16)
        G1A = A.take([8, 128], BF16)
        G1B = A.take([8, 128], BF16)
        W2sb = A.take([512], BF16)
        A2sb = A.take([512], BF16)
        G2sb = A.take([512], BF16)
        L1 = A.take([T], BF16)
        L1g = A.take([T], BF16)
        lnxw = A.take([512], F32)
        lnxb = A.take([512], F32)
        B_lw = Buf('loraw')
        L1B = [Buf(f'L1_{i}') for i in range(4)]
        dma('sp', wl_f[:, :, 0:64], w1_d.rearrange("(c p) k -> p c k", p=128), [], [B_lw])
        dma('sp', wl_f[:, :, 64:128], a1_d.rearrange("(c p) k -> p c k", p=128), [], [B_lw])
        dma('sp', gl_f, g1_d.rearrange("(c p) k -> p c k", p=128), [], [B_lw])
        dma('sp', lnxw, bct_d[:, 3072:3584], [], [B_lw])
        dma('sp', lnxb, bct_d[:, 3584:4096], [], [B_lw])
        S.op('pool', lambda e: e.memset(W2sb, 0.0), writes=[B_lw])
        S.op('pool', lambda e: e.memset(A2sb, 0.0), writes=[B_lw])
        dma('pool', W2sb[0:64, :], w2_d, [B_lw], [B_lw])
        dma('pool', A2sb[64:128, :], a2_d, [B_lw], [B_lw])
        dma('pool', G2sb, g2_d, [], [B_lw])

        def vb(tab, col, k):
            return tab[:, col:col + 8].unsqueeze(2).to_broadcast([128, 8, k])
        tt(W1A[:, :, 0:64], wl_f[:, :, 0:64], vb(om, V_MUW, 64), ALU.mult, [B_lw, B_const], [B_lw])
        tt(W1A[:, :, 64:128], wl_f[:, :, 64:128], vb(om, V_MUA, 64), ALU.mult, [B_lw, B_const], [B_lw])
        tt(W1B[:, :, 0:64], wl_f[:, :, 0:64], vb(vecs, V_MUW, 64), ALU.mult, [B_lw, B_const], [B_lw])
        tt(W1B[:, :, 64:128], wl_f[:, :, 64:128], vb(vecs, V_MUA, 64), ALU.mult, [B_lw, B_const], [B_lw])
        tt(G1A, gl_f, vb(om, V_MUG, 128), ALU.mult, [B_lw, B_const], [B_lw])
        tt(G1B, gl_f, vb(vecs, V_MUG, 128), ALU.mult, [B_lw, B_const], [B_lw])
        for tb in range(4):
            rd = [hTb[4 * tb + i] for i in range(4)] + ([hTb[4 * tb - 1]] if tb > 0 else []) + [B_lw]
            for (WA, WB, pb) in ((W1A, W1B, 0), (G1A, G1B, 1)):
                for c in range(8):
                    mm(bank[pb], WA[:, c, :], hT[:, c, 1 + tb * 512:1 + (tb + 1) * 512], c == 0, False, rd, [bankB[pb]])
                    mm(bank[pb], WB[:, c, :], hT[:, c, tb * 512:(tb + 1) * 512], False, c == 7, rd, [bankB[pb]])
            blk = slice(tb * 512, (tb + 1) * 512)
            act(L1[0:64, blk], bank[0][0:64, :], AF.Tanh, [bankB[0]], [L1B[tb]])
            act(L1[64:128, blk], bank[0][64:128, :], AF.Copy, [bankB[0]], [L1B[tb]])
            act(L1g[:, blk], bank[1], AF.Sigmoid, [bankB[1]], [L1B[tb]])

        wrkv = A.take([8, 3, 128], BF16)
        AR = A.take([NT, 2, 128], BF16)
        BT = A.take([T], BF16)
        KT = A.take([T], BF16)
        vT = A.take([T], BF16)
        rkrT = A.take([T], BF16)
        rm = A.take([513], F32)
        km = A.take([513], F32)
        vm = A.take([513], F32)
        tnames = ['r', 'k0', 'sg', 'asg', 'cum', 'P', 'invP', 'Pp', 'ssk', 'kk', 't1']
        tmp = {k: A.take([512], F32) for k in tnames}
        sqk = A.take([512], BF16)
        PCt = A.take([NT], F32)
        Xb = [A.take([2, 2, 128], BF16) for _ in range(2)]
        Nn = [A.take([2, 128], BF16) for _ in range(2)]
        W1s = [A.take([2, 3, 128], BF16) for _ in range(2)]
        TTs = [A.take([2, 128], BF16) for _ in range(2)]
        BK = [A.take([2, 128], BF16) for _ in range(2)]
        Vt = [A.take([4, 128], BF16) for _ in range(2)]
        Xs = A.take([128], BF16)
        Us = A.take([128], BF16)
        Hs = A.take([64], F32)
        HP = A.take([64], F32)
        Hbz = A.take([2, 64], BF16)
        BKz = [A.take([2, 2, 128], BF16) for _ in range(2)]
        Yp = [A.take([4, 128], F32) for _ in range(2)]
        sqp = A.take([512], F32)
        ynp = A.take([512], F32)
        bon = A.take([512], F32)
        sB = A.take([8], F32)
        yop = A.take([4, 128], BF16)
        rstp = A.take([64], F32)
        Bw = Buf('wrkv')
        Bt_ = {k: Buf('t_' + k) for k in tnames + ['rm', 'km', 'vm', 'sqk', 'PCt']}
        ARb = [Buf(f'AR{n}') for n in range(NT)]
        BTb = [Buf(f'BT{i}') for i in range(4)]
        KTb = [Buf(f'KT{i}') for i in range(4)]
        vTb = [Buf(f'vT{i}') for i in range(4)]
        rkb = [Buf(f'rk{i}') for i in range(4)]
        Bs = {k: Buf('s_' + k) for k in ['X0', 'X1', 'N0', 'N1', 'W10', 'W11', 'TT0', 'TT1', 'BK0', 'BK1', 'Vt0', 'Vt1', 'Xs', 'Us', 'H', 'HP', 'Hb', 'Yp0', 'Yp1', 'BKz0', 'BKz1',
                                          'sqp', 'ynp', 'bon', 'sB', 'yop', 'rstp',
                                          'ps1', 'ps2', 'psN', 'psL', 'pT', 'pT2', 'psX', 'psU', 'psH', 'psY', 'psB', 'psG']}
        for k_, b_ in (('ps1', 0), ('ps2', 2), ('psN', 2), ('psL', 3), ('pT', 4), ('pT2', 4), ('psX', 5), ('psU', 5), ('psH', 5),
                       ('psY', 6), ('psB', 6), ('psG', 7)):
            Bs[k_] = bankB[b_]
        wv3 = w_in.rearrange("(c p) n -> p c n", p=128)
        m4 = cb[:, CB_M4:CB_M4 + 512]
        mS_bc = cb[:, CB_M4:CB_M4 + 128].unsqueeze(1).to_broadcast([128, 2, 128])
        m3_bc = cb[:, CB_M4 + 128:CB_M4 + 512].unsqueeze(1).to_broadcast([128, 2, 384])
        mL_bc = cb[:, CB_ML:CB_ML + 128].unsqueeze(1).to_broadcast([128, 2, 128])
        id_bc = ident.unsqueeze(1).to_broadcast([128, 2, 128])
        sel = cb[:, CB_SEL:CB_SEL + 2]
        ones_bd = cb[:, CB_ONES:CB_ONES + 128]
        ps1 = pp[0][:].rearrange("p (h c) -> p h c", h=2)
        ps2 = bank[2][:, 0:256].rearrange("p (h s) -> p h s", h=2)
        psN = bank[2][:, 256:512].rearrange("p (h s) -> p h s", h=2)
        psL = bank[3].rearrange("p (h c) -> p h c", h=2)
        pTb = bankbf(4)
        pT3 = pTb[:, 0:384].rearrange("p (j t) -> p j t", j=3)
        pT2 = pTb[:, 512:1024].rearrange("p (j t) -> p j t", j=4)
        psX = bank[5][:, 0:128]
        psU = bank[5][:, 128:256]
        psH = bank[5][:, 256:384]
        psY = bank[6][:, 0:128]
        psB = bank[6][:, 128:136]
        psG = bank[7]

        def pair_setup(p):
            vp = V_PAIR + 8 * p
            col = lambda j: vecs[:, vp + j:vp + j + 1]
            ocol = lambda j: om[:, vp + j:vp + j + 1]
            return col, ocol

        def prep_block(p, tb):
            col, ocol = pair_setup(p)
            if tb == 0:
                for j in range(3):
                    dma('pool', wrkv[:, :, j, :], wv3[:, :, j * 512 + p * 128:j * 512 + (p + 1) * 128], [], [Bw])
                for nm_ in ('rm', 'km', 'vm'):
                    tl = {'rm': rm, 'km': km, 'vm': vm}[nm_]
                    S.op('pool', lambda e, tl=tl: e.memset(tl[:, 0:1], 0.0), writes=[Bt_[nm_]])
            blk = slice(tb * 512, (tb + 1) * 512)
            rd = [hTb[4 * tb + i] for i in range(4)] + [Bw]
            for j in range(3):
                for c in range(8):
                    mm(bank[j], wrkv[:, c, j, :], hT[:, c, 1 + tb * 512:1 + (tb + 1) * 512], c == 0, c == 7, rd, [bankB[j]])
            mm(bank[3], W2sb[:, p * 128:(p + 1) * 128], L1[:, blk], True, True, [B_lw, L1B[tb]], [bankB[3]])
            mm(bank[4], A2sb[:, p * 128:(p + 1) * 128], L1[:, blk], True, True, [B_lw, L1B[tb]], [bankB[4]])
            for j, (tl, nm_, dst, dstB) in enumerate(((rm, 'rm', tmp['r'], Bt_['r']), (km, 'km', tmp['k0'], Bt_['k0']), (vm, 'vm', vT[:, blk], vTb[tb]))):
                act(tl[:, 1:513], bank[j], AF.Copy, [bankB[j], B_const], [Bt_[nm_]], scale=col(j))
                stt(dst, bank[j], ocol(j), tl[:, 0:512], ALU.mult, ALU.add, [bankB[j], Bt_[nm_], B_const], [dstB])
                S.op('pool', lambda e, tl=tl: e.tensor_copy(out=tl[:, 0:1], in_=tl[:, 512:513]), reads=[Bt_[nm_]], writes=[Bt_[nm_]])
            r_, k0 = tmp['r'], tmp['k0']
            act(tmp['sg'], bank[3], AF.Sigmoid, [bankB[3], B_const], [Bt_['sg']], bias=col(3))
            act(tmp['asg'], bank[4], AF.Sigmoid, [bankB[4], B_const], [Bt_['asg']], bias=col(4))
            for ch in range(4):
                cs = slice(ch * 128, (ch + 1) * 128)
                S.op('dve', lambda e, cs=cs: e.tensor_tensor_scan(out=tmp['cum'][:, cs], data0=tmp['sg'][:, cs], data1=tmp['sg'][:, cs],
                                                                   initial=0.0, op0=ALU.add, op1=ALU.bypass),
                     reads=[Bt_['sg']], writes=[Bt_['cum']])
            act(tmp['P'], tmp['cum'], AF.Exp, [Bt_['cum']], [Bt_['P']], scale=-C0)
            act(tmp['invP'], tmp['cum'], AF.Exp, [Bt_['cum']], [Bt_['invP']], scale=C0)
            tt(tmp['sg'], tmp['cum'], tmp['sg'], ALU.subtract, [Bt_['cum'], Bt_['sg']], [Bt_['sg']])
            act(tmp['Pp'], tmp['sg'], AF.Exp, [Bt_['sg']], [Bt_['Pp']], scale=-C0)
            S.op('pool', lambda e, tb=tb: e.tensor_copy(out=PCt[:, tb * 4:(tb + 1) * 4],
                                                        in_=tmp['P'].rearrange("p (c t) -> p c t", c=4)[:, :, 127]),
                 reads=[Bt_['P']], writes=[Bt_['PCt']])
            act(sqk, k0, AF.Square, [Bt_['k0'], B_const], [Bt_['sqk']], scale=col(5))
            mm(bank[5], ones_bd, sqk, True, True, [Bt_['sqk'], B_const], [bankB[5]])
            act(tmp['ssk'], bank[5], AF.Ln, [bankB[5]], [Bt_['ssk']])
            act(tmp['ssk'], tmp['ssk'], AF.Exp, [Bt_['ssk']], [Bt_['ssk']], scale=-0.5)
            stt(tmp['kk'], k0, col(5), tmp['ssk'], ALU.mult, ALU.mult, [Bt_['k0'], Bt_['ssk'], B_const], [Bt_['kk']])
            ts(tmp['t1'], tmp['asg'], col(6), ocol(6), ALU.mult, ALU.add, [Bt_['asg'], B_const], [Bt_['t1']])
            tt(tmp['t1'], tmp['t1'], k0, ALU.mult, [Bt_['t1'], Bt_['k0']], [Bt_['t1']])
            arv = AR[:, 4 * tb:4 * tb + 4, :, :]
            c4 = lambda a: a.rearrange("p (c t) -> p c t", c=4)
            stt(arv[:, :, 0, :], c4(tmp['kk']), -1.0, c4(tmp['Pp']), ALU.mult, ALU.mult, [Bt_['kk'], Bt_['Pp']], [ARb[4 * tb + i] for i in range(4)])
            tt(arv[:, :, 1, :], c4(r_), c4(tmp['P']), ALU.mult, [Bt_['r'], Bt_['P']], [ARb[4 * tb + i] for i in range(4)])
            tt(tmp['kk'], tmp['kk'], tmp['asg'], ALU.mult, [Bt_['kk'], Bt_['asg']], [Bt_['kk']])
            tt(BT[:, blk], tmp['kk'], tmp['invP'], ALU.mult, [Bt_['kk'], Bt_['invP']], [BTb[tb]])
            tt(KT[:, blk], tmp['t1'], tmp['invP'], ALU.mult, [Bt_['t1'], Bt_['invP']], [KTb[tb]])
            stt(rkrT[:, blk], r_, col(7), tmp['t1'], ALU.mult, ALU.mult, [Bt_['r'], Bt_['t1'], B_const], [rkb[tb]])

        def make_scan(p):
            col, ocol = pair_setup(p)
            def local(n):
                cs = slice(n * 128, (n + 1) * 128)
                tb = n // 4
                bz = BKz[n % 2]
                W1, TT, W1B, TTB = W1s[n % 2], TTs[n % 2], Bs[f'W1{n % 2}'], Bs[f'TT{n % 2}']
                bzB = Bs[f'BKz{n % 2}']
                for h in range(2):
                    hp = slice(64 * h, 64 * h + 64)
                    S.op('pool', lambda e, h=h, hp=hp: e.tensor_copy(out=bz[hp, 0, h, :], in_=BT[hp, cs]), reads=[BTb[tb]], writes=[bzB])
                    S.op('pool', lambda e, h=h, hp=hp: e.tensor_copy(out=bz[hp, 1, h, :], in_=KT[hp, cs]), reads=[KTb[tb]], writes=[bzB])
                for h in range(2):
                    mm(ps1[:, h, 0:256], bz[:, 0, h, :], AR[:, n, :, :], True, True, [bzB, ARb[n]], [Bs['ps1']])
                    mm(ps1[:, h, 256:512], bz[:, 1, h, :], AR[:, n, :, :], True, True, [bzB, ARb[n]], [Bs['ps1']])
                    mm(ps2[:, h, :], AR[:, n, 0, :], bz[:, 0, h, :], True, True, [bzB, ARb[n]], [Bs['ps2']])
                tt(Xb[0][:, :, 0, :], ps1[:, :, 0:128], mS_bc, ALU.mult, [Bs['ps1'], B_const], [Bs['X0']])
                tt(W1, ps1[:, :, 128:512], m3_bc, ALU.mult, [Bs['ps1'], B_const], [W1B])
                tt(Nn[0], ps2, mL_bc, ALU.mult, [Bs['ps2'], B_const], [Bs['N0']])
                tt(Xb[1][:, :, 1, :], Xb[0][:, :, 0, :], id_bc, ALU.add, [Bs['X0'], B_const], [Bs['X1']])
                yield
                cur = 0
                for k in range(4):
                    nx = 1 - cur
                    lastk = (k == 3)
                    for h in range(2):
                        if lastk:
                            mm(psL[:, h, 128:256], Nn[cur][:, h, :], Xb[cur][:, h, 1, :], True, True, [Bs[f'N{cur}'], Bs[f'X{cur}']], [Bs['psL']])
                        elif k == 0:
                            mm(psL[:, h, 0:128], Nn[cur][:, h, :], Xb[cur][:, h, 0, :], True, True, [Bs[f'N{cur}'], Bs[f'X{cur}']], [Bs['psL']])
                            mm(psN[:, h, :], Xb[cur][:, h, 0, :], Nn[cur][:, h, :], True, True, [Bs[f'N{cur}'], Bs[f'X{cur}']], [Bs['psN']])
                        else:
                            mm(psL[:, h, :], Nn[cur][:, h, :], Xb[cur][:, h, :, :], True, True, [Bs[f'N{cur}'], Bs[f'X{cur}']], [Bs['psL']])
                            mm(psN[:, h, :], Xb[cur][:, h, 0, :], Nn[cur][:, h, :], True, True, [Bs[f'N{cur}'], Bs[f'X{cur}']], [Bs['psN']])
                    if lastk:
                        tt(TT, Xb[cur][:, :, 1, :], psL[:, :, 128:256], ALU.add, [Bs['psL'], Bs[f'X{cur}']], [TTB])
                    else:
                        cp('act', Xb[nx][:, :, 0, :], psL[:, :, 0:128], [Bs['psL']], [Bs[f'X{nx}']])
                        cp('dve', Nn[nx], psN, [Bs['psN']], [Bs[f'N{nx}']])
                        if k > 0:
                            tt(Xb[nx][:, :, 1, :], Xb[cur][:, :, 1, :], psL[:, :, 128:256], ALU.add, [Bs['psL'], Bs[f'X{cur}']], [Bs[f'X{nx}']])
                    cur = nx
                    yield

            def chain(n):
                cs = slice(n * 128, (n + 1) * 128)
                tb = n // 4
                g = (n // 4) % 2
                bk = BK[n % 2]
                bkB = Bs[f'BK{n % 2}']
                vt = Vt[g][:, n % 4, :]
                vtB = Bs[f'Vt{g}']
                W1, TT, W1B, TTB = W1s[n % 2], TTs[n % 2], Bs[f'W1{n % 2}'], Bs[f'TT{n % 2}']
                S.op('pe', lambda e: e.transpose(out=pT3[:, 0, :], in_=vT[:, cs], identity=ident), reads=[vTb[tb], B_const], writes=[Bs['pT']])
                S.op('pe', lambda e: e.transpose(out=pT3[:, 1, :], in_=BT[:, cs], identity=ident), reads=[BTb[tb], B_const], writes=[Bs['pT']])
                S.op('pe', lambda e: e.transpose(out=pT3[:, 2, :], in_=KT[:, cs], identity=ident), reads=[KTb[tb], B_const], writes=[Bs['pT']])
                cp('act', vt, pT3[:, 0, :], [Bs['pT']], [vtB])
                cp('act', bk, pT3[:, 1:3, :], [Bs['pT']], [bkB])
                yield
                for h in range(2):
                    hs = slice(64 * h, 64 * h + 64)
                    if n > 0:
                        mm(psX[:, hs], AR[:, n, 0, :], Hbz[:, h, :], True, False, [ARb[n], Bs['Hb']], [Bs['psX']])
                    mm(psX[:, hs], W1[:, h, 1, :], vt[:, hs], n == 0, True, [W1B, vtB], [Bs['psX']])
                cp('act', Xs, psX, [Bs['psX']], [Bs['Xs']])
                yield
                for h in range(2):
                    hs = slice(64 * h, 64 * h + 64)
                    mm(psU[:, hs], TT[:, h, :], Xs[:, hs], True, True, [TTB, Bs['Xs']], [Bs['psU']])
                cp('dve', Us, psU, [Bs['psU']], [Bs['Us']])
                yield
                for h in range(2):
                    hs = slice(64 * h, 64 * h + 64)
                    if n > 0:
                        mm(psY[:, hs], AR[:, n, 1, :], Hbz[:, h, :], True, False, [ARb[n], Bs['Hb']], [Bs['psY']])
                    mm(psY[:, hs], W1[:, h, 0, :], Us[:, hs], n == 0, False, [W1B, Bs['Us']], [Bs['psY']])
                    mm(psY[:, hs], W1[:, h, 2, :], vt[:, hs], False, True, [W1B, vtB], [Bs['psY']])
                mm(psH, bk[:, 0, :], Us, True, False, [bkB, Bs['Us']], [Bs['psH']])
                mm(psH, bk[:, 1, :], vt, False, True, [bkB, vtB], [Bs['psH']])
                cp('act', Yp[g][:, n % 4, :], psY, [Bs['psY']], [Bs[f'Yp{g}']])
                if n > 0:
                    ts(HP, Hs, PCt[:, n:n + 1], None, ALU.mult, None, [Bs['H'], Bt_['PCt']], [Bs['HP']])
                for h in range(2):
                    hp = slice(64 * h, 64 * h + 64)
                    hs = slice(64 * h, 64 * h + 64)
                    if n > 0:
                        stt(Hs[hp, :], psH[hp, hs], PCt[hp, n:n + 1], HP[hp, :], ALU.mult, ALU.add, [Bs['psH'], Bs['HP'], Bt_['PCt']], [Bs['H']])
                    else:
                        ts(Hs[hp, :], psH[hp, hs], PCt[hp, n:n + 1], None, ALU.mult, None, [Bs['psH'], Bt_['PCt']], [Bs['H']])
                for h in range(2):
                    hp = slice(64 * h, 64 * h + 64)
                    cp('act', Hbz[hp, h, :], Hs[hp, :], [Bs['H']], [Bs['Hb']])
                yield

            def post(tg):
                g = tg % 2
                y3 = Yp[g].rearrange("p j (h e) -> p (j h) e", h=2)
                yB = Bs[f'Yp{g}']
                s1, s2, mean, msq, rstd = (rstp[:, 8 * i:8 * i + 8] for i in range(5))
                v8 = lambda a: a.rearrange("p (j e) -> p j e", j=8)
                S.op('dve', lambda e: e.tensor_reduce(out=s1, in_=y3, axis=AX.X, op=ALU.add), reads=[yB], writes=[Bs['rstp']])
                act(sqp, Yp[g].rearrange("p j c -> p (j c)"), AF.Square, [yB], [Bs['sqp']])
                S.op('dve', lambda e: e.tensor_reduce(out=s2, in_=v8(sqp), axis=AX.X, op=ALU.add), reads=[Bs['sqp']], writes=[Bs['rstp']])
                yield
                ts(mean, s1, 1.0 / 64, None, ALU.mult, None, [Bs['rstp']], [Bs['rstp']])
                tt(msq, mean, mean, ALU.mult, [Bs['rstp']], [Bs['rstp']])
                stt(rstd, s2, 1.0 / 64, msq, ALU.mult, ALU.subtract, [Bs['rstp']], [Bs['rstp']])
                rsqrt_tiny(rstd, rstd, 1.0, RWKV_GN_EPS, [Bs['rstp']], [Bs['rstp']])
                yield
                tt(v8(ynp), y3, mean.unsqueeze(2).to_broadcast([128, 8, 64]), ALU.subtract, [yB, Bs['rstp']], [Bs['ynp']])
                tt(v8(ynp), v8(ynp), rstd.unsqueeze(2).to_broadcast([128, 8, 64]), ALU.mult, [Bs['ynp'], Bs['rstp']], [Bs['ynp']])
                yield
                y4 = ynp.rearrange("p (j c) -> p j c", j=4)
                tt(y4, y4, lnxw[:, p * 128:(p + 1) * 128].unsqueeze(1).to_broadcast([128, 4, 128]), ALU.mult, [Bs['ynp'], B_lw], [Bs['ynp']])
                tt(y4, y4, lnxb[:, p * 128:(p + 1) * 128].unsqueeze(1).to_broadcast([128, 4, 128]), ALU.add, [Bs['ynp'], B_lw], [Bs['ynp']])
                yield
                for j in range(4):
                    n = 4 * tg + j
                    cs = slice(n * 128, (n + 1) * 128)
                    mm(psB[:, 2 * j:2 * j + 2], rkrT[:, cs], sel, True, True, [rkb[tg], B_const], [Bs['psB']])
                    mm(psG[:, j * 128:(j + 1) * 128], L1g[:, cs], G2sb[:, p * 128:(p + 1) * 128], True, True, [L1B[tg], B_lw], [Bs['psG']])
                cp('act', sB, psB, [Bs['psB']], [Bs['sB']])
                yield
                tt(v8(bon), Vt[g].rearrange("p j (h e) -> p (j h) e", h=2), sB.unsqueeze(2).to_broadcast([128, 8, 64]), ALU.mult,
                   [Bs[f'Vt{g}'], Bs['sB']], [Bs['bon']])
                tt(ynp, ynp, bon, ALU.add, [Bs['ynp'], Bs['bon']], [Bs['ynp']])
                yield
                tt(yop.rearrange("p j c -> p (j c)"), ynp, psG, ALU.mult, [Bs['ynp'], Bs['psG']], [Bs['yop']])
                yield
                for j in range(4):
                    S.op('pe', lambda e, j=j: e.transpose(out=pT2[:, j, :], in_=yop[:, j, :], identity=ident), reads=[Bs['yop'], B_const], writes=[Bs['pT2']])
                cp('act', yT[:, p, tg * 512:(tg + 1) * 512], pT2.rearrange("p j t -> p (j t)"), [Bs['pT2']], [yTb[p][4 * tg + j] for j in range(4)])
                yield

            return local, chain, post

        for i_ in range(2):
            S.op('pool', lambda e, i_=i_: e.memset(BKz[i_], 0.0), writes=[Bs[f'BKz{i_}']])
        NP = DBG.get('pairs', 4)
        for tb in range(4):
            prep_block(0, tb)
        scans = [make_scan(p) for p in range(NP)]

        def drain(g):
            for _ in g:
                pass
        S.op('pool', lambda e: e.memset(Hs, 0.0), writes=[Bs['H']])
        S.op('pool', lambda e: e.memset(Hbz, 0.0), writes=[Bs['Hb']])
        drain(scans[0][0](0))
        pend = []

        def prep_gen(p_, tb_):
            prep_block(p_, tb_)
            yield

        for p in range(NP):
            local, chain, post = scans[p]
            for n in range(NT):
                while len(pend) > 2:
                    drain(pend.pop(0))
                a = chain(n)
                if n + 1 < NT:
                    b = local(n + 1)
                elif p + 1 < NP:
                    b = scans[p + 1][0](0)
                else:
                    b = iter(())
                done_a = done_b = False
                while not (done_a and done_b):
                    if not done_a:
                        try:
                            next(a)
                        except StopIteration:
                            done_a = True
                    if not done_b:
                        try:
                            next(b)
                        except StopIteration:
                            done_b = True
                    if pend:
                        try:
                            next(pend[0])
                        except StopIteration:
                            pend.pop(0)
                if n % 4 == 3:
                    pend.append(post(n // 4))
                    if p + 1 < NP:
                        pend.append(prep_gen(p + 1, n // 4))
            if p + 1 == NP:
                while pend:
                    drain(pend.pop(0))
            if p + 1 < NP:
                S.op('dve', lambda e: e.memset(Hs, 0.0), writes=[Bs['H']])
                S.op('pool', lambda e: e.memset(Hbz, 0.0), writes=[Bs['Hb']])
        tap('yT', yT, [128, 8, T], [b for l in yTb for b in l])
        S.barrier()


    if 4 in phases:
        S.barrier()
        A.reset(NORM_END)
        xres = A.take([NT, D], F32)
        xresB = [Buf(f'xres{n}') for n in range(NT)]
        P4 = A.mark()
        wout = A.take([8, D], BF16)
        woutB = Buf('wout')
        wo_v = w_out.rearrange("(c p) n -> p c n", p=128)
        for c in range(8):
            dma('pool', wout[:, c, :], wo_v[:, c, :], [], [woutB])
        dma('sp', gtab, bct_d[:, 1024:2048], [], [B_gtab])
        for n in range(NT):
            dma('sp', xst[n % 3], xv[n], [], [xstB[n % 3]])
            pb = 2 * (n % 2)
            for half in range(2):
                for c in range(8):
                    mm(bank[pb + half], yT[:, c, n * 128:(n + 1) * 128], wout[:, c, half * 512:(half + 1) * 512], c == 0, c == 7,
                       [yTb[c][n], woutB], [bankB[pb + half]])
            tt(xres[:, n, :], pp[n % 2][:], xst[n % 3], ALU.add, [bankB[pb], bankB[pb + 1], xstB[n % 3]], [xresB[n]])
            norm_stats(n, xres[:, n, :], xresB[n], 1)
        norm_rstd(1)
        for n in range(NT):
            norm_apply(n, xres[:, n, :], xresB[n], 1, 4 + n % 2)
        tap('xres', xres, [128, NT, D], xresB)

    if 5 in phases:
        S.barrier()
        A.reset(P4)
        hid = yT[:, 0:6, :]
        hidB = [Buf(f'hid{i}') for i in range(4)]
        wgu = [A.take([2, 8, 256], BF16) for _ in range(2)]
        wguB = [Buf(f'wgu{i}') for i in range(2)]
        wd = A.take([6, D], BF16)
        wdB = Buf('wd')
        gs = [A.take([514], F32) for _ in range(2)]
        gsB = [Buf(f'gs{i}') for i in range(2)]
        acc = [A.take([512], F32) for _ in range(2)]
        accB = [Buf(f'acc{i}') for i in range(2)]
        sl = [A.take([512], F32) for _ in range(2)]
        slB = [Buf(f'sl{i}') for i in range(2)]
        ost = [xst[0], xst[1]]
        ostB = [xstB[0], xstB[1]]
        dma('sp', gtab, bct_d[:, 2048:3072], [], [B_gtab])
        wg_v = wg_d.rearrange("(c p) n -> p c n", p=128)
        wu_v = wu_d.rearrange("(c p) n -> p c n", p=128)
        wd_v = wd_d.rearrange("(m p) n -> p m n", p=128)
        quarters = [(0, 6), (6, 6), (12, 5), (17, 5)]

        def load_wgu(m):
            wb_ = (m // 2) % 2
            dma('pool', wgu[wb_][:, 0, :, :], wg_v[:, :, m * 128:(m + 2) * 128], [], [wguB[wb_]])
            dma('pool', wgu[wb_][:, 1, :, :], wu_v[:, :, m * 128:(m + 2) * 128], [], [wguB[wb_]])
        load_wgu(0)
        it = 0
        for qi, (m0, nq) in enumerate(quarters):
            dma('pool', wd[:, 0:nq, :], wd_v[:, m0:m0 + nq, :], [], [wdB])
            for ml in range(nq):
                m = m0 + ml
                wb = (m // 2) % 2
                if m % 2 == 0 and m + 2 < NFF:
                    load_wgu(m + 2)
                mc = slice((m % 2) * 128, (m % 2) * 128 + 128)
                vf = V_FFN + 4 * m
                cw = lambda j: vecs[:, vf + j:vf + j + 1]
                for blk in range(4):
                    g_, gB = gs[blk % 2], gsB[blk % 2]
                    a_, aB = acc[it % 2], accB[it % 2]
                    s_, sB_ = sl[it % 2], slB[it % 2]
                    pg, pu = 2 * (it % 4), 2 * (it % 4) + 1
                    it += 1
                    rd = [hTb[4 * blk + i] for i in range(4)] + [wguB[wb]]
                    for c in range(8):
                        mm(bank[pg], wgu[wb][:, 0, c, mc], hT[:, c, 1 + blk * 512:1 + (blk + 1) * 512], c == 0, c == 7, rd, [bankB[pg]])
                    for c in range(8):
                        mm(bank[pu], wgu[wb][:, 1, c, mc], hT[:, c, 1 + blk * 512:1 + (blk + 1) * 512], c == 0, c == 7, rd, [bankB[pu]])
                    if blk == 0:
                        S.op('pool', lambda e, g_=g_: e.memset(g_[:, 0:2], 0.0), writes=[gB])
                    else:
                        gp = gs[(blk - 1) % 2]
                        S.op('pool', lambda e, g_=g_, gp=gp: e.tensor_copy(out=g_[:, 0:2], in_=gp[:, 512:514]), reads=[gsB[(blk - 1) % 2]], writes=[gB])
                    act(g_[:, 2:514], bank[pg], AF.Copy, [bankB[pg]], [gB])
                    act(a_, bank[pg], AF.Identity, [bankB[pg], B_const], [aB], bias=cw(3), scale=cw(2))
                    stt(a_, g_[:, 1:513], cw(1), a_, ALU.mult, ALU.add, [gB, aB, B_const], [aB])
                    stt(a_, g_[:, 0:512], cw(0), a_, ALU.mult, ALU.add, [gB, aB, B_const], [aB])
                    act(s_, a_, AF.Silu, [aB], [sB_])
                    tt(hid[:, ml, blk * 512:(blk + 1) * 512], s_, bank[pu], ALU.mult, [sB_, bankB[pu]], [hidB[blk]])
            last = qi == len(quarters) - 1
            for n in range(NT):
                pb = 2 * (n % 4)
                for half in range(2):
                    for ml in range(nq):
                        mm(bank[pb + half], hid[:, ml, n * 128:(n + 1) * 128], wd[:, ml, half * 512:(half + 1) * 512], ml == 0, ml == nq - 1,
                           [hidB[n // 4], wdB], [bankB[pb + half]])
                tt(xres[:, n, :], pp[n % 4][:], xres[:, n, :], ALU.add, [bankB[pb], bankB[pb + 1], xresB[n]], [xresB[n]])
                if last:
                    ssn = ss_all[:, 2, n:n + 1]
                    rsn = rstd_all[:, 2, n:n + 1]
                    sB2 = statB[n % 4]
                    act(sqj, xres[:, n, :], AF.Square, [xresB[n]], [sqjB, sB2], accum=ssn)
                    rsqrt_tiny(rsn, ssn, 1.0 / D, NORM_EPS, [sB2], [sB2])
                    stt(ost[n % 2], xres[:, n, :], rsn, gtab, ALU.mult, ALU.mult, [xresB[n], sB2, B_gtab], [ostB[n % 2]])
                    dma('sp', ov[n], ost[n % 2], [ostB[n % 2]], [])

    S.barrier(('sp',))
    S.emit(st)
    st.close()
    return nc, tap_out, S, A


def _chunkcols(v):
    v = np.asarray(v, np.float32).reshape(-1, 128)
    return np.ascontiguousarray(v.T)


def prep_shared(inp):
    f = lambda k: np.ascontiguousarray(np.asarray(inp[k], np.float32)[0])
    vecs = np.zeros((128, NV), np.float32)
    vecs[:, V_MUW:V_MUW + 8] = _chunkcols(f("rwkv_mu_w"))
    vecs[:, V_MUA:V_MUA + 8] = _chunkcols(f("rwkv_mu_a"))
    vecs[:, V_MUG:V_MUG + 8] = _chunkcols(f("rwkv_mu_g"))
    names = ["rwkv_mu_r", "rwkv_mu_k", "rwkv_mu_v", "rwkv_w0", "rwkv_a0", "rwkv_k_k", "rwkv_k_a", "rwkv_r_k"]
    for j, nm in enumerate(names):
        cc = _chunkcols(f(nm).reshape(-1))
        for p in range(4):
            vecs[:, V_PAIR + 8 * p + j] = cc[:, p]
    cw = f("ffn_conv_w").reshape(3, DFF)
    cbias = f("ffn_conv_b")
    for j in range(3):
        cc = _chunkcols(cw[j])
        for m in range(NFF):
            vecs[:, V_FFN + 4 * m + j] = cc[:, m]
    cc = _chunkcols(cbias)
    for m in range(NFF):
        vecs[:, V_FFN + 4 * m + 3] = cc[:, m]
    row = np.concatenate([f("norm_mix_g"), f("norm_ffn_g"), np.asarray(inp["norm_final_g"], np.float32),
                          f("rwkv_lnx_w"), f("rwkv_lnx_b"), f("ret_gn_w")])
    bct = np.ascontiguousarray(np.broadcast_to(row[None, :], (128, row.shape[0])))
    cf, cb = make_consts()
    shared = {
        "w_in": f("w_in"), "w_out": f("w_out"), "ffn_w_gate": f("ffn_w_gate"), "ffn_w_up": f("ffn_w_up"),
        "ffn_w_down": f("ffn_w_down"), "rwkv_w1": f("rwkv_w1"), "rwkv_a1": f("rwkv_a1"), "rwkv_g1": f("rwkv_g1"),
        "rwkv_w2": f("rwkv_w2"), "rwkv_a2": f("rwkv_a2"), "rwkv_g2": f("rwkv_g2"),
        "vecs": vecs, "bct": bct, "cf": cf, "cb": cb,
    }
    return shared


_PROG = None


def kernel(**inputs):
    global _PROG
    if _PROG is None:
        _PROG = build_program()[0]
    shared = prep_shared(inputs)
    xs = np.asarray(inputs["x"], np.float32)
    in_maps = [dict(shared, x=np.ascontiguousarray(xs[b])) for b in range(8)]
    res = run_bass_kernel_spmd(_PROG, in_maps, core_ids=list(range(8)))
    return np.stack([np.asarray(r["out"], np.float32) for r in res.results], axis=0)
```

```python
import numpy as np
import ml_dtypes
from contextlib import ExitStack
import concourse.bass as bass
import concourse.mybir as mybir
from concourse.bass_utils import run_bass_kernel_spmd

F32 = mybir.dt.float32
BF16 = mybir.dt.bfloat16
AF = mybir.ActivationFunctionType
ALU = mybir.AluOpType
AX = mybir.AxisListType

QUEUES = ('sp', 'act', 'pool', 'pe', 'dve')

T = 2048
D = 1024
NT = 16
DFF = 2816
NFF = 22
C0 = float(np.exp(-0.5))
NORM_EPS = 1e-6
RWKV_GN_EPS = 64e-5
RET_GN_EPS = 1e-5


class Buf:
    __slots__ = ('name', 'w', 'r')

    def __init__(self, name=''):
        self.name = name
        self.w = None
        self.r = {}


class _Op:
    __slots__ = ('q', 's', 'idx', 'fn', 'waits', 'inc', 'dma')


class Sched:
    def __init__(self, nc):
        self.nc = nc
        self.ops = {q: [] for q in QUEUES}
        self.streams = {}
        self.clock = {q: {} for q in QUEUES}
        self.opclock = {}
        self.nwaits = 0
        self.nops = 0

    def op(self, q, fn, reads=(), writes=(), dma=False):
        if dma:
            ref = writes[0] if len(writes) else (reads[0] if len(reads) else None)
            s = 'dq_' + (ref.name if ref is not None and ref.name else q)
        else:
            s = q
        deps = {}

        def need(st, i):
            if deps.get(st, 0) < i:
                deps[st] = i
        for b in reads:
            if b.w is not None:
                st, i = b.w
                if st == q and q == 'pe':
                    continue
                need(st, i)
        for b in writes:
            if b.w is not None:
                st, i = b.w
                if not (st == q and not dma):
                    need(st, i)
            for st, i in b.r.items():
                if st == q and not dma:
                    continue
                need(st, i)
        ck = self.clock[q]
        waits = []
        for st, i in deps.items():
            if ck.get(st, 0) >= i:
                continue
            waits.append((st, i))
            oc = self.opclock[(st, i)]
            for k, v in oc.items():
                if ck.get(k, 0) < v:
                    ck[k] = v
            if ck.get(st, 0) < i:
                ck[st] = i
            self.streams[st][i - 1].inc = True
        o = _Op()
        o.q = q
        o.s = s
        o.fn = fn
        o.waits = waits
        o.inc = dma
        o.dma = dma
        lst = self.streams.setdefault(s, [])
        lst.append(o)
        o.idx = len(lst)
        self.opclock[(s, o.idx)] = dict(ck)
        self.ops[q].append(o)
        self.nwaits += len(waits)
        self.nops += 1
        for b in writes:
            b.w = (s, o.idx)
            b.r = {}
        for b in reads:
            if b.r.get(s, 0) < o.idx:
                b.r[s] = o.idx
        return o

    def barrier(self, queues=QUEUES):
        tips = {s: len(l) for s, l in self.streams.items() if l}
        for q in queues:
            ck = self.clock[q]
            waits = []
            for s, i in tips.items():
                if s == q and q == 'pe':
                    continue
                if ck.get(s, 0) >= i:
                    continue
                waits.append((s, i))
                self.streams[s][i - 1].inc = True
            for s, i in waits:
                oc = self.opclock[(s, i)]
                for k, v in oc.items():
                    if ck.get(k, 0) < v:
                        ck[k] = v
                ck[s] = i
            if waits:
                o = _Op()
                o.q = q
                o.s = None
                o.fn = None
                o.waits = waits
                o.inc = False
                o.dma = False
                self.ops[q].append(o)

    def emit(self, stack):
        nc = self.nc
        sems = {s: stack.enter_context(nc.semaphore('sem_' + s)) for s in self.streams}
        cnt = {}
        for s, lst in self.streams.items():
            c = 0
            for o in lst:
                if o.dma:
                    c += 16
                elif o.inc:
                    c += 1
                cnt[(s, o.idx)] = c
        self.final_counts = {s: (cnt[(s, len(l))] if l else 0) for s, l in self.streams.items()}
        block = stack.enter_context(nc.Block())

        def run(q, eng):
            for o in self.ops[q]:
                for st, i in o.waits:
                    eng.wait_ge(sems[st], cnt[(st, i)])
                if o.fn is None:
                    continue
                ins = o.fn(eng)
                if o.dma:
                    ins.then_inc(sems[o.s], 16)
                elif o.inc:
                    ins.then_inc(sems[o.s], 1)

        @block.sync
        def _(e):
            run('sp', e)

        @block.scalar
        def _(e):
            run('act', e)

        @block.gpsimd
        def _(e):
            run('pool', e)

        @block.tensor
        def _(e):
            run('pe', e)

        @block.vector
        def _(e):
            run('dve', e)


class Arena:
    def __init__(self, ap, nbytes):
        self.ap = ap
        self.nbytes = nbytes
        self.off = 0
        self.peak = 0

    def take(self, shape, dt):
        esz = 4 if dt == F32 else 2
        n = int(np.prod(shape))
        nb = (n * esz + 63) // 64 * 64
        assert self.off + nb <= self.nbytes, ("arena overflow", self.off, nb, self.nbytes)
        v = self.ap[:, self.off // 4:(self.off + nb) // 4]
        if dt != F32:
            v = v.bitcast(dt)
        v = v[:, 0:n]
        if len(shape) == 2:
            v = v.rearrange("p (a b) -> p a b", a=shape[0])
        elif len(shape) == 3:
            v = v.rearrange("p (a b c) -> p a b c", a=shape[0], b=shape[1])
        elif len(shape) == 4:
            v = v.rearrange("p (a b c d) -> p a b c d", a=shape[0], b=shape[1], c=shape[2])
        self.off += nb
        self.peak = max(self.peak, self.off)
        return v

    def mark(self):
        return self.off

    def reset(self, m):
        self.off = m


V_MUW, V_MUA, V_MUG = 0, 8, 16
V_PAIR = 24
V_FFN = 56
NV = 56 + 4 * NFF
CB_ID, CB_MRET, CB_M4, CB_ML, CB_SEL, CB_ONES = 0, 128, 256, 768, 896, 900
NCB = 1028
CF_COS, CF_SIN, CF_NSIN, CF_XIT, CF_KAT, CF_KAPG, CF_GC = 0, 1024, 2048, 3072, 3584, 4096, 4100
NCF = 4104


def make_consts():
    f32 = np.float32
    p = np.arange(128)
    cf = np.zeros((128, NCF), f32)
    half = 64
    inv_freq = (10000.0 ** (-np.arange(half, dtype=np.float64) / half))
    pos = (np.arange(NT)[None, :] * 128 + p[:, None]).astype(np.float64)
    ang = pos[:, :, None] * inv_freq[None, None, :]
    cf[:, CF_COS:CF_COS + 1024] = np.cos(ang).reshape(128, -1)
    cf[:, CF_SIN:CF_SIN + 1024] = np.sin(ang).reshape(128, -1)
    cf[:, CF_NSIN:CF_NSIN + 1024] = -np.sin(ang).reshape(128, -1)
    lg = np.log(1.0 - 2.0 ** (-5.0 - np.arange(4, dtype=np.float64)))
    i = np.arange(128, dtype=np.float64)
    xi = np.exp((i[None, :] + 1.0) * lg[:, None])
    ka = np.exp(-(i[None, :] + 1.0) * lg[:, None]) * (128.0 ** -0.5)
    cf[:, CF_XIT:CF_XIT + 512] = np.broadcast_to(xi.reshape(1, 512), (128, 512))
    cf[:, CF_KAT:CF_KAT + 512] = np.broadcast_to(ka.reshape(1, 512), (128, 512))
    gC = np.exp(128.0 * lg)
    cf[:, CF_KAPG:CF_KAPG + 4] = (ka.T * gC[None, :])
    cf[:, CF_GC:CF_GC + 4] = gC[None, :]
    cb = np.zeros((128, NCB), f32)
    cb[:, CB_ID:CB_ID + 128] = np.eye(128)
    r = p[:, None]
    c = p[None, :]
    cb[:, CB_MRET:CB_MRET + 128] = (r <= c)
    strict = (r < c).astype(f32)
    incl = (r <= c).astype(f32)
    cb[:, CB_M4:CB_M4 + 512] = np.concatenate([strict, incl, strict, incl], axis=1)
    cb[:, CB_ML:CB_ML + 128] = (c < r)
    cb[0:64, CB_SEL] = 1.0
    cb[64:128, CB_SEL + 1] = 1.0
    cb[0:64, CB_ONES:CB_ONES + 64] = 1.0
    cb[64:128, CB_ONES + 64:CB_ONES + 128] = 1.0
    return cf, cb.astype(ml_dtypes.bfloat16)


DBG = {'ret_chunks': NT, 'ret_steps': 99}


def build_program(taps=None, phases=(1, 2, 3, 4, 5)):
    nc = bass.Bass("TRN2", target_bir_lowering=False)

    def din(name, shape, dt=F32):
        return nc.dram_tensor(name, list(shape), dt, kind="ExternalInput").ap()
    x = din("x", [T, D])
    w_in = din("w_in", [D, 3584])
    w_out = din("w_out", [D, D])
    wg_d = din("ffn_w_gate", [D, DFF])
    wu_d = din("ffn_w_up", [D, DFF])
    wd_d = din("ffn_w_down", [DFF, D])
    w1_d = din("rwkv_w1", [D, 64])
    a1_d = din("rwkv_a1", [D, 64])
    g1_d = din("rwkv_g1", [D, 128])
    w2_d = din("rwkv_w2", [64, 512])
    a2_d = din("rwkv_a2", [64, 512])
    g2_d = din("rwkv_g2", [128, 512])
    vecs_d = din("vecs", [128, NV])
    bct_d = din("bct", [128, 4608])
    cf_d = din("cf", [128, NCF])
    cb_d = din("cb", [128, NCB], BF16)
    out = nc.dram_tensor("out", [T, D], F32, kind="ExternalOutput").ap()
    tap_out = {}
    taps = taps or {}

    S = Sched(nc)
    st = ExitStack()
    ARENA_BYTES = 200 * 1024
    arena_t = st.enter_context(nc.sbuf_tensor("arena", [128, ARENA_BYTES // 4], F32))
    A = Arena(arena_t[:], ARENA_BYTES)
    pp = [st.enter_context(nc.psum_tensor(f"pp{i}", [128, 1024], F32)) for i in range(4)]
    bank = [pp[i // 2][:, (i % 2) * 512:(i % 2) * 512 + 512] for i in range(8)]
    bankB = [Buf(f"bank{i}") for i in range(8)]

    def bankbf(i):
        return bank[i].bitcast(BF16)

    def act(out_, in_, func, r, w, bias=None, scale=None, accum=None):
        kw = {}
        if bias is not None:
            kw['bias'] = bias
        if scale is not None:
            kw['scale'] = scale
        if accum is not None:
            kw['accum_out'] = accum
        S.op('act', lambda e: e.activation(out=out_, in_=in_, func=func, **kw), reads=r, writes=w)

    def tt(out_, a, b, op, r, w, q='dve'):
        S.op(q, lambda e: e.tensor_tensor(out=out_, in0=a, in1=b, op=op), reads=r, writes=w)

    def ts(out_, a, s1, s2, op0, op1, r, w, q='dve'):
        if s2 is None:
            S.op(q, lambda e: e.tensor_scalar(out=out_, in0=a, scalar1=s1, scalar2=None, op0=op0), reads=r, writes=w)
        else:
            S.op(q, lambda e: e.tensor_scalar(out=out_, in0=a, scalar1=s1, scalar2=s2, op0=op0, op1=op1), reads=r, writes=w)

    def stt(out_, a, s, b, op0, op1, r, w):
        S.op('dve', lambda e: e.scalar_tensor_tensor(out=out_, in0=a, scalar=s, in1=b, op0=op0, op1=op1), reads=r, writes=w)

    def mm(out_, lhsT, rhs, start, stop, r, w):
        S.op('pe', lambda e: e.matmul(out=out_, lhsT=lhsT, rhs=rhs, start=start, stop=stop), reads=r, writes=w)

    def mm2(out_, lhsT, rhs, start, stop, r, w):
        if lhsT.shape[0] == 128:
            mm(out_, lhsT[0:64], rhs[0:64], start, False, r, w)
            mm(out_, lhsT[64:128], rhs[64:128], False, stop, r, w)
        else:
            mm(out_, lhsT, rhs, start, stop, r, w)

    def dma(q, out_, in_, r, w, **kw):
        S.op(q, lambda e: e.dma_start(out=out_, in_=in_, **kw), reads=r, writes=w, dma=True)

    def cp(q, out_, in_, r, w):
        if q == 'act':
            act(out_, in_, AF.Copy, r, w)
        else:
            S.op(q, lambda e: e.tensor_copy(out=out_, in_=in_), reads=r, writes=w)

    def rsqrt_tiny(dst, src, scale, eps, r, w):
        ts(dst, src, scale, eps, ALU.mult, ALU.add, r, w)
        act(dst, dst, AF.Ln, w, w)
        act(dst, dst, AF.Exp, w, w, scale=-0.5)

    hT = A.take([8, T + 1], BF16)
    yT = A.take([8, T], BF16)
    cb = A.take([NCB], BF16)
    vecs = A.take([NV], F32)
    om = A.take([NV], F32)
    mhalf = A.take([4], F32)
    gtab = A.take([1024], F32)
    stat = A.take([64], F32)
    ss_all = A.take([3, NT], F32)
    rstd_all = A.take([3, NT], F32)
    B_const = Buf('const')
    B_gtab = Buf('gtab')
    hTb = [Buf(f'hT{n}') for n in range(NT)]
    yTb = [[Buf(f'yT{c}_{n}') for n in range(NT)] for c in range(8)]
    ident = cb[:, CB_ID:CB_ID + 128]
    PERSIST = A.mark()

    def tap(name, ap, shape, reads):
        if name in taps:
            d = nc.dram_tensor("tap_" + name, list(shape), ap.dtype, kind="ExternalOutput").ap()
            tap_out[name] = d
            dma('sp', d, ap, reads, [])

    dma('sp', cb, cb_d, [], [B_const])
    dma('sp', vecs, vecs_d, [], [B_const])
    dma('sp', gtab, bct_d[:, 0:1024], [], [B_gtab])
    S.op('pool', lambda e: e.memset(mhalf, -0.5), writes=[B_const])
    ts(om, vecs, -1.0, 1.0, ALU.mult, ALU.add, [B_const], [B_const])
    S.op('pool', lambda e: e.memset(hT[:, :, 0:1], 0.0), writes=[hTb[0]])

    xst = [A.take([D], F32) for _ in range(3)]
    xstB = [Buf(f'xst{i}') for i in range(3)]
    hb = [A.take([D], BF16) for _ in range(2)]
    hbB = [Buf(f'hb{i}') for i in range(2)]
    sqj = A.take([D], BF16)
    sqjB = Buf('sqj')
    statB = [Buf(f'stat{i}') for i in range(4)]
    NORM_END = A.mark()

    ssB = [Buf(f'ss{i}') for i in range(3)]
    rsB = [Buf(f'rs{i}') for i in range(3)]

    def norm_stats(n, src, srcB, which):
        act(sqj, src, AF.Square, [srcB], [sqjB, ssB[which]], accum=ss_all[:, which, n:n + 1])

    def norm_rstd(which):
        rsqrt_tiny(rstd_all[:, which, :], ss_all[:, which, :], 1.0 / D, NORM_EPS, [ssB[which]], [rsB[which]])

    def norm_apply(n, src, srcB, which, pbank):
        h = hb[n % 2]
        stt(h, src, rstd_all[:, which, n:n + 1], gtab, ALU.mult, ALU.mult, [srcB, rsB[which], B_gtab], [hbB[n % 2]])
        pt = bankbf(pbank).rearrange("p (c t) -> p c t", c=8)
        for c in range(8):
            S.op('pe', lambda e, c=c: e.transpose(out=pt[:, c, :], in_=h[:, c * 128:(c + 1) * 128], identity=ident),
                 reads=[hbB[n % 2], B_const], writes=[bankB[pbank]])
        cp('act', hT[:, :, 1 + n * 128:1 + (n + 1) * 128], pt, [bankB[pbank]], [hTb[n]])

    xv = x.rearrange("(n p) d -> n p d", p=128)
    ov = out.rearrange("(n p) d -> n p d", p=128)
    if 2 in phases:
        cf = A.take([NCF], F32)
        dma('sp', cf, cf_d, [], [B_const])
        wret = A.take([8, 2048], BF16)
        wretB = Buf('wret')
        wv = w_in.rearrange("(c p) n -> p c n", p=128)
        for c in range(8):
            dma('pool', wret[:, c, :], wv[:, c, 1536:3584], [], [wretB])
        P2START = A.mark()
    for n in range(NT):
        dma('sp', xst[n % 3], xv[n], [], [xstB[n % 3]])
        norm_stats(n, xst[n % 3], xstB[n % 3], 0)
    norm_rstd(0)
    for n in range(NT):
        dma('sp', xst[n % 3], xv[n], [], [xstB[n % 3]])
        norm_apply(n, xst[n % 3], xstB[n % 3], 0, n % 2)
    tap('hT', hT, [128, 8, T + 1], hTb)

    if 2 in phases:
        A.reset(P2START)
        gnw = A.take([512], F32)
        dma('sp', gnw, bct_d[:, 4096:4608], [], [B_const])
        qa = A.take([512], F32)
        qb = A.take([512], F32)
        qrot = A.take([512], BF16)
        krot = A.take([512], BF16)
        qT = A.take([4, 128], BF16)
        kT = A.take([4, 128], BF16)
        PT = A.take([4, 128], BF16)
        Vb = A.take([512], BF16)
        Vk = A.take([512], BF16)
        R = A.take([512], F32)
        Rt = A.take([512], F32)
        Rb = A.take([512], BF16)
        sqy = A.take([512], F32)
        yn = A.take([512], F32)
        sgt = A.take([512], F32)
        yo = A.take([512], BF16)
        rst = A.take([32], F32)
        Bq = {k: Buf('r_' + k) for k in ['qa', 'qb', 'qrot', 'krot', 'qT', 'kT', 'PT', 'Vb', 'Vk', 'R', 'Rt', 'Rb', 'sqy', 'yn', 'sgt', 'yo', 'rst']}
        kapg_bc = cf[:, CF_KAPG:CF_KAPG + 4].unsqueeze(2).to_broadcast([128, 4, 128])
        gC_bc = cf[:, CF_GC:CF_GC + 4].unsqueeze(2).to_broadcast([128, 4, 128])
        xiT = cf[:, CF_XIT:CF_XIT + 512].rearrange("p (h t) -> p h t", h=4)
        kaT = cf[:, CF_KAT:CF_KAT + 512].rearrange("p (h t) -> p h t", h=4)
        mret_bc = cb[:, CB_MRET:CB_MRET + 128].unsqueeze(1).to_broadcast([128, 4, 128])
        PQ, PK, PV, PG, PTB, PS, PY, PKV = range(8)

        def v4(ap):
            return ap.rearrange("p (h e) -> p h e", h=4)

        def rot(ps, psB, dst, dstB, n):
            cosb = cf[:, CF_COS + n * 64:CF_COS + (n + 1) * 64].unsqueeze(1).unsqueeze(1).to_broadcast([128, 4, 2, 64])
            sinb = cf[:, CF_SIN + n * 64:CF_SIN + (n + 1) * 64].unsqueeze(1).to_broadcast([128, 4, 64])
            nsinb = cf[:, CF_NSIN + n * 64:CF_NSIN + (n + 1) * 64].unsqueeze(1).to_broadcast([128, 4, 64])
            p4 = ps.rearrange("p (h two f) -> p h two f", h=4, two=2)
            tt(qa.rearrange("p (h two f) -> p h two f", h=4, two=2), p4, cosb, ALU.mult, [psB, B_const], [Bq['qa']])
            qb4 = qb.rearrange("p (h two f) -> p h two f", h=4, two=2)
            tt(qb4[:, :, 0, :], p4[:, :, 1, :], nsinb, ALU.mult, [psB, B_const], [Bq['qb']])
            tt(qb4[:, :, 1, :], p4[:, :, 0, :], sinb, ALU.mult, [psB, B_const], [Bq['qb']])
            tt(dst, qa, qb, ALU.add, [Bq['qa'], Bq['qb']], [dstB])

        for n in range(DBG['ret_chunks']):
            RS = DBG['ret_steps']
            def proj(nn):
                tok = slice(1 + nn * 128, 1 + (nn + 1) * 128)
                for j, pb in enumerate((PQ, PK, PV, PG)):
                    for c in range(8):
                        mm(bank[pb], hT[:, c, tok], wret[:, c, j * 512:(j + 1) * 512], c == 0, c == 7,
                           [hTb[nn], wretB], [bankB[pb]])
            if n == 0:
                proj(0)
            act(sgt, bank[PG], AF.Silu, [bankB[PG]], [Bq['sgt']])
            rot(bank[PQ], bankB[PQ], qrot, Bq['qrot'], n)
            rot(bank[PK], bankB[PK], krot, Bq['krot'], n)
            if RS < 3:
                continue
            ptb = bankbf(PTB).rearrange("p (c t) -> p c t", c=8)
            for h in range(4):
                S.op('pe', lambda e, h=h: e.transpose(out=ptb[:, h, :], in_=qrot[:, h * 128:(h + 1) * 128], identity=ident),
                     reads=[Bq['qrot'], B_const], writes=[bankB[PTB]])
            for h in range(4):
                S.op('pe', lambda e, h=h: e.transpose(out=ptb[:, 4 + h, :], in_=krot[:, h * 128:(h + 1) * 128], identity=ident),
                     reads=[Bq['krot'], B_const], writes=[bankB[PTB]])
            tt(qT, ptb[:, 0:4, :], xiT, ALU.mult, [bankB[PTB], B_const], [Bq['qT']])
            tt(kT, ptb[:, 4:8, :], kaT, ALU.mult, [bankB[PTB], B_const], [Bq['kT']])
            if RS < 4:
                continue
            ps4 = v4(bank[PS])
            for h in range(4):
                mm(ps4[:, h, :], kT[:, h, :], qT[:, h, :], True, True, [Bq['kT'], Bq['qT']], [bankB[PS]])
            tt(PT, ps4, mret_bc, ALU.mult, [bankB[PS], B_const], [Bq['PT']])
            if RS < 5:
                continue
            cp('act', Vb, bank[PV], [bankB[PV]], [Bq['Vb']])
            tt(v4(Vk), v4(bank[PV]), kapg_bc, ALU.mult, [bankB[PV], B_const], [Bq['Vk']])
            if RS < 6:
                continue
            py4 = v4(bank[PY])
            for h in range(4):
                mm(py4[:, h, :], PT[:, h, :], Vb[:, h * 128:(h + 1) * 128], True, n == 0, [Bq['PT'], Bq['Vb']], [bankB[PY]])
                if n > 0:
                    mm(py4[:, h, :], qT[:, h, :], Rb[:, h * 128:(h + 1) * 128], False, True, [Bq['qT'], Bq['Rb']], [bankB[PY]])
            if RS < 7:
                continue
            if n < NT - DBG.get('skiplast', 0):
                pkv4 = v4(bank[PKV])
                for h in range(4):
                    mm(pkv4[:, h, :], krot[:, h * 128:(h + 1) * 128], Vk[:, h * 128:(h + 1) * 128], True, True,
                       [Bq['krot'], Bq['Vk']], [bankB[PKV]])
                if n == 0:
                    cp('dve', R, bank[PKV], [bankB[PKV]], [Bq['R']])
                else:
                    tt(v4(Rt), v4(R), gC_bc, ALU.mult, [Bq['R'], B_const], [Bq['Rt']])
                    tt(R, Rt, bank[PKV], ALU.add, [Bq['Rt'], bankB[PKV]], [Bq['R']])
                cp('pool', Rb, R, [Bq['R']], [Bq['Rb']])
            if n + 1 < NT:
                proj(n + 1)
            s1 = rst[:, 0:4]
            s2 = rst[:, 4:8]
            mean = rst[:, 8:12]
            msq = rst[:, 12:16]
            rstd = rst[:, 16:20]
            S.op('dve', lambda e: e.tensor_reduce(out=s1, in_=py4, axis=AX.X, op=ALU.add), reads=[bankB[PY]], writes=[Bq['rst']])
            act(sqy, bank[PY], AF.Square, [bankB[PY]], [Bq['sqy']])
            S.op('dve', lambda e: e.tensor_reduce(out=s2, in_=v4(sqy), axis=AX.X, op=ALU.add), reads=[Bq['sqy']], writes=[Bq['rst']])
            ts(mean, s1, 1.0 / 128, None, ALU.mult, None, [Bq['rst']], [Bq['rst']])
            tt(msq, mean, mean, ALU.mult, [Bq['rst']], [Bq['rst']])
            stt(rstd, s2, 1.0 / 128, msq, ALU.mult, ALU.subtract, [Bq['rst']], [Bq['rst']])
            rsqrt_tiny(rstd, rstd, 1.0, RET_GN_EPS, [Bq['rst']], [Bq['rst']])
            tt(v4(yn), py4, mean.unsqueeze(2).to_broadcast([128, 4, 128]), ALU.subtract, [bankB[PY], Bq['rst']], [Bq['yn']])
            tt(v4(yn), v4(yn), rstd.unsqueeze(2).to_broadcast([128, 4, 128]), ALU.mult, [Bq['yn'], Bq['rst']], [Bq['yn']])
            tt(yn, yn, gnw, ALU.mult, [Bq['yn'], B_const], [Bq['yn']])
            tt(yo, yn, sgt, ALU.mult, [Bq['yn'], Bq['sgt']], [Bq['yo']])
            if RS < 9:
                continue
            for h in range(4):
                S.op('pe', lambda e, h=h: e.transpose(out=ptb[:, h, :], in_=yo[:, h * 128:(h + 1) * 128], identity=ident),
                     reads=[Bq['yo'], B_const], writes=[bankB[PTB]])
            cp('act', yT[:, 4:8, n * 128:(n + 1) * 128], ptb[:, 0:4, :], [bankB[PTB]], [yTb[4 + h][n] for h in range(4)])
        if 3 not in phases:
            tap('yT', yT, [128, 8, T], [b for l in yTb for b in l])
        S.barrier()


    if 3 in phases:
        S.barrier()
        A.reset(PERSIST)
        wl_f = A.take([8, 128], F32)
        gl_f = A.take([8, 128], F32)
        W1A = A.take([8, 128], BF16)
        W1B = A.take([8, 128], BF16)
        G1A = A.take([8, 128], BF16)
        G1B = A.take([8, 128], BF16)
        W2sb = A.take([512], BF16)
        A2sb = A.take([512], BF16)
        G2sb = A.take([512], BF16)
        L1 = A.take([T], BF16)
        L1g = A.take([T], BF16)
        lnxw = A.take([512], F32)
        lnxb = A.take([512], F32)
        B_lw = Buf('loraw')
        L1B = [Buf(f'L1_{i}') for i in range(4)]
        dma('sp', wl_f[:, :, 0:64], w1_d.rearrange("(c p) k -> p c k", p=128), [], [B_lw])
        dma('sp', wl_f[:, :, 64:128], a1_d.rearrange("(c p) k -> p c k", p=128), [], [B_lw])
        dma('sp', gl_f, g1_d.rearrange("(c p) k -> p c k", p=128), [], [B_lw])
        dma('sp', lnxw, bct_d[:, 3072:3584], [], [B_lw])
        dma('sp', lnxb, bct_d[:, 3584:4096], [], [B_lw])
        S.op('pool', lambda e: e.memset(W2sb, 0.0), writes=[B_lw])
        S.op('pool', lambda e: e.memset(A2sb, 0.0), writes=[B_lw])
        dma('pool', W2sb[0:64, :], w2_d, [B_lw], [B_lw])
        dma('pool', A2sb[64:128, :], a2_d, [B_lw], [B_lw])
        dma('pool', G2sb, g2_d, [], [B_lw])

        def vb(tab, col, k):
            return tab[:, col:col + 8].unsqueeze(2).to_broadcast([128, 8, k])
        tt(W1A[:, :, 0:64], wl_f[:, :, 0:64], vb(om, V_MUW, 64), ALU.mult, [B_lw, B_const], [B_lw])
        tt(W1A[:, :, 64:128], wl_f[:, :, 64:128], vb(om, V_MUA, 64), ALU.mult, [B_lw, B_const], [B_lw])
        tt(W1B[:, :, 0:64], wl_f[:, :, 0:64], vb(vecs, V_MUW, 64), ALU.mult, [B_lw, B_const], [B_lw])
        tt(W1B[:, :, 64:128], wl_f[:, :, 64:128], vb(vecs, V_MUA, 64), ALU.mult, [B_lw, B_const], [B_lw])
        tt(G1A, gl_f, vb(om, V_MUG, 128), ALU.mult, [B_lw, B_const], [B_lw])
        tt(G1B, gl_f, vb(vecs, V_MUG, 128), ALU.mult, [B_lw, B_const], [B_lw])
        for tb in range(4):
            rd = [hTb[4 * tb + i] for i in range(4)] + ([hTb[4 * tb - 1]] if tb > 0 else []) + [B_lw]
            for (WA, WB, pb) in ((W1A, W1B, 0), (G1A, G1B, 1)):
                for c in range(8):
                    mm(bank[pb], WA[:, c, :], hT[:, c, 1 + tb * 512:1 + (tb + 1) * 512], c == 0, False, rd, [bankB[pb]])
                    mm(bank[pb], WB[:, c, :], hT[:, c, tb * 512:(tb + 1) * 512], False, c == 7, rd, [bankB[pb]])
            blk = slice(tb * 512, (tb + 1) * 512)
            act(L1[0:64, blk], bank[0][0:64, :], AF.Tanh, [bankB[0]], [L1B[tb]])
            act(L1[64:128, blk], bank[0][64:128, :], AF.Copy, [bankB[0]], [L1B[tb]])
            act(L1g[:, blk], bank[1], AF.Sigmoid, [bankB[1]], [L1B[tb]])

        wrkv = A.take([8, 3, 128], BF16)
        AR = A.take([NT, 2, 128], BF16)
        BT = A.take([T], BF16)
        KT = A.take([T], BF16)
        vT = A.take([T], BF16)
        rkrT = A.take([T], BF16)
        rm = A.take([513], F32)
        km = A.take([513], F32)
        vm = A.take([513], F32)
        tnames = ['r', 'k0', 'sg', 'asg', 'cum', 'P', 'invP', 'Pp', 'ssk', 'kk', 't1']
        tmp = {k: A.take([512], F32) for k in tnames}
        sqk = A.take([512], BF16)
        PCt = A.take([NT], F32)
        Xb = [A.take([2, 2, 128], BF16) for _ in range(2)]
        Nn = [A.take([2, 128], BF16) for _ in range(2)]
        W1s = [A.take([2, 3, 128], BF16) for _ in range(2)]
        TTs = [A.take([2, 128], BF16) for _ in range(2)]
        BK = [A.take([2, 128], BF16) for _ in range(2)]
        Vt = [A.take([4, 128], BF16) for _ in range(2)]
        Xs = A.take([128], BF16)
        Us = A.take([128], BF16)
        Hs = A.take([64], F32)
        HP = A.take([64], F32)
        Hbz = A.take([2, 64], BF16)
        BKz = [A.take([2, 2, 128], BF16) for _ in range(2)]
        Yp = [A.take([4, 128], F32) for _ in range(2)]
        sqp = A.take([512], F32)
        ynp = A.take([512], F32)
        bon = A.take([512], F32)
        sB = A.take([8], F32)
        yop = A.take([4, 128], BF16)
        rstp = A.take([64], F32)
        Bw = Buf('wrkv')
        Bt_ = {k: Buf('t_' + k) for k in tnames + ['rm', 'km', 'vm', 'sqk', 'PCt']}
        ARb = [Buf(f'AR{n}') for n in range(NT)]
        BTb = [Buf(f'BT{i}') for i in range(4)]
        KTb = [Buf(f'KT{i}') for i in range(4)]
        vTb = [Buf(f'vT{i}') for i in range(4)]
        rkb = [Buf(f'rk{i}') for i in range(4)]
        Bs = {k: Buf('s_' + k) for k in ['X0', 'X1', 'N0', 'N1', 'W10', 'W11', 'TT0', 'TT1', 'BK0', 'BK1', 'Vt0', 'Vt1', 'Xs', 'Us', 'H', 'HP', 'Hb', 'Yp0', 'Yp1', 'BKz0', 'BKz1',
                                          'sqp', 'ynp', 'bon', 'sB', 'yop', 'rstp',
                                          'ps1', 'ps2', 'psN', 'psL', 'pT', 'pT2', 'psX', 'psU', 'psH', 'psY', 'psB', 'psG']}
        for k_, b_ in (('ps1', 0), ('ps2', 2), ('psN', 2), ('psL', 3), ('pT', 4), ('pT2', 4), ('psX', 5), ('psU', 5), ('psH', 5),
                       ('psY', 6), ('psB', 6), ('psG', 7)):
            Bs[k_] = bankB[b_]
        wv3 = w_in.rearrange("(c p) n -> p c n", p=128)
        m4 = cb[:, CB_M4:CB_M4 + 512]
        mS_bc = cb[:, CB_M4:CB_M4 + 128].unsqueeze(1).to_broadcast([128, 2, 128])
        m3_bc = cb[:, CB_M4 + 128:CB_M4 + 512].unsqueeze(1).to_broadcast([128, 2, 384])
        mL_bc = cb[:, CB_ML:CB_ML + 128].unsqueeze(1).to_broadcast([128, 2, 128])
        id_bc = ident.unsqueeze(1).to_broadcast([128, 2, 128])
        sel = cb[:, CB_SEL:CB_SEL + 2]
        ones_bd = cb[:, CB_ONES:CB_ONES + 128]
        ps1 = pp[0][:].rearrange("p (h c) -> p h c", h=2)
        ps2 = bank[2][:, 0:256].rearrange("p (h s) -> p h s", h=2)
        psN = bank[2][:, 256:512].rearrange("p (h s) -> p h s", h=2)
        psL = bank[3].rearrange("p (h c) -> p h c", h=2)
        pTb = bankbf(4)
        pT3 = pTb[:, 0:384].rearrange("p (j t) -> p j t", j=3)
        pT2 = pTb[:, 512:1024].rearrange("p (j t) -> p j t", j=4)
        psX = bank[5][:, 0:128]
        psU = bank[5][:, 128:256]
        psH = bank[5][:, 256:384]
        psY = bank[6][:, 0:128]
        psB = bank[6][:, 128:136]
        psG = bank[7]

        def pair_setup(p):
            vp = V_PAIR + 8 * p
            col = lambda j: vecs[:, vp + j:vp + j + 1]
            ocol = lambda j: om[:, vp + j:vp + j + 1]
            return col, ocol

        def prep_block(p, tb):
            col, ocol = pair_setup(p)
            if tb == 0:
                for j in range(3):
                    dma('pool', wrkv[:, :, j, :], wv3[:, :, j * 512 + p * 128:j * 512 + (p + 1) * 128], [], [Bw])
                for nm_ in ('rm', 'km', 'vm'):
                    tl = {'rm': rm, 'km': km, 'vm': vm}[nm_]
                    S.op('pool', lambda e, tl=tl: e.memset(tl[:, 0:1], 0.0), writes=[Bt_[nm_]])
            blk = slice(tb * 512, (tb + 1) * 512)
            rd = [hTb[4 * tb + i] for i in range(4)] + [Bw]
            for j in range(3):
                for c in range(8):
                    mm(bank[j], wrkv[:, c, j, :], hT[:, c, 1 + tb * 512:1 + (tb + 1) * 512], c == 0, c == 7, rd, [bankB[j]])
            mm(bank[3], W2sb[:, p * 128:(p + 1) * 128], L1[:, blk], True, True, [B_lw, L1B[tb]], [bankB[3]])
            mm(bank[4], A2sb[:, p * 128:(p + 1) * 128], L1[:, blk], True, True, [B_lw, L1B[tb]], [bankB[4]])
            for j, (tl, nm_, dst, dstB) in enumerate(((rm, 'rm', tmp['r'], Bt_['r']), (km, 'km', tmp['k0'], Bt_['k0']), (vm, 'vm', vT[:, blk], vTb[tb]))):
                act(tl[:, 1:513], bank[j], AF.Copy, [bankB[j], B_const], [Bt_[nm_]], scale=col(j))
                stt(dst, bank[j], ocol(j), tl[:, 0:512], ALU.mult, ALU.add, [bankB[j], Bt_[nm_], B_const], [dstB])
                S.op('pool', lambda e, tl=tl: e.tensor_copy(out=tl[:, 0:1], in_=tl[:, 512:513]), reads=[Bt_[nm_]], writes=[Bt_[nm_]])
            r_, k0 = tmp['r'], tmp['k0']
            act(tmp['sg'], bank[3], AF.Sigmoid, [bankB[3], B_const], [Bt_['sg']], bias=col(3))
            act(tmp['asg'], bank[4], AF.Sigmoid, [bankB[4], B_const], [Bt_['asg']], bias=col(4))
            for ch in range(4):
                cs = slice(ch * 128, (ch + 1) * 128)
                S.op('dve', lambda e, cs=cs: e.tensor_tensor_scan(out=tmp['cum'][:, cs], data0=tmp['sg'][:, cs], data1=tmp['sg'][:, cs],
                                                                   initial=0.0, op0=ALU.add, op1=ALU.bypass),
                     reads=[Bt_['sg']], writes=[Bt_['cum']])
            act(tmp['P'], tmp['cum'], AF.Exp, [Bt_['cum']], [Bt_['P']], scale=-C0)
            act(tmp['invP'], tmp['cum'], AF.Exp, [Bt_['cum']], [Bt_['invP']], scale=C0)
            tt(tmp['sg'], tmp['cum'], tmp['sg'], ALU.subtract, [Bt_['cum'], Bt_['sg']], [Bt_['sg']])
            act(tmp['Pp'], tmp['sg'], AF.Exp, [Bt_['sg']], [Bt_['Pp']], scale=-C0)
            S.op('pool', lambda e, tb=tb: e.tensor_copy(out=PCt[:, tb * 4:(tb + 1) * 4],
                                                        in_=tmp['P'].rearrange("p (c t) -> p c t", c=4)[:, :, 127]),
                 reads=[Bt_['P']], writes=[Bt_['PCt']])
            act(sqk, k0, AF.Square, [Bt_['k0'], B_const], [Bt_['sqk']], scale=col(5))
            mm(bank[5], ones_bd, sqk, True, True, [Bt_['sqk'], B_const], [bankB[5]])
            act(tmp['ssk'], bank[5], AF.Ln, [bankB[5]], [Bt_['ssk']])
            act(tmp['ssk'], tmp['ssk'], AF.Exp, [Bt_['ssk']], [Bt_['ssk']], scale=-0.5)
            stt(tmp['kk'], k0, col(5), tmp['ssk'], ALU.mult, ALU.mult, [Bt_['k0'], Bt_['ssk'], B_const], [Bt_['kk']])
            ts(tmp['t1'], tmp['asg'], col(6), ocol(6), ALU.mult, ALU.add, [Bt_['asg'], B_const], [Bt_['t1']])
            tt(tmp['t1'], tmp['t1'], k0, ALU.mult, [Bt_['t1'], Bt_['k0']], [Bt_['t1']])
            arv = AR[:, 4 * tb:4 * tb + 4, :, :]
            c4 = lambda a: a.rearrange("p (c t) -> p c t", c=4)
            stt(arv[:, :, 0, :], c4(tmp['kk']), -1.0, c4(tmp['Pp']), ALU.mult, ALU.mult, [Bt_['kk'], Bt_['Pp']], [ARb[4 * tb + i] for i in range(4)])
            tt(arv[:, :, 1, :], c4(r_), c4(tmp['P']), ALU.mult, [Bt_['r'], Bt_['P']], [ARb[4 * tb + i] for i in range(4)])
            tt(tmp['kk'], tmp['kk'], tmp['asg'], ALU.mult, [Bt_['kk'], Bt_['asg']], [Bt_['kk']])
            tt(BT[:, blk], tmp['kk'], tmp['invP'], ALU.mult, [Bt_['kk'], Bt_['invP']], [BTb[tb]])
            tt(KT[:, blk], tmp['t1'], tmp['invP'], ALU.mult, [Bt_['t1'], Bt_['invP']], [KTb[tb]])
            stt(rkrT[:, blk], r_, col(7), tmp['t1'], ALU.mult, ALU.mult, [Bt_['r'], Bt_['t1'], B_const], [rkb[tb]])

        def make_scan(p):
            col, ocol = pair_setup(p)
            def local(n):
                cs = slice(n * 128, (n + 1) * 128)
                tb = n // 4
                bz = BKz[n % 2]
                W1, TT, W1B, TTB = W1s[n % 2], TTs[n % 2], Bs[f'W1{n % 2}'], Bs[f'TT{n % 2}']
                bzB = Bs[f'BKz{n % 2}']
                for h in range(2):
                    hp = slice(64 * h, 64 * h + 64)
                    S.op('pool', lambda e, h=h, hp=hp: e.tensor_copy(out=bz[hp, 0, h, :], in_=BT[hp, cs]), reads=[BTb[tb]], writes=[bzB])
                    S.op('pool', lambda e, h=h, hp=hp: e.tensor_copy(out=bz[hp, 1, h, :], in_=KT[hp, cs]), reads=[KTb[tb]], writes=[bzB])
                for h in range(2):
                    mm(ps1[:, h, 0:256], bz[:, 0, h, :], AR[:, n, :, :], True, True, [bzB, ARb[n]], [Bs['ps1']])
                    mm(ps1[:, h, 256:512], bz[:, 1, h, :], AR[:, n, :, :], True, True, [bzB, ARb[n]], [Bs['ps1']])
                    mm(ps2[:, h, :], AR[:, n, 0, :], bz[:, 0, h, :], True, True, [bzB, ARb[n]], [Bs['ps2']])
                tt(Xb[0][:, :, 0, :], ps1[:, :, 0:128], mS_bc, ALU.mult, [Bs['ps1'], B_const], [Bs['X0']])
                tt(W1, ps1[:, :, 128:512], m3_bc, ALU.mult, [Bs['ps1'], B_const], [W1B])
                tt(Nn[0], ps2, mL_bc, ALU.mult, [Bs['ps2'], B_const], [Bs['N0']])
                tt(Xb[1][:, :, 1, :], Xb[0][:, :, 0, :], id_bc, ALU.add, [Bs['X0'], B_const], [Bs['X1']])
                yield
                cur = 0
                for k in range(4):
                    nx = 1 - cur
                    lastk = (k == 3)
                    for h in range(2):
                        if lastk:
                            mm(psL[:, h, 128:256], Nn[cur][:, h, :], Xb[cur][:, h, 1, :], True, True, [Bs[f'N{cur}'], Bs[f'X{cur}']], [Bs['psL']])
                        elif k == 0:
                            mm(psL[:, h, 0:128], Nn[cur][:, h, :], Xb[cur][:, h, 0, :], True, True, [Bs[f'N{cur}'], Bs[f'X{cur}']], [Bs['psL']])
                            mm(psN[:, h, :], Xb[cur][:, h, 0, :], Nn[cur][:, h, :], True, True, [Bs[f'N{cur}'], Bs[f'X{cur}']], [Bs['psN']])
                        else:
                            mm(psL[:, h, :], Nn[cur][:, h, :], Xb[cur][:, h, :, :], True, True, [Bs[f'N{cur}'], Bs[f'X{cur}']], [Bs['psL']])
                            mm(psN[:, h, :], Xb[cur][:, h, 0, :], Nn[cur][:, h, :], True, True, [Bs[f'N{cur}'], Bs[f'X{cur}']], [Bs['psN']])
                    if lastk:
                        tt(TT, Xb[cur][:, :, 1, :], psL[:, :, 128:256], ALU.add, [Bs['psL'], Bs[f'X{cur}']], [TTB])
                    else:
                        cp('act', Xb[nx][:, :, 0, :], psL[:, :, 0:128], [Bs['psL']], [Bs[f'X{nx}']])
                        cp('dve', Nn[nx], psN, [Bs['psN']], [Bs[f'N{nx}']])
                        if k > 0:
                            tt(Xb[nx][:, :, 1, :], Xb[cur][:, :, 1, :], psL[:, :, 128:256], ALU.add, [Bs['psL'], Bs[f'X{cur}']], [Bs[f'X{nx}']])
                    cur = nx
                    yield

            def chain(n):
                cs = slice(n * 128, (n + 1) * 128)
                tb = n // 4
                g = (n // 4) % 2
                bk = BK[n % 2]
                bkB = Bs[f'BK{n % 2}']
                vt = Vt[g][:, n % 4, :]
                vtB = Bs[f'Vt{g}']
                W1, TT, W1B, TTB = W1s[n % 2], TTs[n % 2], Bs[f'W1{n % 2}'], Bs[f'TT{n % 2}']
                S.op('pe', lambda e: e.transpose(out=pT3[:, 0, :], in_=vT[:, cs], identity=ident), reads=[vTb[tb], B_const], writes=[Bs['pT']])
                S.op('pe', lambda e: e.transpose(out=pT3[:, 1, :], in_=BT[:, cs], identity=ident), reads=[BTb[tb], B_const], writes=[Bs['pT']])
                S.op('pe', lambda e: e.transpose(out=pT3[:, 2, :], in_=KT[:, cs], identity=ident), reads=[KTb[tb], B_const], writes=[Bs['pT']])
                cp('act', vt, pT3[:, 0, :], [Bs['pT']], [vtB])
                cp('act', bk, pT3[:, 1:3, :], [Bs['pT']], [bkB])
                yield
                for h in range(2):
                    hs = slice(64 * h, 64 * h + 64)
                    if n > 0:
                        mm(psX[:, hs], AR[:, n, 0, :], Hbz[:, h, :], True, False, [ARb[n], Bs['Hb']], [Bs['psX']])
                    mm(psX[:, hs], W1[:, h, 1, :], vt[:, hs], n == 0, True, [W1B, vtB], [Bs['psX']])
                cp('act', Xs, psX, [Bs['psX']], [Bs['Xs']])
                yield
                for h in range(2):
                    hs = slice(64 * h, 64 * h + 64)
                    mm(psU[:, hs], TT[:, h, :], Xs[:, hs], True, True, [TTB, Bs['Xs']], [Bs['psU']])
                cp('dve', Us, psU, [Bs['psU']], [Bs['Us']])
                yield
                for h in range(2):
                    hs = slice(64 * h, 64 * h + 64)
                    if n > 0:
                        mm(psY[:, hs], AR[:, n, 1, :], Hbz[:, h, :], True, False, [ARb[n], Bs['Hb']], [Bs['psY']])
                    mm(psY[:, hs], W1[:, h, 0, :], Us[:, hs], n == 0, False, [W1B, Bs['Us']], [Bs['psY']])
                    mm(psY[:, hs], W1[:, h, 2, :], vt[:, hs], False, True, [W1B, vtB], [Bs['psY']])
                mm(psH, bk[:, 0, :], Us, True, False, [bkB, Bs['Us']], [Bs['psH']])
                mm(psH, bk[:, 1, :], vt, False, True, [bkB, vtB], [Bs['psH']])
                cp('act', Yp[g][:, n % 4, :], psY, [Bs['psY']], [Bs[f'Yp{g}']])
                if n > 0:
                    ts(HP, Hs, PCt[:, n:n + 1], None, ALU.mult, None, [Bs['H'], Bt_['PCt']], [Bs['HP']])
                for h in range(2):
                    hp = slice(64 * h, 64 * h + 64)
                    hs = slice(64 * h, 64 * h + 64)
                    if n > 0:
                        stt(Hs[hp, :], psH[hp, hs], PCt[hp, n:n + 1], HP[hp, :], ALU.mult, ALU.add, [Bs['psH'], Bs['HP'], Bt_['PCt']], [Bs['H']])
                    else:
                        ts(Hs[hp, :], psH[hp, hs], PCt[hp, n:n + 1], None, ALU.mult, None, [Bs['psH'], Bt_['PCt']], [Bs['H']])
                for h in range(2):
                    hp = slice(64 * h, 64 * h + 64)
                    cp('act', Hbz[hp, h, :], Hs[hp, :], [Bs['H']], [Bs['Hb']])
                yield

            def post(tg):
                g = tg % 2
                y3 = Yp[g].rearrange("p j (h e) -> p (j h) e", h=2)
                yB = Bs[f'Yp{g}']
                s1, s2, mean, msq, rstd = (rstp[:, 8 * i:8 * i + 8] for i in range(5))
                v8 = lambda a: a.rearrange("p (j e) -> p j e", j=8)
                S.op('dve', lambda e: e.tensor_reduce(out=s1, in_=y3, axis=AX.X, op=ALU.add), reads=[yB], writes=[Bs['rstp']])
                act(sqp, Yp[g].rearrange("p j c -> p (j c)"), AF.Square, [yB], [Bs['sqp']])
                S.op('dve', lambda e: e.tensor_reduce(out=s2, in_=v8(sqp), axis=AX.X, op=ALU.add), reads=[Bs['sqp']], writes=[Bs['rstp']])
                yield
                ts(mean, s1, 1.0 / 64, None, ALU.mult, None, [Bs['rstp']], [Bs['rstp']])
                tt(msq, mean, mean, ALU.mult, [Bs['rstp']], [Bs['rstp']])
                stt(rstd, s2, 1.0 / 64, msq, ALU.mult, ALU.subtract, [Bs['rstp']], [Bs['rstp']])
                rsqrt_tiny(rstd, rstd, 1.0, RWKV_GN_EPS, [Bs['rstp']], [Bs['rstp']])
                yield
                tt(v8(ynp), y3, mean.unsqueeze(2).to_broadcast([128, 8, 64]), ALU.subtract, [yB, Bs['rstp']], [Bs['ynp']])
                tt(v8(ynp), v8(ynp), rstd.unsqueeze(2).to_broadcast([128, 8, 64]), ALU.mult, [Bs['ynp'], Bs['rstp']], [Bs['ynp']])
                yield
                y4 = ynp.rearrange("p (j c) -> p j c", j=4)
                tt(y4, y4, lnxw[:, p * 128:(p + 1) * 128].unsqueeze(1).to_broadcast([128, 4, 128]), ALU.mult, [Bs['ynp'], B_lw], [Bs['ynp']])
                tt(y4, y4, lnxb[:, p * 128:(p + 1) * 128].unsqueeze(1).to_broadcast([128, 4, 128]), ALU.add, [Bs['ynp'], B_lw], [Bs['ynp']])
                yield
                for j in range(4):
                    n = 4 * tg + j
                    cs = slice(n * 128, (n + 1) * 128)
                    mm(psB[:, 2 * j:2 * j + 2], rkrT[:, cs], sel, True, True, [rkb[tg], B_const], [Bs['psB']])
                    mm(psG[:, j * 128:(j + 1) * 128], L1g[:, cs], G2sb[:, p * 128:(p + 1) * 128], True, True, [L1B[tg], B_lw], [Bs['psG']])
                cp('act', sB, psB, [Bs['psB']], [Bs['sB']])
                yield
                tt(v8(bon), Vt[g].rearrange("p j (h e) -> p (j h) e", h=2), sB.unsqueeze(2).to_broadcast([128, 8, 64]), ALU.mult,
                   [Bs[f'Vt{g}'], Bs['sB']], [Bs['bon']])
                tt(ynp, ynp, bon, ALU.add, [Bs['ynp'], Bs['bon']], [Bs['ynp']])
                yield
                tt(yop.rearrange("p j c -> p (j c)"), ynp, psG, ALU.mult, [Bs['ynp'], Bs['psG']], [Bs['yop']])
                yield
                for j in range(4):
                    S.op('pe', lambda e, j=j: e.transpose(out=pT2[:, j, :], in_=yop[:, j, :], identity=ident), reads=[Bs['yop'], B_const], writes=[Bs['pT2']])
                cp('act', yT[:, p, tg * 512:(tg + 1) * 512], pT2.rearrange("p j t -> p (j t)"), [Bs['pT2']], [yTb[p][4 * tg + j] for j in range(4)])
                yield

            return local, chain, post

        for i_ in range(2):
            S.op('pool', lambda e, i_=i_: e.memset(BKz[i_], 0.0), writes=[Bs[f'BKz{i_}']])
        NP = DBG.get('pairs', 4)
        for tb in range(4):
            prep_block(0, tb)
        scans = [make_scan(p) for p in range(NP)]

        def drain(g):
            for _ in g:
                pass
        S.op('pool', lambda e: e.memset(Hs, 0.0), writes=[Bs['H']])
        S.op('pool', lambda e: e.memset(Hbz, 0.0), writes=[Bs['Hb']])
        drain(scans[0][0](0))
        pend = []

        def prep_gen(p_, tb_):
            prep_block(p_, tb_)
            yield

        for p in range(NP):
            local, chain, post = scans[p]
            for n in range(NT):
                while len(pend) > 2:
                    drain(pend.pop(0))
                a = chain(n)
                if n + 1 < NT:
                    b = local(n + 1)
                elif p + 1 < NP:
                    b = scans[p + 1][0](0)
                else:
                    b = iter(())
                done_a = done_b = False
                while not (done_a and done_b):
                    if not done_a:
                        try:
                            next(a)
                        except StopIteration:
                            done_a = True
                    if not done_b:
                        try:
                            next(b)
                        except StopIteration:
                            done_b = True
                    if pend:
                        try:
                            next(pend[0])
                        except StopIteration:
                            pend.pop(0)
                if n % 4 == 3:
                    pend.append(post(n // 4))
                    if p + 1 < NP:
                        pend.append(prep_gen(p + 1, n // 4))
            if p + 1 == NP:
                while pend:
                    drain(pend.pop(0))
            if p + 1 < NP:
                S.op('dve', lambda e: e.memset(Hs, 0.0), writes=[Bs['H']])
                S.op('pool', lambda e: e.memset(Hbz, 0.0), writes=[Bs['Hb']])
        tap('yT', yT, [128, 8, T], [b for l in yTb for b in l])
        S.barrier()


    if 4 in phases:
        S.barrier()
        A.reset(NORM_END)
        xres = A.take([NT, D], F32)
        xresB = [Buf(f'xres{n}') for n in range(NT)]
        P4 = A.mark()
        wout = A.take([8, D], BF16)
        woutB = Buf('wout')
        wo_v = w_out.rearrange("(c p) n -> p c n", p=128)
        for c in range(8):
            dma('pool', wout[:, c, :], wo_v[:, c, :], [], [woutB])
        dma('sp', gtab, bct_d[:, 1024:2048], [], [B_gtab])
        for n in range(NT):
            dma('sp', xst[n % 3], xv[n], [], [xstB[n % 3]])
            pb = 2 * (n % 2)
            for half in range(2):
                for c in range(8):
                    mm(bank[pb + half], yT[:, c, n * 128:(n + 1) * 128], wout[:, c, half * 512:(half + 1) * 512], c == 0, c == 7,
                       [yTb[c][n], woutB], [bankB[pb + half]])
            tt(xres[:, n, :], pp[n % 2][:], xst[n % 3], ALU.add, [bankB[pb], bankB[pb + 1], xstB[n % 3]], [xresB[n]])
            norm_stats(n, xres[:, n, :], xresB[n], 1)
        norm_rstd(1)
        for n in range(NT):
            norm_apply(n, xres[:, n, :], xresB[n], 1, 4 + n % 2)
        tap('xres', xres, [128, NT, D], xresB)

    if 5 in phases:
        S.barrier()
        A.reset(P4)
        hid = yT[:, 0:6, :]
        hidB = [Buf(f'hid{i}') for i in range(4)]
        wgu = [A.take([2, 8, 256], BF16) for _ in range(2)]
        wguB = [Buf(f'wgu{i}') for i in range(2)]
        wd = A.take([6, D], BF16)
        wdB = Buf('wd')
        gs = [A.take([514], F32) for _ in range(2)]
        gsB = [Buf(f'gs{i}') for i in range(2)]
        acc = [A.take([512], F32) for _ in range(2)]
        accB = [Buf(f'acc{i}') for i in range(2)]
        sl = [A.take([512], F32) for _ in range(2)]
        slB = [Buf(f'sl{i}') for i in range(2)]
        ost = [xst[0], xst[1]]
        ostB = [xstB[0], xstB[1]]
        dma('sp', gtab, bct_d[:, 2048:3072], [], [B_gtab])
        wg_v = wg_d.rearrange("(c p) n -> p c n", p=128)
        wu_v = wu_d.rearrange("(c p) n -> p c n", p=128)
        wd_v = wd_d.rearrange("(m p) n -> p m n", p=128)
        quarters = [(0, 6), (6, 6), (12, 5), (17, 5)]

        def load_wgu(m):
            wb_ = (m // 2) % 2
            dma('pool', wgu[wb_][:, 0, :, :], wg_v[:, :, m * 128:(m + 2) * 128], [], [wguB[wb_]])
            dma('pool', wgu[wb_][:, 1, :, :], wu_v[:, :, m * 128:(m + 2) * 128], [], [wguB[wb_]])
        load_wgu(0)
        it = 0
        for qi, (m0, nq) in enumerate(quarters):
            dma('pool', wd[:, 0:nq, :], wd_v[:, m0:m0 + nq, :], [], [wdB])
            for ml in range(nq):
                m = m0 + ml
                wb = (m // 2) % 2
                if m % 2 == 0 and m + 2 < NFF:
                    load_wgu(m + 2)
                mc = slice((m % 2) * 128, (m % 2) * 128 + 128)
                vf = V_FFN + 4 * m
                cw = lambda j: vecs[:, vf + j:vf + j + 1]
                for blk in range(4):
                    g_, gB = gs[blk % 2], gsB[blk % 2]
                    a_, aB = acc[it % 2], accB[it % 2]
                    s_, sB_ = sl[it % 2], slB[it % 2]
                    pg, pu = 2 * (it % 4), 2 * (it % 4) + 1
                    it += 1
                    rd = [hTb[4 * blk + i] for i in range(4)] + [wguB[wb]]
                    for c in range(8):
                        mm(bank[pg], wgu[wb][:, 0, c, mc], hT[:, c, 1 + blk * 512:1 + (blk + 1) * 512], c == 0, c == 7, rd, [bankB[pg]])
                    for c in range(8):
                        mm(bank[pu], wgu[wb][:, 1, c, mc], hT[:, c, 1 + blk * 512:1 + (blk + 1) * 512], c == 0, c == 7, rd, [bankB[pu]])
                    if blk == 0:
                        S.op('pool', lambda e, g_=g_: e.memset(g_[:, 0:2], 0.0), writes=[gB])
                    else:
                        gp = gs[(blk - 1) % 2]
                        S.op('pool', lambda e, g_=g_, gp=gp: e.tensor_copy(out=g_[:, 0:2], in_=gp[:, 512:514]), reads=[gsB[(blk - 1) % 2]], writes=[gB])
                    act(g_[:, 2:514], bank[pg], AF.Copy, [bankB[pg]], [gB])
                    act(a_, bank[pg], AF.Identity, [bankB[pg], B_const], [aB], bias=cw(3), scale=cw(2))
                    stt(a_, g_[:, 1:513], cw(1), a_, ALU.mult, ALU.add, [gB, aB, B_const], [aB])
                    stt(a_, g_[:, 0:512], cw(0), a_, ALU.mult, ALU.add, [gB, aB, B_const], [aB])
                    act(s_, a_, AF.Silu, [aB], [sB_])
                    tt(hid[:, ml, blk * 512:(blk + 1) * 512], s_, bank[pu], ALU.mult, [sB_, bankB[pu]], [hidB[blk]])
            last = qi == len(quarters) - 1
            for n in range(NT):
                pb = 2 * (n % 4)
                for half in range(2):
                    for ml in range(nq):
                        mm(bank[pb + half], hid[:, ml, n * 128:(n + 1) * 128], wd[:, ml, half * 512:(half + 1) * 512], ml == 0, ml == nq - 1,
                           [hidB[n // 4], wdB], [bankB[pb + half]])
                tt(xres[:, n, :], pp[n % 4][:], xres[:, n, :], ALU.add, [bankB[pb], bankB[pb + 1], xresB[n]], [xresB[n]])
                if last:
                    ssn = ss_all[:, 2, n:n + 1]
                    rsn = rstd_all[:, 2, n:n + 1]
                    sB2 = statB[n % 4]
                    act(sqj, xres[:, n, :], AF.Square, [xresB[n]], [sqjB, sB2], accum=ssn)
                    rsqrt_tiny(rsn, ssn, 1.0 / D, NORM_EPS, [sB2], [sB2])
                    stt(ost[n % 2], xres[:, n, :], rsn, gtab, ALU.mult, ALU.mult, [xresB[n], sB2, B_gtab], [ostB[n % 2]])
                    dma('sp', ov[n], ost[n % 2], [ostB[n % 2]], [])

    S.barrier(('sp',))
    S.emit(st)
    st.close()
    return nc, tap_out, S, A


def _chunkcols(v):
    v = np.asarray(v, np.float32).reshape(-1, 128)
    return np.ascontiguousarray(v.T)


def prep_shared(inp):
    f = lambda k: np.ascontiguousarray(np.asarray(inp[k], np.float32)[0])
    vecs = np.zeros((128, NV), np.float32)
    vecs[:, V_MUW:V_MUW + 8] = _chunkcols(f("rwkv_mu_w"))
    vecs[:, V_MUA:V_MUA + 8] = _chunkcols(f("rwkv_mu_a"))
    vecs[:, V_MUG:V_MUG + 8] = _chunkcols(f("rwkv_mu_g"))
    names = ["rwkv_mu_r", "rwkv_mu_k", "rwkv_mu_v", "rwkv_w0", "rwkv_a0", "rwkv_k_k", "rwkv_k_a", "rwkv_r_k"]
    for j, nm in enumerate(names):
        cc = _chunkcols(f(nm).reshape(-1))
        for p in range(4):
            vecs[:, V_PAIR + 8 * p + j] = cc[:, p]
    cw = f("ffn_conv_w").reshape(3, DFF)
    cbias = f("ffn_conv_b")
    for j in range(3):
        cc = _chunkcols(cw[j])
        for m in range(NFF):
            vecs[:, V_FFN + 4 * m + j] = cc[:, m]
    cc = _chunkcols(cbias)
    for m in range(NFF):
        vecs[:, V_FFN + 4 * m + 3] = cc[:, m]
    row = np.concatenate([f("norm_mix_g"), f("norm_ffn_g"), np.asarray(inp["norm_final_g"], np.float32),
                          f("rwkv_lnx_w"), f("rwkv_lnx_b"), f("ret_gn_w")])
    bct = np.ascontiguousarray(np.broadcast_to(row[None, :], (128, row.shape[0])))
    cf, cb = make_consts()
    shared = {
        "w_in": f("w_in"), "w_out": f("w_out"), "ffn_w_gate": f("ffn_w_gate"), "ffn_w_up": f("ffn_w_up"),
        "ffn_w_down": f("ffn_w_down"), "rwkv_w1": f("rwkv_w1"), "rwkv_a1": f("rwkv_a1"), "rwkv_g1": f("rwkv_g1"),
        "rwkv_w2": f("rwkv_w2"), "rwkv_a2": f("rwkv_a2"), "rwkv_g2": f("rwkv_g2"),
        "vecs": vecs, "bct": bct, "cf": cf, "cb": cb,
    }
    return shared


_PROG = None


def kernel(**inputs):
    global _PROG
    if _PROG is None:
        _PROG = build_program()[0]
    shared = prep_shared(inputs)
    xs = np.asarray(inputs["x"], np.float32)
    in_maps = [dict(shared, x=np.ascontiguousarray(xs[b])) for b in range(8)]
    res = run_bass_kernel_spmd(_PROG, in_maps, core_ids=list(range(8)))
    return np.stack([np.asarray(r["out"], np.float32) for r in res.results], axis=0)
```

```python
import numpy as np
import ml_dtypes
from contextlib import ExitStack
import concourse.bass as bass
import concourse.mybir as mybir
from concourse.bass_utils import run_bass_kernel_spmd

F32 = mybir.dt.float32
BF16 = mybir.dt.bfloat16
AF = mybir.ActivationFunctionType
ALU = mybir.AluOpType
AX = mybir.AxisListType

QUEUES = ('sp', 'act', 'pool', 'pe', 'dve')

T = 2048
D = 1024
NT = 16
DFF = 2816
NFF = 22
C0 = float(np.exp(-0.5))
NORM_EPS = 1e-6
RWKV_GN_EPS = 64e-5
RET_GN_EPS = 1e-5


class Buf:
    __slots__ = ('name', 'w', 'r')

    def __init__(self, name=''):
        self.name = name
        self.w = None
        self.r = {}


class _Op:
    __slots__ = ('q', 's', 'idx', 'fn', 'waits', 'inc', 'dma')


class Sched:
    def __init__(self, nc):
        self.nc = nc
        self.ops = {q: [] for q in QUEUES}
        self.streams = {}
        self.clock = {q: {} for q in QUEUES}
        self.opclock = {}
        self.nwaits = 0
        self.nops = 0

    def op(self, q, fn, reads=(), writes=(), dma=False):
        if dma:
            ref = writes[0] if len(writes) else (reads[0] if len(reads) else None)
            s = 'dq_' + (ref.name if ref is not None and ref.name else q)
        else:
            s = q
        deps = {}

        def need(st, i):
            if deps.get(st, 0) < i:
                deps[st] = i
        for b in reads:
            if b.w is not None:
                st, i = b.w
                if st == q and q == 'pe':
                    continue
                need(st, i)
        for b in writes:
            if b.w is not None:
                st, i = b.w
                if not (st == q and not dma):
                    need(st, i)
            for st, i in b.r.items():
                if st == q and not dma:
                    continue
                need(st, i)
        ck = self.clock[q]
        waits = []
        for st, i in deps.items():
            if ck.get(st, 0) >= i:
                continue
            waits.append((st, i))
            oc = self.opclock[(st, i)]
            for k, v in oc.items():
                if ck.get(k, 0) < v:
                    ck[k] = v
            if ck.get(st, 0) < i:
                ck[st] = i
            self.streams[st][i - 1].inc = True
        o = _Op()
        o.q = q
        o.s = s
        o.fn = fn
        o.waits = waits
        o.inc = dma
        o.dma = dma
        lst = self.streams.setdefault(s, [])
        lst.append(o)
        o.idx = len(lst)
        self.opclock[(s, o.idx)] = dict(ck)
        self.ops[q].append(o)
        self.nwaits += len(waits)
        self.nops += 1
        for b in writes:
            b.w = (s, o.idx)
            b.r = {}
        for b in reads:
            if b.r.get(s, 0) < o.idx:
                b.r[s] = o.idx
        return o

    def barrier(self, queues=QUEUES):
        tips = {s: len(l) for s, l in self.streams.items() if l}
        for q in queues:
            ck = self.clock[q]
            waits = []
            for s, i in tips.items():
                if s == q and q == 'pe':
                    continue
                if ck.get(s, 0) >= i:
                    continue
                waits.append((s, i))
                self.streams[s][i - 1].inc = True
            for s, i in waits:
                oc = self.opclock[(s, i)]
                for k, v in oc.items():
                    if ck.get(k, 0) < v:
                        ck[k] = v
                ck[s] = i
            if waits:
                o = _Op()
                o.q = q
                o.s = None
                o.fn = None
                o.waits = waits
                o.inc = False
                o.dma = False
                self.ops[q].append(o)

    def emit(self, stack):
        nc = self.nc
        sems = {s: stack.enter_context(nc.semaphore('sem_' + s)) for s in self.streams}
        cnt = {}
        for s, lst in self.streams.items():
            c = 0
            for o in lst:
                if o.dma:
                    c += 16
                elif o.inc:
                    c += 1
                cnt[(s, o.idx)] = c
        self.final_counts = {s: (cnt[(s, len(l))] if l else 0) for s, l in self.streams.items()}
        block = stack.enter_context(nc.Block())

        def run(q, eng):
            for o in self.ops[q]:
                for st, i in o.waits:
                    eng.wait_ge(sems[st], cnt[(st, i)])
                if o.fn is None:
                    continue
                ins = o.fn(eng)
                if o.dma:
                    ins.then_inc(sems[o.s], 16)
                elif o.inc:
                    ins.then_inc(sems[o.s], 1)

        @block.sync
        def _(e):
            run('sp', e)

        @block.scalar
        def _(e):
            run('act', e)

        @block.gpsimd
        def _(e):
            run('pool', e)

        @block.tensor
        def _(e):
            run('pe', e)

        @block.vector
        def _(e):
            run('dve', e)


class Arena:
    def __init__(self, ap, nbytes):
        self.ap = ap
        self.nbytes = nbytes
        self.off = 0
        self.peak = 0

    def take(self, shape, dt):
        esz = 4 if dt == F32 else 2
        n = int(np.prod(shape))
        nb = (n * esz + 63) // 64 * 64
        assert self.off + nb <= self.nbytes, ("arena overflow", self.off, nb, self.nbytes)
        v = self.ap[:, self.off // 4:(self.off + nb) // 4]
        if dt != F32:
            v = v.bitcast(dt)
        v = v[:, 0:n]
        if len(shape) == 2:
            v = v.rearrange("p (a b) -> p a b", a=shape[0])
        elif len(shape) == 3:
            v = v.rearrange("p (a b c) -> p a b c", a=shape[0], b=shape[1])
        elif len(shape) == 4:
            v = v.rearrange("p (a b c d) -> p a b c d", a=shape[0], b=shape[1], c=shape[2])
        self.off += nb
        self.peak = max(self.peak, self.off)
        return v

    def mark(self):
        return self.off

    def reset(self, m):
        self.off = m


V_MUW, V_MUA, V_MUG = 0, 8, 16
V_PAIR = 24
V_FFN = 56
NV = 56 + 4 * NFF
CB_ID, CB_MRET, CB_M4, CB_ML, CB_SEL, CB_ONES = 0, 128, 256, 768, 896, 900
NCB = 1028
CF_COS, CF_SIN, CF_NSIN, CF_XIT, CF_KAT, CF_KAPG, CF_GC = 0, 1024, 2048, 3072, 3584, 4096, 4100
NCF = 4104


def make_consts():
    f32 = np.float32
    p = np.arange(128)
    cf = np.zeros((128, NCF), f32)
    half = 64
    inv_freq = (10000.0 ** (-np.arange(half, dtype=np.float64) / half))
    pos = (np.arange(NT)[None, :] * 128 + p[:, None]).astype(np.float64)
    ang = pos[:, :, None] * inv_freq[None, None, :]
    cf[:, CF_COS:CF_COS + 1024] = np.cos(ang).reshape(128, -1)
    cf[:, CF_SIN:CF_SIN + 1024] = np.sin(ang).reshape(128, -1)
    cf[:, CF_NSIN:CF_NSIN + 1024] = -np.sin(ang).reshape(128, -1)
    lg = np.log(1.0 - 2.0 ** (-5.0 - np.arange(4, dtype=np.float64)))
    i = np.arange(128, dtype=np.float64)
    xi = np.exp((i[None, :] + 1.0) * lg[:, None])
    ka = np.exp(-(i[None, :] + 1.0) * lg[:, None]) * (128.0 ** -0.5)
    cf[:, CF_XIT:CF_XIT + 512] = np.broadcast_to(xi.reshape(1, 512), (128, 512))
    cf[:, CF_KAT:CF_KAT + 512] = np.broadcast_to(ka.reshape(1, 512), (128, 512))
    gC = np.exp(128.0 * lg)
    cf[:, CF_KAPG:CF_KAPG + 4] = (ka.T * gC[None, :])
    cf[:, CF_GC:CF_GC + 4] = gC[None, :]
    cb = np.zeros((128, NCB), f32)
    cb[:, CB_ID:CB_ID + 128] = np.eye(128)
    r = p[:, None]
    c = p[None, :]
    cb[:, CB_MRET:CB_MRET + 128] = (r <= c)
    strict = (r < c).astype(f32)
    incl = (r <= c).astype(f32)
    cb[:, CB_M4:CB_M4 + 512] = np.concatenate([strict, incl, strict, incl], axis=1)
    cb[:, CB_ML:CB_ML + 128] = (c < r)
    cb[0:64, CB_SEL] = 1.0
    cb[64:128, CB_SEL + 1] = 1.0
    cb[0:64, CB_ONES:CB_ONES + 64] = 1.0
    cb[64:128, CB_ONES + 64:CB_ONES + 128] = 1.0
    return cf, cb.astype(ml_dtypes.bfloat16)


DBG = {'ret_chunks': NT, 'ret_steps': 99}


def build_program(taps=None, phases=(1, 2, 3, 4, 5)):
    nc = bass.Bass("TRN2", target_bir_lowering=False)

    def din(name, shape, dt=F32):
        return nc.dram_tensor(name, list(shape), dt, kind="ExternalInput").ap()
    x = din("x", [T, D])
    w_in = din("w_in", [D, 3584])
    w_out = din("w_out", [D, D])
    wg_d = din("ffn_w_gate", [D, DFF])
    wu_d = din("ffn_w_up", [D, DFF])
    wd_d = din("ffn_w_down", [DFF, D])
    w1_d = din("rwkv_w1", [D, 64])
    a1_d = din("rwkv_a1", [D, 64])
    g1_d = din("rwkv_g1", [D, 128])
    w2_d = din("rwkv_w2", [64, 512])
    a2_d = din("rwkv_a2", [64, 512])
    g2_d = din("rwkv_g2", [128, 512])
    vecs_d = din("vecs", [128, NV])
    bct_d = din("bct", [128, 4608])
    cf_d = din("cf", [128, NCF])
    cb_d = din("cb", [128, NCB], BF16)
    out = nc.dram_tensor("out", [T, D], F32, kind="ExternalOutput").ap()
    tap_out = {}
    taps = taps or {}

    S = Sched(nc)
    st = ExitStack()
    ARENA_BYTES = 200 * 1024
    arena_t = st.enter_context(nc.sbuf_tensor("arena", [128, ARENA_BYTES // 4], F32))
    A = Arena(arena_t[:], ARENA_BYTES)
    pp = [st.enter_context(nc.psum_tensor(f"pp{i}", [128, 1024], F32)) for i in range(4)]
    bank = [pp[i // 2][:, (i % 2) * 512:(i % 2) * 512 + 512] for i in range(8)]
    bankB = [Buf(f"bank{i}") for i in range(8)]

    def bankbf(i):
        return bank[i].bitcast(BF16)

    def act(out_, in_, func, r, w, bias=None, scale=None, accum=None):
        kw = {}
        if bias is not None:
            kw['bias'] = bias
        if scale is not None:
            kw['scale'] = scale
        if accum is not None:
            kw['accum_out'] = accum
        S.op('act', lambda e: e.activation(out=out_, in_=in_, func=func, **kw), reads=r, writes=w)

    def tt(out_, a, b, op, r, w, q='dve'):
        S.op(q, lambda e: e.tensor_tensor(out=out_, in0=a, in1=b, op=op), reads=r, writes=w)

    def ts(out_, a, s1, s2, op0, op1, r, w, q='dve'):
        if s2 is None:
            S.op(q, lambda e: e.tensor_scalar(out=out_, in0=a, scalar1=s1, scalar2=None, op0=op0), reads=r, writes=w)
        else:
            S.op(q, lambda e: e.tensor_scalar(out=out_, in0=a, scalar1=s1, scalar2=s2, op0=op0, op1=op1), reads=r, writes=w)

    def stt(out_, a, s, b, op0, op1, r, w):
        S.op('dve', lambda e: e.scalar_tensor_tensor(out=out_, in0=a, scalar=s, in1=b, op0=op0, op1=op1), reads=r, writes=w)

    def mm(out_, lhsT, rhs, start, stop, r, w):
        S.op('pe', lambda e: e.matmul(out=out_, lhsT=lhsT, rhs=rhs, start=start, stop=stop), reads=r, writes=w)

    def mm2(out_, lhsT, rhs, start, stop, r, w):
        if lhsT.shape[0] == 128:
            mm(out_, lhsT[0:64], rhs[0:64], start, False, r, w)
            mm(out_, lhsT[64:128], rhs[64:128], False, stop, r, w)
        else:
            mm(out_, lhsT, rhs, start, stop, r, w)

    def dma(q, out_, in_, r, w, **kw):
        S.op(q, lambda e: e.dma_start(out=out_, in_=in_, **kw), reads=r, writes=w, dma=True)

    def cp(q, out_, in_, r, w):
        if q == 'act':
            act(out_, in_, AF.Copy, r, w)
        else:
            S.op(q, lambda e: e.tensor_copy(out=out_, in_=in_), reads=r, writes=w)

    def rsqrt_tiny(dst, src, scale, eps, r, w):
        ts(dst, src, scale, eps, ALU.mult, ALU.add, r, w)
        act(dst, dst, AF.Ln, w, w)
        act(dst, dst, AF.Exp, w, w, scale=-0.5)

    hT = A.take([8, T + 1], BF16)
    yT = A.take([8, T], BF16)
    cb = A.take([NCB], BF16)
    vecs = A.take([NV], F32)
    om = A.take([NV], F32)
    mhalf = A.take([4], F32)
    gtab = A.take([1024], F32)
    stat = A.take([64], F32)
    ss_all = A.take([3, NT], F32)
    rstd_all = A.take([3, NT], F32)
    B_const = Buf('const')
    B_gtab = Buf('gtab')
    hTb = [Buf(f'hT{n}') for n in range(NT)]
    yTb = [[Buf(f'yT{c}_{n}') for n in range(NT)] for c in range(8)]
    ident = cb[:, CB_ID:CB_ID + 128]
    PERSIST = A.mark()

    def tap(name, ap, shape, reads):
        if name in taps:
            d = nc.dram_tensor("tap_" + name, list(shape), ap.dtype, kind="ExternalOutput").ap()
            tap_out[name] = d
            dma('sp', d, ap, reads, [])

    dma('sp', cb, cb_d, [], [B_const])
    dma('sp', vecs, vecs_d, [], [B_const])
    dma('sp', gtab, bct_d[:, 0:1024], [], [B_gtab])
    S.op('pool', lambda e: e.memset(mhalf, -0.5), writes=[B_const])
    ts(om, vecs, -1.0, 1.0, ALU.mult, ALU.add, [B_const], [B_const])
    S.op('pool', lambda e: e.memset(hT[:, :, 0:1], 0.0), writes=[hTb[0]])

    xst = [A.take([D], F32) for _ in range(3)]
    xstB = [Buf(f'xst{i}') for i in range(3)]
    hb = [A.take([D], BF16) for _ in range(2)]
    hbB = [Buf(f'hb{i}') for i in range(2)]
    sqj = A.take([D], BF16)
    sqjB = Buf('sqj')
    statB = [Buf(f'stat{i}') for i in range(4)]
    NORM_END = A.mark()

    ssB = [Buf(f'ss{i}') for i in range(3)]
    rsB = [Buf(f'rs{i}') for i in range(3)]

    def norm_stats(n, src, srcB, which):
        act(sqj, src, AF.Square, [srcB], [sqjB, ssB[which]], accum=ss_all[:, which, n:n + 1])

    def norm_rstd(which, lo=0, hi=NT):
        rsqrt_tiny(rstd_all[:, which, lo:hi], ss_all[:, which, lo:hi], 1.0 / D, NORM_EPS, [ssB[which]], [rsB[which]])

    def norm_apply(n, src, srcB, which, pbank):
        h = hb[n % 2]
        stt(h, src, rstd_all[:, which, n:n + 1], gtab, ALU.mult, ALU.mult, [srcB, rsB[which], B_gtab], [hbB[n % 2]])
        pt = bankbf(pbank).rearrange("p (c t) -> p c t", c=8)
        for c in range(8):
            S.op('pe', lambda e, c=c: e.transpose(out=pt[:, c, :], in_=h[:, c * 128:(c + 1) * 128], identity=ident),
                 reads=[hbB[n % 2], B_const], writes=[bankB[pbank]])
        cp('act', hT[:, :, 1 + n * 128:1 + (n + 1) * 128], pt, [bankB[pbank]], [hTb[n]])

    xv = x.rearrange("(n p) d -> n p d", p=128)
    ov = out.rearrange("(n p) d -> n p d", p=128)
    if 2 in phases:
        cf = A.take([NCF], F32)
        dma('sp', cf, cf_d, [], [B_const])
        wret = A.take([8, 2048], BF16)
        wretB = Buf('wret')
        wv = w_in.rearrange("(c p) n -> p c n", p=128)
        for c in range(8):
            dma('pool', wret[:, c, :], wv[:, c, 1536:3584], [], [wretB])
        P2START = A.mark()
    for lo_ in (0, 8):
        for n in range(lo_, lo_ + 8):
            dma('sp', xst[n % 3], xv[n], [], [xstB[n % 3]])
            norm_stats(n, xst[n % 3], xstB[n % 3], 0)
        norm_rstd(0, lo_, lo_ + 8)
        for n in range(lo_, lo_ + 8):
            dma('sp', xst[n % 3], xv[n], [], [xstB[n % 3]])
            norm_apply(n, xst[n % 3], xstB[n % 3], 0, n % 2)
    tap('hT', hT, [128, 8, T + 1], hTb)

    if 2 in phases:
        A.reset(P2START)
        gnw = A.take([512], F32)
        dma('sp', gnw, bct_d[:, 4096:4608], [], [B_const])
        qa = A.take([512], F32)
        qb = A.take([512], F32)
        qrot = A.take([512], BF16)
        krot = A.take([512], BF16)
        qT = A.take([4, 128], BF16)
        kT = A.take([4, 128], BF16)
        PT = A.take([4, 128], BF16)
        Vb = A.take([512], BF16)
        Vk = A.take([512], BF16)
        R = A.take([512], F32)
        Rt = A.take([512], F32)
        Rb = A.take([512], BF16)
        sqy = A.take([512], F32)
        yn = A.take([512], F32)
        sgt = A.take([512], F32)
        yo = A.take([512], BF16)
        rst = A.take([32], F32)
        Bq = {k: Buf('r_' + k) for k in ['qa', 'qb', 'qrot', 'krot', 'qT', 'kT', 'PT', 'Vb', 'Vk', 'R', 'Rt', 'Rb', 'sqy', 'yn', 'sgt', 'yo', 'rst']}
        kapg_bc = cf[:, CF_KAPG:CF_KAPG + 4].unsqueeze(2).to_broadcast([128, 4, 128])
        gC_bc = cf[:, CF_GC:CF_GC + 4].unsqueeze(2).to_broadcast([128, 4, 128])
        xiT = cf[:, CF_XIT:CF_XIT + 512].rearrange("p (h t) -> p h t", h=4)
        kaT = cf[:, CF_KAT:CF_KAT + 512].rearrange("p (h t) -> p h t", h=4)
        mret_bc = cb[:, CB_MRET:CB_MRET + 128].unsqueeze(1).to_broadcast([128, 4, 128])
        PQ, PK, PV, PG, PTB, PS, PY, PKV = range(8)

        def v4(ap):
            return ap.rearrange("p (h e) -> p h e", h=4)

        def rot(ps, psB, dst, dstB, n):
            cosb = cf[:, CF_COS + n * 64:CF_COS + (n + 1) * 64].unsqueeze(1).unsqueeze(1).to_broadcast([128, 4, 2, 64])
            sinb = cf[:, CF_SIN + n * 64:CF_SIN + (n + 1) * 64].unsqueeze(1).to_broadcast([128, 4, 64])
            nsinb = cf[:, CF_NSIN + n * 64:CF_NSIN + (n + 1) * 64].unsqueeze(1).to_broadcast([128, 4, 64])
            p4 = ps.rearrange("p (h two f) -> p h two f", h=4, two=2)
            tt(qa.rearrange("p (h two f) -> p h two f", h=4, two=2), p4, cosb, ALU.mult, [psB, B_const], [Bq['qa']])
            qb4 = qb.rearrange("p (h two f) -> p h two f", h=4, two=2)
            tt(qb4[:, :, 0, :], p4[:, :, 1, :], nsinb, ALU.mult, [psB, B_const], [Bq['qb']])
            tt(qb4[:, :, 1, :], p4[:, :, 0, :], sinb, ALU.mult, [psB, B_const], [Bq['qb']])
            tt(dst, qa, qb, ALU.add, [Bq['qa'], Bq['qb']], [dstB])

        for n in range(DBG['ret_chunks']):
            RS = DBG['ret_steps']
            def proj(nn):
                tok = slice(1 + nn * 128, 1 + (nn + 1) * 128)
                for j, pb in enumerate((PQ, PK, PV, PG)):
                    for c in range(8):
                        mm(bank[pb], hT[:, c, tok], wret[:, c, j * 512:(j + 1) * 512], c == 0, c == 7,
                           [hTb[nn], wretB], [bankB[pb]])
            if n == 0:
                proj(0)
            act(sgt, bank[PG], AF.Silu, [bankB[PG]], [Bq['sgt']])
            rot(bank[PQ], bankB[PQ], qrot, Bq['qrot'], n)
            rot(bank[PK], bankB[PK], krot, Bq['krot'], n)
            if RS < 3:
                continue
            ptb = bankbf(PTB).rearrange("p (c t) -> p c t", c=8)
            for h in range(4):
                S.op('pe', lambda e, h=h: e.transpose(out=ptb[:, h, :], in_=qrot[:, h * 128:(h + 1) * 128], identity=ident),
                     reads=[Bq['qrot'], B_const], writes=[bankB[PTB]])
            for h in range(4):
                S.op('pe', lambda e, h=h: e.transpose(out=ptb[:, 4 + h, :], in_=krot[:, h * 128:(h + 1) * 128], identity=ident),
                     reads=[Bq['krot'], B_const], writes=[bankB[PTB]])
            tt(qT, ptb[:, 0:4, :], xiT, ALU.mult, [bankB[PTB], B_const], [Bq['qT']])
            tt(kT, ptb[:, 4:8, :], kaT, ALU.mult, [bankB[PTB], B_const], [Bq['kT']])
            if RS < 4:
                continue
            ps4 = v4(bank[PS])
            for h in range(4):
                mm(ps4[:, h, :], kT[:, h, :], qT[:, h, :], True, True, [Bq['kT'], Bq['qT']], [bankB[PS]])
            tt(PT, ps4, mret_bc, ALU.mult, [bankB[PS], B_const], [Bq['PT']])
            if RS < 5:
                continue
            cp('act', Vb, bank[PV], [bankB[PV]], [Bq['Vb']])
            tt(v4(Vk), v4(bank[PV]), kapg_bc, ALU.mult, [bankB[PV], B_const], [Bq['Vk']])
            if RS < 6:
                continue
            py4 = v4(bank[PY])
            for h in range(4):
                mm(py4[:, h, :], PT[:, h, :], Vb[:, h * 128:(h + 1) * 128], True, n == 0, [Bq['PT'], Bq['Vb']], [bankB[PY]])
                if n > 0:
                    mm(py4[:, h, :], qT[:, h, :], Rb[:, h * 128:(h + 1) * 128], False, True, [Bq['qT'], Bq['Rb']], [bankB[PY]])
            if RS < 7:
                continue
            if n < NT - DBG.get('skiplast', 0):
                pkv4 = v4(bank[PKV])
                for h in range(4):
                    mm(pkv4[:, h, :], krot[:, h * 128:(h + 1) * 128], Vk[:, h * 128:(h + 1) * 128], True, True,
                       [Bq['krot'], Bq['Vk']], [bankB[PKV]])
                if n == 0:
                    cp('dve', R, bank[PKV], [bankB[PKV]], [Bq['R']])
                else:
                    tt(v4(Rt), v4(R), gC_bc, ALU.mult, [Bq['R'], B_const], [Bq['Rt']])
                    tt(R, Rt, bank[PKV], ALU.add, [Bq['Rt'], bankB[PKV]], [Bq['R']])
                cp('pool', Rb, R, [Bq['R']], [Bq['Rb']])
            if n + 1 < NT:
                proj(n + 1)
            s1 = rst[:, 0:4]
            s2 = rst[:, 4:8]
            mean = rst[:, 8:12]
            msq = rst[:, 12:16]
            rstd = rst[:, 16:20]
            S.op('dve', lambda e: e.tensor_reduce(out=s1, in_=py4, axis=AX.X, op=ALU.add), reads=[bankB[PY]], writes=[Bq['rst']])
            act(sqy, bank[PY], AF.Square, [bankB[PY]], [Bq['sqy']])
            S.op('dve', lambda e: e.tensor_reduce(out=s2, in_=v4(sqy), axis=AX.X, op=ALU.add), reads=[Bq['sqy']], writes=[Bq['rst']])
            ts(mean, s1, 1.0 / 128, None, ALU.mult, None, [Bq['rst']], [Bq['rst']])
            tt(msq, mean, mean, ALU.mult, [Bq['rst']], [Bq['rst']])
            stt(rstd, s2, 1.0 / 128, msq, ALU.mult, ALU.subtract, [Bq['rst']], [Bq['rst']])
            rsqrt_tiny(rstd, rstd, 1.0, RET_GN_EPS, [Bq['rst']], [Bq['rst']])
            tt(v4(yn), py4, mean.unsqueeze(2).to_broadcast([128, 4, 128]), ALU.subtract, [bankB[PY], Bq['rst']], [Bq['yn']])
            tt(v4(yn), v4(yn), rstd.unsqueeze(2).to_broadcast([128, 4, 128]), ALU.mult, [Bq['yn'], Bq['rst']], [Bq['yn']])
            tt(yn, yn, gnw, ALU.mult, [Bq['yn'], B_const], [Bq['yn']])
            tt(yo, yn, sgt, ALU.mult, [Bq['yn'], Bq['sgt']], [Bq['yo']])
            if RS < 9:
                continue
            for h in range(4):
                S.op('pe', lambda e, h=h: e.transpose(out=ptb[:, h, :], in_=yo[:, h * 128:(h + 1) * 128], identity=ident),
                     reads=[Bq['yo'], B_const], writes=[bankB[PTB]])
            cp('act', yT[:, 4:8, n * 128:(n + 1) * 128], ptb[:, 0:4, :], [bankB[PTB]], [yTb[4 + h][n] for h in range(4)])
        if 3 not in phases:
            tap('yT', yT, [128, 8, T], [b for l in yTb for b in l])
        S.barrier()


    if 3 in phases:
        S.barrier()
        A.reset(PERSIST)
        wl_f = A.take([8, 128], F32)
        gl_f = A.take([8, 128], F32)
        W1A = A.take([8, 128], BF16)
        W1B = A.take([8, 128], BF16)
        G1A = A.take([8, 128], BF16)
        G1B = A.take([8, 128], BF16)
        W2sb = A.take([512], BF16)
        A2sb = A.take([512], BF16)
        G2sb = A.take([512], BF16)
        L1 = A.take([T], BF16)
        L1g = A.take([T], BF16)
        lnxw = A.take([512], F32)
        lnxb = A.take([512], F32)
        B_lw = Buf('loraw')
        L1B = [Buf(f'L1_{i}') for i in range(4)]
        dma('sp', wl_f[:, :, 0:64], w1_d.rearrange("(c p) k -> p c k", p=128), [], [B_lw])
        dma('sp', wl_f[:, :, 64:128], a1_d.rearrange("(c p) k -> p c k", p=128), [], [B_lw])
        dma('sp', gl_f, g1_d.rearrange("(c p) k -> p c k", p=128), [], [B_lw])
        dma('sp', lnxw, bct_d[:, 3072:3584], [], [B_lw])
        dma('sp', lnxb, bct_d[:, 3584:4096], [], [B_lw])
        S.op('pool', lambda e: e.memset(W2sb, 0.0), writes=[B_lw])
        S.op('pool', lambda e: e.memset(A2sb, 0.0), writes=[B_lw])
        dma('pool', W2sb[0:64, :], w2_d, [B_lw], [B_lw])
        dma('pool', A2sb[64:128, :], a2_d, [B_lw], [B_lw])
        dma('pool', G2sb, g2_d, [], [B_lw])

        def vb(tab, col, k):
            return tab[:, col:col + 8].unsqueeze(2).to_broadcast([128, 8, k])
        tt(W1A[:, :, 0:64], wl_f[:, :, 0:64], vb(om, V_MUW, 64), ALU.mult, [B_lw, B_const], [B_lw])
        tt(W1A[:, :, 64:128], wl_f[:, :, 64:128], vb(om, V_MUA, 64), ALU.mult, [B_lw, B_const], [B_lw])
        tt(W1B[:, :, 0:64], wl_f[:, :, 0:64], vb(vecs, V_MUW, 64), ALU.mult, [B_lw, B_const], [B_lw])
        tt(W1B[:, :, 64:128], wl_f[:, :, 64:128], vb(vecs, V_MUA, 64), ALU.mult, [B_lw, B_const], [B_lw])
        tt(G1A, gl_f, vb(om, V_MUG, 128), ALU.mult, [B_lw, B_const], [B_lw])
        tt(G1B, gl_f, vb(vecs, V_MUG, 128), ALU.mult, [B_lw, B_const], [B_lw])
        for tb in range(4):
            rd = [hTb[4 * tb + i] for i in range(4)] + ([hTb[4 * tb - 1]] if tb > 0 else []) + [B_lw]
            for (WA, WB, pb) in ((W1A, W1B, 0), (G1A, G1B, 1)):
                for c in range(8):
                    mm(bank[pb], WA[:, c, :], hT[:, c, 1 + tb * 512:1 + (tb + 1) * 512], c == 0, False, rd, [bankB[pb]])
                    mm(bank[pb], WB[:, c, :], hT[:, c, tb * 512:(tb + 1) * 512], False, c == 7, rd, [bankB[pb]])
            blk = slice(tb * 512, (tb + 1) * 512)
            act(L1[0:64, blk], bank[0][0:64, :], AF.Tanh, [bankB[0]], [L1B[tb]])
            act(L1[64:128, blk], bank[0][64:128, :], AF.Copy, [bankB[0]], [L1B[tb]])
            act(L1g[:, blk], bank[1], AF.Sigmoid, [bankB[1]], [L1B[tb]])

        wrkv = A.take([8, 3, 128], BF16)
        AR = A.take([NT, 2, 128], BF16)
        BT = A.take([T], BF16)
        KT = A.take([T], BF16)
        vT = A.take([T], BF16)
        rkrT = A.take([T], BF16)
        rm = A.take([513], F32)
        km = A.take([513], F32)
        vm = A.take([513], F32)
        tnames = ['r', 'k0', 'sg', 'asg', 'cum', 'P', 'invP', 'Pp', 'ssk', 'kk', 't1']
        tmp = {k: A.take([512], F32) for k in tnames}
        sqk = A.take([512], BF16)
        PCt = A.take([NT], F32)
        Xb = [A.take([2, 2, 128], BF16) for _ in range(2)]
        Nn = [A.take([2, 128], BF16) for _ in range(2)]
        W1s = [A.take([2, 3, 128], BF16) for _ in range(2)]
        TTs = [A.take([2, 128], BF16) for _ in range(2)]
        BK = [A.take([2, 128], BF16) for _ in range(2)]
        Vt = [A.take([4, 128], BF16) for _ in range(2)]
        Xs = A.take([128], BF16)
        Us = A.take([128], BF16)
        Hs = A.take([64], F32)
        HP = A.take([64], F32)
        Hbz = A.take([2, 64], BF16)
        BKz = [A.take([2, 2, 128], BF16) for _ in range(2)]
        Yp = [A.take([4, 128], F32) for _ in range(2)]
        sqp = A.take([512], F32)
        ynp = A.take([512], F32)
        bon = A.take([512], F32)
        sB = A.take([8], F32)
        yop = A.take([4, 128], BF16)
        rstp = A.take([64], F32)
        Bw = Buf('wrkv')
        Bt_ = {k: Buf('t_' + k) for k in tnames + ['rm', 'km', 'vm', 'sqk', 'PCt']}
        ARb = [Buf(f'AR{n}') for n in range(NT)]
        BTb = [Buf(f'BT{i}') for i in range(4)]
        KTb = [Buf(f'KT{i}') for i in range(4)]
        vTb = [Buf(f'vT{i}') for i in range(4)]
        rkb = [Buf(f'rk{i}') for i in range(4)]
        Bs = {k: Buf('s_' + k) for k in ['X0', 'X1', 'N0', 'N1', 'W10', 'W11', 'TT0', 'TT1', 'BK0', 'BK1', 'Vt0', 'Vt1', 'Xs', 'Us', 'H', 'HP', 'Hb', 'Yp0', 'Yp1', 'BKz0', 'BKz1',
                                          'sqp', 'ynp', 'bon', 'sB', 'yop', 'rstp',
                                          'ps1', 'ps2', 'psN', 'psL', 'pT', 'pT2', 'psX', 'psU', 'psH', 'psY', 'psB', 'psG']}
        for k_, b_ in (('ps1', 0), ('ps2', 2), ('psN', 2), ('psL', 3), ('pT', 4), ('pT2', 4), ('psX', 5), ('psU', 5), ('psH', 5),
                       ('psY', 6), ('psB', 6), ('psG', 7)):
            Bs[k_] = bankB[b_]
        wv3 = w_in.rearrange("(c p) n -> p c n", p=128)
        m4 = cb[:, CB_M4:CB_M4 + 512]
        mS_bc = cb[:, CB_M4:CB_M4 + 128].unsqueeze(1).to_broadcast([128, 2, 128])
        m3_bc = cb[:, CB_M4 + 128:CB_M4 + 512].unsqueeze(1).to_broadcast([128, 2, 384])
        mL_bc = cb[:, CB_ML:CB_ML + 128].unsqueeze(1).to_broadcast([128, 2, 128])
        id_bc = ident.unsqueeze(1).to_broadcast([128, 2, 128])
        sel = cb[:, CB_SEL:CB_SEL + 2]
        ones_bd = cb[:, CB_ONES:CB_ONES + 128]
        ps1 = pp[0][:].rearrange("p (h c) -> p h c", h=2)
        ps2 = bank[2][:, 0:256].rearrange("p (h s) -> p h s", h=2)
        psN = bank[2][:, 256:512].rearrange("p (h s) -> p h s", h=2)
        psL = bank[3].rearrange("p (h c) -> p h c", h=2)
        pTb = bankbf(4)
        pT3 = pTb[:, 0:384].rearrange("p (j t) -> p j t", j=3)
        pT2 = pTb[:, 512:1024].rearrange("p (j t) -> p j t", j=4)
        psX = bank[5][:, 0:128]
        psU = bank[5][:, 128:256]
        psH = bank[5][:, 256:384]
        psY = bank[6][:, 0:128]
        psB = bank[6][:, 128:136]
        psG = bank[7]

        def pair_setup(p):
            vp = V_PAIR + 8 * p
            col = lambda j: vecs[:, vp + j:vp + j + 1]
            ocol = lambda j: om[:, vp + j:vp + j + 1]
            return col, ocol

        def prep_block(p, tb):
            col, ocol = pair_setup(p)
            if tb == 0:
                for j in range(3):
                    dma('pool', wrkv[:, :, j, :], wv3[:, :, j * 512 + p * 128:j * 512 + (p + 1) * 128], [], [Bw])
                for nm_ in ('rm', 'km', 'vm'):
                    tl = {'rm': rm, 'km': km, 'vm': vm}[nm_]
                    S.op('pool', lambda e, tl=tl: e.memset(tl[:, 0:1], 0.0), writes=[Bt_[nm_]])
            blk = slice(tb * 512, (tb + 1) * 512)
            rd = [hTb[4 * tb + i] for i in range(4)] + [Bw]
            for j in range(3):
                for c in range(8):
                    mm(bank[j], wrkv[:, c, j, :], hT[:, c, 1 + tb * 512:1 + (tb + 1) * 512], c == 0, c == 7, rd, [bankB[j]])
            mm(bank[3], W2sb[:, p * 128:(p + 1) * 128], L1[:, blk], True, True, [B_lw, L1B[tb]], [bankB[3]])
            mm(bank[4], A2sb[:, p * 128:(p + 1) * 128], L1[:, blk], True, True, [B_lw, L1B[tb]], [bankB[4]])
            for j, (tl, nm_, dst, dstB) in enumerate(((rm, 'rm', tmp['r'], Bt_['r']), (km, 'km', tmp['k0'], Bt_['k0']), (vm, 'vm', vT[:, blk], vTb[tb]))):
                act(tl[:, 1:513], bank[j], AF.Copy, [bankB[j], B_const], [Bt_[nm_]], scale=col(j))
                stt(dst, bank[j], ocol(j), tl[:, 0:512], ALU.mult, ALU.add, [bankB[j], Bt_[nm_], B_const], [dstB])
                S.op('pool', lambda e, tl=tl: e.tensor_copy(out=tl[:, 0:1], in_=tl[:, 512:513]), reads=[Bt_[nm_]], writes=[Bt_[nm_]])
            r_, k0 = tmp['r'], tmp['k0']
            act(tmp['sg'], bank[3], AF.Sigmoid, [bankB[3], B_const], [Bt_['sg']], bias=col(3))
            act(tmp['asg'], bank[4], AF.Sigmoid, [bankB[4], B_const], [Bt_['asg']], bias=col(4))
            for ch in range(4):
                cs = slice(ch * 128, (ch + 1) * 128)
                S.op('dve', lambda e, cs=cs: e.tensor_tensor_scan(out=tmp['cum'][:, cs], data0=tmp['sg'][:, cs], data1=tmp['sg'][:, cs],
                                                                   initial=0.0, op0=ALU.add, op1=ALU.bypass),
                     reads=[Bt_['sg']], writes=[Bt_['cum']])
            act(tmp['P'], tmp['cum'], AF.Exp, [Bt_['cum']], [Bt_['P']], scale=-C0)
            act(tmp['invP'], tmp['cum'], AF.Exp, [Bt_['cum']], [Bt_['invP']], scale=C0)
            tt(tmp['sg'], tmp['cum'], tmp['sg'], ALU.subtract, [Bt_['cum'], Bt_['sg']], [Bt_['sg']])
            act(tmp['Pp'], tmp['sg'], AF.Exp, [Bt_['sg']], [Bt_['Pp']], scale=-C0)
            S.op('pool', lambda e, tb=tb: e.tensor_copy(out=PCt[:, tb * 4:(tb + 1) * 4],
                                                        in_=tmp['P'].rearrange("p (c t) -> p c t", c=4)[:, :, 127]),
                 reads=[Bt_['P']], writes=[Bt_['PCt']])
            act(sqk, k0, AF.Square, [Bt_['k0'], B_const], [Bt_['sqk']], scale=col(5))
            mm(bank[5], ones_bd, sqk, True, True, [Bt_['sqk'], B_const], [bankB[5]])
            act(tmp['ssk'], bank[5], AF.Ln, [bankB[5]], [Bt_['ssk']])
            act(tmp['ssk'], tmp['ssk'], AF.Exp, [Bt_['ssk']], [Bt_['ssk']], scale=-0.5)
            stt(tmp['kk'], k0, col(5), tmp['ssk'], ALU.mult, ALU.mult, [Bt_['k0'], Bt_['ssk'], B_const], [Bt_['kk']])
            ts(tmp['t1'], tmp['asg'], col(6), ocol(6), ALU.mult, ALU.add, [Bt_['asg'], B_const], [Bt_['t1']])
            tt(tmp['t1'], tmp['t1'], k0, ALU.mult, [Bt_['t1'], Bt_['k0']], [Bt_['t1']])
            arv = AR[:, 4 * tb:4 * tb + 4, :, :]
            c4 = lambda a: a.rearrange("p (c t) -> p c t", c=4)
            stt(arv[:, :, 0, :], c4(tmp['kk']), -1.0, c4(tmp['Pp']), ALU.mult, ALU.mult, [Bt_['kk'], Bt_['Pp']], [ARb[4 * tb + i] for i in range(4)])
            tt(arv[:, :, 1, :], c4(r_), c4(tmp['P']), ALU.mult, [Bt_['r'], Bt_['P']], [ARb[4 * tb + i] for i in range(4)])
            tt(tmp['kk'], tmp['kk'], tmp['asg'], ALU.mult, [Bt_['kk'], Bt_['asg']], [Bt_['kk']])
            tt(BT[:, blk], tmp['kk'], tmp['invP'], ALU.mult, [Bt_['kk'], Bt_['invP']], [BTb[tb]])
            tt(KT[:, blk], tmp['t1'], tmp['invP'], ALU.mult, [Bt_['t1'], Bt_['invP']], [KTb[tb]])
            stt(rkrT[:, blk], r_, col(7), tmp['t1'], ALU.mult, ALU.mult, [Bt_['r'], Bt_['t1'], B_const], [rkb[tb]])

        def make_scan(p):
            col, ocol = pair_setup(p)
            def local(n):
                cs = slice(n * 128, (n + 1) * 128)
                tb = n // 4
                bz = BKz[n % 2]
                W1, TT, W1B, TTB = W1s[n % 2], TTs[n % 2], Bs[f'W1{n % 2}'], Bs[f'TT{n % 2}']
                bzB = Bs[f'BKz{n % 2}']
                for h in range(2):
                    hp = slice(64 * h, 64 * h + 64)
                    S.op('pool', lambda e, h=h, hp=hp: e.tensor_copy(out=bz[hp, 0, h, :], in_=BT[hp, cs]), reads=[BTb[tb]], writes=[bzB])
                    S.op('pool', lambda e, h=h, hp=hp: e.tensor_copy(out=bz[hp, 1, h, :], in_=KT[hp, cs]), reads=[KTb[tb]], writes=[bzB])
                for h in range(2):
                    mm(ps1[:, h, 0:256], bz[:, 0, h, :], AR[:, n, :, :], True, True, [bzB, ARb[n]], [Bs['ps1']])
                    mm(ps1[:, h, 256:512], bz[:, 1, h, :], AR[:, n, :, :], True, True, [bzB, ARb[n]], [Bs['ps1']])
                    mm(ps2[:, h, :], AR[:, n, 0, :], bz[:, 0, h, :], True, True, [bzB, ARb[n]], [Bs['ps2']])
                tt(Xb[0][:, :, 0, :], ps1[:, :, 0:128], mS_bc, ALU.mult, [Bs['ps1'], B_const], [Bs['X0']])
                tt(Nn[0], ps2, mL_bc, ALU.mult, [Bs['ps2'], B_const], [Bs['N0']])
                tt(Xb[1][:, :, 1, :], Xb[0][:, :, 0, :], id_bc, ALU.add, [Bs['X0'], B_const], [Bs['X1']])
                tt(W1, ps1[:, :, 128:512], m3_bc, ALU.mult, [Bs['ps1'], B_const], [W1B])
                yield
                cur = 0
                for k in range(4):
                    nx = 1 - cur
                    lastk = (k == 3)
                    for h in range(2):
                        if lastk:
                            mm(psL[:, h, 128:256], Nn[cur][:, h, :], Xb[cur][:, h, 1, :], True, True, [Bs[f'N{cur}'], Bs[f'X{cur}']], [Bs['psL']])
                        elif k == 0:
                            mm(psL[:, h, 0:128], Nn[cur][:, h, :], Xb[cur][:, h, 0, :], True, True, [Bs[f'N{cur}'], Bs[f'X{cur}']], [Bs['psL']])
                            mm(psN[:, h, :], Xb[cur][:, h, 0, :], Nn[cur][:, h, :], True, True, [Bs[f'N{cur}'], Bs[f'X{cur}']], [Bs['psN']])
                        else:
                            mm(psL[:, h, :], Nn[cur][:, h, :], Xb[cur][:, h, :, :], True, True, [Bs[f'N{cur}'], Bs[f'X{cur}']], [Bs['psL']])
                            mm(psN[:, h, :], Xb[cur][:, h, 0, :], Nn[cur][:, h, :], True, True, [Bs[f'N{cur}'], Bs[f'X{cur}']], [Bs['psN']])
                    if lastk:
                        tt(TT, Xb[cur][:, :, 1, :], psL[:, :, 128:256], ALU.add, [Bs['psL'], Bs[f'X{cur}']], [TTB])
                    else:
                        cp('act', Xb[nx][:, :, 0, :], psL[:, :, 0:128], [Bs['psL']], [Bs[f'X{nx}']])
                        cp('dve', Nn[nx], psN, [Bs['psN']], [Bs[f'N{nx}']])
                        if k > 0:
                            tt(Xb[nx][:, :, 1, :], Xb[cur][:, :, 1, :], psL[:, :, 128:256], ALU.add, [Bs['psL'], Bs[f'X{cur}']], [Bs[f'X{nx}']])
                    cur = nx
                    yield

            def chain(n):
                cs = slice(n * 128, (n + 1) * 128)
                tb = n // 4
                g = (n // 4) % 2
                bk = BK[n % 2]
                bkB = Bs[f'BK{n % 2}']
                vt = Vt[g][:, n % 4, :]
                vtB = Bs[f'Vt{g}']
                W1, TT, W1B, TTB = W1s[n % 2], TTs[n % 2], Bs[f'W1{n % 2}'], Bs[f'TT{n % 2}']
                S.op('pe', lambda e: e.transpose(out=pT3[:, 0, :], in_=vT[:, cs], identity=ident), reads=[vTb[tb], B_const], writes=[Bs['pT']])
                S.op('pe', lambda e: e.transpose(out=pT3[:, 1, :], in_=BT[:, cs], identity=ident), reads=[BTb[tb], B_const], writes=[Bs['pT']])
                S.op('pe', lambda e: e.transpose(out=pT3[:, 2, :], in_=KT[:, cs], identity=ident), reads=[KTb[tb], B_const], writes=[Bs['pT']])
                cp('act', vt, pT3[:, 0, :], [Bs['pT']], [vtB])
                cp('act', bk, pT3[:, 1:3, :], [Bs['pT']], [bkB])
                yield
                for h in range(2):
                    hs = slice(64 * h, 64 * h + 64)
                    if n > 0:
                        mm(psX[:, hs], AR[:, n, 0, :], Hbz[:, h, :], True, False, [ARb[n], Bs['Hb']], [Bs['psX']])
                    mm(psX[:, hs], W1[:, h, 1, :], vt[:, hs], n == 0, True, [W1B, vtB], [Bs['psX']])
                cp('act', Xs, psX, [Bs['psX']], [Bs['Xs']])
                yield
                for h in range(2):
                    hs = slice(64 * h, 64 * h + 64)
                    mm(psU[:, hs], TT[:, h, :], Xs[:, hs], True, True, [TTB, Bs['Xs']], [Bs['psU']])
                cp('dve', Us, psU, [Bs['psU']], [Bs['Us']])
                yield
                for h in range(2):
                    hs = slice(64 * h, 64 * h + 64)
                    if n > 0:
                        mm(psY[:, hs], AR[:, n, 1, :], Hbz[:, h, :], True, False, [ARb[n], Bs['Hb']], [Bs['psY']])
                    mm(psY[:, hs], W1[:, h, 0, :], Us[:, hs], n == 0, False, [W1B, Bs['Us']], [Bs['psY']])
                    mm(psY[:, hs], W1[:, h, 2, :], vt[:, hs], False, True, [W1B, vtB], [Bs['psY']])
                mm(psH, bk[:, 0, :], Us, True, False, [bkB, Bs['Us']], [Bs['psH']])
                mm(psH, bk[:, 1, :], vt, False, True, [bkB, vtB], [Bs['psH']])
                cp('act', Yp[g][:, n % 4, :], psY, [Bs['psY']], [Bs[f'Yp{g}']])
                if n > 0:
                    ts(HP, Hs, PCt[:, n:n + 1], None, ALU.mult, None, [Bs['H'], Bt_['PCt']], [Bs['HP']])
                for h in range(2):
                    hp = slice(64 * h, 64 * h + 64)
                    hs = slice(64 * h, 64 * h + 64)
                    if n > 0:
                        stt(Hs[hp, :], psH[hp, hs], PCt[hp, n:n + 1], HP[hp, :], ALU.mult, ALU.add, [Bs['psH'], Bs['HP'], Bt_['PCt']], [Bs['H']])
                    else:
                        ts(Hs[hp, :], psH[hp, hs], PCt[hp, n:n + 1], None, ALU.mult, None, [Bs['psH'], Bt_['PCt']], [Bs['H']])
                for h in range(2):
                    hp = slice(64 * h, 64 * h + 64)
                    cp('act', Hbz[hp, h, :], Hs[hp, :], [Bs['H']], [Bs['Hb']])
                yield

            def post(tg):
                g = tg % 2
                y3 = Yp[g].rearrange("p j (h e) -> p (j h) e", h=2)
                yB = Bs[f'Yp{g}']
                s1, s2, mean, msq, rstd = (rstp[:, 8 * i:8 * i + 8] for i in range(5))
                v8 = lambda a: a.rearrange("p (j e) -> p j e", j=8)
                S.op('dve', lambda e: e.tensor_reduce(out=s1, in_=y3, axis=AX.X, op=ALU.add), reads=[yB], writes=[Bs['rstp']])
                act(sqp, Yp[g].rearrange("p j c -> p (j c)"), AF.Square, [yB], [Bs['sqp']])
                S.op('dve', lambda e: e.tensor_reduce(out=s2, in_=v8(sqp), axis=AX.X, op=ALU.add), reads=[Bs['sqp']], writes=[Bs['rstp']])
                yield
                ts(mean, s1, 1.0 / 64, None, ALU.mult, None, [Bs['rstp']], [Bs['rstp']])
                tt(msq, mean, mean, ALU.mult, [Bs['rstp']], [Bs['rstp']])
                stt(rstd, s2, 1.0 / 64, msq, ALU.mult, ALU.subtract, [Bs['rstp']], [Bs['rstp']])
                rsqrt_tiny(rstd, rstd, 1.0, RWKV_GN_EPS, [Bs['rstp']], [Bs['rstp']])
                yield
                tt(v8(ynp), y3, mean.unsqueeze(2).to_broadcast([128, 8, 64]), ALU.subtract, [yB, Bs['rstp']], [Bs['ynp']])
                tt(v8(ynp), v8(ynp), rstd.unsqueeze(2).to_broadcast([128, 8, 64]), ALU.mult, [Bs['ynp'], Bs['rstp']], [Bs['ynp']])
                yield
                y4 = ynp.rearrange("p (j c) -> p j c", j=4)
                tt(y4, y4, lnxw[:, p * 128:(p + 1) * 128].unsqueeze(1).to_broadcast([128, 4, 128]), ALU.mult, [Bs['ynp'], B_lw], [Bs['ynp']])
                tt(y4, y4, lnxb[:, p * 128:(p + 1) * 128].unsqueeze(1).to_broadcast([128, 4, 128]), ALU.add, [Bs['ynp'], B_lw], [Bs['ynp']])
                yield
                for j in range(4):
                    n = 4 * tg + j
                    cs = slice(n * 128, (n + 1) * 128)
                    mm(psB[:, 2 * j:2 * j + 2], rkrT[:, cs], sel, True, True, [rkb[tg], B_const], [Bs['psB']])
                    mm(psG[:, j * 128:(j + 1) * 128], L1g[:, cs], G2sb[:, p * 128:(p + 1) * 128], True, True, [L1B[tg], B_lw], [Bs['psG']])
                cp('act', sB, psB, [Bs['psB']], [Bs['sB']])
                yield
                tt(v8(bon), Vt[g].rearrange("p j (h e) -> p (j h) e", h=2), sB.unsqueeze(2).to_broadcast([128, 8, 64]), ALU.mult,
                   [Bs[f'Vt{g}'], Bs['sB']], [Bs['bon']])
                tt(ynp, ynp, bon, ALU.add, [Bs['ynp'], Bs['bon']], [Bs['ynp']])
                yield
                tt(yop.rearrange("p j c -> p (j c)"), ynp, psG, ALU.mult, [Bs['ynp'], Bs['psG']], [Bs['yop']])
                yield
                for j in range(4):
                    S.op('pe', lambda e, j=j: e.transpose(out=pT2[:, j, :], in_=yop[:, j, :], identity=ident), reads=[Bs['yop'], B_const], writes=[Bs['pT2']])
                cp('act', yT[:, p, tg * 512:(tg + 1) * 512], pT2.rearrange("p j t -> p (j t)"), [Bs['pT2']], [yTb[p][4 * tg + j] for j in range(4)])
                yield

            return local, chain, post

        for i_ in range(2):
            S.op('pool', lambda e, i_=i_: e.memset(BKz[i_], 0.0), writes=[Bs[f'BKz{i_}']])
        NP = DBG.get('pairs', 4)
        for tb in range(4):
            prep_block(0, tb)
        scans = [make_scan(p) for p in range(NP)]

        def drain(g):
            for _ in g:
                pass
        S.op('pool', lambda e: e.memset(Hs, 0.0), writes=[Bs['H']])
        S.op('pool', lambda e: e.memset(Hbz, 0.0), writes=[Bs['Hb']])
        drain(scans[0][0](0))
        pend = []

        def prep_gen(p_, tb_):
            prep_block(p_, tb_)
            yield

        for p in range(NP):
            local, chain, post = scans[p]
            for n in range(NT):
                while len(pend) > 2:
                    drain(pend.pop(0))
                a = chain(n)
                if n + 1 < NT:
                    b = local(n + 1)
                elif p + 1 < NP:
                    b = scans[p + 1][0](0)
                else:
                    b = iter(())
                done_a = done_b = False
                while not (done_a and done_b):
                    if not done_a:
                        try:
                            next(a)
                        except StopIteration:
                            done_a = True
                    if not done_b:
                        try:
                            next(b)
                        except StopIteration:
                            done_b = True
                    if pend:
                        try:
                            next(pend[0])
                        except StopIteration:
                            pend.pop(0)
                if n % 4 == 3:
                    pend.append(post(n // 4))
                    if p + 1 < NP:
                        pend.append(prep_gen(p + 1, n // 4))
            if p + 1 == NP:
                while pend:
                    drain(pend.pop(0))
            if p + 1 < NP:
                S.op('dve', lambda e: e.memset(Hs, 0.0), writes=[Bs['H']])
                S.op('pool', lambda e: e.memset(Hbz, 0.0), writes=[Bs['Hb']])
        tap('yT', yT, [128, 8, T], [b for l in yTb for b in l])
        S.barrier()


    if 4 in phases:
        S.barrier()
        A.reset(NORM_END)
        xres = A.take([NT, D], F32)
        xresB = [Buf(f'xres{n}') for n in range(NT)]
        P4 = A.mark()
        wout = A.take([8, D], BF16)
        woutB = Buf('wout')
        wo_v = w_out.rearrange("(c p) n -> p c n", p=128)
        for c in range(8):
            dma('pool', wout[:, c, :], wo_v[:, c, :], [], [woutB])
        dma('sp', gtab, bct_d[:, 1024:2048], [], [B_gtab])
        for n in range(NT):
            dma('sp', xst[n % 3], xv[n], [], [xstB[n % 3]])
            pb = 2 * (n % 2)
            for half in range(2):
                for c in range(8):
                    mm(bank[pb + half], yT[:, c, n * 128:(n + 1) * 128], wout[:, c, half * 512:(half + 1) * 512], c == 0, c == 7,
                       [yTb[c][n], woutB], [bankB[pb + half]])
            tt(xres[:, n, :], pp[n % 2][:], xst[n % 3], ALU.add, [bankB[pb], bankB[pb + 1], xstB[n % 3]], [xresB[n]])
            norm_stats(n, xres[:, n, :], xresB[n], 1)
        norm_rstd(1)
        for n in range(NT):
            norm_apply(n, xres[:, n, :], xresB[n], 1, 4 + n % 2)
        tap('xres', xres, [128, NT, D], xresB)

    if 5 in phases:
        S.barrier()
        A.reset(P4)
        hid = yT[:, 0:6, :]
        hidB = [Buf(f'hid{i}') for i in range(4)]
        wgu = [A.take([2, 8, 256], BF16) for _ in range(2)]
        wguB = [Buf(f'wgu{i}') for i in range(2)]
        wd = A.take([6, D], BF16)
        wdB = Buf('wd')
        gs = [A.take([514], F32) for _ in range(2)]
        gsB = [Buf(f'gs{i}') for i in range(2)]
        acc = [A.take([512], F32) for _ in range(2)]
        accB = [Buf(f'acc{i}') for i in range(2)]
        sl = [A.take([512], F32) for _ in range(2)]
        slB = [Buf(f'sl{i}') for i in range(2)]
        ost = [xst[0], xst[1]]
        ostB = [xstB[0], xstB[1]]
        dma('sp', gtab, bct_d[:, 2048:3072], [], [B_gtab])
        wg_v = wg_d.rearrange("(c p) n -> p c n", p=128)
        wu_v = wu_d.rearrange("(c p) n -> p c n", p=128)
        wd_v = wd_d.rearrange("(m p) n -> p m n", p=128)
        quarters = [(0, 6), (6, 6), (12, 5), (17, 5)]

        def load_wgu(m):
            wb_ = (m // 2) % 2
            dma('pool', wgu[wb_][:, 0, :, :], wg_v[:, :, m * 128:(m + 2) * 128], [], [wguB[wb_]])
            dma('pool', wgu[wb_][:, 1, :, :], wu_v[:, :, m * 128:(m + 2) * 128], [], [wguB[wb_]])
        load_wgu(0)
        it = 0
        for qi, (m0, nq) in enumerate(quarters):
            dma('pool', wd[:, 0:nq, :], wd_v[:, m0:m0 + nq, :], [], [wdB])
            for ml in range(nq):
                m = m0 + ml
                wb = (m // 2) % 2
                if m % 2 == 0 and m + 2 < NFF:
                    load_wgu(m + 2)
                mc = slice((m % 2) * 128, (m % 2) * 128 + 128)
                vf = V_FFN + 4 * m
                cw = lambda j: vecs[:, vf + j:vf + j + 1]
                for blk in range(4):
                    g_, gB = gs[blk % 2], gsB[blk % 2]
                    a_, aB = acc[it % 2], accB[it % 2]
                    s_, sB_ = sl[it % 2], slB[it % 2]
                    pg, pu = 2 * (it % 4), 2 * (it % 4) + 1
                    it += 1
                    rd = [hTb[4 * blk + i] for i in range(4)] + [wguB[wb]]
                    for c in range(8):
                        mm(bank[pg], wgu[wb][:, 0, c, mc], hT[:, c, 1 + blk * 512:1 + (blk + 1) * 512], c == 0, c == 7, rd, [bankB[pg]])
                    for c in range(8):
                        mm(bank[pu], wgu[wb][:, 1, c, mc], hT[:, c, 1 + blk * 512:1 + (blk + 1) * 512], c == 0, c == 7, rd, [bankB[pu]])
                    if blk == 0:
                        S.op('pool', lambda e, g_=g_: e.memset(g_[:, 0:2], 0.0), writes=[gB])
                    else:
                        gp = gs[(blk - 1) % 2]
                        S.op('pool', lambda e, g_=g_, gp=gp: e.tensor_copy(out=g_[:, 0:2], in_=gp[:, 512:514]), reads=[gsB[(blk - 1) % 2]], writes=[gB])
                    act(g_[:, 2:514], bank[pg], AF.Copy, [bankB[pg]], [gB])
                    act(a_, bank[pg], AF.Identity, [bankB[pg], B_const], [aB], bias=cw(3), scale=cw(2))
                    stt(a_, g_[:, 1:513], cw(1), a_, ALU.mult, ALU.add, [gB, aB, B_const], [aB])
                    stt(a_, g_[:, 0:512], cw(0), a_, ALU.mult, ALU.add, [gB, aB, B_const], [aB])
                    act(s_, a_, AF.Silu, [aB], [sB_])
                    tt(hid[:, ml, blk * 512:(blk + 1) * 512], s_, bank[pu], ALU.mult, [sB_, bankB[pu]], [hidB[blk]])
            last = qi == len(quarters) - 1
            for n in range(NT):
                pb = 2 * (n % 4)
                for half in range(2):
                    for ml in range(nq):
                        mm(bank[pb + half], hid[:, ml, n * 128:(n + 1) * 128], wd[:, ml, half * 512:(half + 1) * 512], ml == 0, ml == nq - 1,
                           [hidB[n // 4], wdB], [bankB[pb + half]])
                tt(xres[:, n, :], pp[n % 4][:], xres[:, n, :], ALU.add, [bankB[pb], bankB[pb + 1], xresB[n]], [xresB[n]])
                if last:
                    ssn = ss_all[:, 2, n:n + 1]
                    rsn = rstd_all[:, 2, n:n + 1]
                    sB2 = statB[n % 4]
                    act(sqj, xres[:, n, :], AF.Square, [xresB[n]], [sqjB, sB2], accum=ssn)
                    rsqrt_tiny(rsn, ssn, 1.0 / D, NORM_EPS, [sB2], [sB2])
                    stt(ost[n % 2], xres[:, n, :], rsn, gtab, ALU.mult, ALU.mult, [xresB[n], sB2, B_gtab], [ostB[n % 2]])
                    dma('sp', ov[n], ost[n % 2], [ostB[n % 2]], [])

    S.barrier(('sp',))
    S.emit(st)
    st.close()
    return nc, tap_out, S, A


def _chunkcols(v):
    v = np.asarray(v, np.float32).reshape(-1, 128)
    return np.ascontiguousarray(v.T)


def prep_shared(inp):
    f = lambda k: np.ascontiguousarray(np.asarray(inp[k], np.float32)[0])
    vecs = np.zeros((128, NV), np.float32)
    vecs[:, V_MUW:V_MUW + 8] = _chunkcols(f("rwkv_mu_w"))
    vecs[:, V_MUA:V_MUA + 8] = _chunkcols(f("rwkv_mu_a"))
    vecs[:, V_MUG:V_MUG + 8] = _chunkcols(f("rwkv_mu_g"))
    names = ["rwkv_mu_r", "rwkv_mu_k", "rwkv_mu_v", "rwkv_w0", "rwkv_a0", "rwkv_k_k", "rwkv_k_a", "rwkv_r_k"]
    for j, nm in enumerate(names):
        cc = _chunkcols(f(nm).reshape(-1))
        for p in range(4):
            vecs[:, V_PAIR + 8 * p + j] = cc[:, p]
    cw = f("ffn_conv_w").reshape(3, DFF)
    cbias = f("ffn_conv_b")
    for j in range(3):
        cc = _chunkcols(cw[j])
        for m in range(NFF):
            vecs[:, V_FFN + 4 * m + j] = cc[:, m]
    cc = _chunkcols(cbias)
    for m in range(NFF):
        vecs[:, V_FFN + 4 * m + 3] = cc[:, m]
    row = np.concatenate([f("norm_mix_g"), f("norm_ffn_g"), np.asarray(inp["norm_final_g"], np.float32),
                          f("rwkv_lnx_w"), f("rwkv_lnx_b"), f("ret_gn_w")])
    bct = np.ascontiguousarray(np.broadcast_to(row[None, :], (128, row.shape[0])))
    cf, cb = make_consts()
    shared = {
        "w_in": f("w_in"), "w_out": f("w_out"), "ffn_w_gate": f("ffn_w_gate"), "ffn_w_up": f("ffn_w_up"),
        "ffn_w_down": f("ffn_w_down"), "rwkv_w1": f("rwkv_w1"), "rwkv_a1": f("rwkv_a1"), "rwkv_g1": f("rwkv_g1"),
        "rwkv_w2": f("rwkv_w2"), "rwkv_a2": f("rwkv_a2"), "rwkv_g2": f("rwkv_g2"),
        "vecs": vecs, "bct": bct, "cf": cf, "cb": cb,
    }
    return shared


_PROG = None


def kernel(**inputs):
    global _PROG
    if _PROG is None:
        _PROG = build_program()[0]
    shared = prep_shared(inputs)
    xs = np.asarray(inputs["x"], np.float32)
    in_maps = [dict(shared, x=np.ascontiguousarray(xs[b])) for b in range(8)]
    res = run_bass_kernel_spmd(_PROG, in_maps, core_ids=list(range(8)))
    return np.stack([np.asarray(r["out"], np.float32) for r in res.results], axis=0)
```

```python
import numpy as np
import ml_dtypes
from contextlib import ExitStack
import concourse.bass as bass
import concourse.mybir as mybir
from concourse.bass_utils import run_bass_kernel_spmd

F32 = mybir.dt.float32
BF16 = mybir.dt.bfloat16
AF = mybir.ActivationFunctionType
ALU = mybir.AluOpType
AX = mybir.AxisListType

QUEUES = ('sp', 'act', 'pool', 'pe', 'dve')

T = 2048
D = 1024
NT = 16
DFF = 2816
NFF = 22
C0 = float(np.exp(-0.5))
NORM_EPS = 1e-6
RWKV_GN_EPS = 64e-5
RET_GN_EPS = 1e-5


class Buf:
    __slots__ = ('name', 'w', 'r')

    def __init__(self, name=''):
        self.name = name
        self.w = None
        self.r = {}


class _Op:
    __slots__ = ('q', 's', 'idx', 'fn', 'waits', 'inc', 'dma')


class Sched:
    def __init__(self, nc):
        self.nc = nc
        self.ops = {q: [] for q in QUEUES}
        self.streams = {}
        self.clock = {q: {} for q in QUEUES}
        self.opclock = {}
        self.nwaits = 0
        self.nops = 0

    def op(self, q, fn, reads=(), writes=(), dma=False):
        if dma:
            ref = writes[0] if len(writes) else (reads[0] if len(reads) else None)
            s = 'dq_' + (ref.name if ref is not None and ref.name else q)
        else:
            s = q
        deps = {}

        def need(st, i):
            if deps.get(st, 0) < i:
                deps[st] = i
        for b in reads:
            if b.w is not None:
                st, i = b.w
                if st == q and q == 'pe':
                    continue
                need(st, i)
        for b in writes:
            if b.w is not None:
                st, i = b.w
                if not (st == q and not dma):
                    need(st, i)
            for st, i in b.r.items():
                if st == q and not dma:
                    continue
                need(st, i)
        ck = self.clock[q]
        waits = []
        for st, i in deps.items():
            if ck.get(st, 0) >= i:
                continue
            waits.append((st, i))
            oc = self.opclock[(st, i)]
            for k, v in oc.items():
                if ck.get(k, 0) < v:
                    ck[k] = v
            if ck.get(st, 0) < i:
                ck[st] = i
            self.streams[st][i - 1].inc = True
        o = _Op()
        o.q = q
        o.s = s
        o.fn = fn
        o.waits = waits
        o.inc = dma
        o.dma = dma
        lst = self.streams.setdefault(s, [])
        lst.append(o)
        o.idx = len(lst)
        self.opclock[(s, o.idx)] = dict(ck)
        self.ops[q].append(o)
        self.nwaits += len(waits)
        self.nops += 1
        for b in writes:
            b.w = (s, o.idx)
            b.r = {}
        for b in reads:
            if b.r.get(s, 0) < o.idx:
                b.r[s] = o.idx
        return o

    def barrier(self, queues=QUEUES):
        tips = {s: len(l) for s, l in self.streams.items() if l}
        for q in queues:
            ck = self.clock[q]
            waits = []
            for s, i in tips.items():
                if s == q and q == 'pe':
                    continue
                if ck.get(s, 0) >= i:
                    continue
                waits.append((s, i))
                self.streams[s][i - 1].inc = True
            for s, i in waits:
                oc = self.opclock[(s, i)]
                for k, v in oc.items():
                    if ck.get(k, 0) < v:
                        ck[k] = v
                ck[s] = i
            if waits:
                o = _Op()
                o.q = q
                o.s = None
                o.fn = None
                o.waits = waits
                o.inc = False
                o.dma = False
                self.ops[q].append(o)

    def emit(self, stack):
        nc = self.nc
        sems = {s: stack.enter_context(nc.semaphore('sem_' + s)) for s in self.streams}
        cnt = {}
        for s, lst in self.streams.items():
            c = 0
            for o in lst:
                if o.dma:
                    c += 16
                elif o.inc:
                    c += 1
                cnt[(s, o.idx)] = c
        self.final_counts = {s: (cnt[(s, len(l))] if l else 0) for s, l in self.streams.items()}
        block = stack.enter_context(nc.Block())

        def run(q, eng):
            for o in self.ops[q]:
                for st, i in o.waits:
                    eng.wait_ge(sems[st], cnt[(st, i)])
                if o.fn is None:
                    continue
                ins = o.fn(eng)
                if o.dma:
                    ins.then_inc(sems[o.s], 16)
                elif o.inc:
                    ins.then_inc(sems[o.s], 1)

        @block.sync
        def _(e):
            run('sp', e)

        @block.scalar
        def _(e):
            run('act', e)

        @block.gpsimd
        def _(e):
            run('pool', e)

        @block.tensor
        def _(e):
            run('pe', e)

        @block.vector
        def _(e):
            run('dve', e)


class Arena:
    def __init__(self, ap, nbytes):
        self.ap = ap
        self.nbytes = nbytes
        self.off = 0
        self.peak = 0

    def take(self, shape, dt):
        esz = 4 if dt == F32 else 2
        n = int(np.prod(shape))
        nb = (n * esz + 63) // 64 * 64
        assert self.off + nb <= self.nbytes, ("arena overflow", self.off, nb, self.nbytes)
        v = self.ap[:, self.off // 4:(self.off + nb) // 4]
        if dt != F32:
            v = v.bitcast(dt)
        v = v[:, 0:n]
        if len(shape) == 2:
            v = v.rearrange("p (a b) -> p a b", a=shape[0])
        elif len(shape) == 3:
            v = v.rearrange("p (a b c) -> p a b c", a=shape[0], b=shape[1])
        elif len(shape) == 4:
            v = v.rearrange("p (a b c d) -> p a b c d", a=shape[0], b=shape[1], c=shape[2])
        self.off += nb
        self.peak = max(self.peak, self.off)
        return v

    def mark(self):
        return self.off

    def reset(self, m):
        self.off = m


V_MUW, V_MUA, V_MUG = 0, 8, 16
V_PAIR = 24
V_FFN = 56
NV = 56 + 4 * NFF
CB_ID, CB_MRET, CB_M4, CB_ML, CB_SEL, CB_ONES = 0, 128, 256, 768, 896, 900
NCB = 1028
CF_COS, CF_SIN, CF_NSIN, CF_XIT, CF_KAT, CF_KAPG, CF_GC = 0, 1024, 2048, 3072, 3584, 4096, 4100
NCF = 4104


def make_consts():
    f32 = np.float32
    p = np.arange(128)
    cf = np.zeros((128, NCF), f32)
    half = 64
    inv_freq = (10000.0 ** (-np.arange(half, dtype=np.float64) / half))
    pos = (np.arange(NT)[None, :] * 128 + p[:, None]).astype(np.float64)
    ang = pos[:, :, None] * inv_freq[None, None, :]
    cf[:, CF_COS:CF_COS + 1024] = np.cos(ang).reshape(128, -1)
    cf[:, CF_SIN:CF_SIN + 1024] = np.sin(ang).reshape(128, -1)
    cf[:, CF_NSIN:CF_NSIN + 1024] = -np.sin(ang).reshape(128, -1)
    lg = np.log(1.0 - 2.0 ** (-5.0 - np.arange(4, dtype=np.float64)))
    i = np.arange(128, dtype=np.float64)
    xi = np.exp((i[None, :] + 1.0) * lg[:, None])
    ka = np.exp(-(i[None, :] + 1.0) * lg[:, None]) * (128.0 ** -0.5)
    cf[:, CF_XIT:CF_XIT + 512] = np.broadcast_to(xi.reshape(1, 512), (128, 512))
    cf[:, CF_KAT:CF_KAT + 512] = np.broadcast_to(ka.reshape(1, 512), (128, 512))
    gC = np.exp(128.0 * lg)
    cf[:, CF_KAPG:CF_KAPG + 4] = (ka.T * gC[None, :])
    cf[:, CF_GC:CF_GC + 4] = gC[None, :]
    cb = np.zeros((128, NCB), f32)
    cb[:, CB_ID:CB_ID + 128] = np.eye(128)
    r = p[:, None]
    c = p[None, :]
    cb[:, CB_MRET:CB_MRET + 128] = (r <= c)
    strict = (r < c).astype(f32)
    incl = (r <= c).astype(f32)
    cb[:, CB_M4:CB_M4 + 512] = np.concatenate([strict, incl, strict, incl], axis=1)
    cb[:, CB_ML:CB_ML + 128] = (c < r)
    cb[0:64, CB_SEL] = 1.0
    cb[64:128, CB_SEL + 1] = 1.0
    cb[0:64, CB_ONES:CB_ONES + 64] = 1.0
    cb[64:128, CB_ONES + 64:CB_ONES + 128] = 1.0
    return cf, cb.astype(ml_dtypes.bfloat16)


DBG = {'ret_chunks': NT, 'ret_steps': 99}


def build_program(taps=None, phases=(1, 2, 3, 4, 5)):
    nc = bass.Bass("TRN2", target_bir_lowering=False)

    def din(name, shape, dt=F32):
        return nc.dram_tensor(name, list(shape), dt, kind="ExternalInput").ap()
    x = din("x", [T, D])
    w_in = din("w_in", [D, 3584])
    w_out = din("w_out", [D, D])
    wg_d = din("ffn_w_gate", [D, DFF])
    wu_d = din("ffn_w_up", [D, DFF])
    wd_d = din("ffn_w_down", [DFF, D])
    w1_d = din("rwkv_w1", [D, 64])
    a1_d = din("rwkv_a1", [D, 64])
    g1_d = din("rwkv_g1", [D, 128])
    w2_d = din("rwkv_w2", [64, 512])
    a2_d = din("rwkv_a2", [64, 512])
    g2_d = din("rwkv_g2", [128, 512])
    vecs_d = din("vecs", [128, NV])
    bct_d = din("bct", [128, 4608])
    cf_d = din("cf", [128, NCF])
    cb_d = din("cb", [128, NCB], BF16)
    out = nc.dram_tensor("out", [T, D], F32, kind="ExternalOutput").ap()
    tap_out = {}
    taps = taps or {}

    S = Sched(nc)
    st = ExitStack()
    ARENA_BYTES = 206 * 1024
    arena_t = st.enter_context(nc.sbuf_tensor("arena", [128, ARENA_BYTES // 4], F32))
    A = Arena(arena_t[:], ARENA_BYTES)
    pp = [st.enter_context(nc.psum_tensor(f"pp{i}", [128, 1024], F32)) for i in range(4)]
    bank = [pp[i // 2][:, (i % 2) * 512:(i % 2) * 512 + 512] for i in range(8)]
    bankB = [Buf(f"bank{i}") for i in range(8)]

    def bankbf(i):
        return bank[i].bitcast(BF16)

    def act(out_, in_, func, r, w, bias=None, scale=None, accum=None):
        kw = {}
        if bias is not None:
            kw['bias'] = bias
        if scale is not None:
            kw['scale'] = scale
        if accum is not None:
            kw['accum_out'] = accum
        S.op('act', lambda e: e.activation(out=out_, in_=in_, func=func, **kw), reads=r, writes=w)

    def tt(out_, a, b, op, r, w, q='dve'):
        S.op(q, lambda e: e.tensor_tensor(out=out_, in0=a, in1=b, op=op), reads=r, writes=w)

    def ts(out_, a, s1, s2, op0, op1, r, w, q='dve'):
        if s2 is None:
            S.op(q, lambda e: e.tensor_scalar(out=out_, in0=a, scalar1=s1, scalar2=None, op0=op0), reads=r, writes=w)
        else:
            S.op(q, lambda e: e.tensor_scalar(out=out_, in0=a, scalar1=s1, scalar2=s2, op0=op0, op1=op1), reads=r, writes=w)

    def stt(out_, a, s, b, op0, op1, r, w):
        S.op('dve', lambda e: e.scalar_tensor_tensor(out=out_, in0=a, scalar=s, in1=b, op0=op0, op1=op1), reads=r, writes=w)

    def mm(out_, lhsT, rhs, start, stop, r, w):
        S.op('pe', lambda e: e.matmul(out=out_, lhsT=lhsT, rhs=rhs, start=start, stop=stop), reads=r, writes=w)

    def mm2(out_, lhsT, rhs, start, stop, r, w):
        if lhsT.shape[0] == 128:
            mm(out_, lhsT[0:64], rhs[0:64], start, False, r, w)
            mm(out_, lhsT[64:128], rhs[64:128], False, stop, r, w)
        else:
            mm(out_, lhsT, rhs, start, stop, r, w)

    def dma(q, out_, in_, r, w, **kw):
        S.op(q, lambda e: e.dma_start(out=out_, in_=in_, **kw), reads=r, writes=w, dma=True)

    def cp(q, out_, in_, r, w):
        if q == 'act':
            act(out_, in_, AF.Copy, r, w)
        else:
            S.op(q, lambda e: e.tensor_copy(out=out_, in_=in_), reads=r, writes=w)

    def rsqrt_tiny(dst, src, scale, eps, r, w):
        ts(dst, src, scale, eps, ALU.mult, ALU.add, r, w)
        act(dst, dst, AF.Ln, w, w)
        act(dst, dst, AF.Exp, w, w, scale=-0.5)

    hT = A.take([8, T + 1], BF16)
    yT = A.take([8, T], BF16)
    cb = A.take([NCB], BF16)
    vecs = A.take([NV], F32)
    om = A.take([NV], F32)
    mhalf = A.take([4], F32)
    gtab = A.take([1024], F32)
    stat = A.take([64], F32)
    ss_all = A.take([3, NT], F32)
    rstd_all = A.take([3, NT], F32)
    B_const = Buf('const')
    B_gtab = Buf('gtab')
    hTb = [Buf(f'hT{n}') for n in range(NT)]
    yTb = [[Buf(f'yT{c}_{n}') for n in range(NT)] for c in range(8)]
    ident = cb[:, CB_ID:CB_ID + 128]
    PERSIST = A.mark()
    WOUT_OFF = ARENA_BYTES - 8 * D * 2
    wout = arena_t[:, WOUT_OFF // 4:ARENA_BYTES // 4].bitcast(BF16).rearrange("p (c n) -> p c n", c=8)
    woutB = Buf('wout')

    def tap(name, ap, shape, reads):
        if name in taps:
            d = nc.dram_tensor("tap_" + name, list(shape), ap.dtype, kind="ExternalOutput").ap()
            tap_out[name] = d
            dma('sp', d, ap, reads, [])

    dma('sp', cb, cb_d, [], [B_const])
    dma('sp', vecs, vecs_d, [], [B_const])
    dma('sp', gtab, bct_d[:, 0:1024], [], [B_gtab])
    S.op('pool', lambda e: e.memset(mhalf, -0.5), writes=[B_const])
    ts(om, vecs, -1.0, 1.0, ALU.mult, ALU.add, [B_const], [B_const])
    S.op('pool', lambda e: e.memset(hT[:, :, 0:1], 0.0), writes=[hTb[0]])

    xst = [A.take([D], F32) for _ in range(3)]
    xstB = [Buf(f'xst{i}') for i in range(3)]
    hb = [A.take([D], BF16) for _ in range(2)]
    hbB = [Buf(f'hb{i}') for i in range(2)]
    sqj = A.take([D], BF16)
    sqjB = Buf('sqj')
    statB = [Buf(f'stat{i}') for i in range(4)]
    NORM_END = A.mark()

    ssB = [Buf(f'ss{i}') for i in range(3)]
    rsB = [Buf(f'rs{i}') for i in range(3)]

    def norm_stats(n, src, srcB, which):
        act(sqj, src, AF.Square, [srcB], [sqjB, ssB[which]], accum=ss_all[:, which, n:n + 1])

    def norm_rstd(which, lo=0, hi=NT):
        rsqrt_tiny(rstd_all[:, which, lo:hi], ss_all[:, which, lo:hi], 1.0 / D, NORM_EPS, [ssB[which]], [rsB[which]])

    def norm_apply(n, src, srcB, which, pbank):
        h = hb[n % 2]
        stt(h, src, rstd_all[:, which, n:n + 1], gtab, ALU.mult, ALU.mult, [srcB, rsB[which], B_gtab], [hbB[n % 2]])
        pt = bankbf(pbank).rearrange("p (c t) -> p c t", c=8)
        for c in range(8):
            S.op('pe', lambda e, c=c: e.transpose(out=pt[:, c, :], in_=h[:, c * 128:(c + 1) * 128], identity=ident),
                 reads=[hbB[n % 2], B_const], writes=[bankB[pbank]])
        cp('act', hT[:, :, 1 + n * 128:1 + (n + 1) * 128], pt, [bankB[pbank]], [hTb[n]])

    xv = x.rearrange("(n p) d -> n p d", p=128)
    ov = out.rearrange("(n p) d -> n p d", p=128)
    if 2 in phases:
        cf = A.take([NCF], F32)
        dma('sp', cf, cf_d, [], [B_const])
        wret = A.take([8, 2048], BF16)
        wretB = Buf('wret')
        wv = w_in.rearrange("(c p) n -> p c n", p=128)
        for c in range(8):
            dma('pool', wret[:, c, :], wv[:, c, 1536:3584], [], [wretB])
        P2START = A.mark()
    for lo_ in (0, 8):
        for n in range(lo_, lo_ + 8):
            dma('sp', xst[n % 3], xv[n], [], [xstB[n % 3]])
            norm_stats(n, xst[n % 3], xstB[n % 3], 0)
        norm_rstd(0, lo_, lo_ + 8)
        for n in range(lo_, lo_ + 8):
            dma('sp', xst[n % 3], xv[n], [], [xstB[n % 3]])
            norm_apply(n, xst[n % 3], xstB[n % 3], 0, n % 2)
    tap('hT', hT, [128, 8, T + 1], hTb)

    if 2 in phases:
        A.reset(P2START)
        gnw = A.take([512], F32)
        dma('sp', gnw, bct_d[:, 4096:4608], [], [B_const])
        qa = A.take([512], F32)
        qb = A.take([512], F32)
        qrot = A.take([512], BF16)
        krot = A.take([512], BF16)
        qT = A.take([4, 128], BF16)
        kT = A.take([4, 128], BF16)
        PT = A.take([4, 128], BF16)
        Vb = A.take([512], BF16)
        Vk = A.take([512], BF16)
        R = A.take([512], F32)
        Rt = A.take([512], F32)
        Rb = A.take([512], BF16)
        sqy = A.take([512], F32)
        yn = A.take([512], F32)
        sgt = A.take([512], F32)
        yo = A.take([512], BF16)
        rst = A.take([32], F32)
        Bq = {k: Buf('r_' + k) for k in ['qa', 'qb', 'qrot', 'krot', 'qT', 'kT', 'PT', 'Vb', 'Vk', 'R', 'Rt', 'Rb', 'sqy', 'yn', 'sgt', 'yo', 'rst']}
        kapg_bc = cf[:, CF_KAPG:CF_KAPG + 4].unsqueeze(2).to_broadcast([128, 4, 128])
        gC_bc = cf[:, CF_GC:CF_GC + 4].unsqueeze(2).to_broadcast([128, 4, 128])
        xiT = cf[:, CF_XIT:CF_XIT + 512].rearrange("p (h t) -> p h t", h=4)
        kaT = cf[:, CF_KAT:CF_KAT + 512].rearrange("p (h t) -> p h t", h=4)
        mret_bc = cb[:, CB_MRET:CB_MRET + 128].unsqueeze(1).to_broadcast([128, 4, 128])
        PQ, PK, PV, PG, PTB, PS, PY, PKV = range(8)

        def v4(ap):
            return ap.rearrange("p (h e) -> p h e", h=4)

        def rot(ps, psB, dst, dstB, n):
            cosb = cf[:, CF_COS + n * 64:CF_COS + (n + 1) * 64].unsqueeze(1).unsqueeze(1).to_broadcast([128, 4, 2, 64])
            sinb = cf[:, CF_SIN + n * 64:CF_SIN + (n + 1) * 64].unsqueeze(1).to_broadcast([128, 4, 64])
            nsinb = cf[:, CF_NSIN + n * 64:CF_NSIN + (n + 1) * 64].unsqueeze(1).to_broadcast([128, 4, 64])
            p4 = ps.rearrange("p (h two f) -> p h two f", h=4, two=2)
            tt(qa.rearrange("p (h two f) -> p h two f", h=4, two=2), p4, cosb, ALU.mult, [psB, B_const], [Bq['qa']])
            qb4 = qb.rearrange("p (h two f) -> p h two f", h=4, two=2)
            tt(qb4[:, :, 0, :], p4[:, :, 1, :], nsinb, ALU.mult, [psB, B_const], [Bq['qb']])
            tt(qb4[:, :, 1, :], p4[:, :, 0, :], sinb, ALU.mult, [psB, B_const], [Bq['qb']])
            tt(dst, qa, qb, ALU.add, [Bq['qa'], Bq['qb']], [dstB])

        for n in range(DBG['ret_chunks']):
            RS = DBG['ret_steps']
            def proj(nn):
                tok = slice(1 + nn * 128, 1 + (nn + 1) * 128)
                for j, pb in enumerate((PQ, PK, PV, PG)):
                    for c in range(8):
                        mm(bank[pb], hT[:, c, tok], wret[:, c, j * 512:(j + 1) * 512], c == 0, c == 7,
                           [hTb[nn], wretB], [bankB[pb]])
            if n == 0:
                proj(0)
            act(sgt, bank[PG], AF.Silu, [bankB[PG]], [Bq['sgt']])
            rot(bank[PQ], bankB[PQ], qrot, Bq['qrot'], n)
            rot(bank[PK], bankB[PK], krot, Bq['krot'], n)
            if RS < 3:
                continue
            ptb = bankbf(PTB).rearrange("p (c t) -> p c t", c=8)
            for h in range(4):
                S.op('pe', lambda e, h=h: e.transpose(out=ptb[:, h, :], in_=qrot[:, h * 128:(h + 1) * 128], identity=ident),
                     reads=[Bq['qrot'], B_const], writes=[bankB[PTB]])
            for h in range(4):
                S.op('pe', lambda e, h=h: e.transpose(out=ptb[:, 4 + h, :], in_=krot[:, h * 128:(h + 1) * 128], identity=ident),
                     reads=[Bq['krot'], B_const], writes=[bankB[PTB]])
            tt(qT, ptb[:, 0:4, :], xiT, ALU.mult, [bankB[PTB], B_const], [Bq['qT']])
            tt(kT, ptb[:, 4:8, :], kaT, ALU.mult, [bankB[PTB], B_const], [Bq['kT']])
            if RS < 4:
                continue
            ps4 = v4(bank[PS])
            for h in range(4):
                mm(ps4[:, h, :], kT[:, h, :], qT[:, h, :], True, True, [Bq['kT'], Bq['qT']], [bankB[PS]])
            tt(PT, ps4, mret_bc, ALU.mult, [bankB[PS], B_const], [Bq['PT']])
            if RS < 5:
                continue
            cp('act', Vb, bank[PV], [bankB[PV]], [Bq['Vb']])
            tt(v4(Vk), v4(bank[PV]), kapg_bc, ALU.mult, [bankB[PV], B_const], [Bq['Vk']])
            if RS < 6:
                continue
            py4 = v4(bank[PY])
            for h in range(4):
                mm(py4[:, h, :], PT[:, h, :], Vb[:, h * 128:(h + 1) * 128], True, n == 0, [Bq['PT'], Bq['Vb']], [bankB[PY]])
                if n > 0:
                    mm(py4[:, h, :], qT[:, h, :], Rb[:, h * 128:(h + 1) * 128], False, True, [Bq['qT'], Bq['Rb']], [bankB[PY]])
            if RS < 7:
                continue
            if n < NT - DBG.get('skiplast', 0):
                pkv4 = v4(bank[PKV])
                for h in range(4):
                    mm(pkv4[:, h, :], krot[:, h * 128:(h + 1) * 128], Vk[:, h * 128:(h + 1) * 128], True, True,
                       [Bq['krot'], Bq['Vk']], [bankB[PKV]])
                if n == 0:
                    cp('dve', R, bank[PKV], [bankB[PKV]], [Bq['R']])
                else:
                    tt(v4(Rt), v4(R), gC_bc, ALU.mult, [Bq['R'], B_const], [Bq['Rt']])
                    tt(R, Rt, bank[PKV], ALU.add, [Bq['Rt'], bankB[PKV]], [Bq['R']])
                cp('pool', Rb, R, [Bq['R']], [Bq['Rb']])
            if n + 1 < NT:
                proj(n + 1)
            s1 = rst[:, 0:4]
            s2 = rst[:, 4:8]
            mean = rst[:, 8:12]
            msq = rst[:, 12:16]
            rstd = rst[:, 16:20]
            S.op('dve', lambda e: e.tensor_reduce(out=s1, in_=py4, axis=AX.X, op=ALU.add), reads=[bankB[PY]], writes=[Bq['rst']])
            act(sqy, bank[PY], AF.Square, [bankB[PY]], [Bq['sqy']])
            S.op('dve', lambda e: e.tensor_reduce(out=s2, in_=v4(sqy), axis=AX.X, op=ALU.add), reads=[Bq['sqy']], writes=[Bq['rst']])
            ts(mean, s1, 1.0 / 128, None, ALU.mult, None, [Bq['rst']], [Bq['rst']])
            tt(msq, mean, mean, ALU.mult, [Bq['rst']], [Bq['rst']])
            stt(rstd, s2, 1.0 / 128, msq, ALU.mult, ALU.subtract, [Bq['rst']], [Bq['rst']])
            rsqrt_tiny(rstd, rstd, 1.0, RET_GN_EPS, [Bq['rst']], [Bq['rst']])
            tt(v4(yn), py4, mean.unsqueeze(2).to_broadcast([128, 4, 128]), ALU.subtract, [bankB[PY], Bq['rst']], [Bq['yn']])
            tt(v4(yn), v4(yn), rstd.unsqueeze(2).to_broadcast([128, 4, 128]), ALU.mult, [Bq['yn'], Bq['rst']], [Bq['yn']])
            tt(yn, yn, gnw, ALU.mult, [Bq['yn'], B_const], [Bq['yn']])
            tt(yo, yn, sgt, ALU.mult, [Bq['yn'], Bq['sgt']], [Bq['yo']])
            if RS < 9:
                continue
            for h in range(4):
                S.op('pe', lambda e, h=h: e.transpose(out=ptb[:, h, :], in_=yo[:, h * 128:(h + 1) * 128], identity=ident),
                     reads=[Bq['yo'], B_const], writes=[bankB[PTB]])
            cp('act', yT[:, 4:8, n * 128:(n + 1) * 128], ptb[:, 0:4, :], [bankB[PTB]], [yTb[4 + h][n] for h in range(4)])
        if 3 not in phases:
            tap('yT', yT, [128, 8, T], [b for l in yTb for b in l])
        S.barrier()


    if 3 in phases:
        S.barrier()
        A.reset(PERSIST)
        wl_f = A.take([8, 128], F32)
        gl_f = A.take([8, 128], F32)
        W1A = A.take([8, 128], BF16)
        W1B = A.take([8, 128], BF16)
        G1A = A.take([8, 128], BF16)
        G1B = A.take([8, 128], BF16)
        W2sb = A.take([512], BF16)
        A2sb = A.take([512], BF16)
        G2sb = A.take([512], BF16)
        L1 = A.take([T], BF16)
        L1g = A.take([T], BF16)
        lnxw = A.take([512], F32)
        lnxb = A.take([512], F32)
        B_lw = Buf('loraw')
        L1B = [Buf(f'L1_{i}') for i in range(4)]
        dma('sp', wl_f[:, :, 0:64], w1_d.rearrange("(c p) k -> p c k", p=128), [], [B_lw])
        dma('sp', wl_f[:, :, 64:128], a1_d.rearrange("(c p) k -> p c k", p=128), [], [B_lw])
        dma('sp', gl_f, g1_d.rearrange("(c p) k -> p c k", p=128), [], [B_lw])
        dma('sp', lnxw, bct_d[:, 3072:3584], [], [B_lw])
        dma('sp', lnxb, bct_d[:, 3584:4096], [], [B_lw])
        S.op('pool', lambda e: e.memset(W2sb, 0.0), writes=[B_lw])
        S.op('pool', lambda e: e.memset(A2sb, 0.0), writes=[B_lw])
        dma('pool', W2sb[0:64, :], w2_d, [B_lw], [B_lw])
        dma('pool', A2sb[64:128, :], a2_d, [B_lw], [B_lw])
        dma('pool', G2sb, g2_d, [], [B_lw])

        def vb(tab, col, k):
            return tab[:, col:col + 8].unsqueeze(2).to_broadcast([128, 8, k])
        tt(W1A[:, :, 0:64], wl_f[:, :, 0:64], vb(om, V_MUW, 64), ALU.mult, [B_lw, B_const], [B_lw])
        tt(W1A[:, :, 64:128], wl_f[:, :, 64:128], vb(om, V_MUA, 64), ALU.mult, [B_lw, B_const], [B_lw])
        tt(W1B[:, :, 0:64], wl_f[:, :, 0:64], vb(vecs, V_MUW, 64), ALU.mult, [B_lw, B_const], [B_lw])
        tt(W1B[:, :, 64:128], wl_f[:, :, 64:128], vb(vecs, V_MUA, 64), ALU.mult, [B_lw, B_const], [B_lw])
        tt(G1A, gl_f, vb(om, V_MUG, 128), ALU.mult, [B_lw, B_const], [B_lw])
        tt(G1B, gl_f, vb(vecs, V_MUG, 128), ALU.mult, [B_lw, B_const], [B_lw])
        for tb in range(4):
            rd = [hTb[4 * tb + i] for i in range(4)] + ([hTb[4 * tb - 1]] if tb > 0 else []) + [B_lw]
            for (WA, WB, pb) in ((W1A, W1B, 0), (G1A, G1B, 1)):
                for c in range(8):
                    mm(bank[pb], WA[:, c, :], hT[:, c, 1 + tb * 512:1 + (tb + 1) * 512], c == 0, False, rd, [bankB[pb]])
                    mm(bank[pb], WB[:, c, :], hT[:, c, tb * 512:(tb + 1) * 512], False, c == 7, rd, [bankB[pb]])
            blk = slice(tb * 512, (tb + 1) * 512)
            act(L1[0:64, blk], bank[0][0:64, :], AF.Tanh, [bankB[0]], [L1B[tb]])
            act(L1[64:128, blk], bank[0][64:128, :], AF.Copy, [bankB[0]], [L1B[tb]])
            act(L1g[:, blk], bank[1], AF.Sigmoid, [bankB[1]], [L1B[tb]])

        wrkv = A.take([8, 3, 128], BF16)
        AR = A.take([NT, 2, 128], BF16)
        BT = A.take([T], BF16)
        KT = A.take([T], BF16)
        vT = A.take([T], BF16)
        rkrT = A.take([T], BF16)
        rm = A.take([513], F32)
        km = A.take([513], F32)
        vm = A.take([513], F32)
        tnames = ['r', 'k0', 'sg', 'asg', 'cum', 'P', 'invP', 'Pp', 'ssk', 'kk', 't1']
        tmp = {k: A.take([512], F32) for k in tnames}
        sqk = A.take([512], BF16)
        PCt = A.take([NT], F32)
        Xb = [A.take([2, 2, 128], BF16) for _ in range(2)]
        Nn = [A.take([2, 128], BF16) for _ in range(2)]
        W1s = [A.take([2, 3, 128], BF16) for _ in range(2)]
        TTs = [A.take([2, 128], BF16) for _ in range(2)]
        BK = [A.take([2, 128], BF16) for _ in range(2)]
        Vt = [A.take([4, 128], BF16) for _ in range(2)]
        Xs = A.take([128], BF16)
        Us = A.take([128], BF16)
        Hs = A.take([64], F32)
        HP = A.take([64], F32)
        Hbz = A.take([2, 64], BF16)
        BKz = [A.take([2, 2, 128], BF16) for _ in range(2)]
        Yp = [A.take([4, 128], F32) for _ in range(2)]
        sqp = A.take([512], F32)
        ynp = A.take([512], F32)
        bon = A.take([512], F32)
        sB = A.take([8], F32)
        yop = A.take([4, 128], BF16)
        rstp = A.take([64], F32)
        Bw = Buf('wrkv')
        Bt_ = {k: Buf('t_' + k) for k in tnames + ['rm', 'km', 'vm', 'sqk', 'PCt']}
        ARb = [Buf(f'AR{n}') for n in range(NT)]
        BTb = [Buf(f'BT{i}') for i in range(4)]
        KTb = [Buf(f'KT{i}') for i in range(4)]
        vTb = [Buf(f'vT{i}') for i in range(4)]
        rkb = [Buf(f'rk{i}') for i in range(4)]
        Bs = {k: Buf('s_' + k) for k in ['X0', 'X1', 'N0', 'N1', 'W10', 'W11', 'TT0', 'TT1', 'BK0', 'BK1', 'Vt0', 'Vt1', 'Xs', 'Us', 'H', 'HP', 'Hb', 'Yp0', 'Yp1', 'BKz0', 'BKz1',
                                          'sqp', 'ynp', 'bon', 'sB', 'yop', 'rstp',
                                          'ps1', 'ps2', 'psN', 'psL', 'pT', 'pT2', 'psX', 'psU', 'psH', 'psY', 'psB', 'psG']}
        for k_, b_ in (('ps1', 0), ('ps2', 2), ('psN', 2), ('psL', 3), ('pT', 4), ('pT2', 4), ('psX', 5), ('psU', 5), ('psH', 5),
                       ('psY', 6), ('psB', 6), ('psG', 7)):
            Bs[k_] = bankB[b_]
        wv3 = w_in.rearrange("(c p) n -> p c n", p=128)
        m4 = cb[:, CB_M4:CB_M4 + 512]
        mS_bc = cb[:, CB_M4:CB_M4 + 128].unsqueeze(1).to_broadcast([128, 2, 128])
        m3_bc = cb[:, CB_M4 + 128:CB_M4 + 512].unsqueeze(1).to_broadcast([128, 2, 384])
        mL_bc = cb[:, CB_ML:CB_ML + 128].unsqueeze(1).to_broadcast([128, 2, 128])
        id_bc = ident.unsqueeze(1).to_broadcast([128, 2, 128])
        sel = cb[:, CB_SEL:CB_SEL + 2]
        ones_bd = cb[:, CB_ONES:CB_ONES + 128]
        ps1 = pp[0][:].rearrange("p (h c) -> p h c", h=2)
        ps2 = bank[2][:, 0:256].rearrange("p (h s) -> p h s", h=2)
        psN = bank[2][:, 256:512].rearrange("p (h s) -> p h s", h=2)
        psL = bank[3].rearrange("p (h c) -> p h c", h=2)
        pTb = bankbf(4)
        pT3 = pTb[:, 0:384].rearrange("p (j t) -> p j t", j=3)
        pT2 = pTb[:, 512:1024].rearrange("p (j t) -> p j t", j=4)
        psX = bank[5][:, 0:128]
        psU = bank[5][:, 128:256]
        psH = bank[5][:, 256:384]
        psY = bank[6][:, 0:128]
        psB = bank[6][:, 128:136]
        psG = bank[7]

        def pair_setup(p):
            vp = V_PAIR + 8 * p
            col = lambda j: vecs[:, vp + j:vp + j + 1]
            ocol = lambda j: om[:, vp + j:vp + j + 1]
            return col, ocol

        def prep_block(p, tb):
            col, ocol = pair_setup(p)
            if tb == 0:
                for j in range(3):
                    dma('pool', wrkv[:, :, j, :], wv3[:, :, j * 512 + p * 128:j * 512 + (p + 1) * 128], [], [Bw])
                for nm_ in ('rm', 'km', 'vm'):
                    tl = {'rm': rm, 'km': km, 'vm': vm}[nm_]
                    S.op('pool', lambda e, tl=tl: e.memset(tl[:, 0:1], 0.0), writes=[Bt_[nm_]])
            blk = slice(tb * 512, (tb + 1) * 512)
            rd = [hTb[4 * tb + i] for i in range(4)] + [Bw]
            for j in range(3):
                for c in range(8):
                    mm(bank[j], wrkv[:, c, j, :], hT[:, c, 1 + tb * 512:1 + (tb + 1) * 512], c == 0, c == 7, rd, [bankB[j]])
            mm(bank[3], W2sb[:, p * 128:(p + 1) * 128], L1[:, blk], True, True, [B_lw, L1B[tb]], [bankB[3]])
            mm(bank[4], A2sb[:, p * 128:(p + 1) * 128], L1[:, blk], True, True, [B_lw, L1B[tb]], [bankB[4]])
            for j, (tl, nm_, dst, dstB) in enumerate(((rm, 'rm', tmp['r'], Bt_['r']), (km, 'km', tmp['k0'], Bt_['k0']), (vm, 'vm', vT[:, blk], vTb[tb]))):
                act(tl[:, 1:513], bank[j], AF.Copy, [bankB[j], B_const], [Bt_[nm_]], scale=col(j))
                stt(dst, bank[j], ocol(j), tl[:, 0:512], ALU.mult, ALU.add, [bankB[j], Bt_[nm_], B_const], [dstB])
                S.op('pool', lambda e, tl=tl: e.tensor_copy(out=tl[:, 0:1], in_=tl[:, 512:513]), reads=[Bt_[nm_]], writes=[Bt_[nm_]])
            r_, k0 = tmp['r'], tmp['k0']
            act(tmp['sg'], bank[3], AF.Sigmoid, [bankB[3], B_const], [Bt_['sg']], bias=col(3))
            act(tmp['asg'], bank[4], AF.Sigmoid, [bankB[4], B_const], [Bt_['asg']], bias=col(4))
            for ch in range(4):
                cs = slice(ch * 128, (ch + 1) * 128)
                S.op('dve', lambda e, cs=cs: e.tensor_tensor_scan(out=tmp['cum'][:, cs], data0=tmp['sg'][:, cs], data1=tmp['sg'][:, cs],
                                                                   initial=0.0, op0=ALU.add, op1=ALU.bypass),
                     reads=[Bt_['sg']], writes=[Bt_['cum']])
            act(tmp['P'], tmp['cum'], AF.Exp, [Bt_['cum']], [Bt_['P']], scale=-C0)
            act(tmp['invP'], tmp['cum'], AF.Exp, [Bt_['cum']], [Bt_['invP']], scale=C0)
            tt(tmp['sg'], tmp['cum'], tmp['sg'], ALU.subtract, [Bt_['cum'], Bt_['sg']], [Bt_['sg']])
            act(tmp['Pp'], tmp['sg'], AF.Exp, [Bt_['sg']], [Bt_['Pp']], scale=-C0)
            S.op('pool', lambda e, tb=tb: e.tensor_copy(out=PCt[:, tb * 4:(tb + 1) * 4],
                                                        in_=tmp['P'].rearrange("p (c t) -> p c t", c=4)[:, :, 127]),
                 reads=[Bt_['P']], writes=[Bt_['PCt']])
            act(sqk, k0, AF.Square, [Bt_['k0'], B_const], [Bt_['sqk']], scale=col(5))
            mm(bank[5], ones_bd, sqk, True, True, [Bt_['sqk'], B_const], [bankB[5]])
            act(tmp['ssk'], bank[5], AF.Ln, [bankB[5]], [Bt_['ssk']])
            act(tmp['ssk'], tmp['ssk'], AF.Exp, [Bt_['ssk']], [Bt_['ssk']], scale=-0.5)
            stt(tmp['kk'], k0, col(5), tmp['ssk'], ALU.mult, ALU.mult, [Bt_['k0'], Bt_['ssk'], B_const], [Bt_['kk']])
            ts(tmp['t1'], tmp['asg'], col(6), ocol(6), ALU.mult, ALU.add, [Bt_['asg'], B_const], [Bt_['t1']])
            tt(tmp['t1'], tmp['t1'], k0, ALU.mult, [Bt_['t1'], Bt_['k0']], [Bt_['t1']])
            arv = AR[:, 4 * tb:4 * tb + 4, :, :]
            c4 = lambda a: a.rearrange("p (c t) -> p c t", c=4)
            stt(arv[:, :, 0, :], c4(tmp['kk']), -1.0, c4(tmp['Pp']), ALU.mult, ALU.mult, [Bt_['kk'], Bt_['Pp']], [ARb[4 * tb + i] for i in range(4)])
            tt(arv[:, :, 1, :], c4(r_), c4(tmp['P']), ALU.mult, [Bt_['r'], Bt_['P']], [ARb[4 * tb + i] for i in range(4)])
            tt(tmp['kk'], tmp['kk'], tmp['asg'], ALU.mult, [Bt_['kk'], Bt_['asg']], [Bt_['kk']])
            tt(BT[:, blk], tmp['kk'], tmp['invP'], ALU.mult, [Bt_['kk'], Bt_['invP']], [BTb[tb]])
            tt(KT[:, blk], tmp['t1'], tmp['invP'], ALU.mult, [Bt_['t1'], Bt_['invP']], [KTb[tb]])
            stt(rkrT[:, blk], r_, col(7), tmp['t1'], ALU.mult, ALU.mult, [Bt_['r'], Bt_['t1'], B_const], [rkb[tb]])

        def make_scan(p):
            col, ocol = pair_setup(p)
            def local(n):
                cs = slice(n * 128, (n + 1) * 128)
                tb = n // 4
                bz = BKz[n % 2]
                W1, TT, W1B, TTB = W1s[n % 2], TTs[n % 2], Bs[f'W1{n % 2}'], Bs[f'TT{n % 2}']
                bzB = Bs[f'BKz{n % 2}']
                for h in range(2):
                    hp = slice(64 * h, 64 * h + 64)
                    S.op('pool', lambda e, h=h, hp=hp: e.tensor_copy(out=bz[hp, 0, h, :], in_=BT[hp, cs]), reads=[BTb[tb]], writes=[bzB])
                    S.op('pool', lambda e, h=h, hp=hp: e.tensor_copy(out=bz[hp, 1, h, :], in_=KT[hp, cs]), reads=[KTb[tb]], writes=[bzB])
                for h in range(2):
                    mm(ps1[:, h, 0:256], bz[:, 0, h, :], AR[:, n, :, :], True, True, [bzB, ARb[n]], [Bs['ps1']])
                    mm(ps1[:, h, 256:512], bz[:, 1, h, :], AR[:, n, :, :], True, True, [bzB, ARb[n]], [Bs['ps1']])
                    mm(ps2[:, h, :], AR[:, n, 0, :], bz[:, 0, h, :], True, True, [bzB, ARb[n]], [Bs['ps2']])
                tt(Xb[0][:, :, 0, :], ps1[:, :, 0:128], mS_bc, ALU.mult, [Bs['ps1'], B_const], [Bs['X0']])
                tt(Nn[0], ps2, mL_bc, ALU.mult, [Bs['ps2'], B_const], [Bs['N0']])
                tt(Xb[1][:, :, 1, :], Xb[0][:, :, 0, :], id_bc, ALU.add, [Bs['X0'], B_const], [Bs['X1']])
                tt(W1, ps1[:, :, 128:512], m3_bc, ALU.mult, [Bs['ps1'], B_const], [W1B])
                yield
                cur = 0
                for k in range(4):
                    nx = 1 - cur
                    lastk = (k == 3)
                    for h in range(2):
                        if lastk:
                            mm(psL[:, h, 128:256], Nn[cur][:, h, :], Xb[cur][:, h, 1, :], True, True, [Bs[f'N{cur}'], Bs[f'X{cur}']], [Bs['psL']])
                        elif k == 0:
                            mm(psL[:, h, 0:128], Nn[cur][:, h, :], Xb[cur][:, h, 0, :], True, True, [Bs[f'N{cur}'], Bs[f'X{cur}']], [Bs['psL']])
                            mm(psN[:, h, :], Xb[cur][:, h, 0, :], Nn[cur][:, h, :], True, True, [Bs[f'N{cur}'], Bs[f'X{cur}']], [Bs['psN']])
                        else:
                            mm(psL[:, h, :], Nn[cur][:, h, :], Xb[cur][:, h, :, :], True, True, [Bs[f'N{cur}'], Bs[f'X{cur}']], [Bs['psL']])
                            mm(psN[:, h, :], Xb[cur][:, h, 0, :], Nn[cur][:, h, :], True, True, [Bs[f'N{cur}'], Bs[f'X{cur}']], [Bs['psN']])
                    if lastk:
                        tt(TT, Xb[cur][:, :, 1, :], psL[:, :, 128:256], ALU.add, [Bs['psL'], Bs[f'X{cur}']], [TTB])
                    else:
                        cp('act', Xb[nx][:, :, 0, :], psL[:, :, 0:128], [Bs['psL']], [Bs[f'X{nx}']])
                        cp('dve', Nn[nx], psN, [Bs['psN']], [Bs[f'N{nx}']])
                        if k > 0:
                            tt(Xb[nx][:, :, 1, :], Xb[cur][:, :, 1, :], psL[:, :, 128:256], ALU.add, [Bs['psL'], Bs[f'X{cur}']], [Bs[f'X{nx}']])
                    cur = nx
                    yield

            def chain(n):
                cs = slice(n * 128, (n + 1) * 128)
                tb = n // 4
                g = (n // 4) % 2
                bk = BK[n % 2]
                bkB = Bs[f'BK{n % 2}']
                vt = Vt[g][:, n % 4, :]
                vtB = Bs[f'Vt{g}']
                W1, TT, W1B, TTB = W1s[n % 2], TTs[n % 2], Bs[f'W1{n % 2}'], Bs[f'TT{n % 2}']
                S.op('pe', lambda e: e.transpose(out=pT3[:, 0, :], in_=vT[:, cs], identity=ident), reads=[vTb[tb], B_const], writes=[Bs['pT']])
                S.op('pe', lambda e: e.transpose(out=pT3[:, 1, :], in_=BT[:, cs], identity=ident), reads=[BTb[tb], B_const], writes=[Bs['pT']])
                S.op('pe', lambda e: e.transpose(out=pT3[:, 2, :], in_=KT[:, cs], identity=ident), reads=[KTb[tb], B_const], writes=[Bs['pT']])
                cp('act', vt, pT3[:, 0, :], [Bs['pT']], [vtB])
                cp('act', bk, pT3[:, 1:3, :], [Bs['pT']], [bkB])
                yield
                for h in range(2):
                    hs = slice(64 * h, 64 * h + 64)
                    if n > 0:
                        mm(psX[:, hs], AR[:, n, 0, :], Hbz[:, h, :], True, False, [ARb[n], Bs['Hb']], [Bs['psX']])
                    mm(psX[:, hs], W1[:, h, 1, :], vt[:, hs], n == 0, True, [W1B, vtB], [Bs['psX']])
                cp('act', Xs, psX, [Bs['psX']], [Bs['Xs']])
                yield
                for h in range(2):
                    hs = slice(64 * h, 64 * h + 64)
                    mm(psU[:, hs], TT[:, h, :], Xs[:, hs], True, True, [TTB, Bs['Xs']], [Bs['psU']])
                cp('dve', Us, psU, [Bs['psU']], [Bs['Us']])
                yield
                for h in range(2):
                    hs = slice(64 * h, 64 * h + 64)
                    if n > 0:
                        mm(psY[:, hs], AR[:, n, 1, :], Hbz[:, h, :], True, False, [ARb[n], Bs['Hb']], [Bs['psY']])
                    mm(psY[:, hs], W1[:, h, 0, :], Us[:, hs], n == 0, False, [W1B, Bs['Us']], [Bs['psY']])
                    mm(psY[:, hs], W1[:, h, 2, :], vt[:, hs], False, True, [W1B, vtB], [Bs['psY']])
                mm(psH, bk[:, 0, :], Us, True, False, [bkB, Bs['Us']], [Bs['psH']])
                mm(psH, bk[:, 1, :], vt, False, True, [bkB, vtB], [Bs['psH']])
                cp('act', Yp[g][:, n % 4, :], psY, [Bs['psY']], [Bs[f'Yp{g}']])
                if n > 0:
                    ts(HP, Hs, PCt[:, n:n + 1], None, ALU.mult, None, [Bs['H'], Bt_['PCt']], [Bs['HP']])
                for h in range(2):
                    hp = slice(64 * h, 64 * h + 64)
                    hs = slice(64 * h, 64 * h + 64)
                    if n > 0:
                        stt(Hs[hp, :], psH[hp, hs], PCt[hp, n:n + 1], HP[hp, :], ALU.mult, ALU.add, [Bs['psH'], Bs['HP'], Bt_['PCt']], [Bs['H']])
                    else:
                        ts(Hs[hp, :], psH[hp, hs], PCt[hp, n:n + 1], None, ALU.mult, None, [Bs['psH'], Bt_['PCt']], [Bs['H']])
                for h in range(2):
                    hp = slice(64 * h, 64 * h + 64)
                    cp('act', Hbz[hp, h, :], Hs[hp, :], [Bs['H']], [Bs['Hb']])
                yield

            def post(tg):
                g = tg % 2
                y3 = Yp[g].rearrange("p j (h e) -> p (j h) e", h=2)
                yB = Bs[f'Yp{g}']
                s1, s2, mean, msq, rstd = (rstp[:, 8 * i:8 * i + 8] for i in range(5))
                v8 = lambda a: a.rearrange("p (j e) -> p j e", j=8)
                S.op('dve', lambda e: e.tensor_reduce(out=s1, in_=y3, axis=AX.X, op=ALU.add), reads=[yB], writes=[Bs['rstp']])
                act(sqp, Yp[g].rearrange("p j c -> p (j c)"), AF.Square, [yB], [Bs['sqp']])
                S.op('dve', lambda e: e.tensor_reduce(out=s2, in_=v8(sqp), axis=AX.X, op=ALU.add), reads=[Bs['sqp']], writes=[Bs['rstp']])
                yield
                ts(mean, s1, 1.0 / 64, None, ALU.mult, None, [Bs['rstp']], [Bs['rstp']])
                tt(msq, mean, mean, ALU.mult, [Bs['rstp']], [Bs['rstp']])
                stt(rstd, s2, 1.0 / 64, msq, ALU.mult, ALU.subtract, [Bs['rstp']], [Bs['rstp']])
                rsqrt_tiny(rstd, rstd, 1.0, RWKV_GN_EPS, [Bs['rstp']], [Bs['rstp']])
                yield
                tt(v8(ynp), y3, mean.unsqueeze(2).to_broadcast([128, 8, 64]), ALU.subtract, [yB, Bs['rstp']], [Bs['ynp']])
                tt(v8(ynp), v8(ynp), rstd.unsqueeze(2).to_broadcast([128, 8, 64]), ALU.mult, [Bs['ynp'], Bs['rstp']], [Bs['ynp']])
                yield
                y4 = ynp.rearrange("p (j c) -> p j c", j=4)
                tt(y4, y4, lnxw[:, p * 128:(p + 1) * 128].unsqueeze(1).to_broadcast([128, 4, 128]), ALU.mult, [Bs['ynp'], B_lw], [Bs['ynp']])
                tt(y4, y4, lnxb[:, p * 128:(p + 1) * 128].unsqueeze(1).to_broadcast([128, 4, 128]), ALU.add, [Bs['ynp'], B_lw], [Bs['ynp']])
                yield
                for j in range(4):
                    n = 4 * tg + j
                    cs = slice(n * 128, (n + 1) * 128)
                    mm(psB[:, 2 * j:2 * j + 2], rkrT[:, cs], sel, True, True, [rkb[tg], B_const], [Bs['psB']])
                    mm(psG[:, j * 128:(j + 1) * 128], L1g[:, cs], G2sb[:, p * 128:(p + 1) * 128], True, True, [L1B[tg], B_lw], [Bs['psG']])
                cp('act', sB, psB, [Bs['psB']], [Bs['sB']])
                yield
                tt(v8(bon), Vt[g].rearrange("p j (h e) -> p (j h) e", h=2), sB.unsqueeze(2).to_broadcast([128, 8, 64]), ALU.mult,
                   [Bs[f'Vt{g}'], Bs['sB']], [Bs['bon']])
                tt(ynp, ynp, bon, ALU.add, [Bs['ynp'], Bs['bon']], [Bs['ynp']])
                yield
                tt(yop.rearrange("p j c -> p (j c)"), ynp, psG, ALU.mult, [Bs['ynp'], Bs['psG']], [Bs['yop']])
                yield
                for j in range(4):
                    S.op('pe', lambda e, j=j: e.transpose(out=pT2[:, j, :], in_=yop[:, j, :], identity=ident), reads=[Bs['yop'], B_const], writes=[Bs['pT2']])
                cp('act', yT[:, p, tg * 512:(tg + 1) * 512], pT2.rearrange("p j t -> p (j t)"), [Bs['pT2']], [yTb[p][4 * tg + j] for j in range(4)])
                yield

            return local, chain, post

        assert A.peak <= WOUT_OFF, ("phase-3 buffers overlap the w_out prefetch region", A.peak, WOUT_OFF)
        wo_v = w_out.rearrange("(c p) n -> p c n", p=128)
        for c in range(8):
            dma('pool', wout[:, c, :], wo_v[:, c, :], [], [woutB])
        for i_ in range(2):
            S.op('pool', lambda e, i_=i_: e.memset(BKz[i_], 0.0), writes=[Bs[f'BKz{i_}']])
        NP = DBG.get('pairs', 4)
        for tb in range(4):
            prep_block(0, tb)
        scans = [make_scan(p) for p in range(NP)]

        def drain(g):
            for _ in g:
                pass
        S.op('pool', lambda e: e.memset(Hs, 0.0), writes=[Bs['H']])
        S.op('pool', lambda e: e.memset(Hbz, 0.0), writes=[Bs['Hb']])
        drain(scans[0][0](0))
        pend = []

        def prep_gen(p_, tb_):
            prep_block(p_, tb_)
            yield

        for p in range(NP):
            local, chain, post = scans[p]
            for n in range(NT):
                while len(pend) > 2:
                    drain(pend.pop(0))
                a = chain(n)
                if n + 1 < NT:
                    b = local(n + 1)
                elif p + 1 < NP:
                    b = scans[p + 1][0](0)
                else:
                    b = iter(())
                done_a = done_b = False
                while not (done_a and done_b):
                    if not done_a:
                        try:
                            next(a)
                        except StopIteration:
                            done_a = True
                    if not done_b:
                        try:
                            next(b)
                        except StopIteration:
                            done_b = True
                    if pend:
                        try:
                            next(pend[0])
                        except StopIteration:
                            pend.pop(0)
                if n % 4 == 3:
                    pend.append(post(n // 4))
                    if p + 1 < NP:
                        pend.append(prep_gen(p + 1, n // 4))
            if p + 1 == NP:
                while pend:
                    drain(pend.pop(0))
            if p + 1 < NP:
                S.op('dve', lambda e: e.memset(Hs, 0.0), writes=[Bs['H']])
                S.op('pool', lambda e: e.memset(Hbz, 0.0), writes=[Bs['Hb']])
        tap('yT', yT, [128, 8, T], [b for l in yTb for b in l])
        S.barrier()


    if 4 in phases:
        S.barrier()
        A.reset(NORM_END)
        xres = A.take([NT, D], F32)
        xresB = [Buf(f'xres{n}') for n in range(NT)]
        P4 = A.mark()
        if 3 not in phases:
            wo_v = w_out.rearrange("(c p) n -> p c n", p=128)
            for c in range(8):
                dma('pool', wout[:, c, :], wo_v[:, c, :], [], [woutB])
        dma('sp', gtab, bct_d[:, 1024:2048], [], [B_gtab])
        for n in range(NT):
            dma('sp', xst[n % 3], xv[n], [], [xstB[n % 3]])
            pb = 2 * (n % 2)
            for half in range(2):
                for c in range(8):
                    mm(bank[pb + half], yT[:, c, n * 128:(n + 1) * 128], wout[:, c, half * 512:(half + 1) * 512], c == 0, c == 7,
                       [yTb[c][n], woutB], [bankB[pb + half]])
            tt(xres[:, n, :], pp[n % 2][:], xst[n % 3], ALU.add, [bankB[pb], bankB[pb + 1], xstB[n % 3]], [xresB[n]])
            norm_stats(n, xres[:, n, :], xresB[n], 1)
        norm_rstd(1)
        for n in range(NT):
            norm_apply(n, xres[:, n, :], xresB[n], 1, 4 + n % 2)
        tap('xres', xres, [128, NT, D], xresB)

    if 5 in phases:
        S.barrier()
        A.reset(P4)
        hid = yT[:, 0:6, :]
        hidB = [Buf(f'hid{i}') for i in range(4)]
        wgu = [A.take([2, 8, 256], BF16) for _ in range(2)]
        wguB = [Buf(f'wgu{i}') for i in range(2)]
        wd = A.take([6, D], BF16)
        wdB = Buf('wd')
        gs = [A.take([514], F32) for _ in range(2)]
        gsB = [Buf(f'gs{i}') for i in range(2)]
        acc = [A.take([512], F32) for _ in range(2)]
        accB = [Buf(f'acc{i}') for i in range(2)]
        sl = [A.take([512], F32) for _ in range(2)]
        slB = [Buf(f'sl{i}') for i in range(2)]
        ost = [xst[0], xst[1]]
        ostB = [xstB[0], xstB[1]]
        dma('sp', gtab, bct_d[:, 2048:3072], [], [B_gtab])
        wg_v = wg_d.rearrange("(c p) n -> p c n", p=128)
        wu_v = wu_d.rearrange("(c p) n -> p c n", p=128)
        wd_v = wd_d.rearrange("(m p) n -> p m n", p=128)
        quarters = [(0, 6), (6, 6), (12, 5), (17, 5)]

        def load_wgu(m):
            wb_ = (m // 2) % 2
            dma('pool', wgu[wb_][:, 0, :, :], wg_v[:, :, m * 128:(m + 2) * 128], [], [wguB[wb_]])
            dma('pool', wgu[wb_][:, 1, :, :], wu_v[:, :, m * 128:(m + 2) * 128], [], [wguB[wb_]])
        load_wgu(0)
        it = 0
        for qi, (m0, nq) in enumerate(quarters):
            dma('pool', wd[:, 0:nq, :], wd_v[:, m0:m0 + nq, :], [], [wdB])
            for ml in range(nq):
                m = m0 + ml
                wb = (m // 2) % 2
                if m % 2 == 0 and m + 2 < NFF:
                    load_wgu(m + 2)
                mc = slice((m % 2) * 128, (m % 2) * 128 + 128)
                vf = V_FFN + 4 * m
                cw = lambda j: vecs[:, vf + j:vf + j + 1]
                for blk in range(4):
                    g_, gB = gs[blk % 2], gsB[blk % 2]
                    a_, aB = acc[it % 2], accB[it % 2]
                    s_, sB_ = sl[it % 2], slB[it % 2]
                    pg, pu = 2 * (it % 4), 2 * (it % 4) + 1
                    it += 1
                    rd = [hTb[4 * blk + i] for i in range(4)] + [wguB[wb]]
                    for c in range(8):
                        mm(bank[pg], wgu[wb][:, 0, c, mc], hT[:, c, 1 + blk * 512:1 + (blk + 1) * 512], c == 0, c == 7, rd, [bankB[pg]])
                    for c in range(8):
                        mm(bank[pu], wgu[wb][:, 1, c, mc], hT[:, c, 1 + blk * 512:1 + (blk + 1) * 512], c == 0, c == 7, rd, [bankB[pu]])
                    if blk == 0:
                        S.op('pool', lambda e, g_=g_: e.memset(g_[:, 0:2], 0.0), writes=[gB])
                    else:
                        gp = gs[(blk - 1) % 2]
                        S.op('pool', lambda e, g_=g_, gp=gp: e.tensor_copy(out=g_[:, 0:2], in_=gp[:, 512:514]), reads=[gsB[(blk - 1) % 2]], writes=[gB])
                    act(g_[:, 2:514], bank[pg], AF.Copy, [bankB[pg]], [gB])
                    act(a_, bank[pg], AF.Identity, [bankB[pg], B_const], [aB], bias=cw(3), scale=cw(2))
                    stt(a_, g_[:, 1:513], cw(1), a_, ALU.mult, ALU.add, [gB, aB, B_const], [aB])
                    stt(a_, g_[:, 0:512], cw(0), a_, ALU.mult, ALU.add, [gB, aB, B_const], [aB])
                    act(s_, a_, AF.Silu, [aB], [sB_])
                    tt(hid[:, ml, blk * 512:(blk + 1) * 512], s_, bank[pu], ALU.mult, [sB_, bankB[pu]], [hidB[blk]])
            last = qi == len(quarters) - 1
            for n in range(NT):
                pb = 2 * (n % 4)
                for half in range(2):
                    for ml in range(nq):
                        mm(bank[pb + half], hid[:, ml, n * 128:(n + 1) * 128], wd[:, ml, half * 512:(half + 1) * 512], ml == 0, ml == nq - 1,
                           [hidB[n // 4], wdB], [bankB[pb + half]])
                tt(xres[:, n, :], pp[n % 4][:], xres[:, n, :], ALU.add, [bankB[pb], bankB[pb + 1], xresB[n]], [xresB[n]])
                if last:
                    ssn = ss_all[:, 2, n:n + 1]
                    rsn = rstd_all[:, 2, n:n + 1]
                    sB2 = statB[n % 4]
                    act(sqj, xres[:, n, :], AF.Square, [xresB[n]], [sqjB, sB2], accum=ssn)
                    rsqrt_tiny(rsn, ssn, 1.0 / D, NORM_EPS, [sB2], [sB2])
                    stt(ost[n % 2], xres[:, n, :], rsn, gtab, ALU.mult, ALU.mult, [xresB[n], sB2, B_gtab], [ostB[n % 2]])
                    dma('sp', ov[n], ost[n % 2], [ostB[n % 2]], [])

    S.barrier(('sp',))
    S.emit(st)
    st.close()
    return nc, tap_out, S, A


def _chunkcols(v):
    v = np.asarray(v, np.float32).reshape(-1, 128)
    return np.ascontiguousarray(v.T)


def prep_shared(inp):
    f = lambda k: np.ascontiguousarray(np.asarray(inp[k], np.float32)[0])
    vecs = np.zeros((128, NV), np.float32)
    vecs[:, V_MUW:V_MUW + 8] = _chunkcols(f("rwkv_mu_w"))
    vecs[:, V_MUA:V_MUA + 8] = _chunkcols(f("rwkv_mu_a"))
    vecs[:, V_MUG:V_MUG + 8] = _chunkcols(f("rwkv_mu_g"))
    names = ["rwkv_mu_r", "rwkv_mu_k", "rwkv_mu_v", "rwkv_w0", "rwkv_a0", "rwkv_k_k", "rwkv_k_a", "rwkv_r_k"]
    for j, nm in enumerate(names):
        cc = _chunkcols(f(nm).reshape(-1))
        for p in range(4):
            vecs[:, V_PAIR + 8 * p + j] = cc[:, p]
    cw = f("ffn_conv_w").reshape(3, DFF)
    cbias = f("ffn_conv_b")
    for j in range(3):
        cc = _chunkcols(cw[j])
        for m in range(NFF):
            vecs[:, V_FFN + 4 * m + j] = cc[:, m]
    cc = _chunkcols(cbias)
    for m in range(NFF):
        vecs[:, V_FFN + 4 * m + 3] = cc[:, m]
    row = np.concatenate([f("norm_mix_g"), f("norm_ffn_g"), np.asarray(inp["norm_final_g"], np.float32),
                          f("rwkv_lnx_w"), f("rwkv_lnx_b"), f("ret_gn_w")])
    bct = np.ascontiguousarray(np.broadcast_to(row[None, :], (128, row.shape[0])))
    cf, cb = make_consts()
    shared = {
        "w_in": f("w_in"), "w_out": f("w_out"), "ffn_w_gate": f("ffn_w_gate"), "ffn_w_up": f("ffn_w_up"),
        "ffn_w_down": f("ffn_w_down"), "rwkv_w1": f("rwkv_w1"), "rwkv_a1": f("rwkv_a1"), "rwkv_g1": f("rwkv_g1"),
        "rwkv_w2": f("rwkv_w2"), "rwkv_a2": f("rwkv_a2"), "rwkv_g2": f("rwkv_g2"),
        "vecs": vecs, "bct": bct, "cf": cf, "cb": cb,
    }
    return shared


_PROG = None


def kernel(**inputs):
    global _PROG
    if _PROG is None:
        _PROG = build_program()[0]
    shared = prep_shared(inputs)
    xs = np.asarray(inputs["x"], np.float32)
    in_maps = [dict(shared, x=np.ascontiguousarray(xs[b])) for b in range(8)]
    res = run_bass_kernel_spmd(_PROG, in_maps, core_ids=list(range(8)))
    return np.stack([np.asarray(r["out"], np.float32) for r in res.results], axis=0)
```

```python
import numpy as np
import ml_dtypes
from contextlib import ExitStack
import concourse.bass as bass
import concourse.mybir as mybir
from concourse.bass_utils import run_bass_kernel_spmd

F32 = mybir.dt.float32
BF16 = mybir.dt.bfloat16
AF = mybir.ActivationFunctionType
ALU = mybir.AluOpType
AX = mybir.AxisListType

QUEUES = ('sp', 'act', 'pool', 'pe', 'dve')

T = 2048
D = 1024
NT = 16
DFF = 2816
NFF = 22
C0 = float(np.exp(-0.5))
NORM_EPS = 1e-6
RWKV_GN_EPS = 64e-5
RET_GN_EPS = 1e-5


class Buf:
    __slots__ = ('name', 'w', 'r')

    def __init__(self, name=''):
        self.name = name
        self.w = None
        self.r = {}


class _Op:
    __slots__ = ('q', 's', 'idx', 'fn', 'waits', 'inc', 'dma')


class Sched:
    def __init__(self, nc):
        self.nc = nc
        self.ops = {q: [] for q in QUEUES}
        self.streams = {}
        self.clock = {q: {} for q in QUEUES}
        self.opclock = {}
        self.nwaits = 0
        self.nops = 0

    def op(self, q, fn, reads=(), writes=(), dma=False):
        if dma:
            ref = writes[0] if len(writes) else (reads[0] if len(reads) else None)
            s = 'dq_' + (ref.name if ref is not None and ref.name else q)
        else:
            s = q
        deps = {}

        def need(st, i):
            if deps.get(st, 0) < i:
                deps[st] = i
        for b in reads:
            if b.w is not None:
                st, i = b.w
                if st == q and q == 'pe':
                    continue
                need(st, i)
        for b in writes:
            if b.w is not None:
                st, i = b.w
                if not (st == q and not dma):
                    need(st, i)
            for st, i in b.r.items():
                if st == q and not dma:
                    continue
                need(st, i)
        ck = self.clock[q]
        waits = []
        for st, i in deps.items():
            if ck.get(st, 0) >= i:
                continue
            waits.append((st, i))
            oc = self.opclock[(st, i)]
            for k, v in oc.items():
                if ck.get(k, 0) < v:
                    ck[k] = v
            if ck.get(st, 0) < i:
                ck[st] = i
            self.streams[st][i - 1].inc = True
        o = _Op()
        o.q = q
        o.s = s
        o.fn = fn
        o.waits = waits
        o.inc = dma
        o.dma = dma
        lst = self.streams.setdefault(s, [])
        lst.append(o)
        o.idx = len(lst)
        self.opclock[(s, o.idx)] = dict(ck)
        self.ops[q].append(o)
        self.nwaits += len(waits)
        self.nops += 1
        for b in writes:
            b.w = (s, o.idx)
            b.r = {}
        for b in reads:
            if b.r.get(s, 0) < o.idx:
                b.r[s] = o.idx
        return o

    def barrier(self, queues=QUEUES):
        tips = {s: len(l) for s, l in self.streams.items() if l}
        for q in queues:
            ck = self.clock[q]
            waits = []
            for s, i in tips.items():
                if s == q and q == 'pe':
                    continue
                if ck.get(s, 0) >= i:
                    continue
                waits.append((s, i))
                self.streams[s][i - 1].inc = True
            for s, i in waits:
                oc = self.opclock[(s, i)]
                for k, v in oc.items():
                    if ck.get(k, 0) < v:
                        ck[k] = v
                ck[s] = i
            if waits:
                o = _Op()
                o.q = q
                o.s = None
                o.fn = None
                o.waits = waits
                o.inc = False
                o.dma = False
                self.ops[q].append(o)

    def emit(self, stack):
        nc = self.nc
        sems = {s: stack.enter_context(nc.semaphore('sem_' + s)) for s in self.streams}
        cnt = {}
        for s, lst in self.streams.items():
            c = 0
            for o in lst:
                if o.dma:
                    c += 16
                elif o.inc:
                    c += 1
                cnt[(s, o.idx)] = c
        self.final_counts = {s: (cnt[(s, len(l))] if l else 0) for s, l in self.streams.items()}
        block = stack.enter_context(nc.Block())

        def run(q, eng):
            for o in self.ops[q]:
                for st, i in o.waits:
                    eng.wait_ge(sems[st], cnt[(st, i)])
                if o.fn is None:
                    continue
                ins = o.fn(eng)
                if o.dma:
                    ins.then_inc(sems[o.s], 16)
                elif o.inc:
                    ins.then_inc(sems[o.s], 1)

        @block.sync
        def _(e):
            run('sp', e)

        @block.scalar
        def _(e):
            run('act', e)

        @block.gpsimd
        def _(e):
            run('pool', e)

        @block.tensor
        def _(e):
            run('pe', e)

        @block.vector
        def _(e):
            run('dve', e)


class Arena:
    def __init__(self, ap, nbytes):
        self.ap = ap
        self.nbytes = nbytes
        self.off = 0
        self.peak = 0

    def take(self, shape, dt):
        esz = 4 if dt == F32 else 2
        n = int(np.prod(shape))
        nb = (n * esz + 63) // 64 * 64
        assert self.off + nb <= self.nbytes, ("arena overflow", self.off, nb, self.nbytes)
        v = self.ap[:, self.off // 4:(self.off + nb) // 4]
        if dt != F32:
            v = v.bitcast(dt)
        v = v[:, 0:n]
        if len(shape) == 2:
            v = v.rearrange("p (a b) -> p a b", a=shape[0])
        elif len(shape) == 3:
            v = v.rearrange("p (a b c) -> p a b c", a=shape[0], b=shape[1])
        elif len(shape) == 4:
            v = v.rearrange("p (a b c d) -> p a b c d", a=shape[0], b=shape[1], c=shape[2])
        self.off += nb
        self.peak = max(self.peak, self.off)
        return v

    def mark(self):
        return self.off

    def reset(self, m):
        self.off = m


V_MUW, V_MUA, V_MUG = 0, 8, 16
V_PAIR = 24
V_FFN = 56
NV = 56 + 4 * NFF
CB_ID, CB_MRET, CB_M4, CB_ML, CB_SEL, CB_ONES = 0, 128, 256, 768, 896, 900
NCB = 1028
CF_COS, CF_SIN, CF_NSIN, CF_XIT, CF_KAT, CF_KAPG, CF_GC = 0, 1024, 2048, 3072, 3584, 4096, 4100
NCF = 4104


def make_consts():
    f32 = np.float32
    p = np.arange(128)
    cf = np.zeros((128, NCF), f32)
    half = 64
    inv_freq = (10000.0 ** (-np.arange(half, dtype=np.float64) / half))
    pos = (np.arange(NT)[None, :] * 128 + p[:, None]).astype(np.float64)
    ang = pos[:, :, None] * inv_freq[None, None, :]
    cf[:, CF_COS:CF_COS + 1024] = np.cos(ang).reshape(128, -1)
    cf[:, CF_SIN:CF_SIN + 1024] = np.sin(ang).reshape(128, -1)
    cf[:, CF_NSIN:CF_NSIN + 1024] = -np.sin(ang).reshape(128, -1)
    lg = np.log(1.0 - 2.0 ** (-5.0 - np.arange(4, dtype=np.float64)))
    i = np.arange(128, dtype=np.float64)
    xi = np.exp((i[None, :] + 1.0) * lg[:, None])
    ka = np.exp(-(i[None, :] + 1.0) * lg[:, None]) * (128.0 ** -0.5)
    cf[:, CF_XIT:CF_XIT + 512] = np.broadcast_to(xi.reshape(1, 512), (128, 512))
    cf[:, CF_KAT:CF_KAT + 512] = np.broadcast_to(ka.reshape(1, 512), (128, 512))
    gC = np.exp(128.0 * lg)
    cf[:, CF_KAPG:CF_KAPG + 4] = (ka.T * gC[None, :])
    cf[:, CF_GC:CF_GC + 4] = gC[None, :]
    cb = np.zeros((128, NCB), f32)
    cb[:, CB_ID:CB_ID + 128] = np.eye(128)
    r = p[:, None]
    c = p[None, :]
    cb[:, CB_MRET:CB_MRET + 128] = (r <= c)
    strict = (r < c).astype(f32)
    incl = (r <= c).astype(f32)
    cb[:, CB_M4:CB_M4 + 512] = np.concatenate([strict, incl, strict, incl], axis=1)
    cb[:, CB_ML:CB_ML + 128] = (c < r)
    cb[0:64, CB_SEL] = 1.0
    cb[64:128, CB_SEL + 1] = 1.0
    cb[0:64, CB_ONES:CB_ONES + 64] = 1.0
    cb[64:128, CB_ONES + 64:CB_ONES + 128] = 1.0
    return cf, cb.astype(ml_dtypes.bfloat16)


DBG = {'ret_chunks': NT, 'ret_steps': 99}


def build_program(taps=None, phases=(1, 2, 3, 4, 5)):
    nc = bass.Bass("TRN2", target_bir_lowering=False)

    def din(name, shape, dt=F32):
        return nc.dram_tensor(name, list(shape), dt, kind="ExternalInput").ap()
    x = din("x", [T, D])
    w_in = din("w_in", [D, 3584])
    w_out = din("w_out", [D, D])
    wg_d = din("ffn_w_gate", [D, DFF])
    wu_d = din("ffn_w_up", [D, DFF])
    wd_d = din("ffn_w_down", [DFF, D])
    w1_d = din("rwkv_w1", [D, 64])
    a1_d = din("rwkv_a1", [D, 64])
    g1_d = din("rwkv_g1", [D, 128])
    w2_d = din("rwkv_w2", [64, 512])
    a2_d = din("rwkv_a2", [64, 512])
    g2_d = din("rwkv_g2", [128, 512])
    vecs_d = din("vecs", [128, NV])
    bct_d = din("bct", [128, 4608])
    cf_d = din("cf", [128, NCF])
    cb_d = din("cb", [128, NCB], BF16)
    out = nc.dram_tensor("out", [T, D], F32, kind="ExternalOutput").ap()
    tap_out = {}
    taps = taps or {}

    S = Sched(nc)
    st = ExitStack()
    ARENA_BYTES = 206 * 1024
    arena_t = st.enter_context(nc.sbuf_tensor("arena", [128, ARENA_BYTES // 4], F32))
    A = Arena(arena_t[:], ARENA_BYTES)
    pp = [st.enter_context(nc.psum_tensor(f"pp{i}", [128, 1024], F32)) for i in range(4)]
    bank = [pp[i // 2][:, (i % 2) * 512:(i % 2) * 512 + 512] for i in range(8)]
    bankB = [Buf(f"bank{i}") for i in range(8)]

    def bankbf(i):
        return bank[i].bitcast(BF16)

    def act(out_, in_, func, r, w, bias=None, scale=None, accum=None):
        kw = {}
        if bias is not None:
            kw['bias'] = bias
        if scale is not None:
            kw['scale'] = scale
        if accum is not None:
            kw['accum_out'] = accum
        S.op('act', lambda e: e.activation(out=out_, in_=in_, func=func, **kw), reads=r, writes=w)

    def tt(out_, a, b, op, r, w, q='dve'):
        S.op(q, lambda e: e.tensor_tensor(out=out_, in0=a, in1=b, op=op), reads=r, writes=w)

    def ts(out_, a, s1, s2, op0, op1, r, w, q='dve'):
        if s2 is None:
            S.op(q, lambda e: e.tensor_scalar(out=out_, in0=a, scalar1=s1, scalar2=None, op0=op0), reads=r, writes=w)
        else:
            S.op(q, lambda e: e.tensor_scalar(out=out_, in0=a, scalar1=s1, scalar2=s2, op0=op0, op1=op1), reads=r, writes=w)

    def stt(out_, a, s, b, op0, op1, r, w):
        S.op('dve', lambda e: e.scalar_tensor_tensor(out=out_, in0=a, scalar=s, in1=b, op0=op0, op1=op1), reads=r, writes=w)

    def mm(out_, lhsT, rhs, start, stop, r, w):
        S.op('pe', lambda e: e.matmul(out=out_, lhsT=lhsT, rhs=rhs, start=start, stop=stop), reads=r, writes=w)

    def mm2(out_, lhsT, rhs, start, stop, r, w):
        if lhsT.shape[0] == 128:
            mm(out_, lhsT[0:64], rhs[0:64], start, False, r, w)
            mm(out_, lhsT[64:128], rhs[64:128], False, stop, r, w)
        else:
            mm(out_, lhsT, rhs, start, stop, r, w)

    def dma(q, out_, in_, r, w, **kw):
        S.op(q, lambda e: e.dma_start(out=out_, in_=in_, **kw), reads=r, writes=w, dma=True)

    def cp(q, out_, in_, r, w):
        if q == 'act':
            act(out_, in_, AF.Copy, r, w)
        else:
            S.op(q, lambda e: e.tensor_copy(out=out_, in_=in_), reads=r, writes=w)

    def rsqrt_tiny(dst, src, scale, eps, r, w):
        ts(dst, src, scale, eps, ALU.mult, ALU.add, r, w)
        act(dst, dst, AF.Ln, w, w)
        act(dst, dst, AF.Exp, w, w, scale=-0.5)

    hT = A.take([8, T + 1], BF16)
    yT = A.take([8, T], BF16)
    cb = A.take([NCB], BF16)
    vecs = A.take([NV], F32)
    om = A.take([NV], F32)
    mhalf = A.take([4], F32)
    gtab = A.take([1024], F32)
    stat = A.take([64], F32)
    ss_all = A.take([3, NT], F32)
    rstd_all = A.take([3, NT], F32)
    B_const = Buf('const')
    B_gtab = Buf('gtab')
    hTb = [Buf(f'hT{n}') for n in range(NT)]
    yTb = [[Buf(f'yT{c}_{n}') for n in range(NT)] for c in range(8)]
    ident = cb[:, CB_ID:CB_ID + 128]
    PERSIST = A.mark()
    WOUT_OFF = ARENA_BYTES - 8 * D * 2
    wout = arena_t[:, WOUT_OFF // 4:ARENA_BYTES // 4].bitcast(BF16).rearrange("p (c n) -> p c n", c=8)
    woutB = Buf('wout')

    def tap(name, ap, shape, reads):
        if name in taps:
            d = nc.dram_tensor("tap_" + name, list(shape), ap.dtype, kind="ExternalOutput").ap()
            tap_out[name] = d
            dma('sp', d, ap, reads, [])

    dma('sp', cb, cb_d, [], [B_const])
    dma('sp', vecs, vecs_d, [], [B_const])
    dma('sp', gtab, bct_d[:, 0:1024], [], [B_gtab])
    S.op('pool', lambda e: e.memset(mhalf, -0.5), writes=[B_const])
    ts(om, vecs, -1.0, 1.0, ALU.mult, ALU.add, [B_const], [B_const])
    S.op('pool', lambda e: e.memset(hT[:, :, 0:1], 0.0), writes=[hTb[0]])

    xst = [A.take([D], F32) for _ in range(3)]
    xstB = [Buf(f'xst{i}') for i in range(3)]
    hb = [A.take([D], BF16) for _ in range(2)]
    hbB = [Buf(f'hb{i}') for i in range(2)]
    sqj = A.take([D], BF16)
    sqjB = Buf('sqj')
    statB = [Buf(f'stat{i}') for i in range(4)]
    NORM_END = A.mark()

    ssB = [Buf(f'ss{i}') for i in range(3)]
    rsB = [Buf(f'rs{i}') for i in range(3)]

    def norm_stats(n, src, srcB, which):
        act(sqj, src, AF.Square, [srcB], [sqjB, ssB[which]], accum=ss_all[:, which, n:n + 1])

    def norm_rstd(which, lo=0, hi=NT):
        rsqrt_tiny(rstd_all[:, which, lo:hi], ss_all[:, which, lo:hi], 1.0 / D, NORM_EPS, [ssB[which]], [rsB[which]])

    def norm_apply(n, src, srcB, which, pbank):
        h = hb[n % 2]
        stt(h, src, rstd_all[:, which, n:n + 1], gtab, ALU.mult, ALU.mult, [srcB, rsB[which], B_gtab], [hbB[n % 2]])
        pt = bankbf(pbank).rearrange("p (c t) -> p c t", c=8)
        for c in range(8):
            S.op('pe', lambda e, c=c: e.transpose(out=pt[:, c, :], in_=h[:, c * 128:(c + 1) * 128], identity=ident),
                 reads=[hbB[n % 2], B_const], writes=[bankB[pbank]])
        cp('act', hT[:, :, 1 + n * 128:1 + (n + 1) * 128], pt, [bankB[pbank]], [hTb[n]])

    xv = x.rearrange("(n p) d -> n p d", p=128)
    ov = out.rearrange("(n p) d -> n p d", p=128)
    if 2 in phases:
        cf = A.take([NCF], F32)
        dma('sp', cf, cf_d, [], [B_const])
        wret = A.take([8, 2048], BF16)
        wretB = Buf('wret')
        wv = w_in.rearrange("(c p) n -> p c n", p=128)
        for c in range(8):
            dma('pool', wret[:, c, :], wv[:, c, 1536:3584], [], [wretB])
        P2START = A.mark()
    XC_OFF = 168 * 1024
    assert XC_OFF + 8 * D * 4 <= ARENA_BYTES
    xc = arena_t[:, XC_OFF // 4:XC_OFF // 4 + 8 * D].rearrange("p (s d) -> p s d", s=8)
    xcB = [Buf(f'xc{i}') for i in range(8)]
    for lo_ in (0, 8):
        for n in range(lo_, lo_ + 8):
            dma('sp', xc[:, n % 8, :], xv[n], [], [xcB[n % 8]])
            norm_stats(n, xc[:, n % 8, :], xcB[n % 8], 0)
        norm_rstd(0, lo_, lo_ + 8)
        for n in range(lo_, lo_ + 8):
            norm_apply(n, xc[:, n % 8, :], xcB[n % 8], 0, n % 2)
    tap('hT', hT, [128, 8, T + 1], hTb)

    if 2 in phases:
        A.reset(P2START)
        gnw = A.take([512], F32)
        dma('sp', gnw, bct_d[:, 4096:4608], [], [B_const])
        qa = A.take([512], F32)
        qb = A.take([512], F32)
        qrot = A.take([512], BF16)
        krot = A.take([512], BF16)
        qT = A.take([4, 128], BF16)
        kT = A.take([4, 128], BF16)
        PT = A.take([4, 128], BF16)
        Vb = A.take([512], BF16)
        Vk = A.take([512], BF16)
        R = A.take([512], F32)
        Rt = A.take([512], F32)
        Rb = A.take([512], BF16)
        sqy = A.take([512], F32)
        yn = A.take([512], F32)
        sgt = A.take([512], F32)
        yo = A.take([512], BF16)
        rst = A.take([32], F32)
        assert A.off <= 168 * 1024, ('retention buffers overlap the x cache', A.off)
        Bq = {k: Buf('r_' + k) for k in ['qa', 'qb', 'qrot', 'krot', 'qT', 'kT', 'PT', 'Vb', 'Vk', 'R', 'Rt', 'Rb', 'sqy', 'yn', 'sgt', 'yo', 'rst']}
        kapg_bc = cf[:, CF_KAPG:CF_KAPG + 4].unsqueeze(2).to_broadcast([128, 4, 128])
        gC_bc = cf[:, CF_GC:CF_GC + 4].unsqueeze(2).to_broadcast([128, 4, 128])
        xiT = cf[:, CF_XIT:CF_XIT + 512].rearrange("p (h t) -> p h t", h=4)
        kaT = cf[:, CF_KAT:CF_KAT + 512].rearrange("p (h t) -> p h t", h=4)
        mret_bc = cb[:, CB_MRET:CB_MRET + 128].unsqueeze(1).to_broadcast([128, 4, 128])
        PQ, PK, PV, PG, PTB, PS, PY, PKV = range(8)

        def v4(ap):
            return ap.rearrange("p (h e) -> p h e", h=4)

        def rot(ps, psB, dst, dstB, n):
            cosb = cf[:, CF_COS + n * 64:CF_COS + (n + 1) * 64].unsqueeze(1).unsqueeze(1).to_broadcast([128, 4, 2, 64])
            sinb = cf[:, CF_SIN + n * 64:CF_SIN + (n + 1) * 64].unsqueeze(1).to_broadcast([128, 4, 64])
            nsinb = cf[:, CF_NSIN + n * 64:CF_NSIN + (n + 1) * 64].unsqueeze(1).to_broadcast([128, 4, 64])
            p4 = ps.rearrange("p (h two f) -> p h two f", h=4, two=2)
            tt(qa.rearrange("p (h two f) -> p h two f", h=4, two=2), p4, cosb, ALU.mult, [psB, B_const], [Bq['qa']])
            qb4 = qb.rearrange("p (h two f) -> p h two f", h=4, two=2)
            tt(qb4[:, :, 0, :], p4[:, :, 1, :], nsinb, ALU.mult, [psB, B_const], [Bq['qb']])
            tt(qb4[:, :, 1, :], p4[:, :, 0, :], sinb, ALU.mult, [psB, B_const], [Bq['qb']])
            tt(dst, qa, qb, ALU.add, [Bq['qa'], Bq['qb']], [dstB])

        for n in range(DBG['ret_chunks']):
            RS = DBG['ret_steps']
            def proj(nn):
                tok = slice(1 + nn * 128, 1 + (nn + 1) * 128)
                for j, pb in enumerate((PQ, PK, PV, PG)):
                    for c in range(8):
                        mm(bank[pb], hT[:, c, tok], wret[:, c, j * 512:(j + 1) * 512], c == 0, c == 7,
                           [hTb[nn], wretB], [bankB[pb]])
            if n == 0:
                proj(0)
            act(sgt, bank[PG], AF.Silu, [bankB[PG]], [Bq['sgt']])
            rot(bank[PQ], bankB[PQ], qrot, Bq['qrot'], n)
            rot(bank[PK], bankB[PK], krot, Bq['krot'], n)
            if RS < 3:
                continue
            ptb = bankbf(PTB).rearrange("p (c t) -> p c t", c=8)
            for h in range(4):
                S.op('pe', lambda e, h=h: e.transpose(out=ptb[:, h, :], in_=qrot[:, h * 128:(h + 1) * 128], identity=ident),
                     reads=[Bq['qrot'], B_const], writes=[bankB[PTB]])
            for h in range(4):
                S.op('pe', lambda e, h=h: e.transpose(out=ptb[:, 4 + h, :], in_=krot[:, h * 128:(h + 1) * 128], identity=ident),
                     reads=[Bq['krot'], B_const], writes=[bankB[PTB]])
            tt(qT, ptb[:, 0:4, :], xiT, ALU.mult, [bankB[PTB], B_const], [Bq['qT']])
            tt(kT, ptb[:, 4:8, :], kaT, ALU.mult, [bankB[PTB], B_const], [Bq['kT']])
            if RS < 4:
                continue
            ps4 = v4(bank[PS])
            for h in range(4):
                mm(ps4[:, h, :], kT[:, h, :], qT[:, h, :], True, True, [Bq['kT'], Bq['qT']], [bankB[PS]])
            tt(PT, ps4, mret_bc, ALU.mult, [bankB[PS], B_const], [Bq['PT']])
            if RS < 5:
                continue
            cp('act', Vb, bank[PV], [bankB[PV]], [Bq['Vb']])
            tt(v4(Vk), v4(bank[PV]), kapg_bc, ALU.mult, [bankB[PV], B_const], [Bq['Vk']])
            if RS < 6:
                continue
            py4 = v4(bank[PY])
            for h in range(4):
                mm(py4[:, h, :], PT[:, h, :], Vb[:, h * 128:(h + 1) * 128], True, n == 0, [Bq['PT'], Bq['Vb']], [bankB[PY]])
                if n > 0:
                    mm(py4[:, h, :], qT[:, h, :], Rb[:, h * 128:(h + 1) * 128], False, True, [Bq['qT'], Bq['Rb']], [bankB[PY]])
            if RS < 7:
                continue
            if n < NT - DBG.get('skiplast', 0):
                pkv4 = v4(bank[PKV])
                for h in range(4):
                    mm(pkv4[:, h, :], krot[:, h * 128:(h + 1) * 128], Vk[:, h * 128:(h + 1) * 128], True, True,
                       [Bq['krot'], Bq['Vk']], [bankB[PKV]])
                if n == 0:
                    cp('dve', R, bank[PKV], [bankB[PKV]], [Bq['R']])
                else:
                    tt(v4(Rt), v4(R), gC_bc, ALU.mult, [Bq['R'], B_const], [Bq['Rt']])
                    tt(R, Rt, bank[PKV], ALU.add, [Bq['Rt'], bankB[PKV]], [Bq['R']])
                cp('pool', Rb, R, [Bq['R']], [Bq['Rb']])
            if n + 1 < NT:
                proj(n + 1)
            s1 = rst[:, 0:4]
            s2 = rst[:, 4:8]
            mean = rst[:, 8:12]
            msq = rst[:, 12:16]
            rstd = rst[:, 16:20]
            S.op('dve', lambda e: e.tensor_reduce(out=s1, in_=py4, axis=AX.X, op=ALU.add), reads=[bankB[PY]], writes=[Bq['rst']])
            act(sqy, bank[PY], AF.Square, [bankB[PY]], [Bq['sqy']])
            S.op('dve', lambda e: e.tensor_reduce(out=s2, in_=v4(sqy), axis=AX.X, op=ALU.add), reads=[Bq['sqy']], writes=[Bq['rst']])
            ts(mean, s1, 1.0 / 128, None, ALU.mult, None, [Bq['rst']], [Bq['rst']])
            tt(msq, mean, mean, ALU.mult, [Bq['rst']], [Bq['rst']])
            stt(rstd, s2, 1.0 / 128, msq, ALU.mult, ALU.subtract, [Bq['rst']], [Bq['rst']])
            rsqrt_tiny(rstd, rstd, 1.0, RET_GN_EPS, [Bq['rst']], [Bq['rst']])
            tt(v4(yn), py4, mean.unsqueeze(2).to_broadcast([128, 4, 128]), ALU.subtract, [bankB[PY], Bq['rst']], [Bq['yn']])
            tt(v4(yn), v4(yn), rstd.unsqueeze(2).to_broadcast([128, 4, 128]), ALU.mult, [Bq['yn'], Bq['rst']], [Bq['yn']])
            tt(yn, yn, gnw, ALU.mult, [Bq['yn'], B_const], [Bq['yn']])
            tt(yo, yn, sgt, ALU.mult, [Bq['yn'], Bq['sgt']], [Bq['yo']])
            if RS < 9:
                continue
            for h in range(4):
                S.op('pe', lambda e, h=h: e.transpose(out=ptb[:, h, :], in_=yo[:, h * 128:(h + 1) * 128], identity=ident),
                     reads=[Bq['yo'], B_const], writes=[bankB[PTB]])
            cp('act', yT[:, 4:8, n * 128:(n + 1) * 128], ptb[:, 0:4, :], [bankB[PTB]], [yTb[4 + h][n] for h in range(4)])
        if 3 not in phases:
            tap('yT', yT, [128, 8, T], [b for l in yTb for b in l])
        S.barrier()


    if 3 in phases:
        S.barrier()
        A.reset(PERSIST)
        wl_f = A.take([8, 128], F32)
        gl_f = A.take([8, 128], F32)
        W1A = A.take([8, 128], BF16)
        W1B = A.take([8, 128], BF16)
        G1A = A.take([8, 128], BF16)
        G1B = A.take([8, 128], BF16)
        W2sb = A.take([512], BF16)
        A2sb = A.take([512], BF16)
        G2sb = A.take([512], BF16)
        L1 = A.take([T], BF16)
        L1g = A.take([T], BF16)
        lnxw = A.take([512], F32)
        lnxb = A.take([512], F32)
        B_lw = Buf('loraw')
        L1B = [Buf(f'L1_{i}') for i in range(4)]
        dma('sp', wl_f[:, :, 0:64], w1_d.rearrange("(c p) k -> p c k", p=128), [], [B_lw])
        dma('sp', wl_f[:, :, 64:128], a1_d.rearrange("(c p) k -> p c k", p=128), [], [B_lw])
        dma('sp', gl_f, g1_d.rearrange("(c p) k -> p c k", p=128), [], [B_lw])
        dma('sp', lnxw, bct_d[:, 3072:3584], [], [B_lw])
        dma('sp', lnxb, bct_d[:, 3584:4096], [], [B_lw])
        S.op('pool', lambda e: e.memset(W2sb, 0.0), writes=[B_lw])
        S.op('pool', lambda e: e.memset(A2sb, 0.0), writes=[B_lw])
        dma('pool', W2sb[0:64, :], w2_d, [B_lw], [B_lw])
        dma('pool', A2sb[64:128, :], a2_d, [B_lw], [B_lw])
        dma('pool', G2sb, g2_d, [], [B_lw])

        def vb(tab, col, k):
            return tab[:, col:col + 8].unsqueeze(2).to_broadcast([128, 8, k])
        tt(W1A[:, :, 0:64], wl_f[:, :, 0:64], vb(om, V_MUW, 64), ALU.mult, [B_lw, B_const], [B_lw])
        tt(W1A[:, :, 64:128], wl_f[:, :, 64:128], vb(om, V_MUA, 64), ALU.mult, [B_lw, B_const], [B_lw])
        tt(W1B[:, :, 0:64], wl_f[:, :, 0:64], vb(vecs, V_MUW, 64), ALU.mult, [B_lw, B_const], [B_lw])
        tt(W1B[:, :, 64:128], wl_f[:, :, 64:128], vb(vecs, V_MUA, 64), ALU.mult, [B_lw, B_const], [B_lw])
        tt(G1A, gl_f, vb(om, V_MUG, 128), ALU.mult, [B_lw, B_const], [B_lw])
        tt(G1B, gl_f, vb(vecs, V_MUG, 128), ALU.mult, [B_lw, B_const], [B_lw])
        for tb in range(4):
            rd = [hTb[4 * tb + i] for i in range(4)] + ([hTb[4 * tb - 1]] if tb > 0 else []) + [B_lw]
            for (WA, WB, pb) in ((W1A, W1B, 0), (G1A, G1B, 1)):
                for c in range(8):
                    mm(bank[pb], WA[:, c, :], hT[:, c, 1 + tb * 512:1 + (tb + 1) * 512], c == 0, False, rd, [bankB[pb]])
                    mm(bank[pb], WB[:, c, :], hT[:, c, tb * 512:(tb + 1) * 512], False, c == 7, rd, [bankB[pb]])
            blk = slice(tb * 512, (tb + 1) * 512)
            act(L1[0:64, blk], bank[0][0:64, :], AF.Tanh, [bankB[0]], [L1B[tb]])
            act(L1[64:128, blk], bank[0][64:128, :], AF.Copy, [bankB[0]], [L1B[tb]])
            act(L1g[:, blk], bank[1], AF.Sigmoid, [bankB[1]], [L1B[tb]])

        wrkv = A.take([8, 3, 128], BF16)
        AR = A.take([NT, 2, 128], BF16)
        BT = A.take([T], BF16)
        KT = A.take([T], BF16)
        vT = A.take([T], BF16)
        rkrT = A.take([T], BF16)
        rm = A.take([513], F32)
        km = A.take([513], F32)
        vm = A.take([513], F32)
        tnames = ['r', 'k0', 'sg', 'asg', 'cum', 'P', 'invP', 'Pp', 'ssk', 'kk', 't1']
        tmp = {k: A.take([512], F32) for k in tnames}
        sqk = A.take([512], BF16)
        PCt = A.take([NT], F32)
        Xb = [A.take([2, 2, 128], BF16) for _ in range(2)]
        Nn = [A.take([2, 128], BF16) for _ in range(2)]
        W1s = [A.take([2, 3, 128], BF16) for _ in range(2)]
        TTs = [A.take([2, 128], BF16) for _ in range(2)]
        BK = [A.take([2, 128], BF16) for _ in range(2)]
        Vt = [A.take([4, 128], BF16) for _ in range(2)]
        Xs = A.take([128], BF16)
        Us = A.take([128], BF16)
        Hs = A.take([64], F32)
        HP = A.take([64], F32)
        Hbz = A.take([2, 64], BF16)
        BKz = [A.take([2, 2, 128], BF16) for _ in range(2)]
        Yp = [A.take([4, 128], F32) for _ in range(2)]
        sqp = A.take([512], F32)
        ynp = A.take([512], F32)
        bon = A.take([512], F32)
        sB = A.take([8], F32)
        yop = A.take([4, 128], BF16)
        rstp = A.take([64], F32)
        Bw = Buf('wrkv')
        Bt_ = {k: Buf('t_' + k) for k in tnames + ['rm', 'km', 'vm', 'sqk', 'PCt']}
        ARb = [Buf(f'AR{n}') for n in range(NT)]
        BTb = [Buf(f'BT{i}') for i in range(4)]
        KTb = [Buf(f'KT{i}') for i in range(4)]
        vTb = [Buf(f'vT{i}') for i in range(4)]
        rkb = [Buf(f'rk{i}') for i in range(4)]
        Bs = {k: Buf('s_' + k) for k in ['X0', 'X1', 'N0', 'N1', 'W10', 'W11', 'TT0', 'TT1', 'BK0', 'BK1', 'Vt0', 'Vt1', 'Xs', 'Us', 'H', 'HP', 'Hb', 'Yp0', 'Yp1', 'BKz0', 'BKz1',
                                          'sqp', 'ynp', 'bon', 'sB', 'yop', 'rstp',
                                          'ps1', 'ps2', 'psN', 'psL', 'pT', 'pT2', 'psX', 'psU', 'psH', 'psY', 'psB', 'psG']}
        for k_, b_ in (('ps1', 0), ('ps2', 2), ('psN', 2), ('psL', 3), ('pT', 4), ('pT2', 4), ('psX', 5), ('psU', 5), ('psH', 5),
                       ('psY', 6), ('psB', 6), ('psG', 7)):
            Bs[k_] = bankB[b_]
        wv3 = w_in.rearrange("(c p) n -> p c n", p=128)
        m4 = cb[:, CB_M4:CB_M4 + 512]
        mS_bc = cb[:, CB_M4:CB_M4 + 128].unsqueeze(1).to_broadcast([128, 2, 128])
        m3_bc = cb[:, CB_M4 + 128:CB_M4 + 512].unsqueeze(1).to_broadcast([128, 2, 384])
        mL_bc = cb[:, CB_ML:CB_ML + 128].unsqueeze(1).to_broadcast([128, 2, 128])
        id_bc = ident.unsqueeze(1).to_broadcast([128, 2, 128])
        sel = cb[:, CB_SEL:CB_SEL + 2]
        ones_bd = cb[:, CB_ONES:CB_ONES + 128]
        ps1 = pp[0][:].rearrange("p (h c) -> p h c", h=2)
        ps2 = bank[2][:, 0:256].rearrange("p (h s) -> p h s", h=2)
        psN = bank[2][:, 256:512].rearrange("p (h s) -> p h s", h=2)
        psL = bank[3].rearrange("p (h c) -> p h c", h=2)
        pTb = bankbf(4)
        pT3 = pTb[:, 0:384].rearrange("p (j t) -> p j t", j=3)
        pT2 = pTb[:, 512:1024].rearrange("p (j t) -> p j t", j=4)
        psX = bank[5][:, 0:128]
        psU = bank[5][:, 128:256]
        psH = bank[5][:, 256:384]
        psY = bank[6][:, 0:128]
        psB = bank[6][:, 128:136]
        psG = bank[7]

        def pair_setup(p):
            vp = V_PAIR + 8 * p
            col = lambda j: vecs[:, vp + j:vp + j + 1]
            ocol = lambda j: om[:, vp + j:vp + j + 1]
            return col, ocol

        def prep_block(p, tb):
            col, ocol = pair_setup(p)
            if tb == 0:
                for j in range(3):
                    dma('pool', wrkv[:, :, j, :], wv3[:, :, j * 512 + p * 128:j * 512 + (p + 1) * 128], [], [Bw])
                for nm_ in ('rm', 'km', 'vm'):
                    tl = {'rm': rm, 'km': km, 'vm': vm}[nm_]
                    S.op('pool', lambda e, tl=tl: e.memset(tl[:, 0:1], 0.0), writes=[Bt_[nm_]])
            blk = slice(tb * 512, (tb + 1) * 512)
            rd = [hTb[4 * tb + i] for i in range(4)] + [Bw]
            for j in range(3):
                for c in range(8):
                    mm(bank[j], wrkv[:, c, j, :], hT[:, c, 1 + tb * 512:1 + (tb + 1) * 512], c == 0, c == 7, rd, [bankB[j]])
            mm(bank[3], W2sb[:, p * 128:(p + 1) * 128], L1[:, blk], True, True, [B_lw, L1B[tb]], [bankB[3]])
            mm(bank[4], A2sb[:, p * 128:(p + 1) * 128], L1[:, blk], True, True, [B_lw, L1B[tb]], [bankB[4]])
            for j, (tl, nm_, dst, dstB) in enumerate(((rm, 'rm', tmp['r'], Bt_['r']), (km, 'km', tmp['k0'], Bt_['k0']), (vm, 'vm', vT[:, blk], vTb[tb]))):
                act(tl[:, 1:513], bank[j], AF.Copy, [bankB[j], B_const], [Bt_[nm_]], scale=col(j))
                stt(dst, bank[j], ocol(j), tl[:, 0:512], ALU.mult, ALU.add, [bankB[j], Bt_[nm_], B_const], [dstB])
                S.op('pool', lambda e, tl=tl: e.tensor_copy(out=tl[:, 0:1], in_=tl[:, 512:513]), reads=[Bt_[nm_]], writes=[Bt_[nm_]])
            r_, k0 = tmp['r'], tmp['k0']
            act(tmp['sg'], bank[3], AF.Sigmoid, [bankB[3], B_const], [Bt_['sg']], bias=col(3))
            act(tmp['asg'], bank[4], AF.Sigmoid, [bankB[4], B_const], [Bt_['asg']], bias=col(4))
            for ch in range(4):
                cs = slice(ch * 128, (ch + 1) * 128)
                S.op('dve', lambda e, cs=cs: e.tensor_tensor_scan(out=tmp['cum'][:, cs], data0=tmp['sg'][:, cs], data1=tmp['sg'][:, cs],
                                                                   initial=0.0, op0=ALU.add, op1=ALU.bypass),
                     reads=[Bt_['sg']], writes=[Bt_['cum']])
            act(tmp['P'], tmp['cum'], AF.Exp, [Bt_['cum']], [Bt_['P']], scale=-C0)
            act(tmp['invP'], tmp['cum'], AF.Exp, [Bt_['cum']], [Bt_['invP']], scale=C0)
            tt(tmp['sg'], tmp['cum'], tmp['sg'], ALU.subtract, [Bt_['cum'], Bt_['sg']], [Bt_['sg']])
            act(tmp['Pp'], tmp['sg'], AF.Exp, [Bt_['sg']], [Bt_['Pp']], scale=-C0)
            S.op('pool', lambda e, tb=tb: e.tensor_copy(out=PCt[:, tb * 4:(tb + 1) * 4],
                                                        in_=tmp['P'].rearrange("p (c t) -> p c t", c=4)[:, :, 127]),
                 reads=[Bt_['P']], writes=[Bt_['PCt']])
            act(sqk, k0, AF.Square, [Bt_['k0'], B_const], [Bt_['sqk']], scale=col(5))
            mm(bank[5], ones_bd, sqk, True, True, [Bt_['sqk'], B_const], [bankB[5]])
            act(tmp['ssk'], bank[5], AF.Ln, [bankB[5]], [Bt_['ssk']])
            act(tmp['ssk'], tmp['ssk'], AF.Exp, [Bt_['ssk']], [Bt_['ssk']], scale=-0.5)
            stt(tmp['kk'], k0, col(5), tmp['ssk'], ALU.mult, ALU.mult, [Bt_['k0'], Bt_['ssk'], B_const], [Bt_['kk']])
            ts(tmp['t1'], tmp['asg'], col(6), ocol(6), ALU.mult, ALU.add, [Bt_['asg'], B_const], [Bt_['t1']])
            tt(tmp['t1'], tmp['t1'], k0, ALU.mult, [Bt_['t1'], Bt_['k0']], [Bt_['t1']])
            arv = AR[:, 4 * tb:4 * tb + 4, :, :]
            c4 = lambda a: a.rearrange("p (c t) -> p c t", c=4)
            stt(arv[:, :, 0, :], c4(tmp['kk']), -1.0, c4(tmp['Pp']), ALU.mult, ALU.mult, [Bt_['kk'], Bt_['Pp']], [ARb[4 * tb + i] for i in range(4)])
            tt(arv[:, :, 1, :], c4(r_), c4(tmp['P']), ALU.mult, [Bt_['r'], Bt_['P']], [ARb[4 * tb + i] for i in range(4)])
            tt(tmp['kk'], tmp['kk'], tmp['asg'], ALU.mult, [Bt_['kk'], Bt_['asg']], [Bt_['kk']])
            tt(BT[:, blk], tmp['kk'], tmp['invP'], ALU.mult, [Bt_['kk'], Bt_['invP']], [BTb[tb]])
            tt(KT[:, blk], tmp['t1'], tmp['invP'], ALU.mult, [Bt_['t1'], Bt_['invP']], [KTb[tb]])
            stt(rkrT[:, blk], r_, col(7), tmp['t1'], ALU.mult, ALU.mult, [Bt_['r'], Bt_['t1'], B_const], [rkb[tb]])

        def make_scan(p):
            col, ocol = pair_setup(p)
            def local(n):
                cs = slice(n * 128, (n + 1) * 128)
                tb = n // 4
                bz = BKz[n % 2]
                W1, TT, W1B, TTB = W1s[n % 2], TTs[n % 2], Bs[f'W1{n % 2}'], Bs[f'TT{n % 2}']
                bzB = Bs[f'BKz{n % 2}']
                for h in range(2):
                    hp = slice(64 * h, 64 * h + 64)
                    S.op('pool', lambda e, h=h, hp=hp: e.tensor_copy(out=bz[hp, 0, h, :], in_=BT[hp, cs]), reads=[BTb[tb]], writes=[bzB])
                    S.op('pool', lambda e, h=h, hp=hp: e.tensor_copy(out=bz[hp, 1, h, :], in_=KT[hp, cs]), reads=[KTb[tb]], writes=[bzB])
                for h in range(2):
                    mm(ps1[:, h, 0:256], bz[:, 0, h, :], AR[:, n, :, :], True, True, [bzB, ARb[n]], [Bs['ps1']])
                    mm(ps1[:, h, 256:512], bz[:, 1, h, :], AR[:, n, :, :], True, True, [bzB, ARb[n]], [Bs['ps1']])
                    mm(ps2[:, h, :], AR[:, n, 0, :], bz[:, 0, h, :], True, True, [bzB, ARb[n]], [Bs['ps2']])
                tt(Xb[0][:, :, 0, :], ps1[:, :, 0:128], mS_bc, ALU.mult, [Bs['ps1'], B_const], [Bs['X0']])
                tt(Nn[0], ps2, mL_bc, ALU.mult, [Bs['ps2'], B_const], [Bs['N0']])
                tt(Xb[1][:, :, 1, :], Xb[0][:, :, 0, :], id_bc, ALU.add, [Bs['X0'], B_const], [Bs['X1']])
                tt(W1, ps1[:, :, 128:512], m3_bc, ALU.mult, [Bs['ps1'], B_const], [W1B])
                yield
                cur = 0
                for k in range(4):
                    nx = 1 - cur
                    lastk = (k == 3)
                    for h in range(2):
                        if lastk:
                            mm(psL[:, h, 128:256], Nn[cur][:, h, :], Xb[cur][:, h, 1, :], True, True, [Bs[f'N{cur}'], Bs[f'X{cur}']], [Bs['psL']])
                        elif k == 0:
                            mm(psL[:, h, 0:128], Nn[cur][:, h, :], Xb[cur][:, h, 0, :], True, True, [Bs[f'N{cur}'], Bs[f'X{cur}']], [Bs['psL']])
                            mm(psN[:, h, :], Xb[cur][:, h, 0, :], Nn[cur][:, h, :], True, True, [Bs[f'N{cur}'], Bs[f'X{cur}']], [Bs['psN']])
                        else:
                            mm(psL[:, h, :], Nn[cur][:, h, :], Xb[cur][:, h, :, :], True, True, [Bs[f'N{cur}'], Bs[f'X{cur}']], [Bs['psL']])
                            mm(psN[:, h, :], Xb[cur][:, h, 0, :], Nn[cur][:, h, :], True, True, [Bs[f'N{cur}'], Bs[f'X{cur}']], [Bs['psN']])
                    if lastk:
                        tt(TT, Xb[cur][:, :, 1, :], psL[:, :, 128:256], ALU.add, [Bs['psL'], Bs[f'X{cur}']], [TTB])
                    else:
                        cp('act', Xb[nx][:, :, 0, :], psL[:, :, 0:128], [Bs['psL']], [Bs[f'X{nx}']])
                        cp('dve', Nn[nx], psN, [Bs['psN']], [Bs[f'N{nx}']])
                        if k > 0:
                            tt(Xb[nx][:, :, 1, :], Xb[cur][:, :, 1, :], psL[:, :, 128:256], ALU.add, [Bs['psL'], Bs[f'X{cur}']], [Bs[f'X{nx}']])
                    cur = nx
                    yield

            def chain(n):
                cs = slice(n * 128, (n + 1) * 128)
                tb = n // 4
                g = (n // 4) % 2
                bk = BK[n % 2]
                bkB = Bs[f'BK{n % 2}']
                vt = Vt[g][:, n % 4, :]
                vtB = Bs[f'Vt{g}']
                W1, TT, W1B, TTB = W1s[n % 2], TTs[n % 2], Bs[f'W1{n % 2}'], Bs[f'TT{n % 2}']
                S.op('pe', lambda e: e.transpose(out=pT3[:, 0, :], in_=vT[:, cs], identity=ident), reads=[vTb[tb], B_const], writes=[Bs['pT']])
                S.op('pe', lambda e: e.transpose(out=pT3[:, 1, :], in_=BT[:, cs], identity=ident), reads=[BTb[tb], B_const], writes=[Bs['pT']])
                S.op('pe', lambda e: e.transpose(out=pT3[:, 2, :], in_=KT[:, cs], identity=ident), reads=[KTb[tb], B_const], writes=[Bs['pT']])
                cp('act', vt, pT3[:, 0, :], [Bs['pT']], [vtB])
                cp('act', bk, pT3[:, 1:3, :], [Bs['pT']], [bkB])
                yield
                for h in range(2):
                    hs = slice(64 * h, 64 * h + 64)
                    if n > 0:
                        mm(psX[:, hs], AR[:, n, 0, :], Hbz[:, h, :], True, False, [ARb[n], Bs['Hb']], [Bs['psX']])
                    mm(psX[:, hs], W1[:, h, 1, :], vt[:, hs], n == 0, True, [W1B, vtB], [Bs['psX']])
                cp('act', Xs, psX, [Bs['psX']], [Bs['Xs']])
                yield
                for h in range(2):
                    hs = slice(64 * h, 64 * h + 64)
                    mm(psU[:, hs], TT[:, h, :], Xs[:, hs], True, True, [TTB, Bs['Xs']], [Bs['psU']])
                cp('dve', Us, psU, [Bs['psU']], [Bs['Us']])
                yield
                for h in range(2):
                    hs = slice(64 * h, 64 * h + 64)
                    if n > 0:
                        mm(psY[:, hs], AR[:, n, 1, :], Hbz[:, h, :], True, False, [ARb[n], Bs['Hb']], [Bs['psY']])
                    mm(psY[:, hs], W1[:, h, 0, :], Us[:, hs], n == 0, False, [W1B, Bs['Us']], [Bs['psY']])
                    mm(psY[:, hs], W1[:, h, 2, :], vt[:, hs], False, True, [W1B, vtB], [Bs['psY']])
                mm(psH, bk[:, 0, :], Us, True, False, [bkB, Bs['Us']], [Bs['psH']])
                mm(psH, bk[:, 1, :], vt, False, True, [bkB, vtB], [Bs['psH']])
                cp('act', Yp[g][:, n % 4, :], psY, [Bs['psY']], [Bs[f'Yp{g}']])
                if n > 0:
                    ts(HP, Hs, PCt[:, n:n + 1], None, ALU.mult, None, [Bs['H'], Bt_['PCt']], [Bs['HP']])
                for h in range(2):
                    hp = slice(64 * h, 64 * h + 64)
                    hs = slice(64 * h, 64 * h + 64)
                    if n > 0:
                        stt(Hs[hp, :], psH[hp, hs], PCt[hp, n:n + 1], HP[hp, :], ALU.mult, ALU.add, [Bs['psH'], Bs['HP'], Bt_['PCt']], [Bs['H']])
                    else:
                        ts(Hs[hp, :], psH[hp, hs], PCt[hp, n:n + 1], None, ALU.mult, None, [Bs['psH'], Bt_['PCt']], [Bs['H']])
                for h in range(2):
                    hp = slice(64 * h, 64 * h + 64)
                    cp('act', Hbz[hp, h, :], Hs[hp, :], [Bs['H']], [Bs['Hb']])
                yield

            def post(tg):
                g = tg % 2
                y3 = Yp[g].rearrange("p j (h e) -> p (j h) e", h=2)
                yB = Bs[f'Yp{g}']
                s1, s2, mean, msq, rstd = (rstp[:, 8 * i:8 * i + 8] for i in range(5))
                v8 = lambda a: a.rearrange("p (j e) -> p j e", j=8)
                S.op('dve', lambda e: e.tensor_reduce(out=s1, in_=y3, axis=AX.X, op=ALU.add), reads=[yB], writes=[Bs['rstp']])
                act(sqp, Yp[g].rearrange("p j c -> p (j c)"), AF.Square, [yB], [Bs['sqp']])
                S.op('dve', lambda e: e.tensor_reduce(out=s2, in_=v8(sqp), axis=AX.X, op=ALU.add), reads=[Bs['sqp']], writes=[Bs['rstp']])
                yield
                ts(mean, s1, 1.0 / 64, None, ALU.mult, None, [Bs['rstp']], [Bs['rstp']])
                tt(msq, mean, mean, ALU.mult, [Bs['rstp']], [Bs['rstp']])
                stt(rstd, s2, 1.0 / 64, msq, ALU.mult, ALU.subtract, [Bs['rstp']], [Bs['rstp']])
                rsqrt_tiny(rstd, rstd, 1.0, RWKV_GN_EPS, [Bs['rstp']], [Bs['rstp']])
                yield
                tt(v8(ynp), y3, mean.unsqueeze(2).to_broadcast([128, 8, 64]), ALU.subtract, [yB, Bs['rstp']], [Bs['ynp']])
                tt(v8(ynp), v8(ynp), rstd.unsqueeze(2).to_broadcast([128, 8, 64]), ALU.mult, [Bs['ynp'], Bs['rstp']], [Bs['ynp']])
                yield
                y4 = ynp.rearrange("p (j c) -> p j c", j=4)
                tt(y4, y4, lnxw[:, p * 128:(p + 1) * 128].unsqueeze(1).to_broadcast([128, 4, 128]), ALU.mult, [Bs['ynp'], B_lw], [Bs['ynp']])
                tt(y4, y4, lnxb[:, p * 128:(p + 1) * 128].unsqueeze(1).to_broadcast([128, 4, 128]), ALU.add, [Bs['ynp'], B_lw], [Bs['ynp']])
                yield
                for j in range(4):
                    n = 4 * tg + j
                    cs = slice(n * 128, (n + 1) * 128)
                    mm(psB[:, 2 * j:2 * j + 2], rkrT[:, cs], sel, True, True, [rkb[tg], B_const], [Bs['psB']])
                    mm(psG[:, j * 128:(j + 1) * 128], L1g[:, cs], G2sb[:, p * 128:(p + 1) * 128], True, True, [L1B[tg], B_lw], [Bs['psG']])
                cp('act', sB, psB, [Bs['psB']], [Bs['sB']])
                yield
                tt(v8(bon), Vt[g].rearrange("p j (h e) -> p (j h) e", h=2), sB.unsqueeze(2).to_broadcast([128, 8, 64]), ALU.mult,
                   [Bs[f'Vt{g}'], Bs['sB']], [Bs['bon']])
                tt(ynp, ynp, bon, ALU.add, [Bs['ynp'], Bs['bon']], [Bs['ynp']])
                yield
                tt(yop.rearrange("p j c -> p (j c)"), ynp, psG, ALU.mult, [Bs['ynp'], Bs['psG']], [Bs['yop']])
                yield
                for j in range(4):
                    S.op('pe', lambda e, j=j: e.transpose(out=pT2[:, j, :], in_=yop[:, j, :], identity=ident), reads=[Bs['yop'], B_const], writes=[Bs['pT2']])
                cp('act', yT[:, p, tg * 512:(tg + 1) * 512], pT2.rearrange("p j t -> p (j t)"), [Bs['pT2']], [yTb[p][4 * tg + j] for j in range(4)])
                yield

            return local, chain, post

        assert A.peak <= WOUT_OFF, ("phase-3 buffers overlap the w_out prefetch region", A.peak, WOUT_OFF)
        wo_v = w_out.rearrange("(c p) n -> p c n", p=128)
        for c in range(8):
            dma('pool', wout[:, c, :], wo_v[:, c, :], [], [woutB])
        for i_ in range(2):
            S.op('pool', lambda e, i_=i_: e.memset(BKz[i_], 0.0), writes=[Bs[f'BKz{i_}']])
        NP = DBG.get('pairs', 4)
        for tb in range(4):
            prep_block(0, tb)
        scans = [make_scan(p) for p in range(NP)]

        def drain(g):
            for _ in g:
                pass
        S.op('pool', lambda e: e.memset(Hs, 0.0), writes=[Bs['H']])
        S.op('pool', lambda e: e.memset(Hbz, 0.0), writes=[Bs['Hb']])
        drain(scans[0][0](0))
        pend = []

        def prep_gen(p_, tb_):
            prep_block(p_, tb_)
            yield

        for p in range(NP):
            local, chain, post = scans[p]
            for n in range(NT):
                while len(pend) > 2:
                    drain(pend.pop(0))
                a = chain(n)
                if n + 1 < NT:
                    b = local(n + 1)
                elif p + 1 < NP:
                    b = scans[p + 1][0](0)
                else:
                    b = iter(())
                done_a = done_b = False
                while not (done_a and done_b):
                    if not done_a:
                        try:
                            next(a)
                        except StopIteration:
                            done_a = True
                    if not done_b:
                        try:
                            next(b)
                        except StopIteration:
                            done_b = True
                    if pend:
                        try:
                            next(pend[0])
                        except StopIteration:
                            pend.pop(0)
                if n % 4 == 3:
                    pend.append(post(n // 4))
                    if p + 1 < NP:
                        pend.append(prep_gen(p + 1, n // 4))
            if p + 1 == NP:
                while pend:
                    drain(pend.pop(0))
            if p + 1 < NP:
                S.op('dve', lambda e: e.memset(Hs, 0.0), writes=[Bs['H']])
                S.op('pool', lambda e: e.memset(Hbz, 0.0), writes=[Bs['Hb']])
        tap('yT', yT, [128, 8, T], [b for l in yTb for b in l])
        S.barrier()


    if 4 in phases:
        S.barrier()
        A.reset(NORM_END)
        xres = A.take([NT, D], F32)
        xresB = [Buf(f'xres{n}') for n in range(NT)]
        P4 = A.mark()
        if 3 not in phases:
            wo_v = w_out.rearrange("(c p) n -> p c n", p=128)
            for c in range(8):
                dma('pool', wout[:, c, :], wo_v[:, c, :], [], [woutB])
        dma('sp', gtab, bct_d[:, 1024:2048], [], [B_gtab])
        for n in range(NT):
            dma('sp', xst[n % 3], xv[n], [], [xstB[n % 3]])
            pb = 2 * (n % 2)
            for half in range(2):
                for c in range(8):
                    mm(bank[pb + half], yT[:, c, n * 128:(n + 1) * 128], wout[:, c, half * 512:(half + 1) * 512], c == 0, c == 7,
                       [yTb[c][n], woutB], [bankB[pb + half]])
            tt(xres[:, n, :], pp[n % 2][:], xst[n % 3], ALU.add, [bankB[pb], bankB[pb + 1], xstB[n % 3]], [xresB[n]])
            norm_stats(n, xres[:, n, :], xresB[n], 1)
        norm_rstd(1)
        for n in range(NT):
            norm_apply(n, xres[:, n, :], xresB[n], 1, 4 + n % 2)
        tap('xres', xres, [128, NT, D], xresB)

    if 5 in phases:
        S.barrier()
        A.reset(P4)
        hid = yT[:, 0:6, :]
        hidB = [Buf(f'hid{i}') for i in range(4)]
        wgu = [A.take([2, 8, 256], BF16) for _ in range(2)]
        wguB = [Buf(f'wgu{i}') for i in range(2)]
        wd = A.take([6, D], BF16)
        wdB = Buf('wd')
        gs = [A.take([514], F32) for _ in range(2)]
        gsB = [Buf(f'gs{i}') for i in range(2)]
        acc = [A.take([512], F32) for _ in range(2)]
        accB = [Buf(f'acc{i}') for i in range(2)]
        sl = [A.take([512], F32) for _ in range(2)]
        slB = [Buf(f'sl{i}') for i in range(2)]
        ost = [xst[0], xst[1]]
        ostB = [xstB[0], xstB[1]]
        dma('sp', gtab, bct_d[:, 2048:3072], [], [B_gtab])
        wg_v = wg_d.rearrange("(c p) n -> p c n", p=128)
        wu_v = wu_d.rearrange("(c p) n -> p c n", p=128)
        wd_v = wd_d.rearrange("(m p) n -> p m n", p=128)
        quarters = [(0, 6), (6, 6), (12, 5), (17, 5)]

        def load_wgu(m):
            wb_ = (m // 2) % 2
            dma('pool', wgu[wb_][:, 0, :, :], wg_v[:, :, m * 128:(m + 2) * 128], [], [wguB[wb_]])
            dma('pool', wgu[wb_][:, 1, :, :], wu_v[:, :, m * 128:(m + 2) * 128], [], [wguB[wb_]])
        load_wgu(0)
        it = 0
        for qi, (m0, nq) in enumerate(quarters):
            dma('pool', wd[:, 0:nq, :], wd_v[:, m0:m0 + nq, :], [], [wdB])
            for ml in range(nq):
                m = m0 + ml
                wb = (m // 2) % 2
                if m % 2 == 0 and m + 2 < NFF:
                    load_wgu(m + 2)
                mc = slice((m % 2) * 128, (m % 2) * 128 + 128)
                vf = V_FFN + 4 * m
                cw = lambda j: vecs[:, vf + j:vf + j + 1]
                for blk in range(4):
                    g_, gB = gs[blk % 2], gsB[blk % 2]
                    a_, aB = acc[it % 2], accB[it % 2]
                    s_, sB_ = sl[it % 2], slB[it % 2]
                    pg, pu = 2 * (it % 4), 2 * (it % 4) + 1
                    it += 1
                    rd = [hTb[4 * blk + i] for i in range(4)] + [wguB[wb]]
                    for c in range(8):
                        mm(bank[pg], wgu[wb][:, 0, c, mc], hT[:, c, 1 + blk * 512:1 + (blk + 1) * 512], c == 0, c == 7, rd, [bankB[pg]])
                    for c in range(8):
                        mm(bank[pu], wgu[wb][:, 1, c, mc], hT[:, c, 1 + blk * 512:1 + (blk + 1) * 512], c == 0, c == 7, rd, [bankB[pu]])
                    if blk == 0:
                        S.op('pool', lambda e, g_=g_: e.memset(g_[:, 0:2], 0.0), writes=[gB])
                    else:
                        gp = gs[(blk - 1) % 2]
                        S.op('pool', lambda e, g_=g_, gp=gp: e.tensor_copy(out=g_[:, 0:2], in_=gp[:, 512:514]), reads=[gsB[(blk - 1) % 2]], writes=[gB])
                    act(g_[:, 2:514], bank[pg], AF.Copy, [bankB[pg]], [gB])
                    act(a_, bank[pg], AF.Identity, [bankB[pg], B_const], [aB], bias=cw(3), scale=cw(2))
                    stt(a_, g_[:, 1:513], cw(1), a_, ALU.mult, ALU.add, [gB, aB, B_const], [aB])
                    stt(a_, g_[:, 0:512], cw(0), a_, ALU.mult, ALU.add, [gB, aB, B_const], [aB])
                    act(s_, a_, AF.Silu, [aB], [sB_])
                    tt(hid[:, ml, blk * 512:(blk + 1) * 512], s_, bank[pu], ALU.mult, [sB_, bankB[pu]], [hidB[blk]])
            last = qi == len(quarters) - 1
            for n in range(NT):
                pb = 2 * (n % 4)
                for half in range(2):
                    for ml in range(nq):
                        mm(bank[pb + half], hid[:, ml, n * 128:(n + 1) * 128], wd[:, ml, half * 512:(half + 1) * 512], ml == 0, ml == nq - 1,
                           [hidB[n // 4], wdB], [bankB[pb + half]])
                tt(xres[:, n, :], pp[n % 4][:], xres[:, n, :], ALU.add, [bankB[pb], bankB[pb + 1], xresB[n]], [xresB[n]])
                if last:
                    ssn = ss_all[:, 2, n:n + 1]
                    rsn = rstd_all[:, 2, n:n + 1]
                    sB2 = statB[n % 4]
                    act(sqj, xres[:, n, :], AF.Square, [xresB[n]], [sqjB, sB2], accum=ssn)
                    rsqrt_tiny(rsn, ssn, 1.0 / D, NORM_EPS, [sB2], [sB2])
                    stt(ost[n % 2], xres[:, n, :], rsn, gtab, ALU.mult, ALU.mult, [xresB[n], sB2, B_gtab], [ostB[n % 2]])
                    dma('sp', ov[n], ost[n % 2], [ostB[n % 2]], [])

    S.barrier(('sp',))
    S.emit(st)
    st.close()
    return nc, tap_out, S, A


def _chunkcols(v):
    v = np.asarray(v, np.float32).reshape(-1, 128)
    return np.ascontiguousarray(v.T)


def prep_shared(inp):
    f = lambda k: np.ascontiguousarray(np.asarray(inp[k], np.float32)[0])
    vecs = np.zeros((128, NV), np.float32)
    vecs[:, V_MUW:V_MUW + 8] = _chunkcols(f("rwkv_mu_w"))
    vecs[:, V_MUA:V_MUA + 8] = _chunkcols(f("rwkv_mu_a"))
    vecs[:, V_MUG:V_MUG + 8] = _chunkcols(f("rwkv_mu_g"))
    names = ["rwkv_mu_r", "rwkv_mu_k", "rwkv_mu_v", "rwkv_w0", "rwkv_a0", "rwkv_k_k", "rwkv_k_a", "rwkv_r_k"]
    for j, nm in enumerate(names):
        cc = _chunkcols(f(nm).reshape(-1))
        for p in range(4):
            vecs[:, V_PAIR + 8 * p + j] = cc[:, p]
    cw = f("ffn_conv_w").reshape(3, DFF)
    cbias = f("ffn_conv_b")
    for j in range(3):
        cc = _chunkcols(cw[j])
        for m in range(NFF):
            vecs[:, V_FFN + 4 * m + j] = cc[:, m]
    cc = _chunkcols(cbias)
    for m in range(NFF):
        vecs[:, V_FFN + 4 * m + 3] = cc[:, m]
    row = np.concatenate([f("norm_mix_g"), f("norm_ffn_g"), np.asarray(inp["norm_final_g"], np.float32),
                          f("rwkv_lnx_w"), f("rwkv_lnx_b"), f("ret_gn_w")])
    bct = np.ascontiguousarray(np.broadcast_to(row[None, :], (128, row.shape[0])))
    cf, cb = make_consts()
    shared = {
        "w_in": f("w_in"), "w_out": f("w_out"), "ffn_w_gate": f("ffn_w_gate"), "ffn_w_up": f("ffn_w_up"),
        "ffn_w_down": f("ffn_w_down"), "rwkv_w1": f("rwkv_w1"), "rwkv_a1": f("rwkv_a1"), "rwkv_g1": f("rwkv_g1"),
        "rwkv_w2": f("rwkv_w2"), "rwkv_a2": f("rwkv_a2"), "rwkv_g2": f("rwkv_g2"),
        "vecs": vecs, "bct": bct, "cf": cf, "cb": cb,
    }
    return shared


_PROG = None


def kernel(**inputs):
    global _PROG
    if _PROG is None:
        _PROG = build_program()[0]
    shared = prep_shared(inputs)
    xs = np.asarray(inputs["x"], np.float32)
    in_maps = [dict(shared, x=np.ascontiguousarray(xs[b])) for b in range(8)]
    res = run_bass_kernel_spmd(_PROG, in_maps, core_ids=list(range(8)))
    return np.stack([np.asarray(r["out"], np.float32) for r in res.results], axis=0)
```

```python
import numpy as np
import ml_dtypes
from contextlib import ExitStack
import concourse.bass as bass
import concourse.mybir as mybir
from concourse.bass_utils import run_bass_kernel_spmd

F32 = mybir.dt.float32
BF16 = mybir.dt.bfloat16
AF = mybir.ActivationFunctionType
ALU = mybir.AluOpType
AX = mybir.AxisListType

QUEUES = ('sp', 'act', 'pool', 'pe', 'dve')

T = 2048
D = 1024
NT = 16
DFF = 2816
NFF = 22
C0 = float(np.exp(-0.5))
NORM_EPS = 1e-6
RWKV_GN_EPS = 64e-5
RET_GN_EPS = 1e-5


class Buf:
    __slots__ = ('name', 'w', 'r')

    def __init__(self, name=''):
        self.name = name
        self.w = None
        self.r = {}


class _Op:
    __slots__ = ('q', 's', 'idx', 'fn', 'waits', 'inc', 'dma')


class Sched:
    def __init__(self, nc):
        self.nc = nc
        self.ops = {q: [] for q in QUEUES}
        self.streams = {}
        self.clock = {q: {} for q in QUEUES}
        self.opclock = {}
        self.nwaits = 0
        self.nops = 0

    def op(self, q, fn, reads=(), writes=(), dma=False):
        if dma:
            ref = writes[0] if len(writes) else (reads[0] if len(reads) else None)
            s = 'dq_' + (ref.name if ref is not None and ref.name else q)
        else:
            s = q
        deps = {}

        def need(st, i):
            if deps.get(st, 0) < i:
                deps[st] = i
        for b in reads:
            if b.w is not None:
                st, i = b.w
                if st == q and q == 'pe':
                    continue
                need(st, i)
        for b in writes:
            if b.w is not None:
                st, i = b.w
                if not (st == q and not dma):
                    need(st, i)
            for st, i in b.r.items():
                if st == q and not dma:
                    continue
                need(st, i)
        ck = self.clock[q]
        waits = []
        for st, i in deps.items():
            if ck.get(st, 0) >= i:
                continue
            waits.append((st, i))
            oc = self.opclock[(st, i)]
            for k, v in oc.items():
                if ck.get(k, 0) < v:
                    ck[k] = v
            if ck.get(st, 0) < i:
                ck[st] = i
            self.streams[st][i - 1].inc = True
        o = _Op()
        o.q = q
        o.s = s
        o.fn = fn
        o.waits = waits
        o.inc = dma
        o.dma = dma
        lst = self.streams.setdefault(s, [])
        lst.append(o)
        o.idx = len(lst)
        self.opclock[(s, o.idx)] = dict(ck)
        self.ops[q].append(o)
        self.nwaits += len(waits)
        self.nops += 1
        for b in writes:
            b.w = (s, o.idx)
            b.r = {}
        for b in reads:
            if b.r.get(s, 0) < o.idx:
                b.r[s] = o.idx
        return o

    def barrier(self, queues=QUEUES):
        tips = {s: len(l) for s, l in self.streams.items() if l}
        for q in queues:
            ck = self.clock[q]
            waits = []
            for s, i in tips.items():
                if s == q and q == 'pe':
                    continue
                if ck.get(s, 0) >= i:
                    continue
                waits.append((s, i))
                self.streams[s][i - 1].inc = True
            for s, i in waits:
                oc = self.opclock[(s, i)]
                for k, v in oc.items():
                    if ck.get(k, 0) < v:
                        ck[k] = v
                ck[s] = i
            if waits:
                o = _Op()
                o.q = q
                o.s = None
                o.fn = None
                o.waits = waits
                o.inc = False
                o.dma = False
                self.ops[q].append(o)

    def emit(self, stack):
        nc = self.nc
        sems = {s: stack.enter_context(nc.semaphore('sem_' + s)) for s in self.streams}
        cnt = {}
        for s, lst in self.streams.items():
            c = 0
            for o in lst:
                if o.dma:
                    c += 16
                elif o.inc:
                    c += 1
                cnt[(s, o.idx)] = c
        self.final_counts = {s: (cnt[(s, len(l))] if l else 0) for s, l in self.streams.items()}
        block = stack.enter_context(nc.Block())

        def run(q, eng):
            for o in self.ops[q]:
                for st, i in o.waits:
                    eng.wait_ge(sems[st], cnt[(st, i)])
                if o.fn is None:
                    continue
                ins = o.fn(eng)
                if o.dma:
                    ins.then_inc(sems[o.s], 16)
                elif o.inc:
                    ins.then_inc(sems[o.s], 1)

        @block.sync
        def _(e):
            run('sp', e)

        @block.scalar
        def _(e):
            run('act', e)

        @block.gpsimd
        def _(e):
            run('pool', e)

        @block.tensor
        def _(e):
            run('pe', e)

        @block.vector
        def _(e):
            run('dve', e)


class Arena:
    def __init__(self, ap, nbytes):
        self.ap = ap
        self.nbytes = nbytes
        self.off = 0
        self.peak = 0

    def take(self, shape, dt):
        esz = 4 if dt == F32 else 2
        n = int(np.prod(shape))
        nb = (n * esz + 63) // 64 * 64
        assert self.off + nb <= self.nbytes, ("arena overflow", self.off, nb, self.nbytes)
        v = self.ap[:, self.off // 4:(self.off + nb) // 4]
        if dt != F32:
            v = v.bitcast(dt)
        v = v[:, 0:n]
        if len(shape) == 2:
            v = v.rearrange("p (a b) -> p a b", a=shape[0])
        elif len(shape) == 3:
            v = v.rearrange("p (a b c) -> p a b c", a=shape[0], b=shape[1])
        elif len(shape) == 4:
            v = v.rearrange("p (a b c d) -> p a b c d", a=shape[0], b=shape[1], c=shape[2])
        self.off += nb
        self.peak = max(self.peak, self.off)
        return v

    def mark(self):
        return self.off

    def reset(self, m):
        self.off = m


V_MUW, V_MUA, V_MUG = 0, 8, 16
V_PAIR = 24
V_FFN = 56
NV = 56 + 4 * NFF
CB_ID, CB_MRET, CB_M4, CB_ML, CB_SEL, CB_ONES = 0, 128, 256, 768, 896, 900
NCB = 1028
CF_COS, CF_SIN, CF_NSIN, CF_XIT, CF_KAT, CF_KAPG, CF_GC = 0, 1024, 2048, 3072, 3584, 4096, 4100
NCF = 4104


def make_consts():
    f32 = np.float32
    p = np.arange(128)
    cf = np.zeros((128, NCF), f32)
    half = 64
    inv_freq = (10000.0 ** (-np.arange(half, dtype=np.float64) / half))
    pos = (np.arange(NT)[None, :] * 128 + p[:, None]).astype(np.float64)
    ang = pos[:, :, None] * inv_freq[None, None, :]
    cf[:, CF_COS:CF_COS + 1024] = np.cos(ang).reshape(128, -1)
    cf[:, CF_SIN:CF_SIN + 1024] = np.sin(ang).reshape(128, -1)
    cf[:, CF_NSIN:CF_NSIN + 1024] = -np.sin(ang).reshape(128, -1)
    lg = np.log(1.0 - 2.0 ** (-5.0 - np.arange(4, dtype=np.float64)))
    i = np.arange(128, dtype=np.float64)
    xi = np.exp((i[None, :] + 1.0) * lg[:, None])
    ka = np.exp(-(i[None, :] + 1.0) * lg[:, None]) * (128.0 ** -0.5)
    cf[:, CF_XIT:CF_XIT + 512] = np.broadcast_to(xi.reshape(1, 512), (128, 512))
    cf[:, CF_KAT:CF_KAT + 512] = np.broadcast_to(ka.reshape(1, 512), (128, 512))
    gC = np.exp(128.0 * lg)
    cf[:, CF_KAPG:CF_KAPG + 4] = (ka.T * gC[None, :])
    cf[:, CF_GC:CF_GC + 4] = gC[None, :]
    cb = np.zeros((128, NCB), f32)
    cb[:, CB_ID:CB_ID + 128] = np.eye(128)
    r = p[:, None]
    c = p[None, :]
    cb[:, CB_MRET:CB_MRET + 128] = (r <= c)
    strict = (r < c).astype(f32)
    incl = (r <= c).astype(f32)
    cb[:, CB_M4:CB_M4 + 512] = np.concatenate([strict, incl, strict, incl], axis=1)
    cb[:, CB_ML:CB_ML + 128] = (c < r)
    cb[0:64, CB_SEL] = 1.0
    cb[64:128, CB_SEL + 1] = 1.0
    cb[0:64, CB_ONES:CB_ONES + 64] = 1.0
    cb[64:128, CB_ONES + 64:CB_ONES + 128] = 1.0
    return cf, cb.astype(ml_dtypes.bfloat16)


DBG = {'ret_chunks': NT, 'ret_steps': 99}


def build_program(taps=None, phases=(1, 2, 3, 4, 5)):
    nc = bass.Bass("TRN2", target_bir_lowering=False)

    def din(name, shape, dt=F32):
        return nc.dram_tensor(name, list(shape), dt, kind="ExternalInput").ap()
    x = din("x", [T, D])
    w_in = din("w_in", [D, 3584])
    w_out = din("w_out", [D, D])
    wg_d = din("ffn_w_gate", [D, DFF])
    wu_d = din("ffn_w_up", [D, DFF])
    wd_d = din("ffn_w_down", [DFF, D])
    w1_d = din("rwkv_w1", [D, 64])
    a1_d = din("rwkv_a1", [D, 64])
    g1_d = din("rwkv_g1", [D, 128])
    w2_d = din("rwkv_w2", [64, 512])
    a2_d = din("rwkv_a2", [64, 512])
    g2_d = din("rwkv_g2", [128, 512])
    vecs_d = din("vecs", [128, NV])
    bct_d = din("bct", [128, 4608])
    cf_d = din("cf", [128, NCF])
    cb_d = din("cb", [128, NCB], BF16)
    out = nc.dram_tensor("out", [T, D], F32, kind="ExternalOutput").ap()
    tap_out = {}
    taps = taps or {}

    S = Sched(nc)
    st = ExitStack()
    ARENA_BYTES = 206 * 1024
    arena_t = st.enter_context(nc.sbuf_tensor("arena", [128, ARENA_BYTES // 4], F32))
    A = Arena(arena_t[:], ARENA_BYTES)
    pp = [st.enter_context(nc.psum_tensor(f"pp{i}", [128, 1024], F32)) for i in range(4)]
    bank = [pp[i // 2][:, (i % 2) * 512:(i % 2) * 512 + 512] for i in range(8)]
    bankB = [Buf(f"bank{i}") for i in range(8)]

    def bankbf(i):
        return bank[i].bitcast(BF16)

    def act(out_, in_, func, r, w, bias=None, scale=None, accum=None):
        kw = {}
        if bias is not None:
            kw['bias'] = bias
        if scale is not None:
            kw['scale'] = scale
        if accum is not None:
            kw['accum_out'] = accum
        S.op('act', lambda e: e.activation(out=out_, in_=in_, func=func, **kw), reads=r, writes=w)

    def tt(out_, a, b, op, r, w, q='dve'):
        S.op(q, lambda e: e.tensor_tensor(out=out_, in0=a, in1=b, op=op), reads=r, writes=w)

    def ts(out_, a, s1, s2, op0, op1, r, w, q='dve'):
        if s2 is None:
            S.op(q, lambda e: e.tensor_scalar(out=out_, in0=a, scalar1=s1, scalar2=None, op0=op0), reads=r, writes=w)
        else:
            S.op(q, lambda e: e.tensor_scalar(out=out_, in0=a, scalar1=s1, scalar2=s2, op0=op0, op1=op1), reads=r, writes=w)

    def stt(out_, a, s, b, op0, op1, r, w):
        S.op('dve', lambda e: e.scalar_tensor_tensor(out=out_, in0=a, scalar=s, in1=b, op0=op0, op1=op1), reads=r, writes=w)

    def mm(out_, lhsT, rhs, start, stop, r, w):
        S.op('pe', lambda e: e.matmul(out=out_, lhsT=lhsT, rhs=rhs, start=start, stop=stop), reads=r, writes=w)

    def mm2(out_, lhsT, rhs, start, stop, r, w):
        if lhsT.shape[0] == 128:
            mm(out_, lhsT[0:64], rhs[0:64], start, False, r, w)
            mm(out_, lhsT[64:128], rhs[64:128], False, stop, r, w)
        else:
            mm(out_, lhsT, rhs, start, stop, r, w)

    def dma(q, out_, in_, r, w, **kw):
        S.op(q, lambda e: e.dma_start(out=out_, in_=in_, **kw), reads=r, writes=w, dma=True)

    def cp(q, out_, in_, r, w):
        if q == 'act':
            act(out_, in_, AF.Copy, r, w)
        else:
            S.op(q, lambda e: e.tensor_copy(out=out_, in_=in_), reads=r, writes=w)

    def rsqrt_tiny(dst, src, scale, eps, r, w):
        ts(dst, src, scale, eps, ALU.mult, ALU.add, r, w)
        act(dst, dst, AF.Ln, w, w)
        act(dst, dst, AF.Exp, w, w, scale=-0.5)

    hT = A.take([8, T + 1], BF16)
    yT = A.take([8, T], BF16)
    cb = A.take([NCB], BF16)
    vecs = A.take([NV], F32)
    om = A.take([NV], F32)
    mhalf = A.take([4], F32)
    gtab = A.take([1024], F32)
    stat = A.take([64], F32)
    ss_all = A.take([3, NT], F32)
    rstd_all = A.take([3, NT], F32)
    B_const = Buf('const')
    B_gtab = Buf('gtab')
    hTb = [Buf(f'hT{n}') for n in range(NT)]
    yTb = [[Buf(f'yT{c}_{n}') for n in range(NT)] for c in range(8)]
    ident = cb[:, CB_ID:CB_ID + 128]
    PERSIST = A.mark()
    WOUT_OFF = ARENA_BYTES - 8 * D * 2
    wout = arena_t[:, WOUT_OFF // 4:ARENA_BYTES // 4].bitcast(BF16).rearrange("p (c n) -> p c n", c=8)
    woutB = Buf('wout')

    def tap(name, ap, shape, reads):
        if name in taps:
            d = nc.dram_tensor("tap_" + name, list(shape), ap.dtype, kind="ExternalOutput").ap()
            tap_out[name] = d
            dma('sp', d, ap, reads, [])

    dma('sp', cb, cb_d, [], [B_const])
    dma('sp', vecs, vecs_d, [], [B_const])
    dma('sp', gtab, bct_d[:, 0:1024], [], [B_gtab])
    S.op('pool', lambda e: e.memset(mhalf, -0.5), writes=[B_const])
    ts(om, vecs, -1.0, 1.0, ALU.mult, ALU.add, [B_const], [B_const])
    S.op('pool', lambda e: e.memset(hT[:, :, 0:1], 0.0), writes=[hTb[0]])

    xst = [A.take([D], F32) for _ in range(3)]
    xstB = [Buf(f'xst{i}') for i in range(3)]
    hb = [A.take([D], BF16) for _ in range(2)]
    hbB = [Buf(f'hb{i}') for i in range(2)]
    sqj = A.take([D], BF16)
    sqjB = Buf('sqj')
    statB = [Buf(f'stat{i}') for i in range(4)]
    NORM_END = A.mark()

    ssB = [Buf(f'ss{i}') for i in range(3)]
    rsB = [Buf(f'rs{i}') for i in range(3)]

    def norm_stats(n, src, srcB, which):
        act(sqj, src, AF.Square, [srcB], [sqjB, ssB[which]], accum=ss_all[:, which, n:n + 1])

    def norm_rstd(which, lo=0, hi=NT):
        rsqrt_tiny(rstd_all[:, which, lo:hi], ss_all[:, which, lo:hi], 1.0 / D, NORM_EPS, [ssB[which]], [rsB[which]])

    def norm_apply(n, src, srcB, which, pbank):
        h = hb[n % 2]
        stt(h, src, rstd_all[:, which, n:n + 1], gtab, ALU.mult, ALU.mult, [srcB, rsB[which], B_gtab], [hbB[n % 2]])
        pt = bankbf(pbank).rearrange("p (c t) -> p c t", c=8)
        for c in range(8):
            S.op('pe', lambda e, c=c: e.transpose(out=pt[:, c, :], in_=h[:, c * 128:(c + 1) * 128], identity=ident),
                 reads=[hbB[n % 2], B_const], writes=[bankB[pbank]])
        cp('act', hT[:, :, 1 + n * 128:1 + (n + 1) * 128], pt, [bankB[pbank]], [hTb[n]])

    xv = x.rearrange("(n p) d -> n p d", p=128)
    ov = out.rearrange("(n p) d -> n p d", p=128)
    if 2 in phases:
        cf = A.take([NCF], F32)
        dma('sp', cf, cf_d, [], [B_const])
        wret = A.take([8, 2048], BF16)
        wretB = Buf('wret')
        wv = w_in.rearrange("(c p) n -> p c n", p=128)
        for c in range(8):
            dma('pool', wret[:, c, :], wv[:, c, 1536:3584], [], [wretB])
        P2START = A.mark()
    XC_OFF = 168 * 1024
    assert XC_OFF + 8 * D * 4 <= ARENA_BYTES
    xc = arena_t[:, XC_OFF // 4:XC_OFF // 4 + 8 * D].rearrange("p (s d) -> p s d", s=8)
    xcB = [Buf(f'xc{i}') for i in range(8)]
    for lo_ in (0, 8):
        for n in range(lo_, lo_ + 8):
            dma('sp', xc[:, n % 8, :], xv[n], [], [xcB[n % 8]])
            norm_stats(n, xc[:, n % 8, :], xcB[n % 8], 0)
        norm_rstd(0, lo_, lo_ + 8)
        for n in range(lo_, lo_ + 8):
            norm_apply(n, xc[:, n % 8, :], xcB[n % 8], 0, n % 2)
    tap('hT', hT, [128, 8, T + 1], hTb)

    if 2 in phases:
        A.reset(P2START)
        gnw = A.take([512], F32)
        dma('sp', gnw, bct_d[:, 4096:4608], [], [B_const])
        qa = A.take([512], F32)
        qb = A.take([512], F32)
        qrot = A.take([512], BF16)
        krot = A.take([512], BF16)
        qT = A.take([4, 128], BF16)
        kT = A.take([4, 128], BF16)
        PT = A.take([4, 128], BF16)
        Vb = A.take([512], BF16)
        Vk = A.take([512], BF16)
        R = A.take([512], F32)
        Rt = A.take([512], F32)
        Rb = A.take([512], BF16)
        sqy = A.take([512], F32)
        yn = A.take([512], F32)
        sgt = A.take([512], F32)
        yo = A.take([512], BF16)
        rst = A.take([32], F32)
        assert A.off <= 168 * 1024, ('retention buffers overlap the x cache', A.off)
        Bq = {k: Buf('r_' + k) for k in ['qa', 'qb', 'qrot', 'krot', 'qT', 'kT', 'PT', 'Vb', 'Vk', 'R', 'Rt', 'Rb', 'sqy', 'yn', 'sgt', 'yo', 'rst']}
        kapg_bc = cf[:, CF_KAPG:CF_KAPG + 4].unsqueeze(2).to_broadcast([128, 4, 128])
        gC_bc = cf[:, CF_GC:CF_GC + 4].unsqueeze(2).to_broadcast([128, 4, 128])
        xiT = cf[:, CF_XIT:CF_XIT + 512].rearrange("p (h t) -> p h t", h=4)
        kaT = cf[:, CF_KAT:CF_KAT + 512].rearrange("p (h t) -> p h t", h=4)
        mret_bc = cb[:, CB_MRET:CB_MRET + 128].unsqueeze(1).to_broadcast([128, 4, 128])
        PQ, PK, PV, PG, PTB, PS, PY, PKV = range(8)

        def v4(ap):
            return ap.rearrange("p (h e) -> p h e", h=4)

        def rot(ps, psB, dst, dstB, n):
            cosb = cf[:, CF_COS + n * 64:CF_COS + (n + 1) * 64].unsqueeze(1).unsqueeze(1).to_broadcast([128, 4, 2, 64])
            sinb = cf[:, CF_SIN + n * 64:CF_SIN + (n + 1) * 64].unsqueeze(1).to_broadcast([128, 4, 64])
            nsinb = cf[:, CF_NSIN + n * 64:CF_NSIN + (n + 1) * 64].unsqueeze(1).to_broadcast([128, 4, 64])
            p4 = ps.rearrange("p (h two f) -> p h two f", h=4, two=2)
            tt(qa.rearrange("p (h two f) -> p h two f", h=4, two=2), p4, cosb, ALU.mult, [psB, B_const], [Bq['qa']])
            qb4 = qb.rearrange("p (h two f) -> p h two f", h=4, two=2)
            tt(qb4[:, :, 0, :], p4[:, :, 1, :], nsinb, ALU.mult, [psB, B_const], [Bq['qb']])
            tt(qb4[:, :, 1, :], p4[:, :, 0, :], sinb, ALU.mult, [psB, B_const], [Bq['qb']])
            tt(dst, qa, qb, ALU.add, [Bq['qa'], Bq['qb']], [dstB])

        for n in range(DBG['ret_chunks']):
            RS = DBG['ret_steps']
            def proj(nn):
                tok = slice(1 + nn * 128, 1 + (nn + 1) * 128)
                for j, pb in enumerate((PQ, PK, PV, PG)):
                    for c in range(8):
                        mm(bank[pb], hT[:, c, tok], wret[:, c, j * 512:(j + 1) * 512], c == 0, c == 7,
                           [hTb[nn], wretB], [bankB[pb]])
            if n == 0:
                proj(0)
            act(sgt, bank[PG], AF.Silu, [bankB[PG]], [Bq['sgt']])
            rot(bank[PQ], bankB[PQ], qrot, Bq['qrot'], n)
            rot(bank[PK], bankB[PK], krot, Bq['krot'], n)
            if RS < 3:
                continue
            ptb = bankbf(PTB).rearrange("p (c t) -> p c t", c=8)
            for h in range(4):
                S.op('pe', lambda e, h=h: e.transpose(out=ptb[:, h, :], in_=qrot[:, h * 128:(h + 1) * 128], identity=ident),
                     reads=[Bq['qrot'], B_const], writes=[bankB[PTB]])
            for h in range(4):
                S.op('pe', lambda e, h=h: e.transpose(out=ptb[:, 4 + h, :], in_=krot[:, h * 128:(h + 1) * 128], identity=ident),
                     reads=[Bq['krot'], B_const], writes=[bankB[PTB]])
            tt(qT, ptb[:, 0:4, :], xiT, ALU.mult, [bankB[PTB], B_const], [Bq['qT']])
            tt(kT, ptb[:, 4:8, :], kaT, ALU.mult, [bankB[PTB], B_const], [Bq['kT']])
            if RS < 4:
                continue
            ps4 = v4(bank[PS])
            for h in range(4):
                mm(ps4[:, h, :], kT[:, h, :], qT[:, h, :], True, True, [Bq['kT'], Bq['qT']], [bankB[PS]])
            tt(PT, ps4, mret_bc, ALU.mult, [bankB[PS], B_const], [Bq['PT']])
            if RS < 5:
                continue
            cp('act', Vb, bank[PV], [bankB[PV]], [Bq['Vb']])
            tt(v4(Vk), v4(bank[PV]), kapg_bc, ALU.mult, [bankB[PV], B_const], [Bq['Vk']])
            if RS < 6:
                continue
            py4 = v4(bank[PY])
            for h in range(4):
                mm(py4[:, h, :], PT[:, h, :], Vb[:, h * 128:(h + 1) * 128], True, n == 0, [Bq['PT'], Bq['Vb']], [bankB[PY]])
                if n > 0:
                    mm(py4[:, h, :], qT[:, h, :], Rb[:, h * 128:(h + 1) * 128], False, True, [Bq['qT'], Bq['Rb']], [bankB[PY]])
            if RS < 7:
                continue
            if n < NT - DBG.get('skiplast', 0):
                pkv4 = v4(bank[PKV])
                for h in range(4):
                    mm(pkv4[:, h, :], krot[:, h * 128:(h + 1) * 128], Vk[:, h * 128:(h + 1) * 128], True, True,
                       [Bq['krot'], Bq['Vk']], [bankB[PKV]])
                if n == 0:
                    cp('dve', R, bank[PKV], [bankB[PKV]], [Bq['R']])
                else:
                    tt(v4(Rt), v4(R), gC_bc, ALU.mult, [Bq['R'], B_const], [Bq['Rt']])
                    tt(R, Rt, bank[PKV], ALU.add, [Bq['Rt'], bankB[PKV]], [Bq['R']])
                cp('pool', Rb, R, [Bq['R']], [Bq['Rb']])
            if n + 1 < NT:
                proj(n + 1)
            s1 = rst[:, 0:4]
            s2 = rst[:, 4:8]
            mean = rst[:, 8:12]
            msq = rst[:, 12:16]
            rstd = rst[:, 16:20]
            S.op('dve', lambda e: e.tensor_reduce(out=s1, in_=py4, axis=AX.X, op=ALU.add), reads=[bankB[PY]], writes=[Bq['rst']])
            act(sqy, bank[PY], AF.Square, [bankB[PY]], [Bq['sqy']])
            S.op('dve', lambda e: e.tensor_reduce(out=s2, in_=v4(sqy), axis=AX.X, op=ALU.add), reads=[Bq['sqy']], writes=[Bq['rst']])
            ts(mean, s1, 1.0 / 128, None, ALU.mult, None, [Bq['rst']], [Bq['rst']])
            tt(msq, mean, mean, ALU.mult, [Bq['rst']], [Bq['rst']])
            stt(rstd, s2, 1.0 / 128, msq, ALU.mult, ALU.subtract, [Bq['rst']], [Bq['rst']])
            rsqrt_tiny(rstd, rstd, 1.0, RET_GN_EPS, [Bq['rst']], [Bq['rst']])
            tt(v4(yn), py4, mean.unsqueeze(2).to_broadcast([128, 4, 128]), ALU.subtract, [bankB[PY], Bq['rst']], [Bq['yn']])
            tt(v4(yn), v4(yn), rstd.unsqueeze(2).to_broadcast([128, 4, 128]), ALU.mult, [Bq['yn'], Bq['rst']], [Bq['yn']])
            tt(yn, yn, gnw, ALU.mult, [Bq['yn'], B_const], [Bq['yn']])
            tt(yo, yn, sgt, ALU.mult, [Bq['yn'], Bq['sgt']], [Bq['yo']])
            if RS < 9:
                continue
            for h in range(4):
                S.op('pe', lambda e, h=h: e.transpose(out=ptb[:, h, :], in_=yo[:, h * 128:(h + 1) * 128], identity=ident),
                     reads=[Bq['yo'], B_const], writes=[bankB[PTB]])
            cp('act', yT[:, 4:8, n * 128:(n + 1) * 128], ptb[:, 0:4, :], [bankB[PTB]], [yTb[4 + h][n] for h in range(4)])
        if 3 not in phases:
            tap('yT', yT, [128, 8, T], [b for l in yTb for b in l])
        S.barrier()


    if 3 in phases:
        S.barrier()
        A.reset(PERSIST)
        wl_f = A.take([8, 128], F32)
        gl_f = A.take([8, 128], F32)
        W1A = A.take([8, 128], BF16)
        W1B = A.take([8, 128], BF16)
        G1A = A.take([8, 128], BF16)
        G1B = A.take([8, 128], BF16)
        W2sb = A.take([512], BF16)
        A2sb = A.take([512], BF16)
        G2sb = A.take([512], BF16)
        L1 = A.take([T], BF16)
        L1g = A.take([T], BF16)
        lnxw = A.take([512], F32)
        lnxb = A.take([512], F32)
        B_lw = Buf('loraw')
        L1B = [Buf(f'L1_{i}') for i in range(4)]
        dma('sp', wl_f[:, :, 0:64], w1_d.rearrange("(c p) k -> p c k", p=128), [], [B_lw])
        dma('sp', wl_f[:, :, 64:128], a1_d.rearrange("(c p) k -> p c k", p=128), [], [B_lw])
        dma('sp', gl_f, g1_d.rearrange("(c p) k -> p c k", p=128), [], [B_lw])
        dma('sp', lnxw, bct_d[:, 3072:3584], [], [B_lw])
        dma('sp', lnxb, bct_d[:, 3584:4096], [], [B_lw])
        S.op('pool', lambda e: e.memset(W2sb, 0.0), writes=[B_lw])
        S.op('pool', lambda e: e.memset(A2sb, 0.0), writes=[B_lw])
        dma('pool', W2sb[0:64, :], w2_d, [B_lw], [B_lw])
        dma('pool', A2sb[64:128, :], a2_d, [B_lw], [B_lw])
        dma('pool', G2sb, g2_d, [], [B_lw])

        def vb(tab, col, k):
            return tab[:, col:col + 8].unsqueeze(2).to_broadcast([128, 8, k])
        tt(W1A[:, :, 0:64], wl_f[:, :, 0:64], vb(om, V_MUW, 64), ALU.mult, [B_lw, B_const], [B_lw])
        tt(W1A[:, :, 64:128], wl_f[:, :, 64:128], vb(om, V_MUA, 64), ALU.mult, [B_lw, B_const], [B_lw])
        tt(W1B[:, :, 0:64], wl_f[:, :, 0:64], vb(vecs, V_MUW, 64), ALU.mult, [B_lw, B_const], [B_lw])
        tt(W1B[:, :, 64:128], wl_f[:, :, 64:128], vb(vecs, V_MUA, 64), ALU.mult, [B_lw, B_const], [B_lw])
        tt(G1A, gl_f, vb(om, V_MUG, 128), ALU.mult, [B_lw, B_const], [B_lw])
        tt(G1B, gl_f, vb(vecs, V_MUG, 128), ALU.mult, [B_lw, B_const], [B_lw])
        for tb in range(4):
            rd = [hTb[4 * tb + i] for i in range(4)] + ([hTb[4 * tb - 1]] if tb > 0 else []) + [B_lw]
            for (WA, WB, pb) in ((W1A, W1B, 0), (G1A, G1B, 1)):
                for c in range(8):
                    mm(bank[pb], WA[:, c, :], hT[:, c, 1 + tb * 512:1 + (tb + 1) * 512], c == 0, False, rd, [bankB[pb]])
                    mm(bank[pb], WB[:, c, :], hT[:, c, tb * 512:(tb + 1) * 512], False, c == 7, rd, [bankB[pb]])
            blk = slice(tb * 512, (tb + 1) * 512)
            act(L1[0:64, blk], bank[0][0:64, :], AF.Tanh, [bankB[0]], [L1B[tb]])
            act(L1[64:128, blk], bank[0][64:128, :], AF.Copy, [bankB[0]], [L1B[tb]])
            act(L1g[:, blk], bank[1], AF.Sigmoid, [bankB[1]], [L1B[tb]])

        wrkv = A.take([8, 3, 128], BF16)
        AR = A.take([NT, 2, 128], BF16)
        BT = A.take([T], BF16)
        KT = A.take([T], BF16)
        vT = A.take([T], BF16)
        rkrT = A.take([T], BF16)
        rm = A.take([513], F32)
        km = A.take([513], F32)
        vm = A.take([513], F32)
        tnames = ['r', 'k0', 'sg', 'asg', 'cum', 'P', 'invP', 'Pp', 'ssk', 'kk', 't1']
        tmp = {k: A.take([512], F32) for k in tnames}
        sqk = A.take([512], BF16)
        PCt = A.take([NT], F32)
        Xb = [A.take([2, 2, 128], BF16) for _ in range(2)]
        Nn = [A.take([2, 128], BF16) for _ in range(2)]
        W1s = [A.take([2, 3, 128], BF16) for _ in range(2)]
        TTs = [A.take([2, 128], BF16) for _ in range(2)]
        BK = [A.take([2, 128], BF16) for _ in range(2)]
        Vt = [A.take([4, 128], BF16) for _ in range(2)]
        Xs = A.take([128], BF16)
        Us = A.take([128], BF16)
        Hs = A.take([64], F32)
        HP = A.take([64], F32)
        Hbz = A.take([2, 64], BF16)
        BKz = [A.take([2, 2, 128], BF16) for _ in range(2)]
        Yp = [A.take([4, 128], F32) for _ in range(2)]
        sqp = A.take([512], F32)
        ynp = A.take([512], F32)
        bon = A.take([512], F32)
        sB = A.take([8], F32)
        yop = A.take([4, 128], BF16)
        rstp = A.take([64], F32)
        Bw = Buf('wrkv')
        Bt_ = {k: Buf('t_' + k) for k in tnames + ['rm', 'km', 'vm', 'sqk', 'PCt']}
        ARb = [Buf(f'AR{n}') for n in range(NT)]
        BTb = [Buf(f'BT{i}') for i in range(4)]
        KTb = [Buf(f'KT{i}') for i in range(4)]
        vTb = [Buf(f'vT{i}') for i in range(4)]
        rkb = [Buf(f'rk{i}') for i in range(4)]
        Bs = {k: Buf('s_' + k) for k in ['X0', 'X1', 'N0', 'N1', 'W10', 'W11', 'TT0', 'TT1', 'BK0', 'BK1', 'Vt0', 'Vt1', 'Xs', 'Us', 'H', 'HP', 'Hb', 'Yp0', 'Yp1', 'BKz0', 'BKz1',
                                          'sqp', 'ynp', 'bon', 'sB', 'yop', 'rstp',
                                          'ps1', 'ps2', 'psN', 'psL', 'pT', 'pT2', 'psX', 'psU', 'psH', 'psY', 'psB', 'psG']}
        for k_, b_ in (('ps1', 0), ('ps2', 2), ('psN', 2), ('psL', 3), ('pT', 4), ('pT2', 4), ('psX', 5), ('psU', 5), ('psH', 5),
                       ('psY', 6), ('psB', 6), ('psG', 7)):
            Bs[k_] = bankB[b_]
        wv3 = w_in.rearrange("(c p) n -> p c n", p=128)
        m4 = cb[:, CB_M4:CB_M4 + 512]
        mS_bc = cb[:, CB_M4:CB_M4 + 128].unsqueeze(1).to_broadcast([128, 2, 128])
        m3_bc = cb[:, CB_M4 + 128:CB_M4 + 512].unsqueeze(1).to_broadcast([128, 2, 384])
        mL_bc = cb[:, CB_ML:CB_ML + 128].unsqueeze(1).to_broadcast([128, 2, 128])
        id_bc = ident.unsqueeze(1).to_broadcast([128, 2, 128])
        sel = cb[:, CB_SEL:CB_SEL + 2]
        ones_bd = cb[:, CB_ONES:CB_ONES + 128]
        ps1 = pp[0][:].rearrange("p (h c) -> p h c", h=2)
        ps2 = bank[2][:, 0:256].rearrange("p (h s) -> p h s", h=2)
        psN = bank[2][:, 256:512].rearrange("p (h s) -> p h s", h=2)
        psL = bank[3].rearrange("p (h c) -> p h c", h=2)
        pTb = bankbf(4)
        pT3 = pTb[:, 0:384].rearrange("p (j t) -> p j t", j=3)
        pT2 = pTb[:, 512:1024].rearrange("p (j t) -> p j t", j=4)
        psX = bank[5][:, 0:128]
        psU = bank[5][:, 128:256]
        psH = bank[5][:, 256:384]
        psY = bank[6][:, 0:128]
        psB = bank[6][:, 128:136]
        psG = bank[7]

        def pair_setup(p):
            vp = V_PAIR + 8 * p
            col = lambda j: vecs[:, vp + j:vp + j + 1]
            ocol = lambda j: om[:, vp + j:vp + j + 1]
            return col, ocol

        def prep_block(p, tb):
            col, ocol = pair_setup(p)
            if tb == 0:
                for j in range(3):
                    dma('pool', wrkv[:, :, j, :], wv3[:, :, j * 512 + p * 128:j * 512 + (p + 1) * 128], [], [Bw])
                for nm_ in ('rm', 'km', 'vm'):
                    tl = {'rm': rm, 'km': km, 'vm': vm}[nm_]
                    S.op('pool', lambda e, tl=tl: e.memset(tl[:, 0:1], 0.0), writes=[Bt_[nm_]])
            blk = slice(tb * 512, (tb + 1) * 512)
            rd = [hTb[4 * tb + i] for i in range(4)] + [Bw]
            for j in range(3):
                for c in range(8):
                    mm(bank[j], wrkv[:, c, j, :], hT[:, c, 1 + tb * 512:1 + (tb + 1) * 512], c == 0, c == 7, rd, [bankB[j]])
            mm(bank[3], W2sb[:, p * 128:(p + 1) * 128], L1[:, blk], True, True, [B_lw, L1B[tb]], [bankB[3]])
            mm(bank[4], A2sb[:, p * 128:(p + 1) * 128], L1[:, blk], True, True, [B_lw, L1B[tb]], [bankB[4]])
            for j, (tl, nm_, dst, dstB) in enumerate(((rm, 'rm', tmp['r'], Bt_['r']), (km, 'km', tmp['k0'], Bt_['k0']), (vm, 'vm', vT[:, blk], vTb[tb]))):
                act(tl[:, 1:513], bank[j], AF.Copy, [bankB[j], B_const], [Bt_[nm_]], scale=col(j))
                stt(dst, bank[j], ocol(j), tl[:, 0:512], ALU.mult, ALU.add, [bankB[j], Bt_[nm_], B_const], [dstB])
                S.op('pool', lambda e, tl=tl: e.tensor_copy(out=tl[:, 0:1], in_=tl[:, 512:513]), reads=[Bt_[nm_]], writes=[Bt_[nm_]])
            r_, k0 = tmp['r'], tmp['k0']
            act(tmp['sg'], bank[3], AF.Sigmoid, [bankB[3], B_const], [Bt_['sg']], bias=col(3))
            act(tmp['asg'], bank[4], AF.Sigmoid, [bankB[4], B_const], [Bt_['asg']], bias=col(4))
            for ch in range(4):
                cs = slice(ch * 128, (ch + 1) * 128)
                S.op('dve', lambda e, cs=cs: e.tensor_tensor_scan(out=tmp['cum'][:, cs], data0=tmp['sg'][:, cs], data1=tmp['sg'][:, cs],
                                                                   initial=0.0, op0=ALU.add, op1=ALU.bypass),
                     reads=[Bt_['sg']], writes=[Bt_['cum']])
            act(tmp['P'], tmp['cum'], AF.Exp, [Bt_['cum']], [Bt_['P']], scale=-C0)
            act(tmp['invP'], tmp['cum'], AF.Exp, [Bt_['cum']], [Bt_['invP']], scale=C0)
            tt(tmp['sg'], tmp['cum'], tmp['sg'], ALU.subtract, [Bt_['cum'], Bt_['sg']], [Bt_['sg']])
            act(tmp['Pp'], tmp['sg'], AF.Exp, [Bt_['sg']], [Bt_['Pp']], scale=-C0)
            S.op('pool', lambda e, tb=tb: e.tensor_copy(out=PCt[:, tb * 4:(tb + 1) * 4],
                                                        in_=tmp['P'].rearrange("p (c t) -> p c t", c=4)[:, :, 127]),
                 reads=[Bt_['P']], writes=[Bt_['PCt']])
            act(sqk, k0, AF.Square, [Bt_['k0'], B_const], [Bt_['sqk']], scale=col(5))
            mm(bank[5], ones_bd, sqk, True, True, [Bt_['sqk'], B_const], [bankB[5]])
            act(tmp['ssk'], bank[5], AF.Ln, [bankB[5]], [Bt_['ssk']])
            act(tmp['ssk'], tmp['ssk'], AF.Exp, [Bt_['ssk']], [Bt_['ssk']], scale=-0.5)
            stt(tmp['kk'], k0, col(5), tmp['ssk'], ALU.mult, ALU.mult, [Bt_['k0'], Bt_['ssk'], B_const], [Bt_['kk']])
            ts(tmp['t1'], tmp['asg'], col(6), ocol(6), ALU.mult, ALU.add, [Bt_['asg'], B_const], [Bt_['t1']])
            tt(tmp['t1'], tmp['t1'], k0, ALU.mult, [Bt_['t1'], Bt_['k0']], [Bt_['t1']])
            arv = AR[:, 4 * tb:4 * tb + 4, :, :]
            c4 = lambda a: a.rearrange("p (c t) -> p c t", c=4)
            stt(arv[:, :, 0, :], c4(tmp['kk']), -1.0, c4(tmp['Pp']), ALU.mult, ALU.mult, [Bt_['kk'], Bt_['Pp']], [ARb[4 * tb + i] for i in range(4)])
            tt(arv[:, :, 1, :], c4(r_), c4(tmp['P']), ALU.mult, [Bt_['r'], Bt_['P']], [ARb[4 * tb + i] for i in range(4)])
            tt(tmp['kk'], tmp['kk'], tmp['asg'], ALU.mult, [Bt_['kk'], Bt_['asg']], [Bt_['kk']])
            tt(BT[:, blk], tmp['kk'], tmp['invP'], ALU.mult, [Bt_['kk'], Bt_['invP']], [BTb[tb]])
            tt(KT[:, blk], tmp['t1'], tmp['invP'], ALU.mult, [Bt_['t1'], Bt_['invP']], [KTb[tb]])
            stt(rkrT[:, blk], r_, col(7), tmp['t1'], ALU.mult, ALU.mult, [Bt_['r'], Bt_['t1'], B_const], [rkb[tb]])

        def make_scan(p):
            col, ocol = pair_setup(p)
            def local(n):
                cs = slice(n * 128, (n + 1) * 128)
                tb = n // 4
                bz = BKz[n % 2]
                W1, TT, W1B, TTB = W1s[n % 2], TTs[n % 2], Bs[f'W1{n % 2}'], Bs[f'TT{n % 2}']
                bzB = Bs[f'BKz{n % 2}']
                for h in range(2):
                    hp = slice(64 * h, 64 * h + 64)
                    S.op('pool', lambda e, h=h, hp=hp: e.tensor_copy(out=bz[hp, 0, h, :], in_=BT[hp, cs]), reads=[BTb[tb]], writes=[bzB])
                    S.op('pool', lambda e, h=h, hp=hp: e.tensor_copy(out=bz[hp, 1, h, :], in_=KT[hp, cs]), reads=[KTb[tb]], writes=[bzB])
                for h in range(2):
                    mm(ps1[:, h, 0:256], bz[:, 0, h, :], AR[:, n, :, :], True, True, [bzB, ARb[n]], [Bs['ps1']])
                    mm(ps1[:, h, 256:512], bz[:, 1, h, :], AR[:, n, :, :], True, True, [bzB, ARb[n]], [Bs['ps1']])
                    mm(ps2[:, h, :], AR[:, n, 0, :], bz[:, 0, h, :], True, True, [bzB, ARb[n]], [Bs['ps2']])
                tt(Xb[0][:, :, 0, :], ps1[:, :, 0:128], mS_bc, ALU.mult, [Bs['ps1'], B_const], [Bs['X0']])
                tt(Nn[0], ps2, mL_bc, ALU.mult, [Bs['ps2'], B_const], [Bs['N0']])
                tt(Xb[1][:, :, 1, :], Xb[0][:, :, 0, :], id_bc, ALU.add, [Bs['X0'], B_const], [Bs['X1']])
                tt(W1, ps1[:, :, 128:512], m3_bc, ALU.mult, [Bs['ps1'], B_const], [W1B])
                yield
                cur = 0
                for k in range(4):
                    nx = 1 - cur
                    lastk = (k == 3)
                    for h in range(2):
                        if lastk:
                            mm(psL[:, h, 128:256], Nn[cur][:, h, :], Xb[cur][:, h, 1, :], True, True, [Bs[f'N{cur}'], Bs[f'X{cur}']], [Bs['psL']])
                        elif k == 0:
                            mm(psL[:, h, 0:128], Nn[cur][:, h, :], Xb[cur][:, h, 0, :], True, True, [Bs[f'N{cur}'], Bs[f'X{cur}']], [Bs['psL']])
                            mm(psN[:, h, :], Xb[cur][:, h, 0, :], Nn[cur][:, h, :], True, True, [Bs[f'N{cur}'], Bs[f'X{cur}']], [Bs['psN']])
                        else:
                            mm(psL[:, h, :], Nn[cur][:, h, :], Xb[cur][:, h, :, :], True, True, [Bs[f'N{cur}'], Bs[f'X{cur}']], [Bs['psL']])
                            mm(psN[:, h, :], Xb[cur][:, h, 0, :], Nn[cur][:, h, :], True, True, [Bs[f'N{cur}'], Bs[f'X{cur}']], [Bs['psN']])
                    if lastk:
                        tt(TT, Xb[cur][:, :, 1, :], psL[:, :, 128:256], ALU.add, [Bs['psL'], Bs[f'X{cur}']], [TTB])
                    else:
                        cp('act', Xb[nx][:, :, 0, :], psL[:, :, 0:128], [Bs['psL']], [Bs[f'X{nx}']])
                        cp('dve', Nn[nx], psN, [Bs['psN']], [Bs[f'N{nx}']])
                        if k > 0:
                            tt(Xb[nx][:, :, 1, :], Xb[cur][:, :, 1, :], psL[:, :, 128:256], ALU.add, [Bs['psL'], Bs[f'X{cur}']], [Bs[f'X{nx}']])
                    cur = nx
                    yield

            def chain(n):
                cs = slice(n * 128, (n + 1) * 128)
                tb = n // 4
                g = (n // 4) % 2
                bk = BK[n % 2]
                bkB = Bs[f'BK{n % 2}']
                vt = Vt[g][:, n % 4, :]
                vtB = Bs[f'Vt{g}']
                W1, TT, W1B, TTB = W1s[n % 2], TTs[n % 2], Bs[f'W1{n % 2}'], Bs[f'TT{n % 2}']
                S.op('pe', lambda e: e.transpose(out=pT3[:, 0, :], in_=vT[:, cs], identity=ident), reads=[vTb[tb], B_const], writes=[Bs['pT']])
                S.op('pe', lambda e: e.transpose(out=pT3[:, 1, :], in_=BT[:, cs], identity=ident), reads=[BTb[tb], B_const], writes=[Bs['pT']])
                S.op('pe', lambda e: e.transpose(out=pT3[:, 2, :], in_=KT[:, cs], identity=ident), reads=[KTb[tb], B_const], writes=[Bs['pT']])
                cp('act', vt, pT3[:, 0, :], [Bs['pT']], [vtB])
                cp('act', bk, pT3[:, 1:3, :], [Bs['pT']], [bkB])
                yield
                for h in range(2):
                    hs = slice(64 * h, 64 * h + 64)
                    if n > 0:
                        mm(psX[:, hs], AR[:, n, 0, :], Hbz[:, h, :], True, False, [ARb[n], Bs['Hb']], [Bs['psX']])
                    mm(psX[:, hs], W1[:, h, 1, :], vt[:, hs], n == 0, True, [W1B, vtB], [Bs['psX']])
                cp('act', Xs, psX, [Bs['psX']], [Bs['Xs']])
                if n > 0:
                    ts(HP, Hs, PCt[:, n:n + 1], None, ALU.mult, None, [Bs['H'], Bt_['PCt']], [Bs['HP']])
                yield
                for h in range(2):
                    hs = slice(64 * h, 64 * h + 64)
                    mm(psU[:, hs], TT[:, h, :], Xs[:, hs], True, True, [TTB, Bs['Xs']], [Bs['psU']])
                cp('dve', Us, psU, [Bs['psU']], [Bs['Us']])
                yield
                for h in range(2):
                    hs = slice(64 * h, 64 * h + 64)
                    if n > 0:
                        mm(psY[:, hs], AR[:, n, 1, :], Hbz[:, h, :], True, False, [ARb[n], Bs['Hb']], [Bs['psY']])
                    mm(psY[:, hs], W1[:, h, 0, :], Us[:, hs], n == 0, False, [W1B, Bs['Us']], [Bs['psY']])
                    mm(psY[:, hs], W1[:, h, 2, :], vt[:, hs], False, True, [W1B, vtB], [Bs['psY']])
                mm(psH, bk[:, 0, :], Us, True, False, [bkB, Bs['Us']], [Bs['psH']])
                mm(psH, bk[:, 1, :], vt, False, True, [bkB, vtB], [Bs['psH']])
                cp('act', Yp[g][:, n % 4, :], psY, [Bs['psY']], [Bs[f'Yp{g}']])
                for h in range(2):
                    hp = slice(64 * h, 64 * h + 64)
                    hs = slice(64 * h, 64 * h + 64)
                    if n > 0:
                        stt(Hs[hp, :], psH[hp, hs], PCt[hp, n:n + 1], HP[hp, :], ALU.mult, ALU.add, [Bs['psH'], Bs['HP'], Bt_['PCt']], [Bs['H']])
                    else:
                        ts(Hs[hp, :], psH[hp, hs], PCt[hp, n:n + 1], None, ALU.mult, None, [Bs['psH'], Bt_['PCt']], [Bs['H']])
                for h in range(2):
                    hp = slice(64 * h, 64 * h + 64)
                    cp('act', Hbz[hp, h, :], Hs[hp, :], [Bs['H']], [Bs['Hb']])
                yield

            def post(tg):
                g = tg % 2
                y3 = Yp[g].rearrange("p j (h e) -> p (j h) e", h=2)
                yB = Bs[f'Yp{g}']
                s1, s2, mean, msq, rstd = (rstp[:, 8 * i:8 * i + 8] for i in range(5))
                v8 = lambda a: a.rearrange("p (j e) -> p j e", j=8)
                S.op('dve', lambda e: e.tensor_reduce(out=s1, in_=y3, axis=AX.X, op=ALU.add), reads=[yB], writes=[Bs['rstp']])
                act(sqp, Yp[g].rearrange("p j c -> p (j c)"), AF.Square, [yB], [Bs['sqp']])
                S.op('dve', lambda e: e.tensor_reduce(out=s2, in_=v8(sqp), axis=AX.X, op=ALU.add), reads=[Bs['sqp']], writes=[Bs['rstp']])
                yield
                ts(mean, s1, 1.0 / 64, None, ALU.mult, None, [Bs['rstp']], [Bs['rstp']])
                tt(msq, mean, mean, ALU.mult, [Bs['rstp']], [Bs['rstp']])
                stt(rstd, s2, 1.0 / 64, msq, ALU.mult, ALU.subtract, [Bs['rstp']], [Bs['rstp']])
                rsqrt_tiny(rstd, rstd, 1.0, RWKV_GN_EPS, [Bs['rstp']], [Bs['rstp']])
                yield
                tt(v8(ynp), y3, mean.unsqueeze(2).to_broadcast([128, 8, 64]), ALU.subtract, [yB, Bs['rstp']], [Bs['ynp']])
                tt(v8(ynp), v8(ynp), rstd.unsqueeze(2).to_broadcast([128, 8, 64]), ALU.mult, [Bs['ynp'], Bs['rstp']], [Bs['ynp']])
                yield
                y4 = ynp.rearrange("p (j c) -> p j c", j=4)
                tt(y4, y4, lnxw[:, p * 128:(p + 1) * 128].unsqueeze(1).to_broadcast([128, 4, 128]), ALU.mult, [Bs['ynp'], B_lw], [Bs['ynp']])
                tt(y4, y4, lnxb[:, p * 128:(p + 1) * 128].unsqueeze(1).to_broadcast([128, 4, 128]), ALU.add, [Bs['ynp'], B_lw], [Bs['ynp']])
                yield
                for j in range(4):
                    n = 4 * tg + j
                    cs = slice(n * 128, (n + 1) * 128)
                    mm(psB[:, 2 * j:2 * j + 2], rkrT[:, cs], sel, True, True, [rkb[tg], B_const], [Bs['psB']])
                    mm(psG[:, j * 128:(j + 1) * 128], L1g[:, cs], G2sb[:, p * 128:(p + 1) * 128], True, True, [L1B[tg], B_lw], [Bs['psG']])
                cp('act', sB, psB, [Bs['psB']], [Bs['sB']])
                yield
                tt(v8(bon), Vt[g].rearrange("p j (h e) -> p (j h) e", h=2), sB.unsqueeze(2).to_broadcast([128, 8, 64]), ALU.mult,
                   [Bs[f'Vt{g}'], Bs['sB']], [Bs['bon']])
                tt(ynp, ynp, bon, ALU.add, [Bs['ynp'], Bs['bon']], [Bs['ynp']])
                yield
                tt(yop.rearrange("p j c -> p (j c)"), ynp, psG, ALU.mult, [Bs['ynp'], Bs['psG']], [Bs['yop']])
                yield
                for j in range(4):
                    S.op('pe', lambda e, j=j: e.transpose(out=pT2[:, j, :], in_=yop[:, j, :], identity=ident), reads=[Bs['yop'], B_const], writes=[Bs['pT2']])
                cp('act', yT[:, p, tg * 512:(tg + 1) * 512], pT2.rearrange("p j t -> p (j t)"), [Bs['pT2']], [yTb[p][4 * tg + j] for j in range(4)])
                yield

            return local, chain, post

        assert A.peak <= WOUT_OFF, ("phase-3 buffers overlap the w_out prefetch region", A.peak, WOUT_OFF)
        wo_v = w_out.rearrange("(c p) n -> p c n", p=128)
        for c in range(8):
            dma('pool', wout[:, c, :], wo_v[:, c, :], [], [woutB])
        for i_ in range(2):
            S.op('pool', lambda e, i_=i_: e.memset(BKz[i_], 0.0), writes=[Bs[f'BKz{i_}']])
        NP = DBG.get('pairs', 4)
        for tb in range(4):
            prep_block(0, tb)
        scans = [make_scan(p) for p in range(NP)]

        def drain(g):
            for _ in g:
                pass
        S.op('pool', lambda e: e.memset(Hs, 0.0), writes=[Bs['H']])
        S.op('pool', lambda e: e.memset(Hbz, 0.0), writes=[Bs['Hb']])
        drain(scans[0][0](0))
        pend = []

        def prep_gen(p_, tb_):
            prep_block(p_, tb_)
            yield

        for p in range(NP):
            local, chain, post = scans[p]
            for n in range(NT):
                while len(pend) > 2:
                    drain(pend.pop(0))
                a = chain(n)
                if n + 1 < NT:
                    b = local(n + 1)
                elif p + 1 < NP:
                    b = scans[p + 1][0](0)
                else:
                    b = iter(())
                done_a = done_b = False
                while not (done_a and done_b):
                    if not done_a:
                        try:
                            next(a)
                        except StopIteration:
                            done_a = True
                    if not done_b:
                        try:
                            next(b)
                        except StopIteration:
                            done_b = True
                    if pend:
                        try:
                            next(pend[0])
                        except StopIteration:
                            pend.pop(0)
                if n % 4 == 3:
                    pend.append(post(n // 4))
                    if p + 1 < NP:
                        pend.append(prep_gen(p + 1, n // 4))
            if p + 1 == NP:
                while pend:
                    drain(pend.pop(0))
            if p + 1 < NP:
                S.op('dve', lambda e: e.memset(Hs, 0.0), writes=[Bs['H']])
                S.op('pool', lambda e: e.memset(Hbz, 0.0), writes=[Bs['Hb']])
        tap('yT', yT, [128, 8, T], [b for l in yTb for b in l])
        S.barrier()


    if 4 in phases:
        S.barrier()
        A.reset(NORM_END)
        xres = A.take([NT, D], F32)
        xresB = [Buf(f'xres{n}') for n in range(NT)]
        P4 = A.mark()
        if 3 not in phases:
            wo_v = w_out.rearrange("(c p) n -> p c n", p=128)
            for c in range(8):
                dma('pool', wout[:, c, :], wo_v[:, c, :], [], [woutB])
        dma('sp', gtab, bct_d[:, 1024:2048], [], [B_gtab])
        for n in range(NT):
            dma('sp', xst[n % 3], xv[n], [], [xstB[n % 3]])
            pb = 2 * (n % 2)
            for half in range(2):
                for c in range(8):
                    mm(bank[pb + half], yT[:, c, n * 128:(n + 1) * 128], wout[:, c, half * 512:(half + 1) * 512], c == 0, c == 7,
                       [yTb[c][n], woutB], [bankB[pb + half]])
            tt(xres[:, n, :], pp[n % 2][:], xst[n % 3], ALU.add, [bankB[pb], bankB[pb + 1], xstB[n % 3]], [xresB[n]])
            norm_stats(n, xres[:, n, :], xresB[n], 1)
        norm_rstd(1)
        for n in range(NT):
            norm_apply(n, xres[:, n, :], xresB[n], 1, 4 + n % 2)
        tap('xres', xres, [128, NT, D], xresB)

    if 5 in phases:
        S.barrier()
        A.reset(P4)
        hid = yT[:, 0:6, :]
        hidB = [Buf(f'hid{i}') for i in range(4)]
        wgu = [A.take([2, 8, 256], BF16) for _ in range(2)]
        wguB = [Buf(f'wgu{i}') for i in range(2)]
        wd = A.take([6, D], BF16)
        wdB = Buf('wd')
        gs = [A.take([514], F32) for _ in range(2)]
        gsB = [Buf(f'gs{i}') for i in range(2)]
        acc = [A.take([512], F32) for _ in range(2)]
        accB = [Buf(f'acc{i}') for i in range(2)]
        sl = [A.take([512], F32) for _ in range(2)]
        slB = [Buf(f'sl{i}') for i in range(2)]
        ost = [xst[0], xst[1]]
        ostB = [xstB[0], xstB[1]]
        dma('sp', gtab, bct_d[:, 2048:3072], [], [B_gtab])
        wg_v = wg_d.rearrange("(c p) n -> p c n", p=128)
        wu_v = wu_d.rearrange("(c p) n -> p c n", p=128)
        wd_v = wd_d.rearrange("(m p) n -> p m n", p=128)
        quarters = [(0, 6), (6, 6), (12, 5), (17, 5)]

        def load_wgu(m):
            wb_ = (m // 2) % 2
            dma('pool', wgu[wb_][:, 0, :, :], wg_v[:, :, m * 128:(m + 2) * 128], [], [wguB[wb_]])
            dma('pool', wgu[wb_][:, 1, :, :], wu_v[:, :, m * 128:(m + 2) * 128], [], [wguB[wb_]])
        load_wgu(0)
        it = 0
        for qi, (m0, nq) in enumerate(quarters):
            dma('pool', wd[:, 0:nq, :], wd_v[:, m0:m0 + nq, :], [], [wdB])
            for ml in range(nq):
                m = m0 + ml
                wb = (m // 2) % 2
                if m % 2 == 0 and m + 2 < NFF:
                    load_wgu(m + 2)
                mc = slice((m % 2) * 128, (m % 2) * 128 + 128)
                vf = V_FFN + 4 * m
                cw = lambda j: vecs[:, vf + j:vf + j + 1]
                for blk in range(4):
                    g_, gB = gs[blk % 2], gsB[blk % 2]
                    a_, aB = acc[it % 2], accB[it % 2]
                    s_, sB_ = sl[it % 2], slB[it % 2]
                    pg, pu = 2 * (it % 4), 2 * (it % 4) + 1
                    it += 1
                    rd = [hTb[4 * blk + i] for i in range(4)] + [wguB[wb]]
                    for c in range(8):
                        mm(bank[pg], wgu[wb][:, 0, c, mc], hT[:, c, 1 + blk * 512:1 + (blk + 1) * 512], c == 0, c == 7, rd, [bankB[pg]])
                    for c in range(8):
                        mm(bank[pu], wgu[wb][:, 1, c, mc], hT[:, c, 1 + blk * 512:1 + (blk + 1) * 512], c == 0, c == 7, rd, [bankB[pu]])
                    if blk == 0:
                        S.op('pool', lambda e, g_=g_: e.memset(g_[:, 0:2], 0.0), writes=[gB])
                    else:
                        gp = gs[(blk - 1) % 2]
                        S.op('pool', lambda e, g_=g_, gp=gp: e.tensor_copy(out=g_[:, 0:2], in_=gp[:, 512:514]), reads=[gsB[(blk - 1) % 2]], writes=[gB])
                    act(g_[:, 2:514], bank[pg], AF.Copy, [bankB[pg]], [gB])
                    act(a_, bank[pg], AF.Identity, [bankB[pg], B_const], [aB], bias=cw(3), scale=cw(2))
                    stt(a_, g_[:, 1:513], cw(1), a_, ALU.mult, ALU.add, [gB, aB, B_const], [aB])
                    stt(a_, g_[:, 0:512], cw(0), a_, ALU.mult, ALU.add, [gB, aB, B_const], [aB])
                    act(s_, a_, AF.Silu, [aB], [sB_])
                    tt(hid[:, ml, blk * 512:(blk + 1) * 512], s_, bank[pu], ALU.mult, [sB_, bankB[pu]], [hidB[blk]])
            last = qi == len(quarters) - 1
            for n in range(NT):
                pb = 2 * (n % 4)
                for half in range(2):
                    for ml in range(nq):
                        mm(bank[pb + half], hid[:, ml, n * 128:(n + 1) * 128], wd[:, ml, half * 512:(half + 1) * 512], ml == 0, ml == nq - 1,
                           [hidB[n // 4], wdB], [bankB[pb + half]])
                tt(xres[:, n, :], pp[n % 4][:], xres[:, n, :], ALU.add, [bankB[pb], bankB[pb + 1], xresB[n]], [xresB[n]])
                if last:
                    ssn = ss_all[:, 2, n:n + 1]
                    rsn = rstd_all[:, 2, n:n + 1]
                    sB2 = statB[n % 4]
                    act(sqj, xres[:, n, :], AF.Square, [xresB[n]], [sqjB, sB2], accum=ssn)
                    rsqrt_tiny(rsn, ssn, 1.0 / D, NORM_EPS, [sB2], [sB2])
                    stt(ost[n % 2], xres[:, n, :], rsn, gtab, ALU.mult, ALU.mult, [xresB[n], sB2, B_gtab], [ostB[n % 2]])
                    dma('sp', ov[n], ost[n % 2], [ostB[n % 2]], [])

    S.barrier(('sp',))
    S.emit(st)
    st.close()
    return nc, tap_out, S, A


def _chunkcols(v):
    v = np.asarray(v, np.float32).reshape(-1, 128)
    return np.ascontiguousarray(v.T)


def prep_shared(inp):
    f = lambda k: np.ascontiguousarray(np.asarray(inp[k], np.float32)[0])
    vecs = np.zeros((128, NV), np.float32)
    vecs[:, V_MUW:V_MUW + 8] = _chunkcols(f("rwkv_mu_w"))
    vecs[:, V_MUA:V_MUA + 8] = _chunkcols(f("rwkv_mu_a"))
    vecs[:, V_MUG:V_MUG + 8] = _chunkcols(f("rwkv_mu_g"))
    names = ["rwkv_mu_r", "rwkv_mu_k", "rwkv_mu_v", "rwkv_w0", "rwkv_a0", "rwkv_k_k", "rwkv_k_a", "rwkv_r_k"]
    for j, nm in enumerate(names):
        cc = _chunkcols(f(nm).reshape(-1))
        for p in range(4):
            vecs[:, V_PAIR + 8 * p + j] = cc[:, p]
    cw = f("ffn_conv_w").reshape(3, DFF)
    cbias = f("ffn_conv_b")
    for j in range(3):
        cc = _chunkcols(cw[j])
        for m in range(NFF):
            vecs[:, V_FFN + 4 * m + j] = cc[:, m]
    cc = _chunkcols(cbias)
    for m in range(NFF):
        vecs[:, V_FFN + 4 * m + 3] = cc[:, m]
    row = np.concatenate([f("norm_mix_g"), f("norm_ffn_g"), np.asarray(inp["norm_final_g"], np.float32),
                          f("rwkv_lnx_w"), f("rwkv_lnx_b"), f("ret_gn_w")])
    bct = np.ascontiguousarray(np.broadcast_to(row[None, :], (128, row.shape[0])))
    cf, cb = make_consts()
    shared = {
        "w_in": f("w_in"), "w_out": f("w_out"), "ffn_w_gate": f("ffn_w_gate"), "ffn_w_up": f("ffn_w_up"),
        "ffn_w_down": f("ffn_w_down"), "rwkv_w1": f("rwkv_w1"), "rwkv_a1": f("rwkv_a1"), "rwkv_g1": f("rwkv_g1"),
        "rwkv_w2": f("rwkv_w2"), "rwkv_a2": f("rwkv_a2"), "rwkv_g2": f("rwkv_g2"),
        "vecs": vecs, "bct": bct, "cf": cf, "cb": cb,
    }
    return shared


_PROG = None


def kernel(**inputs):
    global _PROG
    if _PROG is None:
        _PROG = build_program()[0]
    shared = prep_shared(inputs)
    xs = np.asarray(inputs["x"], np.float32)
    in_maps = [dict(shared, x=np.ascontiguousarray(xs[b])) for b in range(8)]
    res = run_bass_kernel_spmd(_PROG, in_maps, core_ids=list(range(8)))
    return np.stack([np.asarray(r["out"], np.float32) for r in res.results], axis=0)
```

```python
import numpy as np
import ml_dtypes
from contextlib import ExitStack
import concourse.bass as bass
import concourse.mybir as mybir
from concourse.bass_utils import run_bass_kernel_spmd

F32 = mybir.dt.float32
BF16 = mybir.dt.bfloat16
AF = mybir.ActivationFunctionType
ALU = mybir.AluOpType
AX = mybir.AxisListType

QUEUES = ('sp', 'act', 'pool', 'pe', 'dve')

T = 2048
D = 1024
NT = 16
DFF = 2816
NFF = 22
C0 = float(np.exp(-0.5))
NORM_EPS = 1e-6
RWKV_GN_EPS = 64e-5
RET_GN_EPS = 1e-5


class Buf:
    __slots__ = ('name', 'w', 'r')

    def __init__(self, name=''):
        self.name = name
        self.w = None
        self.r = {}


class _Op:
    __slots__ = ('q', 's', 'idx', 'fn', 'waits', 'inc', 'dma')


class Sched:
    def __init__(self, nc):
        self.nc = nc
        self.ops = {q: [] for q in QUEUES}
        self.streams = {}
        self.clock = {q: {} for q in QUEUES}
        self.opclock = {}
        self.nwaits = 0
        self.nops = 0

    def op(self, q, fn, reads=(), writes=(), dma=False):
        if dma:
            ref = writes[0] if len(writes) else (reads[0] if len(reads) else None)
            s = 'dq_' + (ref.name if ref is not None and ref.name else q)
        else:
            s = q
        deps = {}

        def need(st, i):
            if deps.get(st, 0) < i:
                deps[st] = i
        for b in reads:
            if b.w is not None:
                st, i = b.w
                if st == q and q == 'pe':
                    continue
                need(st, i)
        for b in writes:
            if b.w is not None:
                st, i = b.w
                if not (st == q and not dma):
                    need(st, i)
            for st, i in b.r.items():
                if st == q and not dma:
                    continue
                need(st, i)
        ck = self.clock[q]
        waits = []
        for st, i in deps.items():
            if ck.get(st, 0) >= i:
                continue
            waits.append((st, i))
            oc = self.opclock[(st, i)]
            for k, v in oc.items():
                if ck.get(k, 0) < v:
                    ck[k] = v
            if ck.get(st, 0) < i:
                ck[st] = i
            self.streams[st][i - 1].inc = True
        o = _Op()
        o.q = q
        o.s = s
        o.fn = fn
        o.waits = waits
        o.inc = dma
        o.dma = dma
        lst = self.streams.setdefault(s, [])
        lst.append(o)
        o.idx = len(lst)
        self.opclock[(s, o.idx)] = dict(ck)
        self.ops[q].append(o)
        self.nwaits += len(waits)
        self.nops += 1
        for b in writes:
            b.w = (s, o.idx)
            b.r = {}
        for b in reads:
            if b.r.get(s, 0) < o.idx:
                b.r[s] = o.idx
        return o

    def barrier(self, queues=QUEUES):
        tips = {s: len(l) for s, l in self.streams.items() if l}
        for q in queues:
            ck = self.clock[q]
            waits = []
            for s, i in tips.items():
                if s == q and q == 'pe':
                    continue
                if ck.get(s, 0) >= i:
                    continue
                waits.append((s, i))
                self.streams[s][i - 1].inc = True
            for s, i in waits:
                oc = self.opclock[(s, i)]
                for k, v in oc.items():
                    if ck.get(k, 0) < v:
                        ck[k] = v
                ck[s] = i
            if waits:
                o = _Op()
                o.q = q
                o.s = None
                o.fn = None
                o.waits = waits
                o.inc = False
                o.dma = False
                self.ops[q].append(o)

    def emit(self, stack):
        nc = self.nc
        sems = {s: stack.enter_context(nc.semaphore('sem_' + s)) for s in self.streams}
        cnt = {}
        for s, lst in self.streams.items():
            c = 0
            for o in lst:
                if o.dma:
                    c += 16
                elif o.inc:
                    c += 1
                cnt[(s, o.idx)] = c
        self.final_counts = {s: (cnt[(s, len(l))] if l else 0) for s, l in self.streams.items()}
        block = stack.enter_context(nc.Block())

        def run(q, eng):
            for o in self.ops[q]:
                for st, i in o.waits:
                    eng.wait_ge(sems[st], cnt[(st, i)])
                if o.fn is None:
                    continue
                ins = o.fn(eng)
                if o.dma:
                    ins.then_inc(sems[o.s], 16)
                elif o.inc:
                    ins.then_inc(sems[o.s], 1)

        @block.sync
        def _(e):
            run('sp', e)

        @block.scalar
        def _(e):
            run('act', e)

        @block.gpsimd
        def _(e):
            run('pool', e)

        @block.tensor
        def _(e):
            run('pe', e)

        @block.vector
        def _(e):
            run('dve', e)


class Arena:
    def __init__(self, ap, nbytes):
        self.ap = ap
        self.nbytes = nbytes
        self.off = 0
        self.peak = 0

    def take(self, shape, dt):
        esz = 4 if dt == F32 else 2
        n = int(np.prod(shape))
        nb = (n * esz + 63) // 64 * 64
        assert self.off + nb <= self.nbytes, ("arena overflow", self.off, nb, self.nbytes)
        v = self.ap[:, self.off // 4:(self.off + nb) // 4]
        if dt != F32:
            v = v.bitcast(dt)
        v = v[:, 0:n]
        if len(shape) == 2:
            v = v.rearrange("p (a b) -> p a b", a=shape[0])
        elif len(shape) == 3:
            v = v.rearrange("p (a b c) -> p a b c", a=shape[0], b=shape[1])
        elif len(shape) == 4:
            v = v.rearrange("p (a b c d) -> p a b c d", a=shape[0], b=shape[1], c=shape[2])
        self.off += nb
        self.peak = max(self.peak, self.off)
        return v

    def mark(self):
        return self.off

    def reset(self, m):
        self.off = m


V_MUW, V_MUA, V_MUG = 0, 8, 16
V_PAIR = 24
V_FFN = 56
NV = 56 + 4 * NFF
CB_ID, CB_MRET, CB_M4, CB_ML, CB_SEL, CB_ONES = 0, 128, 256, 768, 896, 900
NCB = 1028
CF_COS, CF_SIN, CF_NSIN, CF_XIT, CF_KAT, CF_KAPG, CF_GC = 0, 1024, 2048, 3072, 3584, 4096, 4100
NCF = 4104


def make_consts():
    f32 = np.float32
    p = np.arange(128)
    cf = np.zeros((128, NCF), f32)
    half = 64
    inv_freq = (10000.0 ** (-np.arange(half, dtype=np.float64) / half))
    pos = (np.arange(NT)[None, :] * 128 + p[:, None]).astype(np.float64)
    ang = pos[:, :, None] * inv_freq[None, None, :]
    cf[:, CF_COS:CF_COS + 1024] = np.cos(ang).reshape(128, -1)
    cf[:, CF_SIN:CF_SIN + 1024] = np.sin(ang).reshape(128, -1)
    cf[:, CF_NSIN:CF_NSIN + 1024] = -np.sin(ang).reshape(128, -1)
    lg = np.log(1.0 - 2.0 ** (-5.0 - np.arange(4, dtype=np.float64)))
    i = np.arange(128, dtype=np.float64)
    xi = np.exp((i[None, :] + 1.0) * lg[:, None])
    ka = np.exp(-(i[None, :] + 1.0) * lg[:, None]) * (128.0 ** -0.5)
    cf[:, CF_XIT:CF_XIT + 512] = np.broadcast_to(xi.reshape(1, 512), (128, 512))
    cf[:, CF_KAT:CF_KAT + 512] = np.broadcast_to(ka.reshape(1, 512), (128, 512))
    gC = np.exp(128.0 * lg)
    cf[:, CF_KAPG:CF_KAPG + 4] = (ka.T * gC[None, :])
    cf[:, CF_GC:CF_GC + 4] = gC[None, :]
    cb = np.zeros((128, NCB), f32)
    cb[:, CB_ID:CB_ID + 128] = np.eye(128)
    r = p[:, None]
    c = p[None, :]
    cb[:, CB_MRET:CB_MRET + 128] = (r <= c)
    strict = (r < c).astype(f32)
    incl = (r <= c).astype(f32)
    cb[:, CB_M4:CB_M4 + 512] = np.concatenate([strict, incl, strict, incl], axis=1)
    cb[:, CB_ML:CB_ML + 128] = (c < r)
    cb[0:64, CB_SEL] = 1.0
    cb[64:128, CB_SEL + 1] = 1.0
    cb[0:64, CB_ONES:CB_ONES + 64] = 1.0
    cb[64:128, CB_ONES + 64:CB_ONES + 128] = 1.0
    return cf, cb.astype(ml_dtypes.bfloat16)


DBG = {'ret_chunks': NT, 'ret_steps': 99}


def build_program(taps=None, phases=(1, 2, 3, 4, 5)):
    nc = bass.Bass("TRN2", target_bir_lowering=False)

    def din(name, shape, dt=F32):
        return nc.dram_tensor(name, list(shape), dt, kind="ExternalInput").ap()
    x = din("x", [T, D])
    w_in = din("w_in", [D, 3584])
    w_out = din("w_out", [D, D])
    wg_d = din("ffn_w_gate", [D, DFF])
    wu_d = din("ffn_w_up", [D, DFF])
    wd_d = din("ffn_w_down", [DFF, D])
    w1_d = din("rwkv_w1", [D, 64])
    a1_d = din("rwkv_a1", [D, 64])
    g1_d = din("rwkv_g1", [D, 128])
    w2_d = din("rwkv_w2", [64, 512])
    a2_d = din("rwkv_a2", [64, 512])
    g2_d = din("rwkv_g2", [128, 512])
    vecs_d = din("vecs", [128, NV])
    bct_d = din("bct", [128, 4608])
    cf_d = din("cf", [128, NCF])
    cb_d = din("cb", [128, NCB], BF16)
    out = nc.dram_tensor("out", [T, D], F32, kind="ExternalOutput").ap()
    tap_out = {}
    taps = taps or {}

    S = Sched(nc)
    st = ExitStack()
    ARENA_BYTES = 206 * 1024
    arena_t = st.enter_context(nc.sbuf_tensor("arena", [128, ARENA_BYTES // 4], F32))
    A = Arena(arena_t[:], ARENA_BYTES)
    pp = [st.enter_context(nc.psum_tensor(f"pp{i}", [128, 1024], F32)) for i in range(4)]
    bank = [pp[i // 2][:, (i % 2) * 512:(i % 2) * 512 + 512] for i in range(8)]
    bankB = [Buf(f"bank{i}") for i in range(8)]

    def bankbf(i):
        return bank[i].bitcast(BF16)

    def act(out_, in_, func, r, w, bias=None, scale=None, accum=None):
        kw = {}
        if bias is not None:
            kw['bias'] = bias
        if scale is not None:
            kw['scale'] = scale
        if accum is not None:
            kw['accum_out'] = accum
        S.op('act', lambda e: e.activation(out=out_, in_=in_, func=func, **kw), reads=r, writes=w)

    def tt(out_, a, b, op, r, w, q='dve'):
        S.op(q, lambda e: e.tensor_tensor(out=out_, in0=a, in1=b, op=op), reads=r, writes=w)

    def ts(out_, a, s1, s2, op0, op1, r, w, q='dve'):
        if s2 is None:
            S.op(q, lambda e: e.tensor_scalar(out=out_, in0=a, scalar1=s1, scalar2=None, op0=op0), reads=r, writes=w)
        else:
            S.op(q, lambda e: e.tensor_scalar(out=out_, in0=a, scalar1=s1, scalar2=s2, op0=op0, op1=op1), reads=r, writes=w)

    def stt(out_, a, s, b, op0, op1, r, w):
        S.op('dve', lambda e: e.scalar_tensor_tensor(out=out_, in0=a, scalar=s, in1=b, op0=op0, op1=op1), reads=r, writes=w)

    def mm(out_, lhsT, rhs, start, stop, r, w):
        S.op('pe', lambda e: e.matmul(out=out_, lhsT=lhsT, rhs=rhs, start=start, stop=stop), reads=r, writes=w)

    def mm2(out_, lhsT, rhs, start, stop, r, w):
        if lhsT.shape[0] == 128:
            mm(out_, lhsT[0:64], rhs[0:64], start, False, r, w)
            mm(out_, lhsT[64:128], rhs[64:128], False, stop, r, w)
        else:
            mm(out_, lhsT, rhs, start, stop, r, w)

    def dma(q, out_, in_, r, w, **kw):
        S.op(q, lambda e: e.dma_start(out=out_, in_=in_, **kw), reads=r, writes=w, dma=True)

    def cp(q, out_, in_, r, w):
        if q == 'act':
            act(out_, in_, AF.Copy, r, w)
        else:
            S.op(q, lambda e: e.tensor_copy(out=out_, in_=in_), reads=r, writes=w)

    def rsqrt_tiny(dst, src, scale, eps, r, w):
        ts(dst, src, scale, eps, ALU.mult, ALU.add, r, w)
        act(dst, dst, AF.Ln, w, w)
        act(dst, dst, AF.Exp, w, w, scale=-0.5)

    hT = A.take([8, T + 1], BF16)
    yT = A.take([8, T], BF16)
    cb = A.take([NCB], BF16)
    vecs = A.take([NV], F32)
    om = A.take([NV], F32)
    mhalf = A.take([4], F32)
    gtab = A.take([1024], F32)
    stat = A.take([64], F32)
    ss_all = A.take([3, NT], F32)
    rstd_all = A.take([3, NT], F32)
    B_const = Buf('const')
    B_gtab = Buf('gtab')
    hTb = [Buf(f'hT{n}') for n in range(NT)]
    yTb = [[Buf(f'yT{c}_{n}') for n in range(NT)] for c in range(8)]
    ident = cb[:, CB_ID:CB_ID + 128]
    PERSIST = A.mark()
    WOUT_OFF = ARENA_BYTES - 8 * D * 2
    wout = arena_t[:, WOUT_OFF // 4:ARENA_BYTES // 4].bitcast(BF16).rearrange("p (c n) -> p c n", c=8)
    woutB = Buf('wout')

    def tap(name, ap, shape, reads):
        if name in taps:
            d = nc.dram_tensor("tap_" + name, list(shape), ap.dtype, kind="ExternalOutput").ap()
            tap_out[name] = d
            dma('sp', d, ap, reads, [])

    dma('sp', cb, cb_d, [], [B_const])
    dma('sp', vecs, vecs_d, [], [B_const])
    dma('sp', gtab, bct_d[:, 0:1024], [], [B_gtab])
    S.op('pool', lambda e: e.memset(mhalf, -0.5), writes=[B_const])
    ts(om, vecs, -1.0, 1.0, ALU.mult, ALU.add, [B_const], [B_const])
    S.op('pool', lambda e: e.memset(hT[:, :, 0:1], 0.0), writes=[hTb[0]])

    xst = [A.take([D], F32) for _ in range(3)]
    xstB = [Buf(f'xst{i}') for i in range(3)]
    hb = [A.take([D], BF16) for _ in range(2)]
    hbB = [Buf(f'hb{i}') for i in range(2)]
    sqj = A.take([D], BF16)
    sqjB = Buf('sqj')
    statB = [Buf(f'stat{i}') for i in range(4)]
    NORM_END = A.mark()

    ssB = [Buf(f'ss{i}') for i in range(3)]
    rsB = [Buf(f'rs{i}') for i in range(3)]

    def norm_stats(n, src, srcB, which):
        act(sqj, src, AF.Square, [srcB], [sqjB, ssB[which]], accum=ss_all[:, which, n:n + 1])

    def norm_rstd(which, lo=0, hi=NT):
        rsqrt_tiny(rstd_all[:, which, lo:hi], ss_all[:, which, lo:hi], 1.0 / D, NORM_EPS, [ssB[which]], [rsB[which]])

    def norm_apply(n, src, srcB, which, pbank):
        h = hb[n % 2]
        stt(h, src, rstd_all[:, which, n:n + 1], gtab, ALU.mult, ALU.mult, [srcB, rsB[which], B_gtab], [hbB[n % 2]])
        pt = bankbf(pbank).rearrange("p (c t) -> p c t", c=8)
        for c in range(8):
            S.op('pe', lambda e, c=c: e.transpose(out=pt[:, c, :], in_=h[:, c * 128:(c + 1) * 128], identity=ident),
                 reads=[hbB[n % 2], B_const], writes=[bankB[pbank]])
        cp('act', hT[:, :, 1 + n * 128:1 + (n + 1) * 128], pt, [bankB[pbank]], [hTb[n]])

    xv = x.rearrange("(n p) d -> n p d", p=128)
    ov = out.rearrange("(n p) d -> n p d", p=128)
    if 2 in phases:
        cf = A.take([NCF], F32)
        dma('sp', cf, cf_d, [], [B_const])
        wret = A.take([8, 2048], BF16)
        wretB = Buf('wret')
        wv = w_in.rearrange("(c p) n -> p c n", p=128)
        for c in range(8):
            dma('pool', wret[:, c, :], wv[:, c, 1536:3584], [], [wretB])
        P2START = A.mark()
    XC_OFF = 168 * 1024
    assert XC_OFF + 8 * D * 4 <= ARENA_BYTES
    xc = arena_t[:, XC_OFF // 4:XC_OFF // 4 + 8 * D].rearrange("p (s d) -> p s d", s=8)
    xcB = [Buf(f'xc{i}') for i in range(8)]
    for lo_ in (0, 8):
        for n in range(lo_, lo_ + 8):
            dma('sp', xc[:, n % 8, :], xv[n], [], [xcB[n % 8]])
            norm_stats(n, xc[:, n % 8, :], xcB[n % 8], 0)
        norm_rstd(0, lo_, lo_ + 8)
        for n in range(lo_, lo_ + 8):
            norm_apply(n, xc[:, n % 8, :], xcB[n % 8], 0, n % 2)
    tap('hT', hT, [128, 8, T + 1], hTb)

    if 2 in phases:
        A.reset(P2START)
        gnw = A.take([512], F32)
        dma('sp', gnw, bct_d[:, 4096:4608], [], [B_const])
        qa = A.take([512], F32)
        qb = A.take([512], F32)
        qrot = A.take([512], BF16)
        krot = A.take([512], BF16)
        qT = A.take([4, 128], BF16)
        kT = A.take([4, 128], BF16)
        PT = A.take([4, 128], BF16)
        Vb = A.take([512], BF16)
        Vk = A.take([512], BF16)
        R = A.take([512], F32)
        Rt = A.take([512], F32)
        Rb = A.take([512], BF16)
        sqy = A.take([512], F32)
        yn = A.take([512], F32)
        sgt = A.take([512], F32)
        yo = A.take([512], BF16)
        rst = A.take([32], F32)
        assert A.off <= 168 * 1024, ('retention buffers overlap the x cache', A.off)
        Bq = {k: Buf('r_' + k) for k in ['qa', 'qb', 'qrot', 'krot', 'qT', 'kT', 'PT', 'Vb', 'Vk', 'R', 'Rt', 'Rb', 'sqy', 'yn', 'sgt', 'yo', 'rst']}
        kapg_bc = cf[:, CF_KAPG:CF_KAPG + 4].unsqueeze(2).to_broadcast([128, 4, 128])
        gC_bc = cf[:, CF_GC:CF_GC + 4].unsqueeze(2).to_broadcast([128, 4, 128])
        xiT = cf[:, CF_XIT:CF_XIT + 512].rearrange("p (h t) -> p h t", h=4)
        kaT = cf[:, CF_KAT:CF_KAT + 512].rearrange("p (h t) -> p h t", h=4)
        mret_bc = cb[:, CB_MRET:CB_MRET + 128].unsqueeze(1).to_broadcast([128, 4, 128])
        PQ, PK, PV, PG, PTB, PS, PY, PKV = range(8)

        def v4(ap):
            return ap.rearrange("p (h e) -> p h e", h=4)

        def rot(ps, psB, dst, dstB, n):
            cosb = cf[:, CF_COS + n * 64:CF_COS + (n + 1) * 64].unsqueeze(1).unsqueeze(1).to_broadcast([128, 4, 2, 64])
            sinb = cf[:, CF_SIN + n * 64:CF_SIN + (n + 1) * 64].unsqueeze(1).to_broadcast([128, 4, 64])
            nsinb = cf[:, CF_NSIN + n * 64:CF_NSIN + (n + 1) * 64].unsqueeze(1).to_broadcast([128, 4, 64])
            p4 = ps.rearrange("p (h two f) -> p h two f", h=4, two=2)
            tt(qa.rearrange("p (h two f) -> p h two f", h=4, two=2), p4, cosb, ALU.mult, [psB, B_const], [Bq['qa']])
            qb4 = qb.rearrange("p (h two f) -> p h two f", h=4, two=2)
            tt(qb4[:, :, 0, :], p4[:, :, 1, :], nsinb, ALU.mult, [psB, B_const], [Bq['qb']])
            tt(qb4[:, :, 1, :], p4[:, :, 0, :], sinb, ALU.mult, [psB, B_const], [Bq['qb']])
            tt(dst, qa, qb, ALU.add, [Bq['qa'], Bq['qb']], [dstB])

        for n in range(DBG['ret_chunks']):
            RS = DBG['ret_steps']
            def proj(nn):
                tok = slice(1 + nn * 128, 1 + (nn + 1) * 128)
                for j, pb in enumerate((PQ, PK, PV, PG)):
                    for c in range(8):
                        mm(bank[pb], hT[:, c, tok], wret[:, c, j * 512:(j + 1) * 512], c == 0, c == 7,
                           [hTb[nn], wretB], [bankB[pb]])
            if n == 0:
                proj(0)
            act(sgt, bank[PG], AF.Silu, [bankB[PG]], [Bq['sgt']])
            rot(bank[PQ], bankB[PQ], qrot, Bq['qrot'], n)
            rot(bank[PK], bankB[PK], krot, Bq['krot'], n)
            if RS < 3:
                continue
            ptb = bankbf(PTB).rearrange("p (c t) -> p c t", c=8)
            for h in range(4):
                S.op('pe', lambda e, h=h: e.transpose(out=ptb[:, h, :], in_=qrot[:, h * 128:(h + 1) * 128], identity=ident),
                     reads=[Bq['qrot'], B_const], writes=[bankB[PTB]])
            for h in range(4):
                S.op('pe', lambda e, h=h: e.transpose(out=ptb[:, 4 + h, :], in_=krot[:, h * 128:(h + 1) * 128], identity=ident),
                     reads=[Bq['krot'], B_const], writes=[bankB[PTB]])
            tt(qT, ptb[:, 0:4, :], xiT, ALU.mult, [bankB[PTB], B_const], [Bq['qT']])
            tt(kT, ptb[:, 4:8, :], kaT, ALU.mult, [bankB[PTB], B_const], [Bq['kT']])
            if RS < 4:
                continue
            ps4 = v4(bank[PS])
            for h in range(4):
                mm(ps4[:, h, :], kT[:, h, :], qT[:, h, :], True, True, [Bq['kT'], Bq['qT']], [bankB[PS]])
            tt(PT, ps4, mret_bc, ALU.mult, [bankB[PS], B_const], [Bq['PT']])
            if RS < 5:
                continue
            cp('act', Vb, bank[PV], [bankB[PV]], [Bq['Vb']])
            tt(v4(Vk), v4(bank[PV]), kapg_bc, ALU.mult, [bankB[PV], B_const], [Bq['Vk']])
            if RS < 6:
                continue
            py4 = v4(bank[PY])
            for h in range(4):
                mm(py4[:, h, :], PT[:, h, :], Vb[:, h * 128:(h + 1) * 128], True, n == 0, [Bq['PT'], Bq['Vb']], [bankB[PY]])
                if n > 0:
                    mm(py4[:, h, :], qT[:, h, :], Rb[:, h * 128:(h + 1) * 128], False, True, [Bq['qT'], Bq['Rb']], [bankB[PY]])
            if RS < 7:
                continue
            if n < NT - DBG.get('skiplast', 0):
                pkv4 = v4(bank[PKV])
                for h in range(4):
                    mm(pkv4[:, h, :], krot[:, h * 128:(h + 1) * 128], Vk[:, h * 128:(h + 1) * 128], True, True,
                       [Bq['krot'], Bq['Vk']], [bankB[PKV]])
                if n == 0:
                    cp('dve', R, bank[PKV], [bankB[PKV]], [Bq['R']])
                else:
                    tt(v4(Rt), v4(R), gC_bc, ALU.mult, [Bq['R'], B_const], [Bq['Rt']])
                    tt(R, Rt, bank[PKV], ALU.add, [Bq['Rt'], bankB[PKV]], [Bq['R']])
                cp('pool', Rb, R, [Bq['R']], [Bq['Rb']])
            if n + 1 < NT:
                proj(n + 1)
            s1 = rst[:, 0:4]
            s2 = rst[:, 4:8]
            mean = rst[:, 8:12]
            msq = rst[:, 12:16]
            rstd = rst[:, 16:20]
            S.op('dve', lambda e: e.tensor_reduce(out=s1, in_=py4, axis=AX.X, op=ALU.add), reads=[bankB[PY]], writes=[Bq['rst']])
            act(sqy, bank[PY], AF.Square, [bankB[PY]], [Bq['sqy']])
            S.op('dve', lambda e: e.tensor_reduce(out=s2, in_=v4(sqy), axis=AX.X, op=ALU.add), reads=[Bq['sqy']], writes=[Bq['rst']])
            ts(mean, s1, 1.0 / 128, None, ALU.mult, None, [Bq['rst']], [Bq['rst']])
            tt(msq, mean, mean, ALU.mult, [Bq['rst']], [Bq['rst']])
            stt(rstd, s2, 1.0 / 128, msq, ALU.mult, ALU.subtract, [Bq['rst']], [Bq['rst']])
            rsqrt_tiny(rstd, rstd, 1.0, RET_GN_EPS, [Bq['rst']], [Bq['rst']])
            tt(v4(yn), py4, mean.unsqueeze(2).to_broadcast([128, 4, 128]), ALU.subtract, [bankB[PY], Bq['rst']], [Bq['yn']])
            tt(v4(yn), v4(yn), rstd.unsqueeze(2).to_broadcast([128, 4, 128]), ALU.mult, [Bq['yn'], Bq['rst']], [Bq['yn']])
            tt(yn, yn, gnw, ALU.mult, [Bq['yn'], B_const], [Bq['yn']])
            tt(yo, yn, sgt, ALU.mult, [Bq['yn'], Bq['sgt']], [Bq['yo']])
            if RS < 9:
                continue
            for h in range(4):
                S.op('pe', lambda e, h=h: e.transpose(out=ptb[:, h, :], in_=yo[:, h * 128:(h + 1) * 128], identity=ident),
                     reads=[Bq['yo'], B_const], writes=[bankB[PTB]])
            cp('act', yT[:, 4:8, n * 128:(n + 1) * 128], ptb[:, 0:4, :], [bankB[PTB]], [yTb[4 + h][n] for h in range(4)])
        if 3 not in phases:
            tap('yT', yT, [128, 8, T], [b for l in yTb for b in l])
        S.barrier()


    if 3 in phases:
        S.barrier()
        A.reset(PERSIST)
        wl_f = A.take([8, 128], F32)
        gl_f = A.take([8, 128], F32)
        W1A = A.take([8, 128], BF16)
        W1B = A.take([8, 128], BF16)
        G1A = A.take([8, 128], BF16)
        G1B = A.take([8, 128], BF16)
        W2sb = A.take([512], BF16)
        A2sb = A.take([512], BF16)
        G2sb = A.take([512], BF16)
        L1 = A.take([T], BF16)
        L1g = A.take([T], BF16)
        lnxw = A.take([512], F32)
        lnxb = A.take([512], F32)
        B_lw = Buf('loraw')
        L1B = [Buf(f'L1_{i}') for i in range(4)]
        dma('sp', wl_f[:, :, 0:64], w1_d.rearrange("(c p) k -> p c k", p=128), [], [B_lw])
        dma('sp', wl_f[:, :, 64:128], a1_d.rearrange("(c p) k -> p c k", p=128), [], [B_lw])
        dma('sp', gl_f, g1_d.rearrange("(c p) k -> p c k", p=128), [], [B_lw])
        dma('sp', lnxw, bct_d[:, 3072:3584], [], [B_lw])
        dma('sp', lnxb, bct_d[:, 3584:4096], [], [B_lw])
        S.op('pool', lambda e: e.memset(W2sb, 0.0), writes=[B_lw])
        S.op('pool', lambda e: e.memset(A2sb, 0.0), writes=[B_lw])
        dma('pool', W2sb[0:64, :], w2_d, [B_lw], [B_lw])
        dma('pool', A2sb[64:128, :], a2_d, [B_lw], [B_lw])
        dma('pool', G2sb, g2_d, [], [B_lw])

        def vb(tab, col, k):
            return tab[:, col:col + 8].unsqueeze(2).to_broadcast([128, 8, k])
        tt(W1A[:, :, 0:64], wl_f[:, :, 0:64], vb(om, V_MUW, 64), ALU.mult, [B_lw, B_const], [B_lw])
        tt(W1A[:, :, 64:128], wl_f[:, :, 64:128], vb(om, V_MUA, 64), ALU.mult, [B_lw, B_const], [B_lw])
        tt(W1B[:, :, 0:64], wl_f[:, :, 0:64], vb(vecs, V_MUW, 64), ALU.mult, [B_lw, B_const], [B_lw])
        tt(W1B[:, :, 64:128], wl_f[:, :, 64:128], vb(vecs, V_MUA, 64), ALU.mult, [B_lw, B_const], [B_lw])
        tt(G1A, gl_f, vb(om, V_MUG, 128), ALU.mult, [B_lw, B_const], [B_lw])
        tt(G1B, gl_f, vb(vecs, V_MUG, 128), ALU.mult, [B_lw, B_const], [B_lw])
        for tb in range(4):
            rd = [hTb[4 * tb + i] for i in range(4)] + ([hTb[4 * tb - 1]] if tb > 0 else []) + [B_lw]
            for (WA, WB, pb) in ((W1A, W1B, 0), (G1A, G1B, 1)):
                for c in range(8):
                    mm(bank[pb], WA[:, c, :], hT[:, c, 1 + tb * 512:1 + (tb + 1) * 512], c == 0, False, rd, [bankB[pb]])
                    mm(bank[pb], WB[:, c, :], hT[:, c, tb * 512:(tb + 1) * 512], False, c == 7, rd, [bankB[pb]])
            blk = slice(tb * 512, (tb + 1) * 512)
            act(L1[0:64, blk], bank[0][0:64, :], AF.Tanh, [bankB[0]], [L1B[tb]])
            act(L1[64:128, blk], bank[0][64:128, :], AF.Copy, [bankB[0]], [L1B[tb]])
            act(L1g[:, blk], bank[1], AF.Sigmoid, [bankB[1]], [L1B[tb]])

        wrkv = A.take([8, 3, 128], BF16)
        AR = A.take([NT, 2, 128], BF16)
        BT = A.take([T], BF16)
        KT = A.take([T], BF16)
        vT = A.take([T], BF16)
        rkrT = A.take([T], BF16)
        rm = A.take([513], F32)
        km = A.take([513], F32)
        vm = A.take([513], F32)
        tnames = ['r', 'k0', 'sg', 'asg', 'cum', 'P', 'invP', 'Pp', 'ssk', 'kk', 't1']
        tmp = {k: A.take([512], F32) for k in tnames}
        sqk = A.take([512], BF16)
        PCt = A.take([NT], F32)
        Xb = [A.take([2, 2, 128], BF16) for _ in range(2)]
        Nn = [A.take([2, 128], BF16) for _ in range(2)]
        W1s = [A.take([2, 3, 128], BF16) for _ in range(2)]
        TTs = [A.take([2, 128], BF16) for _ in range(2)]
        BK = [A.take([2, 128], BF16) for _ in range(2)]
        Vt = [A.take([4, 128], BF16) for _ in range(2)]
        Xs = A.take([128], BF16)
        Us = A.take([128], BF16)
        Hs = A.take([64], F32)
        HP = A.take([64], F32)
        Hbz = A.take([2, 64], BF16)
        BKz = [A.take([2, 2, 128], BF16) for _ in range(2)]
        Yp = [A.take([4, 128], F32) for _ in range(2)]
        sqp = A.take([512], F32)
        ynp = A.take([512], F32)
        bon = A.take([512], F32)
        sB = A.take([8], F32)
        yop = A.take([4, 128], BF16)
        rstp = A.take([64], F32)
        Bw = Buf('wrkv')
        Bt_ = {k: Buf('t_' + k) for k in tnames + ['rm', 'km', 'vm', 'sqk', 'PCt']}
        ARb = [Buf(f'AR{n}') for n in range(NT)]
        BTb = [Buf(f'BT{i}') for i in range(4)]
        KTb = [Buf(f'KT{i}') for i in range(4)]
        vTb = [Buf(f'vT{i}') for i in range(4)]
        rkb = [Buf(f'rk{i}') for i in range(4)]
        Bs = {k: Buf('s_' + k) for k in ['X0', 'X1', 'N0', 'N1', 'W10', 'W11', 'TT0', 'TT1', 'BK0', 'BK1', 'Vt0', 'Vt1', 'Xs', 'Us', 'H', 'HP', 'Hb', 'Yp0', 'Yp1', 'BKz0', 'BKz1',
                                          'sqp', 'ynp', 'bon', 'sB', 'yop', 'rstp',
                                          'ps1', 'ps2', 'psN', 'psL', 'pT', 'pT2', 'psX', 'psU', 'psH', 'psY', 'psB', 'psG']}
        for k_, b_ in (('ps1', 0), ('ps2', 2), ('psN', 2), ('psL', 3), ('pT', 4), ('pT2', 4), ('psX', 5), ('psU', 5), ('psH', 5),
                       ('psY', 6), ('psB', 6), ('psG', 7)):
            Bs[k_] = bankB[b_]
        wv3 = w_in.rearrange("(c p) n -> p c n", p=128)
        m4 = cb[:, CB_M4:CB_M4 + 512]
        mS_bc = cb[:, CB_M4:CB_M4 + 128].unsqueeze(1).to_broadcast([128, 2, 128])
        m3_bc = cb[:, CB_M4 + 128:CB_M4 + 512].unsqueeze(1).to_broadcast([128, 2, 384])
        mL_bc = cb[:, CB_ML:CB_ML + 128].unsqueeze(1).to_broadcast([128, 2, 128])
        id_bc = ident.unsqueeze(1).to_broadcast([128, 2, 128])
        sel = cb[:, CB_SEL:CB_SEL + 2]
        ones_bd = cb[:, CB_ONES:CB_ONES + 128]
        ps1 = pp[0][:].rearrange("p (h c) -> p h c", h=2)
        ps2 = bank[2][:, 0:256].rearrange("p (h s) -> p h s", h=2)
        psN = bank[2][:, 256:512].rearrange("p (h s) -> p h s", h=2)
        psL = bank[3].rearrange("p (h c) -> p h c", h=2)
        pTb = bankbf(4)
        pT3 = pTb[:, 0:384].rearrange("p (j t) -> p j t", j=3)
        pT2 = pTb[:, 512:1024].rearrange("p (j t) -> p j t", j=4)
        psX = bank[5][:, 0:128]
        psU = bank[5][:, 128:256]
        psH = bank[5][:, 256:384]
        psY = bank[6][:, 0:128]
        psB = bank[6][:, 128:136]
        psG = bank[7]

        def pair_setup(p):
            vp = V_PAIR + 8 * p
            col = lambda j: vecs[:, vp + j:vp + j + 1]
            ocol = lambda j: om[:, vp + j:vp + j + 1]
            return col, ocol

        def prep_block(p, tb):
            col, ocol = pair_setup(p)
            if tb == 0:
                for j in range(3):
                    dma('pool', wrkv[:, :, j, :], wv3[:, :, j * 512 + p * 128:j * 512 + (p + 1) * 128], [], [Bw])
                for nm_ in ('rm', 'km', 'vm'):
                    tl = {'rm': rm, 'km': km, 'vm': vm}[nm_]
                    S.op('pool', lambda e, tl=tl: e.memset(tl[:, 0:1], 0.0), writes=[Bt_[nm_]])
            blk = slice(tb * 512, (tb + 1) * 512)
            rd = [hTb[4 * tb + i] for i in range(4)] + [Bw]
            for j in range(3):
                for c in range(8):
                    mm(bank[j], wrkv[:, c, j, :], hT[:, c, 1 + tb * 512:1 + (tb + 1) * 512], c == 0, c == 7, rd, [bankB[j]])
            mm(bank[3], W2sb[:, p * 128:(p + 1) * 128], L1[:, blk], True, True, [B_lw, L1B[tb]], [bankB[3]])
            mm(bank[4], A2sb[:, p * 128:(p + 1) * 128], L1[:, blk], True, True, [B_lw, L1B[tb]], [bankB[4]])
            for j, (tl, nm_, dst, dstB) in enumerate(((rm, 'rm', tmp['r'], Bt_['r']), (km, 'km', tmp['k0'], Bt_['k0']), (vm, 'vm', vT[:, blk], vTb[tb]))):
                act(tl[:, 1:513], bank[j], AF.Copy, [bankB[j], B_const], [Bt_[nm_]], scale=col(j))
                stt(dst, bank[j], ocol(j), tl[:, 0:512], ALU.mult, ALU.add, [bankB[j], Bt_[nm_], B_const], [dstB])
                S.op('pool', lambda e, tl=tl: e.tensor_copy(out=tl[:, 0:1], in_=tl[:, 512:513]), reads=[Bt_[nm_]], writes=[Bt_[nm_]])
            r_, k0 = tmp['r'], tmp['k0']
            act(tmp['sg'], bank[3], AF.Sigmoid, [bankB[3], B_const], [Bt_['sg']], bias=col(3))
            act(tmp['asg'], bank[4], AF.Sigmoid, [bankB[4], B_const], [Bt_['asg']], bias=col(4))
            for ch in range(4):
                cs = slice(ch * 128, (ch + 1) * 128)
                S.op('dve', lambda e, cs=cs: e.tensor_tensor_scan(out=tmp['cum'][:, cs], data0=tmp['sg'][:, cs], data1=tmp['sg'][:, cs],
                                                                   initial=0.0, op0=ALU.add, op1=ALU.bypass),
                     reads=[Bt_['sg']], writes=[Bt_['cum']])
            act(tmp['P'], tmp['cum'], AF.Exp, [Bt_['cum']], [Bt_['P']], scale=-C0)
            act(tmp['invP'], tmp['cum'], AF.Exp, [Bt_['cum']], [Bt_['invP']], scale=C0)
            tt(tmp['sg'], tmp['cum'], tmp['sg'], ALU.subtract, [Bt_['cum'], Bt_['sg']], [Bt_['sg']])
            act(tmp['Pp'], tmp['sg'], AF.Exp, [Bt_['sg']], [Bt_['Pp']], scale=-C0)
            S.op('pool', lambda e, tb=tb: e.tensor_copy(out=PCt[:, tb * 4:(tb + 1) * 4],
                                                        in_=tmp['P'].rearrange("p (c t) -> p c t", c=4)[:, :, 127]),
                 reads=[Bt_['P']], writes=[Bt_['PCt']])
            act(sqk, k0, AF.Square, [Bt_['k0'], B_const], [Bt_['sqk']], scale=col(5))
            mm(bank[5], ones_bd, sqk, True, True, [Bt_['sqk'], B_const], [bankB[5]])
            act(tmp['ssk'], bank[5], AF.Ln, [bankB[5]], [Bt_['ssk']])
            act(tmp['ssk'], tmp['ssk'], AF.Exp, [Bt_['ssk']], [Bt_['ssk']], scale=-0.5)
            stt(tmp['kk'], k0, col(5), tmp['ssk'], ALU.mult, ALU.mult, [Bt_['k0'], Bt_['ssk'], B_const], [Bt_['kk']])
            ts(tmp['t1'], tmp['asg'], col(6), ocol(6), ALU.mult, ALU.add, [Bt_['asg'], B_const], [Bt_['t1']])
            tt(tmp['t1'], tmp['t1'], k0, ALU.mult, [Bt_['t1'], Bt_['k0']], [Bt_['t1']])
            arv = AR[:, 4 * tb:4 * tb + 4, :, :]
            c4 = lambda a: a.rearrange("p (c t) -> p c t", c=4)
            stt(arv[:, :, 0, :], c4(tmp['kk']), -1.0, c4(tmp['Pp']), ALU.mult, ALU.mult, [Bt_['kk'], Bt_['Pp']], [ARb[4 * tb + i] for i in range(4)])
            tt(arv[:, :, 1, :], c4(r_), c4(tmp['P']), ALU.mult, [Bt_['r'], Bt_['P']], [ARb[4 * tb + i] for i in range(4)])
            tt(tmp['kk'], tmp['kk'], tmp['asg'], ALU.mult, [Bt_['kk'], Bt_['asg']], [Bt_['kk']])
            tt(BT[:, blk], tmp['kk'], tmp['invP'], ALU.mult, [Bt_['kk'], Bt_['invP']], [BTb[tb]])
            tt(KT[:, blk], tmp['t1'], tmp['invP'], ALU.mult, [Bt_['t1'], Bt_['invP']], [KTb[tb]])
            stt(rkrT[:, blk], r_, col(7), tmp['t1'], ALU.mult, ALU.mult, [Bt_['r'], Bt_['t1'], B_const], [rkb[tb]])

        def make_scan(p):
            col, ocol = pair_setup(p)
            def local(n):
                cs = slice(n * 128, (n + 1) * 128)
                tb = n // 4
                bz = BKz[n % 2]
                W1, TT, W1B, TTB = W1s[n % 2], TTs[n % 2], Bs[f'W1{n % 2}'], Bs[f'TT{n % 2}']
                bzB = Bs[f'BKz{n % 2}']
                for h in range(2):
                    hp = slice(64 * h, 64 * h + 64)
                    S.op('pool', lambda e, h=h, hp=hp: e.tensor_copy(out=bz[hp, 0, h, :], in_=BT[hp, cs]), reads=[BTb[tb]], writes=[bzB])
                    S.op('pool', lambda e, h=h, hp=hp: e.tensor_copy(out=bz[hp, 1, h, :], in_=KT[hp, cs]), reads=[KTb[tb]], writes=[bzB])
                for h in range(2):
                    mm(ps1[:, h, 0:256], bz[:, 0, h, :], AR[:, n, :, :], True, True, [bzB, ARb[n]], [Bs['ps1']])
                    mm(ps1[:, h, 256:512], bz[:, 1, h, :], AR[:, n, :, :], True, True, [bzB, ARb[n]], [Bs['ps1']])
                    mm(ps2[:, h, :], AR[:, n, 0, :], bz[:, 0, h, :], True, True, [bzB, ARb[n]], [Bs['ps2']])
                tt(Xb[0][:, :, 0, :], ps1[:, :, 0:128], mS_bc, ALU.mult, [Bs['ps1'], B_const], [Bs['X0']])
                tt(Nn[0], ps2, mL_bc, ALU.mult, [Bs['ps2'], B_const], [Bs['N0']])
                tt(Xb[1][:, :, 1, :], Xb[0][:, :, 0, :], id_bc, ALU.add, [Bs['X0'], B_const], [Bs['X1']])
                tt(W1, ps1[:, :, 128:512], m3_bc, ALU.mult, [Bs['ps1'], B_const], [W1B])
                yield
                cur = 0
                for k in range(4):
                    nx = 1 - cur
                    lastk = (k == 3)
                    for h in range(2):
                        if lastk:
                            mm(psL[:, h, 128:256], Nn[cur][:, h, :], Xb[cur][:, h, 1, :], True, True, [Bs[f'N{cur}'], Bs[f'X{cur}']], [Bs['psL']])
                        elif k == 0:
                            mm(psL[:, h, 0:128], Nn[cur][:, h, :], Xb[cur][:, h, 0, :], True, True, [Bs[f'N{cur}'], Bs[f'X{cur}']], [Bs['psL']])
                            mm(psN[:, h, :], Xb[cur][:, h, 0, :], Nn[cur][:, h, :], True, True, [Bs[f'N{cur}'], Bs[f'X{cur}']], [Bs['psN']])
                        else:
                            mm(psL[:, h, :], Nn[cur][:, h, :], Xb[cur][:, h, :, :], True, True, [Bs[f'N{cur}'], Bs[f'X{cur}']], [Bs['psL']])
                            mm(psN[:, h, :], Xb[cur][:, h, 0, :], Nn[cur][:, h, :], True, True, [Bs[f'N{cur}'], Bs[f'X{cur}']], [Bs['psN']])
                    if lastk:
                        tt(TT, Xb[cur][:, :, 1, :], psL[:, :, 128:256], ALU.add, [Bs['psL'], Bs[f'X{cur}']], [TTB])
                    else:
                        cp('act', Xb[nx][:, :, 0, :], psL[:, :, 0:128], [Bs['psL']], [Bs[f'X{nx}']])
                        cp('dve', Nn[nx], psN, [Bs['psN']], [Bs[f'N{nx}']])
                        if k > 0:
                            tt(Xb[nx][:, :, 1, :], Xb[cur][:, :, 1, :], psL[:, :, 128:256], ALU.add, [Bs['psL'], Bs[f'X{cur}']], [Bs[f'X{nx}']])
                    cur = nx
                    yield

            def chain(n):
                cs = slice(n * 128, (n + 1) * 128)
                tb = n // 4
                g = (n // 4) % 2
                bk = BK[n % 2]
                bkB = Bs[f'BK{n % 2}']
                vt = Vt[g][:, n % 4, :]
                vtB = Bs[f'Vt{g}']
                W1, TT, W1B, TTB = W1s[n % 2], TTs[n % 2], Bs[f'W1{n % 2}'], Bs[f'TT{n % 2}']
                S.op('pe', lambda e: e.transpose(out=pT3[:, 0, :], in_=vT[:, cs], identity=ident), reads=[vTb[tb], B_const], writes=[Bs['pT']])
                S.op('pe', lambda e: e.transpose(out=pT3[:, 1, :], in_=BT[:, cs], identity=ident), reads=[BTb[tb], B_const], writes=[Bs['pT']])
                S.op('pe', lambda e: e.transpose(out=pT3[:, 2, :], in_=KT[:, cs], identity=ident), reads=[KTb[tb], B_const], writes=[Bs['pT']])
                cp('act', vt, pT3[:, 0, :], [Bs['pT']], [vtB])
                cp('act', bk, pT3[:, 1:3, :], [Bs['pT']], [bkB])
                yield
                for h in range(2):
                    hs = slice(64 * h, 64 * h + 64)
                    if n > 0:
                        mm(psX[:, hs], AR[:, n, 0, :], Hbz[:, h, :], True, False, [ARb[n], Bs['Hb']], [Bs['psX']])
                    mm(psX[:, hs], W1[:, h, 1, :], vt[:, hs], n == 0, True, [W1B, vtB], [Bs['psX']])
                cp('act', Xs, psX, [Bs['psX']], [Bs['Xs']])
                if n > 0:
                    ts(HP, Hs, PCt[:, n:n + 1], None, ALU.mult, None, [Bs['H'], Bt_['PCt']], [Bs['HP']])
                yield
                for h in range(2):
                    hs = slice(64 * h, 64 * h + 64)
                    mm(psU[:, hs], TT[:, h, :], Xs[:, hs], True, True, [TTB, Bs['Xs']], [Bs['psU']])
                cp('dve', Us, psU, [Bs['psU']], [Bs['Us']])
                yield
                for h in range(2):
                    hs = slice(64 * h, 64 * h + 64)
                    if n > 0:
                        mm(psY[:, hs], AR[:, n, 1, :], Hbz[:, h, :], True, False, [ARb[n], Bs['Hb']], [Bs['psY']])
                    mm(psY[:, hs], W1[:, h, 0, :], Us[:, hs], n == 0, False, [W1B, Bs['Us']], [Bs['psY']])
                    mm(psY[:, hs], W1[:, h, 2, :], vt[:, hs], False, True, [W1B, vtB], [Bs['psY']])
                mm(psH, bk[:, 0, :], Us, True, False, [bkB, Bs['Us']], [Bs['psH']])
                mm(psH, bk[:, 1, :], vt, False, True, [bkB, vtB], [Bs['psH']])
                cp('act', Yp[g][:, n % 4, :], psY, [Bs['psY']], [Bs[f'Yp{g}']])
                for dst_kind in ('bf', 'fp'):
                    for h in range(2):
                        hp = slice(64 * h, 64 * h + 64)
                        hs = slice(64 * h, 64 * h + 64)
                        dst_ = Hbz[hp, h, :] if dst_kind == 'bf' else Hs[hp, :]
                        dB_ = Bs['Hb'] if dst_kind == 'bf' else Bs['H']
                        if n > 0:
                            stt(dst_, psH[hp, hs], PCt[hp, n:n + 1], HP[hp, :], ALU.mult, ALU.add, [Bs['psH'], Bs['HP'], Bt_['PCt']], [dB_])
                        else:
                            ts(dst_, psH[hp, hs], PCt[hp, n:n + 1], None, ALU.mult, None, [Bs['psH'], Bt_['PCt']], [dB_])
                yield

            def post(tg):
                g = tg % 2
                y3 = Yp[g].rearrange("p j (h e) -> p (j h) e", h=2)
                yB = Bs[f'Yp{g}']
                s1, s2, mean, msq, rstd = (rstp[:, 8 * i:8 * i + 8] for i in range(5))
                v8 = lambda a: a.rearrange("p (j e) -> p j e", j=8)
                S.op('dve', lambda e: e.tensor_reduce(out=s1, in_=y3, axis=AX.X, op=ALU.add), reads=[yB], writes=[Bs['rstp']])
                act(sqp, Yp[g].rearrange("p j c -> p (j c)"), AF.Square, [yB], [Bs['sqp']])
                S.op('dve', lambda e: e.tensor_reduce(out=s2, in_=v8(sqp), axis=AX.X, op=ALU.add), reads=[Bs['sqp']], writes=[Bs['rstp']])
                yield
                ts(mean, s1, 1.0 / 64, None, ALU.mult, None, [Bs['rstp']], [Bs['rstp']])
                tt(msq, mean, mean, ALU.mult, [Bs['rstp']], [Bs['rstp']])
                stt(rstd, s2, 1.0 / 64, msq, ALU.mult, ALU.subtract, [Bs['rstp']], [Bs['rstp']])
                rsqrt_tiny(rstd, rstd, 1.0, RWKV_GN_EPS, [Bs['rstp']], [Bs['rstp']])
                yield
                tt(v8(ynp), y3, mean.unsqueeze(2).to_broadcast([128, 8, 64]), ALU.subtract, [yB, Bs['rstp']], [Bs['ynp']])
                tt(v8(ynp), v8(ynp), rstd.unsqueeze(2).to_broadcast([128, 8, 64]), ALU.mult, [Bs['ynp'], Bs['rstp']], [Bs['ynp']])
                yield
                y4 = ynp.rearrange("p (j c) -> p j c", j=4)
                tt(y4, y4, lnxw[:, p * 128:(p + 1) * 128].unsqueeze(1).to_broadcast([128, 4, 128]), ALU.mult, [Bs['ynp'], B_lw], [Bs['ynp']])
                tt(y4, y4, lnxb[:, p * 128:(p + 1) * 128].unsqueeze(1).to_broadcast([128, 4, 128]), ALU.add, [Bs['ynp'], B_lw], [Bs['ynp']])
                yield
                for j in range(4):
                    n = 4 * tg + j
                    cs = slice(n * 128, (n + 1) * 128)
                    mm(psB[:, 2 * j:2 * j + 2], rkrT[:, cs], sel, True, True, [rkb[tg], B_const], [Bs['psB']])
                    mm(psG[:, j * 128:(j + 1) * 128], L1g[:, cs], G2sb[:, p * 128:(p + 1) * 128], True, True, [L1B[tg], B_lw], [Bs['psG']])
                cp('act', sB, psB, [Bs['psB']], [Bs['sB']])
                yield
                tt(v8(bon), Vt[g].rearrange("p j (h e) -> p (j h) e", h=2), sB.unsqueeze(2).to_broadcast([128, 8, 64]), ALU.mult,
                   [Bs[f'Vt{g}'], Bs['sB']], [Bs['bon']])
                tt(ynp, ynp, bon, ALU.add, [Bs['ynp'], Bs['bon']], [Bs['ynp']])
                yield
                tt(yop.rearrange("p j c -> p (j c)"), ynp, psG, ALU.mult, [Bs['ynp'], Bs['psG']], [Bs['yop']])
                yield
                for j in range(4):
                    S.op('pe', lambda e, j=j: e.transpose(out=pT2[:, j, :], in_=yop[:, j, :], identity=ident), reads=[Bs['yop'], B_const], writes=[Bs['pT2']])
                cp('act', yT[:, p, tg * 512:(tg + 1) * 512], pT2.rearrange("p j t -> p (j t)"), [Bs['pT2']], [yTb[p][4 * tg + j] for j in range(4)])
                yield

            return local, chain, post

        assert A.peak <= WOUT_OFF, ("phase-3 buffers overlap the w_out prefetch region", A.peak, WOUT_OFF)
        wo_v = w_out.rearrange("(c p) n -> p c n", p=128)
        for c in range(8):
            dma('pool', wout[:, c, :], wo_v[:, c, :], [], [woutB])
        for i_ in range(2):
            S.op('pool', lambda e, i_=i_: e.memset(BKz[i_], 0.0), writes=[Bs[f'BKz{i_}']])
        NP = DBG.get('pairs', 4)
        for tb in range(4):
            prep_block(0, tb)
        scans = [make_scan(p) for p in range(NP)]

        def drain(g):
            for _ in g:
                pass
        S.op('pool', lambda e: e.memset(Hs, 0.0), writes=[Bs['H']])
        S.op('pool', lambda e: e.memset(Hbz, 0.0), writes=[Bs['Hb']])
        drain(scans[0][0](0))
        pend = []

        def prep_gen(p_, tb_):
            prep_block(p_, tb_)
            yield

        for p in range(NP):
            local, chain, post = scans[p]
            for n in range(NT):
                while len(pend) > 2:
                    drain(pend.pop(0))
                a = chain(n)
                if n + 1 < NT:
                    b = local(n + 1)
                elif p + 1 < NP:
                    b = scans[p + 1][0](0)
                else:
                    b = iter(())
                done_a = done_b = False
                while not (done_a and done_b):
                    if not done_a:
                        try:
                            next(a)
                        except StopIteration:
                            done_a = True
                    if not done_b:
                        try:
                            next(b)
                        except StopIteration:
                            done_b = True
                    if pend:
                        try:
                            next(pend[0])
                        except StopIteration:
                            pend.pop(0)
                if n % 4 == 3:
                    pend.append(post(n // 4))
                    if p + 1 < NP:
                        pend.append(prep_gen(p + 1, n // 4))
            if p + 1 == NP:
                while pend:
                    drain(pend.pop(0))
            if p + 1 < NP:
                S.op('dve', lambda e: e.memset(Hs, 0.0), writes=[Bs['H']])
                S.op('pool', lambda e: e.memset(Hbz, 0.0), writes=[Bs['Hb']])
        tap('yT', yT, [128, 8, T], [b for l in yTb for b in l])
        S.barrier()


    if 4 in phases:
        S.barrier()
        A.reset(NORM_END)
        xres = A.take([NT, D], F32)
        xresB = [Buf(f'xres{n}') for n in range(NT)]
        P4 = A.mark()
        if 3 not in phases:
            wo_v = w_out.rearrange("(c p) n -> p c n", p=128)
            for c in range(8):
                dma('pool', wout[:, c, :], wo_v[:, c, :], [], [woutB])
        dma('sp', gtab, bct_d[:, 1024:2048], [], [B_gtab])
        for n in range(NT):
            dma('sp', xst[n % 3], xv[n], [], [xstB[n % 3]])
            pb = 2 * (n % 2)
            for half in range(2):
                for c in range(8):
                    mm(bank[pb + half], yT[:, c, n * 128:(n + 1) * 128], wout[:, c, half * 512:(half + 1) * 512], c == 0, c == 7,
                       [yTb[c][n], woutB], [bankB[pb + half]])
            tt(xres[:, n, :], pp[n % 2][:], xst[n % 3], ALU.add, [bankB[pb], bankB[pb + 1], xstB[n % 3]], [xresB[n]])
            norm_stats(n, xres[:, n, :], xresB[n], 1)
        norm_rstd(1)
        for n in range(NT):
            norm_apply(n, xres[:, n, :], xresB[n], 1, 4 + n % 2)
        tap('xres', xres, [128, NT, D], xresB)

    if 5 in phases:
        S.barrier()
        A.reset(P4)
        hid = yT[:, 0:6, :]
        hidB = [Buf(f'hid{i}') for i in range(4)]
        wgu = [A.take([2, 8, 256], BF16) for _ in range(2)]
        wguB = [Buf(f'wgu{i}') for i in range(2)]
        wd = A.take([6, D], BF16)
        wdB = Buf('wd')
        gs = [A.take([514], F32) for _ in range(2)]
        gsB = [Buf(f'gs{i}') for i in range(2)]
        acc = [A.take([512], F32) for _ in range(2)]
        accB = [Buf(f'acc{i}') for i in range(2)]
        sl = [A.take([512], F32) for _ in range(2)]
        slB = [Buf(f'sl{i}') for i in range(2)]
        ost = [xst[0], xst[1]]
        ostB = [xstB[0], xstB[1]]
        dma('sp', gtab, bct_d[:, 2048:3072], [], [B_gtab])
        wg_v = wg_d.rearrange("(c p) n -> p c n", p=128)
        wu_v = wu_d.rearrange("(c p) n -> p c n", p=128)
        wd_v = wd_d.rearrange("(m p) n -> p m n", p=128)
        quarters = [(0, 6), (6, 6), (12, 5), (17, 5)]

        def load_wgu(m):
            wb_ = (m // 2) % 2
            dma('pool', wgu[wb_][:, 0, :, :], wg_v[:, :, m * 128:(m + 2) * 128], [], [wguB[wb_]])
            dma('pool', wgu[wb_][:, 1, :, :], wu_v[:, :, m * 128:(m + 2) * 128], [], [wguB[wb_]])
        load_wgu(0)
        it = 0
        for qi, (m0, nq) in enumerate(quarters):
            dma('pool', wd[:, 0:nq, :], wd_v[:, m0:m0 + nq, :], [], [wdB])
            for ml in range(nq):
                m = m0 + ml
                wb = (m // 2) % 2
                if m % 2 == 0 and m + 2 < NFF:
                    load_wgu(m + 2)
                mc = slice((m % 2) * 128, (m % 2) * 128 + 128)
                vf = V_FFN + 4 * m
                cw = lambda j: vecs[:, vf + j:vf + j + 1]
                for blk in range(4):
                    g_, gB = gs[blk % 2], gsB[blk % 2]
                    a_, aB = acc[it % 2], accB[it % 2]
                    s_, sB_ = sl[it % 2], slB[it % 2]
                    pg, pu = 2 * (it % 4), 2 * (it % 4) + 1
                    it += 1
                    rd = [hTb[4 * blk + i] for i in range(4)] + [wguB[wb]]
                    for c in range(8):
                        mm(bank[pg], wgu[wb][:, 0, c, mc], hT[:, c, 1 + blk * 512:1 + (blk + 1) * 512], c == 0, c == 7, rd, [bankB[pg]])
                    for c in range(8):
                        mm(bank[pu], wgu[wb][:, 1, c, mc], hT[:, c, 1 + blk * 512:1 + (blk + 1) * 512], c == 0, c == 7, rd, [bankB[pu]])
                    if blk == 0:
                        S.op('pool', lambda e, g_=g_: e.memset(g_[:, 0:2], 0.0), writes=[gB])
                    else:
                        gp = gs[(blk - 1) % 2]
                        S.op('pool', lambda e, g_=g_, gp=gp: e.tensor_copy(out=g_[:, 0:2], in_=gp[:, 512:514]), reads=[gsB[(blk - 1) % 2]], writes=[gB])
                    act(g_[:, 2:514], bank[pg], AF.Copy, [bankB[pg]], [gB])
                    act(a_, bank[pg], AF.Identity, [bankB[pg], B_const], [aB], bias=cw(3), scale=cw(2))
                    stt(a_, g_[:, 1:513], cw(1), a_, ALU.mult, ALU.add, [gB, aB, B_const], [aB])
                    stt(a_, g_[:, 0:512], cw(0), a_, ALU.mult, ALU.add, [gB, aB, B_const], [aB])
                    act(s_, a_, AF.Silu, [aB], [sB_])
                    tt(hid[:, ml, blk * 512:(blk + 1) * 512], s_, bank[pu], ALU.mult, [sB_, bankB[pu]], [hidB[blk]])
            last = qi == len(quarters) - 1
            for n in range(NT):
                pb = 2 * (n % 4)
                for half in range(2):
                    for ml in range(nq):
                        mm(bank[pb + half], hid[:, ml, n * 128:(n + 1) * 128], wd[:, ml, half * 512:(half + 1) * 512], ml == 0, ml == nq - 1,
                           [hidB[n // 4], wdB], [bankB[pb + half]])
                tt(xres[:, n, :], pp[n % 4][:], xres[:, n, :], ALU.add, [bankB[pb], bankB[pb + 1], xresB[n]], [xresB[n]])
                if last:
                    ssn = ss_all[:, 2, n:n + 1]
                    rsn = rstd_all[:, 2, n:n + 1]
                    sB2 = statB[n % 4]
                    act(sqj, xres[:, n, :], AF.Square, [xresB[n]], [sqjB, sB2], accum=ssn)
                    rsqrt_tiny(rsn, ssn, 1.0 / D, NORM_EPS, [sB2], [sB2])
                    stt(ost[n % 2], xres[:, n, :], rsn, gtab, ALU.mult, ALU.mult, [xresB[n], sB2, B_gtab], [ostB[n % 2]])
                    dma('sp', ov[n], ost[n % 2], [ostB[n % 2]], [])

    S.barrier(('sp',))
    S.emit(st)
    st.close()
    return nc, tap_out, S, A


def _chunkcols(v):
    v = np.asarray(v, np.float32).reshape(-1, 128)
    return np.ascontiguousarray(v.T)


def prep_shared(inp):
    f = lambda k: np.ascontiguousarray(np.asarray(inp[k], np.float32)[0])
    vecs = np.zeros((128, NV), np.float32)
    vecs[:, V_MUW:V_MUW + 8] = _chunkcols(f("rwkv_mu_w"))
    vecs[:, V_MUA:V_MUA + 8] = _chunkcols(f("rwkv_mu_a"))
    vecs[:, V_MUG:V_MUG + 8] = _chunkcols(f("rwkv_mu_g"))
    names = ["rwkv_mu_r", "rwkv_mu_k", "rwkv_mu_v", "rwkv_w0", "rwkv_a0", "rwkv_k_k", "rwkv_k_a", "rwkv_r_k"]
    for j, nm in enumerate(names):
        cc = _chunkcols(f(nm).reshape(-1))
        for p in range(4):
            vecs[:, V_PAIR + 8 * p + j] = cc[:, p]
    cw = f("ffn_conv_w").reshape(3, DFF)
    cbias = f("ffn_conv_b")
    for j in range(3):
        cc = _chunkcols(cw[j])
        for m in range(NFF):
            vecs[:, V_FFN + 4 * m + j] = cc[:, m]
    cc = _chunkcols(cbias)
    for m in range(NFF):
        vecs[:, V_FFN + 4 * m + 3] = cc[:, m]
    row = np.concatenate([f("norm_mix_g"), f("norm_ffn_g"), np.asarray(inp["norm_final_g"], np.float32),
                          f("rwkv_lnx_w"), f("rwkv_lnx_b"), f("ret_gn_w")])
    bct = np.ascontiguousarray(np.broadcast_to(row[None, :], (128, row.shape[0])))
    cf, cb = make_consts()
    shared = {
        "w_in": f("w_in"), "w_out": f("w_out"), "ffn_w_gate": f("ffn_w_gate"), "ffn_w_up": f("ffn_w_up"),
        "ffn_w_down": f("ffn_w_down"), "rwkv_w1": f("rwkv_w1"), "rwkv_a1": f("rwkv_a1"), "rwkv_g1": f("rwkv_g1"),
        "rwkv_w2": f("rwkv_w2"), "rwkv_a2": f("rwkv_a2"), "rwkv_g2": f("rwkv_g2"),
        "vecs": vecs, "bct": bct, "cf": cf, "cb": cb,
    }
    return shared


_PROG = None


def kernel(**inputs):
    global _PROG
    if _PROG is None:
        _PROG = build_program()[0]
    shared = prep_shared(inputs)
    xs = np.asarray(inputs["x"], np.float32)
    in_maps = [dict(shared, x=np.ascontiguousarray(xs[b])) for b in range(8)]
    res = run_bass_kernel_spmd(_PROG, in_maps, core_ids=list(range(8)))
    return np.stack([np.asarray(r["out"], np.float32) for r in res.results], axis=0)
```
